# Optimizing a Trainium2 kernel written in Bass

```python
import math
import jax, jax.numpy as jnp
from jax import lax
import numpy as np

D_MODEL = 1024
BATCH = 4
SEQ = 8192
DEPTH = 4
DEC_BATCH = 32
DEC_SEQ = 2048
PAST_LEN = 128

EPS = 1e-6
D_FF = 2816
CONV_WIDTH = 4
LRU_WIDTH = D_MODEL
LRU_BLOCKS = 16
LRU_BLOCK = LRU_WIDTH // LRU_BLOCKS
LRU_C = 8.0
S5_WIDTH = D_MODEL // 2
S5_GROUP = 16
S5_GROUPS = S5_WIDTH // S5_GROUP
S5_STATE = 64
EVEN_IN = 2 * LRU_WIDTH + S5_WIDTH
EVEN_MIX = LRU_WIDTH + S5_WIDTH
SSD_INNER = 2 * D_MODEL
SSD_HEADDIM = 64
SSD_HEADS = SSD_INNER // SSD_HEADDIM
SSD_GROUPS = 4
SSD_HPG = SSD_HEADS // SSD_GROUPS
SSD_STATE = 128
SSD_CHUNK = 128
SSD_CONV_DIM = SSD_INNER + 2 * SSD_GROUPS * SSD_STATE
ODD_IN = SSD_INNER + SSD_CONV_DIM + 2 * SSD_HEADS
N_EVEN = (DEPTH + 1) // 2
N_ODD = DEPTH // 2
F32 = jnp.float32

kernel_name = 'hybrid_bidir_rglru_s5_ssd_encoder'


def rmsnorm(x, w):
    xf = x.astype(F32)
    y = xf * lax.rsqrt(jnp.mean(xf * xf, axis=-1, keepdims=True) + EPS)
    return (y * w.astype(F32)).astype(x.dtype)


def swiglu_ffn(x, w_gate, w_up, w_down):
    return (jax.nn.silu(x @ w_gate) * (x @ w_up)) @ w_down


def centred_depthwise_conv(x, w, b):
    k_len = w.shape[0]
    seq = x.shape[1]
    left = k_len // 2
    xp = jnp.pad(x, ((0, 0), (left, k_len - 1 - left), (0, 0)))
    y = b
    for k in range(k_len):
        y = y + xp[:, k:k + seq] * w[k]
    return y


def _lin_combine(l, r):
    a_l, b_l = l
    a_r, b_r = r
    return a_l * a_r, a_r * b_l + b_r


def linear_recurrence(a, b, reverse):
    _, h = lax.associative_scan(_lin_combine, (a, b), reverse=reverse, axis=1)
    return h


def rglru_direction(x, w_a, b_a, w_x, b_x, lam, reverse):
    bsz, seq, _ = x.shape
    xb = x.reshape(bsz, seq, LRU_BLOCKS, LRU_BLOCK)
    r = jax.nn.sigmoid(jnp.einsum('blhi,hij->blhj', xb, w_a).reshape(bsz, seq, LRU_WIDTH) + b_a)
    i = jax.nn.sigmoid(jnp.einsum('blhi,hij->blhj', xb, w_x).reshape(bsz, seq, LRU_WIDTH) + b_x)
    log_a = -LRU_C * r * jax.nn.softplus(-lam)
    gated_x = jnp.sqrt(-jnp.expm1(2.0 * log_a)) * (i * x)
    return linear_recurrence(jnp.exp(log_a), gated_x, reverse)


def _complex_combine(l, r):
    alr, ali, blr, bli = l
    arr, ari, brr, bri = r
    return (alr * arr - ali * ari, alr * ari + ali * arr,
            arr * blr - ari * bli + brr, arr * bli + ari * blr + bri)


def s5_direction(u, lam_re, lam_im, log_dt, b_re, b_im, c_re, c_im, reverse):
    seq = u.shape[1]
    dt = jnp.exp(log_dt)[:, None]
    mag = jnp.exp(lam_re * dt)
    ab_re = mag * jnp.cos(lam_im * dt)
    ab_im = mag * jnp.sin(lam_im * dt)
    den = lam_re * lam_re + lam_im * lam_im
    coef_re = ((ab_re - 1.0) * lam_re + ab_im * lam_im) / den
    coef_im = (ab_im * lam_re - (ab_re - 1.0) * lam_im) / den
    bb_re = coef_re[..., None] * b_re - coef_im[..., None] * b_im
    bb_im = coef_re[..., None] * b_im + coef_im[..., None] * b_re
    bu_re = jnp.einsum('blgc,gnc->blgn', u, bb_re)
    bu_im = jnp.einsum('blgc,gnc->blgn', u, bb_im)
    shape_a = (1, seq) + ab_re.shape
    a_re = jnp.broadcast_to(ab_re, shape_a)
    a_im = jnp.broadcast_to(ab_im, shape_a)
    _, _, h_re, h_im = lax.associative_scan(
        _complex_combine, (a_re, a_im, bu_re, bu_im), reverse=reverse, axis=1)
    return jnp.einsum('blgn,gcn->blgc', h_re, c_re) - jnp.einsum('blgn,gcn->blgc', h_im, c_im)


def ssd_scan(x, dt, a, bm, cm):
    bsz, seq, g, r, pdim = x.shape
    n = bm.shape[-1]
    q = SSD_CHUNK
    c = seq // q
    xdt = (x * dt[..., None]).reshape(bsz, c, q, g, r, pdim)
    a_dt = (dt * a).reshape(bsz, c, q, g, r).transpose(0, 3, 4, 1, 2)
    a_cs = jnp.cumsum(a_dt, axis=-1)
    bc = bm.reshape(bsz, c, q, g, n)
    cc = cm.reshape(bsz, c, q, g, n)
    causal = jnp.tril(jnp.ones((q, q), dtype=bool))
    decay_in = jnp.exp(jnp.where(causal, a_cs[..., :, None] - a_cs[..., None, :], -jnp.inf))
    scores = jnp.einsum('bclgn,bcsgn->bgcls', cc, bc)
    y_diag = jnp.einsum('bgrcls,bcsgrp->bclgrp', scores[:, :, None] * decay_in, xdt)
    decay_to_end = jnp.exp(a_cs[..., -1:] - a_cs)
    chunk_states = jnp.einsum('bcsgn,bgrcs,bcsgrp->bcgrpn', bc, decay_to_end, xdt)
    chunk_states = jnp.concatenate([jnp.zeros_like(chunk_states[:, :1]), chunk_states], axis=1)
    a_tot = jnp.pad(a_cs[..., -1], ((0, 0), (0, 0), (0, 0), (1, 0)))
    a_tot_cs = jnp.cumsum(a_tot, axis=-1)
    causal_c = jnp.tril(jnp.ones((c + 1, c + 1), dtype=bool))
    decay_chunk = jnp.exp(jnp.where(causal_c, a_tot_cs[..., :, None] - a_tot_cs[..., None, :], -jnp.inf))
    states_in = jnp.einsum('bgrzc,bcgrpn->bzgrpn', decay_chunk, chunk_states)[:, :-1]
    y_off = jnp.einsum('bclgn,bcgrpn,bgrcl->bclgrp', cc, states_in, jnp.exp(a_cs))
    return (y_diag + y_off).reshape(bsz, seq, g, r, pdim)


def even_mixer(xn, j, p):
    bsz, seq, _ = xn.shape
    h = xn @ p['ev_w_in'][j]
    gate, xr, u = jnp.split(h, [LRU_WIDTH, 2 * LRU_WIDTH], axis=-1)
    xr = centred_depthwise_conv(xr, p['lru_conv_w'][j], p['lru_conv_b'][j]).astype(F32)
    wa = p['lru_w_a'][j].astype(F32)
    ba = p['lru_b_a'][j].astype(F32)
    wx = p['lru_w_x'][j].astype(F32)
    bx = p['lru_b_x'][j].astype(F32)
    lam = p['lru_lam'][j].astype(F32)
    h_fwd = rglru_direction(xr, wa[0], ba[0], wx[0], bx[0], lam[0], False)
    h_bwd = rglru_direction(xr, wa[1], ba[1], wx[1], bx[1], lam[1], True)
    y_a = jax.nn.gelu(gate.astype(F32)) * (h_fwd + h_bwd)
    uf = u.astype(F32)
    ug = uf.reshape(bsz, seq, S5_GROUPS, S5_GROUP)
    lre = p['s5_lam_re'][j].astype(F32)
    lim = p['s5_lam_im'][j].astype(F32)
    ldt = p['s5_log_dt'][j].astype(F32)
    bre = p['s5_b_re'][j].astype(F32)
    bim = p['s5_b_im'][j].astype(F32)
    cre = p['s5_c_re'][j].astype(F32)
    cim = p['s5_c_im'][j].astype(F32)
    s_fwd = s5_direction(ug, lre[0], lim[0], ldt[0], bre[0], bim[0], cre[0], cim[0], False)
    s_bwd = s5_direction(ug, lre[1], lim[1], ldt[1], bre[1], bim[1], cre[1], cim[1], True)
    y_s = (s_fwd + s_bwd).reshape(bsz, seq, S5_WIDTH) + p['s5_d'][j].astype(F32) * uf
    g = jax.nn.gelu(y_s)
    y_b = g * jax.nn.sigmoid(g @ p['s5_glu_w'][j].astype(F32) + p['s5_glu_b'][j].astype(F32))
    y = jnp.concatenate([y_a, y_b], axis=-1).astype(xn.dtype)
    return y @ p['ev_w_out'][j]


def odd_mixer(xn, j, p):
    bsz, seq, _ = xn.shape
    h = xn @ p['od_w_in'][j]
    z, xbc, dt_raw = jnp.split(h, [SSD_INNER, SSD_INNER + SSD_CONV_DIM], axis=-1)
    xbc = jax.nn.silu(centred_depthwise_conv(xbc, p['ssd_conv_w'][j], p['ssd_conv_b'][j])).astype(F32)
    xs, bm, cm = jnp.split(xbc, [SSD_INNER, SSD_INNER + SSD_GROUPS * SSD_STATE], axis=-1)
    xs = xs.reshape(bsz, seq, SSD_GROUPS, SSD_HPG, SSD_HEADDIM)
    bm = bm.reshape(bsz, seq, SSD_GROUPS, SSD_STATE)
    cm = cm.reshape(bsz, seq, SSD_GROUPS, SSD_STATE)
    dt_all = jax.nn.softplus(dt_raw.astype(F32).reshape(bsz, seq, 2, SSD_HEADS)
                             + p['ssd_dt_bias'][j].astype(F32))
    dt_all = dt_all.reshape(bsz, seq, 2, SSD_GROUPS, SSD_HPG)
    a = -jnp.exp(p['ssd_a_log'][j].astype(F32)).reshape(2, SSD_GROUPS, SSD_HPG)
    y_fwd = ssd_scan(xs, dt_all[:, :, 0], a[0], bm, cm)
    flip = lambda t: jnp.flip(t, axis=1)
    y_bwd = flip(ssd_scan(flip(xs), flip(dt_all[:, :, 1]), a[1], flip(bm), flip(cm)))
    d_skip = p['ssd_d'][j].astype(F32).reshape(SSD_GROUPS, SSD_HPG)[..., None]
    y = (y_fwd + y_bwd + d_skip * xs).reshape(bsz, seq, SSD_INNER) * jax.nn.silu(z.astype(F32))
    yg = y.reshape(bsz, seq, SSD_GROUPS, SSD_INNER // SSD_GROUPS)
    yg = yg * lax.rsqrt(jnp.mean(yg * yg, axis=-1, keepdims=True) + EPS)
    y = yg.reshape(bsz, seq, SSD_INNER) * p['ssd_norm'][j].astype(F32)
    return y.astype(xn.dtype) @ p['od_w_out'][j]


def trunk(x, p):
    for layer in range(DEPTH):
        j = layer // 2
        x = x + 0.5 * swiglu_ffn(rmsnorm(x, p['ffn1_norm'][layer]), p['ffn1_w_gate'][layer],
                                 p['ffn1_w_up'][layer], p['ffn1_w_down'][layer])
        xn = rmsnorm(x, p['mix_norm'][layer])
        if layer % 2 == 0:
            x = x + even_mixer(xn, j, p)
        else:
            x = x + odd_mixer(xn, j, p)
        x = x + 0.5 * swiglu_ffn(rmsnorm(x, p['ffn2_norm'][layer]), p['ffn2_w_gate'][layer],
                                 p['ffn2_w_up'][layer], p['ffn2_w_down'][layer])
    return rmsnorm(x, p['final_norm'])


def setup_inputs(seed: int = 0) -> dict:
    key = jax.random.key(seed)
    ks = iter(jax.random.split(key, 64))

    def normal(shape, scale):
        return jax.random.normal(next(ks), shape, F32) * scale

    def gain(shape):
        return 1.0 + 0.02 * jax.random.normal(next(ks), shape, F32)

    def uniform(shape, lo, hi):
        return jax.random.uniform(next(ks), shape, F32, lo, hi)

    inp = {}
    inp['x_prompt'] = normal((BATCH, SEQ, D_MODEL), 1.0)
    inp['x_sample'] = normal((DEC_BATCH, DEC_SEQ, D_MODEL), 1.0)
    inp['ffn1_norm'] = gain((DEPTH, D_MODEL))
    inp['ffn1_w_gate'] = normal((DEPTH, D_MODEL, D_FF), D_MODEL ** -0.5)
    inp['ffn1_w_up'] = normal((DEPTH, D_MODEL, D_FF), D_MODEL ** -0.5)
    inp['ffn1_w_down'] = normal((DEPTH, D_FF, D_MODEL), D_FF ** -0.5)
    inp['mix_norm'] = gain((DEPTH, D_MODEL))
    inp['ffn2_norm'] = gain((DEPTH, D_MODEL))
    inp['ffn2_w_gate'] = normal((DEPTH, D_MODEL, D_FF), D_MODEL ** -0.5)
    inp['ffn2_w_up'] = normal((DEPTH, D_MODEL, D_FF), D_MODEL ** -0.5)
    inp['ffn2_w_down'] = normal((DEPTH, D_FF, D_MODEL), D_FF ** -0.5)
    inp['ev_w_in'] = normal((N_EVEN, D_MODEL, EVEN_IN), D_MODEL ** -0.5)
    inp['lru_conv_w'] = normal((N_EVEN, CONV_WIDTH, LRU_WIDTH), CONV_WIDTH ** -0.5)
    inp['lru_conv_b'] = normal((N_EVEN, LRU_WIDTH), 0.02)
    inp['lru_w_a'] = normal((N_EVEN, 2, LRU_BLOCKS, LRU_BLOCK, LRU_BLOCK), LRU_BLOCK ** -0.5)
    inp['lru_b_a'] = normal((N_EVEN, 2, LRU_WIDTH), 0.02)
    inp['lru_w_x'] = normal((N_EVEN, 2, LRU_BLOCKS, LRU_BLOCK, LRU_BLOCK), LRU_BLOCK ** -0.5)
    inp['lru_b_x'] = normal((N_EVEN, 2, LRU_WIDTH), 0.02)
    a_pow = uniform((N_EVEN, 2, LRU_WIDTH), 0.9, 0.999) ** (1.0 / LRU_C)
    inp['lru_lam'] = jnp.log(a_pow) - jnp.log1p(-a_pow)
    inp['s5_lam_re'] = -0.5 + normal((N_EVEN, 2, S5_GROUPS, S5_STATE), 0.01)
    inp['s5_lam_im'] = math.pi * jnp.arange(S5_STATE, dtype=F32) + normal((N_EVEN, 2, S5_GROUPS, S5_STATE), 0.01)
    inp['s5_log_dt'] = uniform((N_EVEN, 2, S5_GROUPS), math.log(1e-3), math.log(1e-1))
    inp['s5_b_re'] = normal((N_EVEN, 2, S5_GROUPS, S5_STATE, S5_GROUP), (2 * S5_GROUP) ** -0.5)
    inp['s5_b_im'] = normal((N_EVEN, 2, S5_GROUPS, S5_STATE, S5_GROUP), (2 * S5_GROUP) ** -0.5)
    inp['s5_c_re'] = normal((N_EVEN, 2, S5_GROUPS, S5_GROUP, S5_STATE), S5_STATE ** -0.5)
    inp['s5_c_im'] = normal((N_EVEN, 2, S5_GROUPS, S5_GROUP, S5_STATE), S5_STATE ** -0.5)
    inp['s5_d'] = normal((N_EVEN, S5_WIDTH), 1.0)
    inp['s5_glu_w'] = normal((N_EVEN, S5_WIDTH, S5_WIDTH), S5_WIDTH ** -0.5)
    inp['s5_glu_b'] = normal((N_EVEN, S5_WIDTH), 0.02)
    inp['ev_w_out'] = normal((N_EVEN, EVEN_MIX, D_MODEL), EVEN_MIX ** -0.5)
    inp['od_w_in'] = normal((N_ODD, D_MODEL, ODD_IN), D_MODEL ** -0.5)
    inp['ssd_conv_w'] = normal((N_ODD, CONV_WIDTH, SSD_CONV_DIM), CONV_WIDTH ** -0.5)
    inp['ssd_conv_b'] = normal((N_ODD, SSD_CONV_DIM), 0.02)
    dt0 = jnp.exp(uniform((N_ODD, 2, SSD_HEADS), math.log(1e-3), math.log(1e-1)))
    inp['ssd_dt_bias'] = dt0 + jnp.log(-jnp.expm1(-dt0))
    inp['ssd_a_log'] = jnp.log(uniform((N_ODD, 2, SSD_HEADS), 1.0, 16.0))
    inp['ssd_d'] = gain((N_ODD, SSD_HEADS))
    inp['ssd_norm'] = gain((N_ODD, SSD_INNER))
    inp['od_w_out'] = normal((N_ODD, SSD_INNER, D_MODEL), SSD_INNER ** -0.5)
    inp['final_norm'] = gain((D_MODEL,))
    return inp


def reference(x_prompt, x_sample, ffn1_norm, ffn1_w_gate, ffn1_w_up, ffn1_w_down, mix_norm,
              ffn2_norm, ffn2_w_gate, ffn2_w_up, ffn2_w_down, ev_w_in, lru_conv_w, lru_conv_b,
              lru_w_a, lru_b_a, lru_w_x, lru_b_x, lru_lam, s5_lam_re, s5_lam_im, s5_log_dt,
              s5_b_re, s5_b_im, s5_c_re, s5_c_im, s5_d, s5_glu_w, s5_glu_b, ev_w_out,
              od_w_in, ssd_conv_w, ssd_conv_b, ssd_dt_bias, ssd_a_log, ssd_d, ssd_norm, od_w_out,
              final_norm):
    p = dict(ffn1_norm=ffn1_norm, ffn1_w_gate=ffn1_w_gate, ffn1_w_up=ffn1_w_up,
             ffn1_w_down=ffn1_w_down, mix_norm=mix_norm, ffn2_norm=ffn2_norm,
             ffn2_w_gate=ffn2_w_gate, ffn2_w_up=ffn2_w_up, ffn2_w_down=ffn2_w_down,
             ev_w_in=ev_w_in, lru_conv_w=lru_conv_w, lru_conv_b=lru_conv_b, lru_w_a=lru_w_a,
             lru_b_a=lru_b_a, lru_w_x=lru_w_x, lru_b_x=lru_b_x, lru_lam=lru_lam,
             s5_lam_re=s5_lam_re, s5_lam_im=s5_lam_im, s5_log_dt=s5_log_dt, s5_b_re=s5_b_re,
             s5_b_im=s5_b_im, s5_c_re=s5_c_re, s5_c_im=s5_c_im, s5_d=s5_d, s5_glu_w=s5_glu_w,
             s5_glu_b=s5_glu_b, ev_w_out=ev_w_out, od_w_in=od_w_in, ssd_conv_w=ssd_conv_w,
             ssd_conv_b=ssd_conv_b, ssd_dt_bias=ssd_dt_bias, ssd_a_log=ssd_a_log, ssd_d=ssd_d,
             ssd_norm=ssd_norm, od_w_out=od_w_out, final_norm=final_norm)
    y_prompt = trunk(x_prompt, p)
    y_sample = trunk(x_sample, p)
    return (y_prompt, y_sample)
```

```python
import contextlib
import numpy as np
import concourse.bass as bass
import concourse.mybir as mybir
from concourse.bass_utils import run_bass_kernel_spmd

F32 = mybir.dt.float32
BF16 = mybir.dt.bfloat16
ALU = mybir.AluOpType
AF = mybir.ActivationFunctionType

D = 1024
DFF = 2816
NFC = DFF // 128
EPS = 1e-6


class Trk:
    __slots__ = ("w", "r", "dsem", "dcnt", "name", "dq")

    def __init__(self, name=""):
        self.w = {}
        self.r = {}
        self.dsem = None
        self.dcnt = 0
        self.name = name


class Eng:
    def __init__(self, name, eng, sem):
        self.name = name
        self.eng = eng
        self.sem = sem
        self.cnt = 0
        self.waited = {}


class KB:
    def __init__(self, nc, es):
        self.nc = nc
        self.es = es
        self.nsem = 0
        self.pe = Eng("pe", nc.tensor, self.sem("pe"))
        self.dve = Eng("dve", nc.vector, self.sem("dve"))
        self.act = Eng("act", nc.scalar, self.sem("act"))
        self.pool = Eng("pool", nc.gpsimd, self.sem("pool"))
        self.sp = Eng("sp", nc.sync, self.sem("sp"))
        self.out_tokens = []
        self.dtrks = []
        self.free_dsems = {}

    def sem(self, name):
        self.nsem += 1
        s = self.es.enter_context(self.nc.semaphore(f"s{self.nsem}_{name}"))
        return s

    def sb(self, name, shape, dt):
        return self.es.enter_context(self.nc.sbuf_tensor(name, shape, dt))

    def ps(self, name, shape, dt=F32):
        return self.es.enter_context(self.nc.psum_tensor(name, shape, dt))

    def _waits(self, E, reads, writes):
        need = {}
        for t in reads:
            for s, v in t.w.items():
                if need.get(s, 0) < v:
                    need[s] = v
        for t in writes:
            for s, v in t.w.items():
                if s is E.sem:
                    continue
                if need.get(s, 0) < v:
                    need[s] = v
            for s, v in t.r.items():
                if s is E.sem:
                    continue
                if need.get(s, 0) < v:
                    need[s] = v
        for s, v in need.items():
            if s is E.sem and v > E.cnt:
                continue
            if E.waited.get(id(s), 0) < v:
                E.eng.wait_ge(s, v)
                E.waited[id(s)] = v

    def op(self, E, make, reads=(), writes=(), pub=True):
        self._waits(E, reads, writes)
        inst = make()
        tokv = E.cnt + 1
        if pub:
            inst.then_inc(E.sem, 1)
            E.cnt += 1
        for t in reads:
            if t.r.get(E.sem, 0) < tokv:
                t.r[E.sem] = tokv
        for t in writes:
            t.w = {E.sem: tokv}
            t.r = {}
        return inst

    def dma(self, Q, out, in_, reads=(), writes=(), st=None, is_out=False):
        if st is None:
            st = writes[0] if writes else reads[0]
        if st.dsem is None:
            fl_ = self.free_dsems.setdefault(Q.name, [])
            if fl_:
                st.dsem, st.dcnt = fl_.pop()
            else:
                st.dsem = self.sem("d" + Q.name)
            st.dq = Q.name
            self.dtrks.append(st)
        assert st.dq == Q.name, "one DMA queue per tracker semaphore"
        self._waits(Q, reads, writes)
        inst = Q.eng.dma_start(out=out, in_=in_)
        inst.then_inc(st.dsem, 16)
        st.dcnt += 16
        for t in reads:
            if t.r.get(st.dsem, 0) < st.dcnt:
                t.r[st.dsem] = st.dcnt
        for t in writes:
            t.w = {st.dsem: st.dcnt}
            t.r = {}
        if is_out:
            self.out_tokens.append((st.dsem, st.dcnt))
        return inst

    def barrier(self):
        engs = [self.pe, self.dve, self.act, self.pool, self.sp]
        for E in engs:
            for O in engs:
                if O is E or O.cnt == 0:
                    continue
                if E.waited.get(id(O.sem), 0) < O.cnt:
                    E.eng.wait_ge(O.sem, O.cnt)
                    E.waited[id(O.sem)] = O.cnt
            for t in self.dtrks:
                if t.dcnt and E.waited.get(id(t.dsem), 0) < t.dcnt:
                    E.eng.wait_ge(t.dsem, t.dcnt)
                    E.waited[id(t.dsem)] = t.dcnt

    def release_dsems(self):
        for t in self.dtrks:
            self.free_dsems.setdefault(t.dq, []).append((t.dsem, t.dcnt))
            t.dsem = None
            t.w = {}
            t.r = {}
        self.dtrks = []

    def finish(self):
        need = {}
        for s, v in self.out_tokens:
            if need.get(id(s), (None, 0))[1] < v:
                need[id(s)] = (s, v)
        for s, v in need.values():
            self.sp.eng.wait_ge(s, v)


def build_program(NTOK, SEG, layers, T=256, phases=None):
    nc = bass.Bass("TRN2", target_bir_lowering=False)
    NSEG = NTOK // SEG
    PW = min(512, SEG)
    NPQ = SEG // PW

    def inc(name):
        return phases is None or name in phases

    def din(name, shape):
        return nc.dram_tensor(name, list(shape), F32, kind="ExternalInput").ap()

    def dscr(name, shape, dt=F32):
        return nc.dram_tensor(name, list(shape), dt, kind="Internal").ap()

    xT = din("xT", [D, NTOK])
    flags_d = din("flags", [128, 8])
    ident_d = din("ident", [128, 128])
    ffn_norm = [din("ffn1_norm", [4, D]), din("ffn2_norm", [4, D])]
    ffn_wg = [din("ffn1_w_gate", [4, D, DFF]), din("ffn2_w_gate", [4, D, DFF])]
    ffn_wu = [din("ffn1_w_up", [4, D, DFF]), din("ffn2_w_up", [4, D, DFF])]
    ffn_wd = [din("ffn1_w_down", [4, DFF, D]), din("ffn2_w_down", [4, DFF, D])]
    mix_norm = din("mix_norm", [4, D])
    final_norm = din("final_norm", [D])
    ev_w_in = din("ev_w_in", [2, D, 2560])
    lru_conv_w = din("lru_conv_w", [2, 4, 1024])
    lru_conv_b = din("lru_conv_b", [2, 1024])
    lru_w_a = din("lru_w_a", [2, 2, 16, 64, 64])
    lru_b_a = din("lru_b_a", [2, 2, 1024])
    lru_w_x = din("lru_w_x", [2, 2, 16, 64, 64])
    lru_b_x = din("lru_b_x", [2, 2, 1024])
    lru_lam = din("lru_lam", [2, 2, 1024])
    s5_lam_re = din("s5_lam_re", [2, 2, 32, 64])
    s5_lam_im = din("s5_lam_im", [2, 2, 32, 64])
    s5_log_dt = din("s5_log_dt", [2, 2, 32])
    s5_b_re = din("s5_b_re", [2, 2, 32, 64, 16])
    s5_b_im = din("s5_b_im", [2, 2, 32, 64, 16])
    s5_c_re = din("s5_c_re", [2, 2, 32, 16, 64])
    s5_c_im = din("s5_c_im", [2, 2, 32, 16, 64])
    s5_d = din("s5_d", [2, 512])
    s5_glu_w = din("s5_glu_w", [2, 512, 512])
    s5_glu_b = din("s5_glu_b", [2, 512])
    ev_w_out = din("ev_w_out", [2, 1536, 1024])
    od_w_in = din("od_w_in", [2, D, 5184])
    ssd_conv_w = din("ssd_conv_w", [2, 4, 3072])
    ssd_conv_b = din("ssd_conv_b", [2, 3072])
    ssd_dt_bias = din("ssd_dt_bias", [2, 2, 32])
    ssd_a_log = din("ssd_a_log", [2, 2, 32])
    ssd_d = din("ssd_d", [2, 32])
    ssd_norm = din("ssd_norm", [2, 2048])
    od_w_out = din("od_w_out", [2, 2048, 1024])
    tri_d = din("tri", [128, 256])
    yT = nc.dram_tensor("yT", [D, NTOK], F32, kind="ExternalOutput").ap()
    XA = dscr("XA", [D, NTOK])
    XB = dscr("XB", [D, NTOK])
    Gd = dscr("Gd", [1024, NTOK])
    XRd = dscr("XRd", [1024, NTOK])
    Ud = dscr("Ud", [512, NTOK])
    YMd = dscr("YMd", [1024, NTOK], BF16)
    YSd = dscr("YSd", [512, NTOK])
    ZSd = dscr("ZSd", [2048, NTOK])
    XBCd = dscr("XBCd", [3072, NTOK])
    DCd = dscr("DCd", [128, NTOK])
    XSfd = dscr("XSfd", [2048, NTOK])
    XStd = dscr("XStd", [NTOK, 2048], BF16)
    BFd = dscr("BFd", [4, 128, NTOK], BF16)
    CFd = dscr("CFd", [4, 128, NTOK], BF16)
    BTd = dscr("BTd", [NTOK, 512], BF16)
    Yd = dscr("Yd", [2, 2048, NTOK])
    dram_ap = {"xT": xT, "XA": XA, "XB": XB, "yT": yT}

    with contextlib.ExitStack() as es:
        es.enter_context(nc.allow_non_contiguous_dma(reason="small parameter loads"))
        kb = KB(nc, es)
        pe, dve, act, pool, sp = kb.pe, kb.dve, kb.act, kb.pool, kb.sp
        V, S, G_, PE_ = nc.vector, nc.scalar, nc.gpsimd, nc.tensor

        ones_bf = kb.sb("ones_bf", [128, 128], BF16)
        t_const = Trk("const")
        kb.op(dve, lambda: V.memset(ones_bf[:], 1.0), writes=[t_const])
        eps_t = kb.sb("eps_t", [128, 1], F32)
        kb.op(dve, lambda: V.memset(eps_t[:], EPS), writes=[t_const])
        one_t = kb.sb("one_t", [128, 1], F32)
        kb.op(dve, lambda: V.memset(one_t[:], 1.0), writes=[t_const])
        hpi_t = kb.sb("hpi_t", [128, 1], F32)
        kb.op(dve, lambda: V.memset(hpi_t[:], float(np.pi / 2)), writes=[t_const])
        fl = kb.sb("fl_sb", [128, 8], F32)
        t_fl = Trk("fl")
        kb.dma(sp, fl[:], flags_d[:, :], writes=[t_fl])
        ident = kb.sb("ident_sb", [128, 128], F32)
        t_id = Trk("ident")
        kb.dma(sp, ident[:], ident_d[:, :], writes=[t_id])
        kb.barrier()
        kb.release_dsems()

        def chunked(ap):
            return ap.rearrange("(c p) t -> p c t", p=128)

        class Phase:
            def __init__(self, tag):
                self.tag = tag
                self.pes = contextlib.ExitStack()
                self.n = 0

            def sb(self, name, shape, dt=F32):
                self.n += 1
                return self.pes.enter_context(nc.sbuf_tensor(f"{self.tag}_{name}_{self.n}", list(shape), dt))

            def ps(self, name, shape, dt=F32):
                self.n += 1
                return self.pes.enter_context(nc.psum_tensor(f"{self.tag}_{name}_{self.n}", list(shape), dt))

            def close(self):
                kb.barrier()
                kb.release_dsems()
                self.pes.close()

        cast_rr = [0]

        def cast(out, in_, reads, writes, engs=None):
            engs = engs or [pool, dve, act]
            E = engs[cast_rr[0] % len(engs)]
            cast_rr[0] += 1
            if E is act:
                kb.op(act, lambda: S.copy(out=out, in_=in_), reads=reads, writes=writes)
            elif E is dve:
                kb.op(dve, lambda: V.tensor_copy(out=out, in_=in_), reads=reads, writes=writes)
            else:
                kb.op(pool, lambda: G_.tensor_copy(out=out, in_=in_), reads=reads, writes=writes)

        def load_weight_rows(ph, wdst, t_w, src_rows_fn, nrows, ncols, stg, t_stg):
            W = stg[0].shape[1]
            si = 0
            for r in range(nrows):
                src = src_rows_fn(r)
                for c0 in range(0, ncols, W):
                    c1 = min(ncols, c0 + W)
                    b = si % len(stg)
                    si += 1
                    kb.dma(sp, stg[b][:, :c1 - c0], src[:, c0:c1], writes=[t_stg[b]])
                    cast(wdst[:, r, c0:c1], stg[b][:, :c1 - c0], [t_stg[b]], [t_w])

        class Norm:
            def __init__(self, ph, TT):
                self.TT = TT
                self.sq = ph.sb("sq", [128, 8, TT], BF16)
                self.t_sq = Trk("sq")
                self.rs = ph.sb("rs", [128, TT])
                self.t_rs = Trk("rs")
                self.rs2 = ph.sb("rs2", [128, TT])
                self.t_rs2 = Trk("rs2")
                self.p_n = ph.ps("p_n", [128, 512])
                self.t_pn = Trk("pn")

            def run(self, xt, t_xt, gam, t_gam, out, t_out):
                TT = self.TT
                sq, rs, rs2, p_n = self.sq, self.rs, self.rs2, self.p_n
                for k in range(8):
                    kb.op(act, lambda k=k: S.activation(out=sq[:, k, :], in_=xt[:, k, :], func=AF.Square),
                          reads=[t_xt], writes=[self.t_sq])
                for k in range(8):
                    kb.op(pe, lambda k=k: PE_.matmul(p_n[:, :TT], lhsT=ones_bf[:], rhs=sq[:, k, :],
                                                     start=(k == 0), stop=(k == 7)),
                          reads=[t_const, self.t_sq], writes=[self.t_pn], pub=(k == 7))
                kb.op(act, lambda: S.activation(out=rs[:], in_=p_n[:, :TT], func=AF.Sqrt,
                                                scale=1.0 / D, bias=eps_t[:]),
                      reads=[self.t_pn, t_const], writes=[self.t_rs])
                kb.op(dve, lambda: V.reciprocal(out=rs2[:], in_=rs[:]), reads=[self.t_rs], writes=[self.t_rs2])
                for k in range(8):
                    kb.op(dve, lambda k=k: V.scalar_tensor_tensor(
                        out=out[:, k, :], in0=xt[:, k, :], scalar=gam[:, k:k + 1], in1=rs2[:],
                        op0=ALU.mult, op1=ALU.mult), reads=[t_xt, t_gam, self.t_rs2], writes=[t_out])

        def load_gamma(ph, src_vec):
            gam = ph.sb("gam", [128, 8])
            t_gam = Trk("gam")
            kb.dma(sp, gam[:], src_vec.rearrange("(c p) -> p c", p=128), writes=[t_gam])
            return gam, t_gam

        def ffn_phase(layer, which, src, dst):
            ph = Phase(f"f{layer}{which}")
            NT = NTOK // T
            gam, t_gam = load_gamma(ph, ffn_norm[which][layer])
            wg = ph.sb("wg", [128, 8, DFF], BF16)
            wu = ph.sb("wu", [128, 8, DFF], BF16)
            wd = ph.sb("wd", [128, NFC, D], BF16)
            t_wg, t_wu, t_wd = Trk("wg"), Trk("wu"), Trk("wd")
            HW = DFF // 2
            stg = [ph.sb(f"stg{i}", [128, HW]) for i in range(3)]
            t_stg = [Trk(f"stg{i}") for i in range(3)]
            load_weight_rows(ph, wg, t_wg, lambda r: ffn_wg[which][layer, r * 128:(r + 1) * 128, :], 8, DFF, stg, t_stg)
            load_weight_rows(ph, wu, t_wu, lambda r: ffn_wu[which][layer, r * 128:(r + 1) * 128, :], 8, DFF, stg, t_stg)
            load_weight_rows(ph, wd, t_wd, lambda r: ffn_wd[which][layer, r * 128:(r + 1) * 128, :], NFC, D, stg, t_stg)
            nrm = Norm(ph, T)
            xt = ph.sb("xt", [128, 8, T])
            t_xt = Trk("xt")
            xn = [ph.sb(f"xn{i}", [128, 8, T], BF16) for i in range(2)]
            t_xn = [Trk(f"xn{i}") for i in range(2)]
            h = ph.sb("h", [128, NFC, T], BF16)
            t_h = Trk("h")
            sg = [ph.sb(f"sg{i}", [128, T]) for i in range(2)]
            t_sg = [Trk(f"sg{i}") for i in range(2)]
            xres = [ph.sb(f"xres{i}", [128, T]) for i in range(4)]
            t_xres = [Trk(f"xres{i}") for i in range(4)]
            ores = [ph.sb(f"ores{i}", [128, T]) for i in range(4)]
            t_ores = [Trk(f"ores{i}") for i in range(4)]
            p_g = [ph.ps(f"p_g{i}", [128, 512]) for i in range(2)]
            t_pg = [Trk(f"pg{i}") for i in range(2)]
            p_u = [ph.ps(f"p_u{i}", [128, 512]) for i in range(2)]
            t_pu = [Trk(f"pu{i}") for i in range(2)]
            p_d = [ph.ps(f"p_d{i}", [128, 512]) for i in range(2)]
            t_pd = [Trk(f"pd{i}") for i in range(2)]
            srcc = chunked(dram_ap[src])
            dstc = chunked(dram_ap[dst])
            ri = 0
            for i in range(NT):
                ts = slice(i * T, (i + 1) * T)
                kb.dma(sp, xt[:], srcc[:, :, ts], writes=[t_xt])
                xb, txb = xn[i % 2], t_xn[i % 2]
                nrm.run(xt, t_xt, gam, t_gam, xb, txb)
                for j in range(NFC):
                    js = slice(j * 128, (j + 1) * 128)
                    pg, tpg = p_g[j % 2], t_pg[j % 2]
                    pu, tpu = p_u[j % 2], t_pu[j % 2]
                    for k in range(8):
                        kb.op(pe, lambda k=k: PE_.matmul(pg[:, :T], lhsT=wg[:, k, js], rhs=xb[:, k, :],
                                                         start=(k == 0), stop=(k == 7)),
                              reads=[t_wg, txb], writes=[tpg], pub=(k == 7))
                    for k in range(8):
                        kb.op(pe, lambda k=k: PE_.matmul(pu[:, :T], lhsT=wu[:, k, js], rhs=xb[:, k, :],
                                                         start=(k == 0), stop=(k == 7)),
                              reads=[t_wu, txb], writes=[tpu], pub=(k == 7))
                    sgb, tsg = sg[j % 2], t_sg[j % 2]
                    kb.op(act, lambda: S.activation(out=sgb[:], in_=pg[:, :T], func=AF.Silu),
                          reads=[tpg], writes=[tsg])
                    kb.op(dve, lambda: V.tensor_tensor(out=h[:, j, :], in0=sgb[:], in1=pu[:, :T], op=ALU.mult),
                          reads=[tsg, tpu], writes=[t_h])
                for m in range(8):
                    ms = slice(m * 128, (m + 1) * 128)
                    pd, tpd = p_d[m % 2], t_pd[m % 2]
                    xr, txr = xres[ri % 4], t_xres[ri % 4]
                    orr, tor = ores[ri % 4], t_ores[ri % 4]
                    ri += 1
                    kb.dma(sp, xr[:], srcc[:, m, ts], writes=[txr])
                    for j in range(NFC):
                        kb.op(pe, lambda j=j: PE_.matmul(pd[:, :T], lhsT=wd[:, j, ms], rhs=h[:, j, :],
                                                         start=(j == 0), stop=(j == NFC - 1)),
                              reads=[t_wd, t_h], writes=[tpd], pub=(j == NFC - 1))
                    kb.op(dve, lambda: V.scalar_tensor_tensor(
                        out=orr[:], in0=pd[:, :T], scalar=0.5, in1=xr[:], op0=ALU.mult, op1=ALU.add),
                        reads=[tpd, txr], writes=[tor])
                    kb.dma(pool, dstc[:, m, ts], orr[:], reads=[tor])
            ph.close()

        def final_phase(src):
            ph = Phase("fin")
            gam, t_gam = load_gamma(ph, final_norm)
            TF = 512 if NTOK % 512 == 0 else T
            nrm = Norm(ph, TF)
            xt = [ph.sb(f"fxt{i}", [128, 8, TF]) for i in range(2)]
            t_xt = [Trk("fxt") for _ in range(2)]
            ot = [ph.sb(f"fot{i}", [128, 8, TF]) for i in range(2)]
            t_ot = [Trk("fot") for _ in range(2)]
            srcc = chunked(dram_ap[src])
            dstc = chunked(yT)
            for i in range(NTOK // TF):
                ts = slice(i * TF, (i + 1) * TF)
                xb, txb = xt[i % 2], t_xt[i % 2]
                ob, tob = ot[i % 2], t_ot[i % 2]
                kb.dma(sp, xb[:], srcc[:, :, ts], writes=[txb])
                nrm.run(xb, txb, gam, t_gam, ob, tob)
                kb.dma(pool, dstc[:, :, ts], ob[:], reads=[tob], is_out=True)
            ph.close()

        def even_in_phase(j, layer, src):
            ph = Phase(f"e1_{layer}")
            TT = 512 if NTOK % 512 == 0 else 256
            gam, t_gam = load_gamma(ph, mix_norm[layer])
            NO = 2560
            win = ph.sb("win", [128, 8, NO], BF16)
            t_win = Trk("win")
            stg = [ph.sb(f"stg{i}", [128, NO // 2]) for i in range(3)]
            t_stg = [Trk(f"stg{i}") for i in range(3)]
            load_weight_rows(ph, win, t_win, lambda r: ev_w_in[j, r * 128:(r + 1) * 128, :], 8, NO, stg, t_stg)
            nrm = Norm(ph, TT)
            xt = [ph.sb(f"xt{i}", [128, 8, TT]) for i in range(2)]
            t_xt = [Trk("xt") for _ in range(2)]
            xn = [ph.sb(f"xn{i}", [128, 8, TT], BF16) for i in range(2)]
            t_xn = [Trk("xn") for _ in range(2)]
            ev = [ph.sb(f"ev{i}", [128, TT]) for i in range(4)]
            t_ev = [Trk("ev") for _ in range(4)]
            pp = [ph.ps(f"pp{i}", [128, 512]) for i in range(3)]
            t_pp = [Trk("pp") for _ in range(3)]
            srcc = chunked(dram_ap[src])
            ei = 0
            for i in range(NTOK // TT):
                ts = slice(i * TT, (i + 1) * TT)
                xb, txb = xt[i % 2], t_xt[i % 2]
                kb.dma(sp, xb[:], srcc[:, :, ts], writes=[txb])
                xnb, txnb = xn[i % 2], t_xn[i % 2]
                nrm.run(xb, txb, gam, t_gam, xnb, txnb)
                for oc in range(20):
                    p, tp = pp[oc % 3], t_pp[oc % 3]
                    for k in range(8):
                        kb.op(pe, lambda k=k: PE_.matmul(p[:, :TT], lhsT=win[:, k, oc * 128:(oc + 1) * 128],
                                                         rhs=xnb[:, k, :], start=(k == 0), stop=(k == 7)),
                              reads=[t_win, txnb], writes=[tp], pub=(k == 7))
                    e, te = ev[ei % 4], t_ev[ei % 4]
                    ei += 1
                    if oc < 8:
                        kb.op(act, lambda: S.activation(out=e[:], in_=p[:, :TT], func=AF.Gelu_apprx_tanh),
                              reads=[tp], writes=[te])
                        dst = Gd[oc * 128:(oc + 1) * 128, ts]
                    elif oc < 16:
                        kb.op(dve, lambda: V.tensor_copy(out=e[:], in_=p[:, :TT]), reads=[tp], writes=[te])
                        dst = XRd[(oc - 8) * 128:(oc - 7) * 128, ts]
                    else:
                        kb.op(dve, lambda: V.tensor_copy(out=e[:], in_=p[:, :TT]), reads=[tp], writes=[te])
                        dst = Ud[(oc - 16) * 128:(oc - 15) * 128, ts]
                    kb.dma(pool, dst, e[:], reads=[te])
            ph.close()

        def load_halo(xp, t_xp, src_rows, s):
            lo = s * SEG - 2
            hi = s * SEG + SEG + 1
            clo, chi = max(lo, 0), min(hi, NTOK)
            kb.dma(sp, xp[:, clo - lo:chi - lo], src_rows[:, clo:chi], writes=[t_xp])
            if s == 0:
                kb.op(dve, lambda: V.memset(xp[:, 0:2], 0.0), writes=[t_xp])
            else:
                kb.op(dve, lambda: V.tensor_scalar(out=xp[:, 0:2], in0=xp[:, 0:2], scalar1=fl[:, s - 1:s], scalar2=None,
                                                   op0=ALU.mult), reads=[t_xp, t_fl], writes=[t_xp])
            if s == NSEG - 1:
                kb.op(dve, lambda: V.memset(xp[:, SEG + 2:SEG + 3], 0.0), writes=[t_xp])
            else:
                kb.op(dve, lambda: V.tensor_scalar(out=xp[:, SEG + 2:SEG + 3], in0=xp[:, SEG + 2:SEG + 3],
                                                   scalar1=fl[:, s:s + 1], scalar2=None, op0=ALU.mult),
                      reads=[t_xp, t_fl], writes=[t_xp])

        def conv4(xc, t_xc, xp, t_xp, w4, bcol, t_par):
            kb.op(dve, lambda: V.tensor_scalar(out=xc[:], in0=xp[:, 0:SEG], scalar1=w4[:, 0:1], scalar2=bcol,
                                               op0=ALU.mult, op1=ALU.add), reads=[t_xp, t_par], writes=[t_xc])
            for k in range(1, 4):
                kb.op(dve, lambda k=k: V.scalar_tensor_tensor(out=xc[:], in0=xp[:, k:k + SEG], scalar=w4[:, k:k + 1],
                                                              in1=xc[:], op0=ALU.mult, op1=ALU.add),
                      reads=[t_xp, t_par, t_xc], writes=[t_xc])

        def lru_phase(j):
            ph = Phase(f"e2_{j}")
            t_par = Trk("par")
            cw = ph.sb("cw", [128, 8, 4])
            for k in range(4):
                kb.dma(sp, cw[:, :, k], lru_conv_w[j, k].rearrange("(c p) -> p c", p=128), writes=[t_par], st=t_par)
            cb = ph.sb("cb", [128, 8])
            kb.dma(sp, cb[:], lru_conv_b[j].rearrange("(c p) -> p c", p=128), writes=[t_par])
            ba = ph.sb("ba", [128, 2, 8])
            for d in range(2):
                kb.dma(sp, ba[:, d, :], lru_b_a[j, d].rearrange("(c p) -> p c", p=128), writes=[t_par], st=t_par)
            bx = ph.sb("bx", [128, 2, 8])
            for d in range(2):
                kb.dma(sp, bx[:, d, :], lru_b_x[j, d].rearrange("(c p) -> p c", p=128), writes=[t_par], st=t_par)
            lam = ph.sb("lam", [128, 2, 8])
            t_lam = Trk("lam")
            for d in range(2):
                kb.dma(sp, lam[:, d, :], lru_lam[j, d].rearrange("(c p) -> p c", p=128), writes=[t_lam], st=t_lam)
            l1 = ph.sb("l1", [128, 2, 8])
            c8 = ph.sb("c8", [128, 2, 8])
            c16 = ph.sb("c16", [128, 2, 8])
            kb.op(act, lambda: S.activation(out=l1[:], in_=lam[:], func=AF.Exp, scale=-1.0), reads=[t_lam], writes=[t_lam])
            kb.op(act, lambda: S.activation(out=l1[:], in_=l1[:], func=AF.Ln, bias=1.0, scale=1.0),
                  reads=[t_lam], writes=[t_lam])
            kb.op(dve, lambda: V.tensor_scalar(out=c8[:], in0=l1[:], scalar1=-8.0, scalar2=None, op0=ALU.mult),
                  reads=[t_lam], writes=[t_par])
            kb.op(dve, lambda: V.tensor_scalar(out=c16[:], in0=l1[:], scalar1=-16.0, scalar2=None, op0=ALU.mult),
                  reads=[t_lam], writes=[t_par])
            wst = ph.sb("wst", [128, 32, 128])
            t_wst = Trk("wst")
            kb.op(dve, lambda: V.memset(wst[:], 0.0), writes=[t_wst])

            def widx(g, d, c):
                return (g * 2 + d) * 8 + c

            for g, Wd_ in enumerate((lru_w_a, lru_w_x)):
                for d in range(2):
                    for c in range(8):
                        for hh in range(2):
                            kb.dma(sp, wst[hh * 64:(hh + 1) * 64, widx(g, d, c), hh * 64:(hh + 1) * 64],
                                   Wd_[j, d, 2 * c + hh], writes=[t_wst], st=t_wst)
            wbd = ph.sb("wbd", [128, 32, 128], BF16)
            t_wbd = Trk("wbd")
            kb.op(dve, lambda: V.tensor_copy(out=wbd[:], in_=wst[:]), reads=[t_wst], writes=[t_wbd])

            hb = ph.sb("hb", [128, NTOK])
            t_hb = Trk("hb")
            xp = [ph.sb(f"xp{i}", [128, SEG + 3]) for i in range(2)]
            t_xp = [Trk("xp") for _ in range(2)]
            xc = ph.sb("xc", [128, SEG])
            t_xc = Trk("xc")
            xcb = ph.sb("xcb", [128, SEG], BF16)
            t_xcb = Trk("xcb")
            Rb = ph.sb("Rb", [128, SEG]); t_R = Trk("R")
            Ab = ph.sb("Ab", [128, SEG]); t_A = Trk("A")
            Sb = ph.sb("Sb", [128, SEG]); t_S = Trk("S")
            Ib = ph.sb("Ib", [128, SEG]); t_I = Trk("I")
            hf = [ph.sb(f"hf{i}", [128, SEG]) for i in range(2)]
            t_hf = [Trk("hf") for _ in range(2)]
            Gt = ph.sb("Gt", [128, SEG]); t_G = Trk("G")
            yb = [ph.sb(f"yb{i}", [128, SEG], BF16) for i in range(2)]
            t_yb = [Trk("yb") for _ in range(2)]
            ini = ph.sb("ini", [128, 2]); t_ini = Trk("ini")
            pa = ph.ps("pa", [128, SEG]); t_pa = Trk("pa")
            px = ph.ps("px", [128, SEG]); t_px = Trk("px")
            cnt = 0
            for c in range(8):
                rows = XRd[c * 128:(c + 1) * 128, :]
                for d in (1, 0):
                    segs = list(range(NSEG))[::-1] if d == 1 else list(range(NSEG))
                    for s in segs:
                        ss = slice(s * SEG, (s + 1) * SEG)
                        xpb, txp = xp[cnt % 2], t_xp[cnt % 2]
                        load_halo(xpb, txp, rows, s)
                        conv4(xc, t_xc, xpb, txp, cw[:, c, :], cb[:, c:c + 1], t_par)
                        kb.op(act, lambda: S.copy(out=xcb[:], in_=xc[:]), reads=[t_xc], writes=[t_xcb])
                        for q in range(NPQ):
                            qs = slice(q * PW, (q + 1) * PW)
                            kb.op(pe, lambda qs=qs: PE_.matmul(pa[:, qs], lhsT=wbd[:, widx(0, d, c), :], rhs=xcb[:, qs],
                                                               start=True, stop=True),
                                  reads=[t_wbd, t_xcb], writes=[t_pa], pub=(q == NPQ - 1))
                        for q in range(NPQ):
                            qs = slice(q * PW, (q + 1) * PW)
                            kb.op(pe, lambda qs=qs: PE_.matmul(px[:, qs], lhsT=wbd[:, widx(1, d, c), :], rhs=xcb[:, qs],
                                                               start=True, stop=True),
                                  reads=[t_wbd, t_xcb], writes=[t_px], pub=(q == NPQ - 1))
                        kb.op(act, lambda: S.activation(out=Rb[:], in_=pa[:], func=AF.Sigmoid, bias=ba[:, d, c:c + 1],
                                                        scale=1.0), reads=[t_pa, t_par], writes=[t_R])
                        kb.op(act, lambda: S.activation(out=Ab[:], in_=Rb[:], func=AF.Exp, scale=c8[:, d, c:c + 1]),
                              reads=[t_R, t_par], writes=[t_A])
                        kb.op(act, lambda: S.activation(out=Sb[:], in_=Rb[:], func=AF.Exp, scale=c16[:, d, c:c + 1]),
                              reads=[t_R, t_par], writes=[t_S])
                        kb.op(act, lambda: S.activation(out=Sb[:], in_=Sb[:], func=AF.Sqrt, scale=-1.0, bias=one_t[:]),
                              reads=[t_S, t_const], writes=[t_S])
                        kb.op(act, lambda: S.activation(out=Ib[:], in_=px[:], func=AF.Sigmoid, bias=bx[:, d, c:c + 1],
                                                        scale=1.0), reads=[t_px, t_par], writes=[t_I])
                        kb.op(dve, lambda: V.tensor_tensor(out=Ib[:], in0=Ib[:], in1=xc[:], op=ALU.mult),
                              reads=[t_I, t_xc], writes=[t_I])
                        kb.op(dve, lambda: V.tensor_tensor(out=Ib[:], in0=Ib[:], in1=Sb[:], op=ALU.mult),
                              reads=[t_I, t_S], writes=[t_I])
                        if d == 1:
                            if s == NSEG - 1:
                                init = 0.0
                                rd = []
                            else:
                                kb.op(dve, lambda: V.tensor_scalar(out=ini[:, 0:1], in0=hb[:, (s + 1) * SEG:(s + 1) * SEG + 1],
                                                                   scalar1=fl[:, s:s + 1], scalar2=None, op0=ALU.mult),
                                      reads=[t_hb, t_fl], writes=[t_ini])
                                init = ini[:, 0:1]
                                rd = [t_ini]
                            kb.op(dve, lambda: V.tensor_tensor_scan(out=hb[:, ss][:, ::-1], data0=Ab[:, ::-1],
                                                                    data1=Ib[:, ::-1], initial=init,
                                                                    op0=ALU.mult, op1=ALU.add),
                                  reads=[t_A, t_I] + rd, writes=[t_hb])
                        else:
                            hfb, thf = hf[cnt % 2], t_hf[cnt % 2]
                            hfp, thfp = hf[(cnt + 1) % 2], t_hf[(cnt + 1) % 2]
                            if s == 0:
                                init = 0.0
                                rd = []
                            else:
                                kb.op(dve, lambda: V.tensor_scalar(out=ini[:, 1:2], in0=hfp[:, SEG - 1:SEG],
                                                                   scalar1=fl[:, s - 1:s], scalar2=None, op0=ALU.mult),
                                      reads=[thfp, t_fl], writes=[t_ini])
                                init = ini[:, 1:2]
                                rd = [t_ini]
                            kb.op(dve, lambda: V.tensor_tensor_scan(out=hfb[:], data0=Ab[:], data1=Ib[:], initial=init,
                                                                    op0=ALU.mult, op1=ALU.add),
                                  reads=[t_A, t_I] + rd, writes=[thf])
                            kb.dma(sp, Gt[:], Gd[c * 128:(c + 1) * 128, ss], writes=[t_G])
                            kb.op(pool, lambda: G_.tensor_tensor(out=Ib[:], in0=hfb[:], in1=hb[:, ss], op=ALU.add),
                                  reads=[thf, t_hb], writes=[t_I])
                            ybb, tyb = yb[cnt % 2], t_yb[cnt % 2]
                            kb.op(pool, lambda: G_.tensor_tensor(out=ybb[:], in0=Ib[:], in1=Gt[:], op=ALU.mult),
                                  reads=[t_I, t_G], writes=[tyb])
                            kb.dma(pool, YMd[c * 128:(c + 1) * 128, ss], ybb[:], reads=[tyb])
                        cnt += 1
            ph.close()

        def s5_phase(j):
            ph = Phase(f"e3_{j}")
            t_p = Trk("s5par")
            NC_ = 32

            def pt(name, shape=None):
                return ph.sb(name, shape or [128, NC_])

            lre, lim, ldt = pt("lre"), pt("lim"), pt("ldt")
            for d in range(2):
                kb.dma(sp, lre[:, d * 16:(d + 1) * 16],
                       s5_lam_re[j, d].rearrange("(t g) n -> (g n) t", g=2), writes=[t_p], st=t_p)
                kb.dma(sp, lim[:, d * 16:(d + 1) * 16],
                       s5_lam_im[j, d].rearrange("(t g) n -> (g n) t", g=2), writes=[t_p], st=t_p)
                for g in range(2):
                    kb.dma(sp, ldt[g * 64:(g + 1) * 64, d * 16:(d + 1) * 16],
                           s5_log_dt[j, d].rearrange("(t g) -> g t", g=2)[g].partition_broadcast(64), writes=[t_p], st=t_p)

            def vv(out, a, b, op):
                kb.op(dve, lambda: V.tensor_tensor(out=out, in0=a, in1=b, op=op), reads=[t_p], writes=[t_p])

            def vs(out, a, s1, op):
                kb.op(dve, lambda: V.tensor_scalar(out=out, in0=a, scalar1=s1, scalar2=None, op0=op),
                      reads=[t_p], writes=[t_p])

            dtt, th, lr, rho = pt("dtt"), pt("th"), pt("lr"), pt("rho")
            kb.op(act, lambda: S.activation(out=dtt[:], in_=ldt[:], func=AF.Exp), reads=[t_p], writes=[t_p])
            vv(th[:], lim[:], dtt[:], ALU.mult)
            vv(lr[:], lre[:], dtt[:], ALU.mult)
            kb.op(act, lambda: S.activation(out=rho[:], in_=lr[:], func=AF.Exp), reads=[t_p], writes=[t_p])
            cs_, sn_ = pt("cs"), pt("sn")
            kb.op(act, lambda: S.activation(out=sn_[:], in_=th[:], func=AF.Sin, scale=1.0 / 32.0), reads=[t_p], writes=[t_p])
            kb.op(act, lambda: S.activation(out=cs_[:], in_=th[:], func=AF.Sin, scale=1.0 / 32.0, bias=hpi_t[:]),
                  reads=[t_p, t_const], writes=[t_p])
            cc, s2, sc = pt("cc"), pt("s2"), pt("sc")
            for _ in range(5):
                vv(cc[:], cs_[:], cs_[:], ALU.mult)
                vv(s2[:], sn_[:], sn_[:], ALU.mult)
                vv(sc[:], sn_[:], cs_[:], ALU.mult)
                vv(cs_[:], cc[:], s2[:], ALU.subtract)
                vs(sn_[:], sc[:], 2.0, ALU.mult)
            abr, abi = pt("abr"), pt("abi")
            vv(abr[:], rho[:], cs_[:], ALU.mult)
            vv(abi[:], rho[:], sn_[:], ALU.mult)
            t1, t2, den, abm1 = pt("t1"), pt("t2"), pt("den"), pt("abm1")
            vv(t1[:], lre[:], lre[:], ALU.mult)
            vv(t2[:], lim[:], lim[:], ALU.mult)
            vv(den[:], t1[:], t2[:], ALU.add)
            kb.op(dve, lambda: V.reciprocal(out=den[:], in_=den[:]), reads=[t_p], writes=[t_p])
            vs(abm1[:], abr[:], -1.0, ALU.add)
            cre, cim = pt("cre"), pt("cim")
            vv(t1[:], abm1[:], lre[:], ALU.mult)
            vv(t2[:], abi[:], lim[:], ALU.mult)
            vv(t1[:], t1[:], t2[:], ALU.add)
            vv(cre[:], t1[:], den[:], ALU.mult)
            vv(t1[:], abi[:], lre[:], ALU.mult)
            vv(t2[:], abm1[:], lim[:], ALU.mult)
            vv(t1[:], t1[:], t2[:], ALU.subtract)
            vv(cim[:], t1[:], den[:], ALU.mult)
            H2 = min(1024, SEG)
            NQH = H2 // PW
            NH = SEG // H2
            pbr = ph.ps("pbr", [128, 1024]); t_pbr = Trk("pbr")
            pbi = ph.ps("pbi", [128, 1024]); t_pbi = Trk("pbi")
            py = ph.ps("py", [128, max(SEG, 1024)]); t_py = Trk("py")
            lB = [ph.sb("lBre", [32, NC_, 128], BF16), ph.sb("lBim", [32, NC_, 128], BF16)]
            t_lB = Trk("lB")
            lC = [ph.sb("lCre", [128, NC_, 32], BF16), ph.sb("lCimn", [128, NC_, 32], BF16)]
            t_lC = Trk("lC")
            dsk = ph.sb("dsk", [32, 16])
            kb.dma(sp, dsk[:], s5_d[j].rearrange("(t c) -> c t", c=32), writes=[t_p], st=t_p)
            pp_ = Phase(f"e3p_{j}")
            bre = pp_.sb("bre", [128, NC_, 32])
            bim = pp_.sb("bim", [128, NC_, 32])
            kb.op(dve, lambda: V.memset(bre[:], 0.0), writes=[t_p])
            kb.op(dve, lambda: V.memset(bim[:], 0.0), writes=[t_p])
            for g in range(2):
                for (dst_, src_) in ((bre, s5_b_re), (bim, s5_b_im)):
                    for d in range(2):
                        kb.dma(sp, dst_[g * 64:(g + 1) * 64, d * 16:(d + 1) * 16, g * 16:(g + 1) * 16],
                               src_[j, d].rearrange("(t g) n c -> g n t c", g=2)[g], writes=[t_p], st=t_p)
            bbre = pp_.sb("bbre", [128, NC_, 32])
            bbim = pp_.sb("bbim", [128, NC_, 32])
            tb = pp_.sb("tb", [128, NC_, 32])

            def bc(col):
                return col[:].unsqueeze(2).to_broadcast([128, NC_, 32])

            vv(bbre[:], bre[:], bc(cre), ALU.mult)
            vv(tb[:], bim[:], bc(cim), ALU.mult)
            vv(bbre[:], bbre[:], tb[:], ALU.subtract)
            vv(bbim[:], bim[:], bc(cre), ALU.mult)
            vv(tb[:], bre[:], bc(cim), ALU.mult)
            vv(bbim[:], bbim[:], tb[:], ALU.add)
            for ri_, srcb in enumerate((bbre, bbim)):
                for b0 in range(0, NC_, 8):
                    for q in range(8):
                        kb.op(pe, lambda q=q: PE_.transpose(out=py[0:32, q * 128:(q + 1) * 128], in_=srcb[:, b0 + q, :],
                                                            identity=ident[:]),
                              reads=[t_p, t_id], writes=[t_py], pub=(q == 7))
                    kb.op(dve, lambda: V.tensor_copy(out=lB[ri_][:, b0:b0 + 8, :],
                                                     in_=py[0:32, 0:1024].rearrange("p (q n) -> p q n", q=8)),
                          reads=[t_py], writes=[t_lB])
            craw = [pp_.sb("crre", [32, NC_, 128]), pp_.sb("crim", [32, NC_, 128])]
            t_c = Trk("craw")
            for ri_, src_ in enumerate((s5_c_re, s5_c_im)):
                kb.op(dve, lambda: V.memset(craw[ri_][:], 0.0), writes=[t_c])
                for g in range(2):
                    for d in range(2):
                        kb.dma(sp, craw[ri_][g * 16:(g + 1) * 16, d * 16:(d + 1) * 16, g * 64:(g + 1) * 64],
                               src_[j, d].rearrange("(t g) c n -> g c t n", g=2)[g], writes=[t_c], st=t_c)
            for ri_ in range(2):
                for q in range(NC_):
                    kb.op(pe, lambda q=q: PE_.transpose(out=pbr[:, q * 32:(q + 1) * 32], in_=craw[ri_][:, q, :],
                                                        identity=ident[0:32, 0:32]),
                          reads=[t_c, t_id], writes=[t_pbr], pub=(q == NC_ - 1))
                kb.op(dve, lambda: V.tensor_scalar(out=lC[ri_][:], in0=pbr[:, 0:NC_ * 32].rearrange("p (q c) -> p q c", q=NC_),
                                                   scalar1=(1.0 if ri_ == 0 else -1.0), scalar2=None, op0=ALU.mult),
                      reads=[t_pbr], writes=[t_lC])
            pp_.close()
            cosT = ph.sb("cosT", [128, SEG + 1]); sinT = ph.sb("sinT", [128, SEG + 1]); t_tab = Trk("tab")
            RT = ph.sb("RT", [128, SEG]); t_RT = Trk("RT")
            tt_ = ph.sb("ttmp", [128, 1]);
            uf = ph.sb("uf", [32, SEG]); t_uf = Trk("uf")
            ub = ph.sb("ub", [32, SEG], BF16); t_ub = Trk("ub")
            w1 = ph.sb("w1", [128, H2]); w2 = ph.sb("w2", [128, H2]); t_w1 = Trk("w1"); t_w2 = Trk("w2")
            btre = ph.sb("btre", [128, SEG]); btim = ph.sb("btim", [128, SEG]); t_bt = Trk("bt")
            gre = ph.sb("gre", [128, SEG]); gim = ph.sb("gim", [128, SEG]); t_g = Trk("g")
            q1 = ph.sb("q1", [128, SEG]); q2 = ph.sb("q2", [128, SEG]); t_q1 = Trk("q1"); t_q2 = Trk("q2")
            hbre = ph.sb("hbre", [128, SEG], BF16); hbim = ph.sb("hbim", [128, SEG], BF16); t_hb = Trk("hb")
            ysacc = ph.sb("ysacc", [32, NTOK]); t_ys = Trk("ysacc")
            ot_ = ph.sb("ot", [32, SEG]); t_ot = Trk("ot")
            ini = ph.sb("ini", [128, 8]); t_ini = Trk("ini")
            for ti in range(16):
                for d in (1, 0):
                    col = d * 16 + ti
                    kb.op(dve, lambda: V.memset(cosT[:, 0:1], 1.0), writes=[t_tab])
                    kb.op(dve, lambda: V.memset(sinT[:, 0:1], 0.0), writes=[t_tab])
                    kb.op(dve, lambda: V.tensor_copy(out=cosT[:, 1:2], in_=cs_[:, col:col + 1]), reads=[t_p], writes=[t_tab])
                    kb.op(dve, lambda: V.tensor_copy(out=sinT[:, 1:2], in_=sn_[:, col:col + 1]), reads=[t_p], writes=[t_tab])
                    m = 2
                    while m < SEG + 1:
                        n = min(m - 1, SEG + 1 - m)
                        cr, ci = cosT[:, m - 1:m], sinT[:, m - 1:m]
                        kb.op(dve, lambda: V.tensor_scalar(out=q1[:, 0:n], in0=sinT[:, 1:1 + n], scalar1=ci, scalar2=None,
                                                           op0=ALU.mult), reads=[t_tab], writes=[t_q1])
                        kb.op(dve, lambda: V.tensor_scalar(out=q2[:, 0:n], in0=cosT[:, 1:1 + n], scalar1=ci, scalar2=None,
                                                           op0=ALU.mult), reads=[t_tab], writes=[t_q2])
                        kb.op(dve, lambda: V.scalar_tensor_tensor(out=cosT[:, m:m + n], in0=cosT[:, 1:1 + n], scalar=cr,
                                                                  in1=q1[:, 0:n], op0=ALU.mult, op1=ALU.subtract),
                              reads=[t_tab, t_q1], writes=[t_tab])
                        kb.op(dve, lambda: V.scalar_tensor_tensor(out=sinT[:, m:m + n], in0=sinT[:, 1:1 + n], scalar=cr,
                                                                  in1=q2[:, 0:n], op0=ALU.mult, op1=ALU.add),
                              reads=[t_tab, t_q2], writes=[t_tab])
                        m += n
                    kb.op(dve, lambda: V.tensor_copy(out=RT[:], in_=rho[:, col:col + 1].to_broadcast([128, SEG])),
                          reads=[t_p], writes=[t_RT])
                    EQr, EQi = cosT[:, SEG:SEG + 1], sinT[:, SEG:SEG + 1]
                    segs = list(range(NSEG))[::-1] if d == 1 else list(range(NSEG))
                    for si_, s in enumerate(segs):
                        ss = slice(s * SEG, (s + 1) * SEG)
                        kb.dma(sp, uf[:], Ud[32 * ti:32 * ti + 32, ss], writes=[t_uf])
                        kb.op(act, lambda: S.copy(out=ub[:], in_=uf[:]), reads=[t_uf], writes=[t_ub])

                        def tab(tbl, t0, t1):
                            if d == 0:
                                return tbl[:, t0:t1]
                            return tbl[:, SEG - t1:SEG - t0][:, ::-1]

                        for hh in range(NH):
                            t0, t1_ = hh * H2, (hh + 1) * H2
                            for q in range(NQH):
                                qs = slice(q * PW, (q + 1) * PW)
                                us = slice(t0 + q * PW, t0 + (q + 1) * PW)
                                kb.op(pe, lambda qs=qs, us=us: PE_.matmul(pbr[:, qs], lhsT=lB[0][:, col, :], rhs=ub[:, us],
                                                                          start=True, stop=True),
                                      reads=[t_lB, t_ub], writes=[t_pbr], pub=(q == NQH - 1))
                            for q in range(NQH):
                                qs = slice(q * PW, (q + 1) * PW)
                                us = slice(t0 + q * PW, t0 + (q + 1) * PW)
                                kb.op(pe, lambda qs=qs, us=us: PE_.matmul(pbi[:, qs], lhsT=lB[1][:, col, :], rhs=ub[:, us],
                                                                          start=True, stop=True),
                                      reads=[t_lB, t_ub], writes=[t_pbi], pub=(q == NQH - 1))
                            Ec, Es = tab(cosT, t0, t1_), tab(sinT, t0, t1_)
                            hs = slice(t0, t1_)
                            kb.op(dve, lambda: V.tensor_tensor(out=w1[:], in0=Ec, in1=pbr[:, 0:H2], op=ALU.mult),
                                  reads=[t_tab, t_pbr], writes=[t_w1])
                            kb.op(dve, lambda: V.tensor_tensor(out=w2[:], in0=Es, in1=pbi[:, 0:H2], op=ALU.mult),
                                  reads=[t_tab, t_pbi], writes=[t_w2])
                            kb.op(dve, lambda: V.tensor_tensor(out=btre[:, hs], in0=w1[:], in1=w2[:], op=ALU.add),
                                  reads=[t_w1, t_w2], writes=[t_bt])
                            kb.op(dve, lambda: V.tensor_tensor(out=w1[:], in0=Ec, in1=pbi[:, 0:H2], op=ALU.mult),
                                  reads=[t_tab, t_pbi], writes=[t_w1])
                            kb.op(dve, lambda: V.tensor_tensor(out=w2[:], in0=Es, in1=pbr[:, 0:H2], op=ALU.mult),
                                  reads=[t_tab, t_pbr], writes=[t_w2])
                            kb.op(dve, lambda: V.tensor_tensor(out=btim[:, hs], in0=w1[:], in1=w2[:], op=ALU.subtract),
                                  reads=[t_w1, t_w2], writes=[t_bt])
                        if si_ == 0:
                            i_re, i_im, rd = 0.0, 0.0, []
                        else:
                            ecol = (SEG - 1) if d == 0 else 0
                            fcol = (s - 1) if d == 0 else s
                            ge_r, ge_i = gre[:, ecol:ecol + 1], gim[:, ecol:ecol + 1]
                            fcl = fl[:, fcol:fcol + 1]

                            def ts2(out, a, s1):
                                kb.op(dve, lambda: V.tensor_scalar(out=out, in0=a, scalar1=s1, scalar2=fcl,
                                                                   op0=ALU.mult, op1=ALU.mult),
                                      reads=[t_g, t_tab, t_fl], writes=[t_ini])

                            ts2(ini[:, 0:1], ge_r, EQr)
                            ts2(ini[:, 1:2], ge_i, EQi)
                            ts2(ini[:, 2:3], ge_r, EQi)
                            ts2(ini[:, 3:4], ge_i, EQr)
                            kb.op(dve, lambda: V.tensor_tensor(out=ini[:, 4:5], in0=ini[:, 0:1], in1=ini[:, 1:2],
                                                               op=ALU.subtract), reads=[t_ini], writes=[t_ini])
                            kb.op(dve, lambda: V.tensor_tensor(out=ini[:, 5:6], in0=ini[:, 2:3], in1=ini[:, 3:4],
                                                               op=ALU.add), reads=[t_ini], writes=[t_ini])
                            i_re, i_im, rd = ini[:, 4:5], ini[:, 5:6], [t_ini]

                        def rv(ap):
                            return ap if d == 0 else ap[:, ::-1]

                        kb.op(dve, lambda: V.tensor_tensor_scan(out=rv(gre[:]), data0=RT[:], data1=rv(btre[:]),
                                                                initial=i_re, op0=ALU.mult, op1=ALU.add),
                              reads=[t_RT, t_bt] + rd, writes=[t_g])
                        kb.op(dve, lambda: V.tensor_tensor_scan(out=rv(gim[:]), data0=RT[:], data1=rv(btim[:]),
                                                                initial=i_im, op0=ALU.mult, op1=ALU.add),
                              reads=[t_RT, t_bt] + rd, writes=[t_g])
                        Ec, Es = tab(cosT, 0, SEG), tab(sinT, 0, SEG)
                        kb.op(pool, lambda: G_.tensor_tensor(out=q1[:], in0=Ec, in1=gre[:], op=ALU.mult),
                              reads=[t_tab, t_g], writes=[t_q1])
                        kb.op(pool, lambda: G_.tensor_tensor(out=q2[:], in0=Es, in1=gim[:], op=ALU.mult),
                              reads=[t_tab, t_g], writes=[t_q2])
                        kb.op(pool, lambda: G_.tensor_tensor(out=hbre[:], in0=q1[:], in1=q2[:], op=ALU.subtract),
                              reads=[t_q1, t_q2], writes=[t_hb])
                        kb.op(pool, lambda: G_.tensor_tensor(out=q1[:], in0=Es, in1=gre[:], op=ALU.mult),
                              reads=[t_tab, t_g], writes=[t_q1])
                        kb.op(pool, lambda: G_.tensor_tensor(out=q2[:], in0=Ec, in1=gim[:], op=ALU.mult),
                              reads=[t_tab, t_g], writes=[t_q2])
                        kb.op(pool, lambda: G_.tensor_tensor(out=hbim[:], in0=q1[:], in1=q2[:], op=ALU.add),
                              reads=[t_q1, t_q2], writes=[t_hb])
                        for q in range(NPQ):
                            qs = slice(q * PW, (q + 1) * PW)
                            kb.op(pe, lambda qs=qs: PE_.matmul(py[0:32, qs], lhsT=lC[0][:, col, :], rhs=hbre[:, qs],
                                                               start=True, stop=False),
                                  reads=[t_lC, t_hb], writes=[t_py], pub=False)
                            kb.op(pe, lambda qs=qs: PE_.matmul(py[0:32, qs], lhsT=lC[1][:, col, :], rhs=hbim[:, qs],
                                                               start=False, stop=True),
                                  reads=[t_lC, t_hb], writes=[t_py], pub=(q == NPQ - 1))
                        if d == 1:
                            kb.op(act, lambda: S.copy(out=ysacc[:, ss], in_=py[0:32, 0:SEG]), reads=[t_py], writes=[t_ys])
                        else:
                            kb.op(dve, lambda: V.scalar_tensor_tensor(out=ot_[:], in0=uf[:], scalar=dsk[:, ti:ti + 1],
                                                                      in1=ysacc[:, ss], op0=ALU.mult, op1=ALU.add),
                                  reads=[t_uf, t_p, t_ys], writes=[t_ot])
                            kb.op(dve, lambda: V.tensor_tensor(out=ot_[:], in0=ot_[:], in1=py[0:32, 0:SEG], op=ALU.add),
                                  reads=[t_ot, t_py], writes=[t_ot])
                            kb.dma(pool, YSd[32 * ti:32 * ti + 32, ss], ot_[:], reads=[t_ot])
            ph.close()

        def even_out_phase(j, src, dst):
            ph = Phase(f"e4_{j}")
            TT = 512 if NTOK % 512 == 0 else 256
            gw = ph.sb("gw", [128, 4, 512], BF16); t_gw = Trk("gw")
            wo = ph.sb("wo", [128, 12, 1024], BF16); t_wo = Trk("wo")
            stg = [ph.sb(f"stg{i}", [128, 1024]) for i in range(3)]
            t_stg = [Trk("stg") for _ in range(3)]
            load_weight_rows(ph, gw, t_gw, lambda r: s5_glu_w[j, r * 128:(r + 1) * 128, :], 4, 512, stg, t_stg)
            load_weight_rows(ph, wo, t_wo, lambda r: ev_w_out[j, r * 128:(r + 1) * 128, :], 12, 1024, stg, t_stg)
            gb_ = ph.sb("glub", [128, 4]); t_gb = Trk("glub")
            kb.dma(sp, gb_[:], s5_glu_b[j].rearrange("(c p) -> p c", p=128), writes=[t_gb])
            ys = [ph.sb(f"ys{i}", [128, 4, TT]) for i in range(2)]; t_ysb = [Trk("ys") for _ in range(2)]
            ya = [ph.sb(f"ya{i}", [128, 8, TT], BF16) for i in range(2)]; t_ya = [Trk("ya") for _ in range(2)]
            gf = ph.sb("gf", [128, 4, TT]); t_gf = Trk("gf")
            gbf = ph.sb("gbf", [128, 4, TT], BF16); t_gbf = Trk("gbf")
            sgm = [ph.sb(f"sgm{i}", [128, TT]) for i in range(2)]; t_sgm = [Trk("sgm") for _ in range(2)]
            ymb = ph.sb("ymb", [128, 4, TT], BF16); t_ymb = Trk("ymb")
            xres = [ph.sb(f"xres{i}", [128, TT]) for i in range(4)]; t_xres = [Trk("xres") for _ in range(4)]
            ores = [ph.sb(f"ores{i}", [128, TT]) for i in range(4)]; t_ores = [Trk("ores") for _ in range(4)]
            pg = [ph.ps(f"pg{i}", [128, 512]) for i in range(2)]; t_pg = [Trk("pg") for _ in range(2)]
            po = [ph.ps(f"po{i}", [128, 512]) for i in range(2)]; t_po = [Trk("po") for _ in range(2)]
            srcc = chunked(dram_ap[src]); dstc = chunked(dram_ap[dst])
            ysc = chunked(YSd); yac = chunked(YMd)
            ri = 0
            for i in range(NTOK // TT):
                ts = slice(i * TT, (i + 1) * TT)
                ysb, tys = ys[i % 2], t_ysb[i % 2]
                yab, tya = ya[i % 2], t_ya[i % 2]
                kb.dma(sp, ysb[:], ysc[:, :, ts], writes=[tys])
                kb.dma(sp, yab[:], yac[:, :, ts], writes=[tya])
                for k in range(4):
                    kb.op(act, lambda k=k: S.activation(out=gf[:, k, :], in_=ysb[:, k, :], func=AF.Gelu_apprx_tanh),
                          reads=[tys], writes=[t_gf])
                    kb.op(dve, lambda k=k: V.tensor_copy(out=gbf[:, k, :], in_=gf[:, k, :]), reads=[t_gf], writes=[t_gbf])
                for m in range(4):
                    p, tp = pg[m % 2], t_pg[m % 2]
                    for k in range(4):
                        kb.op(pe, lambda k=k: PE_.matmul(p[:, :TT], lhsT=gw[:, k, m * 128:(m + 1) * 128], rhs=gbf[:, k, :],
                                                         start=(k == 0), stop=(k == 3)),
                              reads=[t_gw, t_gbf], writes=[tp], pub=(k == 3))
                    sg_, tsg = sgm[m % 2], t_sgm[m % 2]
                    kb.op(act, lambda: S.activation(out=sg_[:], in_=p[:, :TT], func=AF.Sigmoid, bias=gb_[:, m:m + 1], scale=1.0),
                          reads=[tp, t_gb], writes=[tsg])
                    kb.op(dve, lambda: V.tensor_tensor(out=ymb[:, m, :], in0=gf[:, m, :], in1=sg_[:], op=ALU.mult),
                          reads=[t_gf, tsg], writes=[t_ymb])
                for m in range(8):
                    ms = slice(m * 128, (m + 1) * 128)
                    p, tp = po[m % 2], t_po[m % 2]
                    xr, txr = xres[ri % 4], t_xres[ri % 4]
                    orr, tor = ores[ri % 4], t_ores[ri % 4]
                    ri += 1
                    kb.dma(sp, xr[:], srcc[:, m, ts], writes=[txr])
                    for k in range(12):
                        rhs = yab[:, k, :] if k < 8 else ymb[:, k - 8, :]
                        kb.op(pe, lambda k=k, rhs=rhs: PE_.matmul(p[:, :TT], lhsT=wo[:, k, ms], rhs=rhs,
                                                                  start=(k == 0), stop=(k == 11)),
                              reads=[t_wo, tya, t_ymb], writes=[tp], pub=(k == 11))
                    kb.op(dve, lambda: V.tensor_tensor(out=orr[:], in0=p[:, :TT], in1=xr[:], op=ALU.add),
                          reads=[tp, txr], writes=[tor])
                    kb.dma(pool, dstc[:, m, ts], orr[:], reads=[tor])
            ph.close()

        def odd_in_phase(j, layer, src):
            ph = Phase(f"o1_{layer}")
            TT = 512 if NTOK % 512 == 0 else 256
            gam, t_gam = load_gamma(ph, mix_norm[layer])
            NO = 5184
            win = ph.sb("win", [128, 8, NO], BF16)
            t_win = Trk("win")
            stg = [ph.sb(f"stg{i}", [128, NO // 3]) for i in range(3)]
            t_stg = [Trk(f"stg{i}") for i in range(3)]
            load_weight_rows(ph, win, t_win, lambda r: od_w_in[j, r * 128:(r + 1) * 128, :], 8, NO, stg, t_stg)
            t_par = Trk("par")
            dtb = ph.sb("dtb", [64, 1])
            kb.dma(sp, dtb[:], ssd_dt_bias[j].rearrange("d (r o) -> (d r) o", o=1), writes=[t_par], st=t_par)
            acol = ph.sb("acol", [64, 1])
            kb.dma(sp, acol[:], ssd_a_log[j].rearrange("d (r o) -> (d r) o", o=1), writes=[t_par], st=t_par)
            kb.op(act, lambda: S.activation(out=acol[:], in_=acol[:], func=AF.Exp), reads=[t_par], writes=[t_par])
            kb.op(dve, lambda: V.tensor_scalar(out=acol[:], in0=acol[:], scalar1=-1.0, scalar2=None, op0=ALU.mult),
                  reads=[t_par], writes=[t_par])
            mk = ph.sb("mk", [64, TT])
            kb.op(dve, lambda: V.memset(mk[:], 1.0), writes=[t_par])
            kb.op(dve, lambda: V.memset(mk[0:32, 0:TT:128], 0.0), writes=[t_par])
            kb.op(dve, lambda: V.memset(mk[32:64, 127:TT:128], 0.0), writes=[t_par])
            nrm = Norm(ph, TT)
            xt = [ph.sb(f"xt{i}", [128, 8, TT]) for i in range(2)]
            t_xt = [Trk("xt") for _ in range(2)]
            xn = [ph.sb(f"xn{i}", [128, 8, TT], BF16) for i in range(2)]
            t_xn = [Trk("xn") for _ in range(2)]
            ev = [ph.sb(f"ev{i}", [128, TT]) for i in range(4)]
            t_ev = [Trk("ev") for _ in range(4)]
            pp = [ph.ps(f"pp{i}", [128, 512]) for i in range(3)]
            t_pp = [Trk("pp") for _ in range(3)]
            pdt = ph.ps("pdt", [64, 512]); t_pdt = Trk("pdt")
            dte = ph.sb("dte", [64, TT]); t_dte = Trk("dte")
            dtv = ph.sb("dtv", [64, TT]); t_dtv = Trk("dtv")
            dta = ph.sb("dta", [64, TT]); t_dta = Trk("dta")
            csb = ph.sb("csb", [64, TT]); t_csb = Trk("csb")
            srcc = chunked(dram_ap[src])
            ei = 0
            for i in range(NTOK // TT):
                ts = slice(i * TT, (i + 1) * TT)
                xb, txb = xt[i % 2], t_xt[i % 2]
                kb.dma(sp, xb[:], srcc[:, :, ts], writes=[txb])
                xnb, txnb = xn[i % 2], t_xn[i % 2]
                nrm.run(xb, txb, gam, t_gam, xnb, txnb)
                for oc in range(40):
                    p, tp = pp[oc % 3], t_pp[oc % 3]
                    for k in range(8):
                        kb.op(pe, lambda k=k: PE_.matmul(p[:, :TT], lhsT=win[:, k, oc * 128:(oc + 1) * 128],
                                                         rhs=xnb[:, k, :], start=(k == 0), stop=(k == 7)),
                              reads=[t_win, txnb], writes=[tp], pub=(k == 7))
                    e, te = ev[ei % 4], t_ev[ei % 4]
                    ei += 1
                    if oc < 16:
                        kb.op(act, lambda: S.activation(out=e[:], in_=p[:, :TT], func=AF.Silu), reads=[tp], writes=[te])
                        dst = ZSd[oc * 128:(oc + 1) * 128, ts]
                    else:
                        kb.op(dve, lambda: V.tensor_copy(out=e[:], in_=p[:, :TT]), reads=[tp], writes=[te])
                        dst = XBCd[(oc - 16) * 128:(oc - 15) * 128, ts]
                    kb.dma(pool, dst, e[:], reads=[te])
                for k in range(8):
                    kb.op(pe, lambda k=k: PE_.matmul(pdt[:, :TT], lhsT=win[:, k, 5120:5184], rhs=xnb[:, k, :],
                                                     start=(k == 0), stop=(k == 7)),
                          reads=[t_win, txnb], writes=[t_pdt], pub=(k == 7))
                kb.op(act, lambda: S.activation(out=dte[:], in_=pdt[:, :TT], func=AF.Exp, bias=dtb[:], scale=1.0),
                      reads=[t_pdt, t_par], writes=[t_dte])
                kb.op(act, lambda: S.activation(out=dtv[:], in_=dte[:], func=AF.Ln, bias=1.0, scale=1.0),
                      reads=[t_dte], writes=[t_dtv])
                kb.op(dve, lambda: V.tensor_scalar(out=dta[:], in0=dtv[:], scalar1=acol[:], scalar2=None, op0=ALU.mult),
                      reads=[t_dtv, t_par], writes=[t_dta])
                kb.op(dve, lambda: V.tensor_tensor_scan(out=csb[0:32, :], data0=mk[0:32, :], data1=dta[0:32, :],
                                                        initial=0.0, op0=ALU.mult, op1=ALU.add),
                      reads=[t_dta, t_par], writes=[t_csb])
                kb.op(dve, lambda: V.tensor_tensor_scan(out=csb[32:64, ::-1], data0=mk[32:64, ::-1], data1=dta[32:64, ::-1],
                                                        initial=0.0, op0=ALU.mult, op1=ALU.add),
                      reads=[t_dta, t_par], writes=[t_csb])
                for d in range(2):
                    kb.dma(pool, DCd[d * 64:d * 64 + 32, ts], csb[d * 32:(d + 1) * 32, :], reads=[t_csb], st=t_csb)
                    kb.dma(pool, DCd[d * 64 + 32:d * 64 + 64, ts], dtv[d * 32:(d + 1) * 32, :], reads=[t_dtv], st=t_dtv)
            ph.close()

        def odd_conv_phase(j):
            ph = Phase(f"o2_{j}")
            t_par = Trk("par")
            cw = ph.sb("cw", [128, 24, 4])
            for k in range(4):
                kb.dma(sp, cw[:, :, k], ssd_conv_w[j, k].rearrange("(c p) -> p c", p=128), writes=[t_par], st=t_par)
            cb = ph.sb("cb", [128, 24])
            kb.dma(sp, cb[:], ssd_conv_b[j].rearrange("(c p) -> p c", p=128), writes=[t_par], st=t_par)
            idb = ph.sb("idb", [128, 128], BF16)
            kb.op(dve, lambda: V.tensor_copy(out=idb[:], in_=ident[:]), writes=[t_par])
            xp = [ph.sb(f"xp{i}", [128, SEG + 3]) for i in range(2)]
            t_xp = [Trk("xp") for _ in range(2)]
            xc = ph.sb("xc", [128, SEG]); t_xc = Trk("xc")
            xs_ = [ph.sb(f"xs{i}", [128, SEG]) for i in range(2)]; t_xs = [Trk("xs") for _ in range(2)]
            xb = [ph.sb(f"xb{i}", [128, SEG], BF16) for i in range(2)]; t_xb = [Trk("xb") for _ in range(2)]
            tr = [ph.sb(f"tr{i}", [128, 4, 128], BF16) for i in range(2)]; t_tr = [Trk("tr") for _ in range(2)]
            ptr = [ph.ps(f"ptr{i}", [128, 512], BF16) for i in range(2)]; t_ptr = [Trk("ptr") for _ in range(2)]
            cnt = 0
            tc_ = 0
            NB = SEG // 128
            for cc in range(24):
                rows = XBCd[cc * 128:(cc + 1) * 128, :]
                for s in range(NSEG):
                    ss = slice(s * SEG, (s + 1) * SEG)
                    xpb, txp = xp[cnt % 2], t_xp[cnt % 2]
                    xsb, txs = xs_[cnt % 2], t_xs[cnt % 2]
                    xbb, txb = xb[cnt % 2], t_xb[cnt % 2]
                    cnt += 1
                    load_halo(xpb, txp, rows, s)
                    conv4(xc, t_xc, xpb, txp, cw[:, cc, :], cb[:, cc:cc + 1], t_par)
                    kb.op(act, lambda: S.activation(out=xsb[:], in_=xc[:], func=AF.Silu), reads=[t_xc], writes=[txs])
                    kb.op(pool, lambda: G_.tensor_copy(out=xbb[:], in_=xsb[:]), reads=[txs], writes=[txb])
                    if cc < 16:
                        kb.dma(pool, XSfd[cc * 128:(cc + 1) * 128, ss], xsb[:], reads=[txs], st=txs)
                    elif cc < 20:
                        kb.dma(pool, BFd[cc - 16, :, ss], xbb[:], reads=[txb], st=txb)
                    else:
                        kb.dma(pool, CFd[cc - 20, :, ss], xbb[:], reads=[txb], st=txb)
                    if cc < 20:
                        for b0 in range(0, NB, 4):
                            pt_, tpt = ptr[tc_ % 2], t_ptr[tc_ % 2]
                            trb, ttr = tr[tc_ % 2], t_tr[tc_ % 2]
                            tc_ += 1
                            nb = min(4, NB - b0)
                            for q in range(nb):
                                kb.op(pe, lambda q=q: PE_.transpose(out=pt_[:, q * 128:(q + 1) * 128],
                                                                    in_=xbb[:, (b0 + q) * 128:(b0 + q + 1) * 128],
                                                                    identity=idb[:]),
                                      reads=[txb, t_par], writes=[tpt], pub=(q == nb - 1))
                            kb.op(dve, lambda: V.tensor_copy(out=trb[:, 0:nb, :],
                                                             in_=pt_[:, 0:nb * 128].rearrange("p (q c) -> p q c", q=nb)),
                                  reads=[tpt], writes=[ttr])
                            t0 = s * SEG + b0 * 128
                            if cc < 16:
                                dst = XStd[t0:t0 + nb * 128, cc * 128:(cc + 1) * 128]
                            else:
                                dst = BTd[t0:t0 + nb * 128, (cc - 16) * 128:(cc - 15) * 128]
                            kb.dma(pool, dst.rearrange("(b p) c -> p b c", p=128), trb[:, 0:nb, :], reads=[ttr], st=ttr)
            ph.close()

        def ssd_core_phase(j):
            ph = Phase(f"o3_{j}")
            t_c = Trk("c")
            sel = ph.sb("sel", [32, 32, 128])
            kb.op(dve, lambda: V.tensor_copy(out=sel[:], in_=ident[0:32, 0:32].unsqueeze(2).to_broadcast([32, 32, 128])),
                  writes=[t_c])
            tri = ph.sb("tri", [128, 2, 128])
            kb.dma(sp, tri[:], tri_d.rearrange("p (a b) -> p a b", a=2), writes=[t_c], st=t_c)
            Sst = ph.sb("Sst", [128, 32, 64]); t_S = Trk("S")
            Sbf = ph.sb("Sbf", [128, 32, 64], BF16); t_Sb = Trk("Sb")
            xst = [ph.sb(f"xst{i}", [128, 2048], BF16) for i in range(2)]; t_xst = [Trk("xst") for _ in range(2)]
            Bfm = [ph.sb(f"Bfm{i}", [128, 4, 128], BF16) for i in range(2)]; t_Bfm = [Trk("Bfm") for _ in range(2)]
            Cfm = [ph.sb(f"Cfm{i}", [128, 4, 128], BF16) for i in range(2)]; t_Cfm = [Trk("Cfm") for _ in range(2)]
            Btk = [ph.sb(f"Btk{i}", [128, 512], BF16) for i in range(2)]; t_Btk = [Trk("Btk") for _ in range(2)]
            dc = [ph.sb(f"dc{i}", [64, 128]) for i in range(2)]; t_dc = [Trk("dc") for _ in range(2)]
            tok = ph.sb("tok", [128, 64]); t_tok = Trk("tok")
            scm = [ph.sb(f"scm{i}", [128, 128]) for i in range(2)]; t_scm = [Trk("scm") for _ in range(2)]
            Eg = [ph.sb(f"Eg{i}", [128, 8, 128]) for i in range(2)]; t_Eg = [Trk("Eg") for _ in range(2)]
            Xg = [ph.sb(f"Xg{i}", [128, 8, 128]) for i in range(2)]; t_Xg = [Trk("Xg") for _ in range(2)]
            Gg = [ph.sb(f"Gg{i}", [128, 8, 128], BF16) for i in range(2)]; t_Gg = [Trk("Gg") for _ in range(2)]
            CEg = [ph.sb(f"CEg{i}", [128, 8, 128], BF16) for i in range(2)]; t_CEg = [Trk("CEg") for _ in range(2)]
            xdt = [ph.sb(f"xdt{i}", [128, 8, 64], BF16) for i in range(2)]; t_xdt = [Trk("xdt") for _ in range(2)]
            wdt = [ph.sb(f"wdt{i}", [128, 8]) for i in range(2)]; t_wdt = [Trk("wdt") for _ in range(2)]
            wx = [ph.sb(f"wx{i}", [128, 8, 64], BF16) for i in range(2)]; t_wx = [Trk("wx") for _ in range(2)]
            yo = [ph.sb(f"yo{i}", [64, 8, 128]) for i in range(2)]; t_yo = [Trk("yo") for _ in range(2)]
            pT = ph.ps("pT", [128, 512]); t_pT = Trk("pT")
            pcs = ph.ps("pcs", [128, 1024]); t_pcs = Trk("pcs")
            psc = ph.ps("psc", [128, 512]); t_psc = Trk("psc")
            py = ph.ps("py", [64, 1024]); t_py = Trk("py")
            pst = ph.ps("pst", [128, 512]); t_pst = Trk("pst")
            NCH = NTOK // 128
            CPS = SEG // 128
            gi = 0
            li = 0
            for d in (1, 0):
                order = list(range(NCH))[::-1] if d == 1 else list(range(NCH))
                lend = 127 if d == 0 else 0
                for ci in order:
                    t0 = ci * 128
                    tsl = slice(t0, t0 + 128)
                    s = t0 // SEG
                    at_bound = (ci % CPS == 0) if d == 0 else (ci % CPS == CPS - 1)
                    if at_bound:
                        first = (s == 0) if d == 0 else (s == NSEG - 1)
                        if first:
                            kb.op(dve, lambda: V.memset(Sst[:], 0.0), writes=[t_S])
                        else:
                            fc = (s - 1) if d == 0 else s
                            kb.op(dve, lambda: V.tensor_scalar(out=Sst[:], in0=Sst[:], scalar1=fl[:, fc:fc + 1], scalar2=None,
                                                               op0=ALU.mult), reads=[t_S, t_fl], writes=[t_S])
                        kb.op(act, lambda: S.copy(out=Sbf[:], in_=Sst[:]), reads=[t_S], writes=[t_Sb])
                    xb_, txb = xst[li % 2], t_xst[li % 2]
                    bf_, tbf = Bfm[li % 2], t_Bfm[li % 2]
                    cf_, tcf = Cfm[li % 2], t_Cfm[li % 2]
                    bt_, tbt = Btk[li % 2], t_Btk[li % 2]
                    dc_, tdc = dc[li % 2], t_dc[li % 2]
                    li += 1
                    kb.dma(sp, xb_[:], XStd[tsl, :], writes=[txb])
                    kb.dma(sp, bf_[:], BFd[:, :, tsl].rearrange("g n s -> n g s"), writes=[tbf])
                    kb.dma(sp, cf_[:], CFd[:, :, tsl].rearrange("g n s -> n g s"), writes=[tcf])
                    kb.dma(sp, bt_[:], BTd[tsl, :], writes=[tbt])
                    kb.dma(sp, dc_[:], DCd[d * 64:(d + 1) * 64, tsl], writes=[tdc])
                    kb.op(pe, lambda: PE_.transpose(out=pT[:, 0:64], in_=dc_[:, :], identity=ident[0:64, 0:64]),
                          reads=[tdc, t_id], writes=[t_pT])
                    kb.op(act, lambda: S.copy(out=tok[:], in_=pT[:, 0:64]), reads=[t_pT], writes=[t_tok])
                    cs_tok = tok[:, 0:32]
                    dt_tok = tok[:, 32:64]
                    for g in range(4):
                        b2 = gi % 2
                        gi += 1
                        hs = slice(8 * g, 8 * g + 8)
                        for r in range(8):
                            kb.op(pe, lambda r=r: PE_.matmul(pcs[:, r * 128:(r + 1) * 128], lhsT=sel[:, 8 * g + r, :],
                                                             rhs=dc_[0:32, :], start=True, stop=True),
                                  reads=[t_c, tdc], writes=[t_pcs], pub=(r == 7))
                        kb.op(pe, lambda: PE_.matmul(psc[:, 0:128], lhsT=bf_[:, g, :], rhs=cf_[:, g, :], start=True, stop=True),
                              reads=[tbf, tcf], writes=[t_psc])
                        kb.op(dve, lambda: V.tensor_tensor(out=scm[b2][:], in0=psc[:, 0:128], in1=tri[:, d, :], op=ALU.mult),
                              reads=[t_psc, t_c], writes=[t_scm[b2]])
                        pcs3 = pcs[:, :].rearrange("p (r l) -> p r l", r=8)
                        kb.op(dve, lambda: V.tensor_tensor(out=Eg[b2][:], in0=pcs3,
                                                           in1=cs_tok[:, hs].unsqueeze(2).to_broadcast([128, 8, 128]),
                                                           op=ALU.subtract),
                              reads=[t_pcs, t_tok], writes=[t_Eg[b2]])
                        kb.op(act, lambda: S.activation(out=Eg[b2][:], in_=Eg[b2][:], func=AF.Exp),
                              reads=[t_Eg[b2]], writes=[t_Eg[b2]])
                        kb.op(act, lambda: S.activation(out=Xg[b2][:], in_=pcs3, func=AF.Exp),
                              reads=[t_pcs], writes=[t_Xg[b2]])
                        kb.op(dve, lambda: V.scalar_tensor_tensor(out=Gg[b2][:], in0=Eg[b2][:], scalar=1.0,
                                                                  in1=scm[b2][:].unsqueeze(1).to_broadcast([128, 8, 128]),
                                                                  op0=ALU.min, op1=ALU.mult),
                              reads=[t_Eg[b2], t_scm[b2]], writes=[t_Gg[b2]])
                        kb.op(pool, lambda: G_.tensor_tensor(out=CEg[b2][:], in0=Xg[b2][:],
                                                             in1=cf_[:, g, :].unsqueeze(1).to_broadcast([128, 8, 128]),
                                                             op=ALU.mult),
                              reads=[t_Xg[b2], tcf], writes=[t_CEg[b2]])
                        xg = xb_[:, g * 512:(g + 1) * 512].rearrange("p (r q) -> p r q", r=8)
                        kb.op(pool, lambda: G_.tensor_tensor(out=xdt[b2][:], in0=xg,
                                                             in1=dt_tok[:, hs].unsqueeze(2).to_broadcast([128, 8, 64]),
                                                             op=ALU.mult),
                              reads=[txb, t_tok], writes=[t_xdt[b2]])
                        kb.op(dve, lambda: V.tensor_tensor(out=wdt[b2][:], in0=Eg[b2][:, :, lend], in1=dt_tok[:, hs], op=ALU.mult),
                              reads=[t_Eg[b2], t_tok], writes=[t_wdt[b2]])
                        kb.op(pool, lambda: G_.tensor_tensor(out=wx[b2][:], in0=xg,
                                                             in1=wdt[b2][:].unsqueeze(2).to_broadcast([128, 8, 64]),
                                                             op=ALU.mult),
                              reads=[txb, t_wdt[b2]], writes=[t_wx[b2]])
                        for r in range(8):
                            kb.op(pe, lambda r=r: PE_.matmul(py[:, r * 128:(r + 1) * 128], lhsT=xdt[b2][:, r, :],
                                                             rhs=Gg[b2][:, r, :], start=True, stop=False),
                                  reads=[t_xdt[b2], t_Gg[b2]], writes=[t_py], pub=False)
                            kb.op(pe, lambda r=r: PE_.matmul(py[:, r * 128:(r + 1) * 128], lhsT=Sbf[:, 8 * g + r, :],
                                                             rhs=CEg[b2][:, r, :], start=False, stop=True),
                                  reads=[t_Sb, t_CEg[b2]], writes=[t_py], pub=(r == 7))
                        kb.op(act, lambda: S.copy(out=yo[b2][:], in_=py[:, :].rearrange("p (r l) -> p r l", r=8)),
                              reads=[t_py], writes=[t_yo[b2]])
                        kb.dma(pool, Yd[d, g * 512:(g + 1) * 512, tsl].rearrange("(r p) l -> p r l", p=64), yo[b2][:],
                               reads=[t_yo[b2]], st=t_yo[b2])
                        kb.op(pe, lambda: PE_.matmul(pst[:, :], lhsT=bt_[:, g * 128:(g + 1) * 128],
                                                     rhs=wx[b2][:].rearrange("p r q -> p (r q)"), start=True, stop=True),
                              reads=[tbt, t_wx[b2]], writes=[t_pst])
                        Sg = Sst[:, hs, :]
                        kb.op(dve, lambda: V.tensor_tensor(out=Sg, in0=Sg,
                                                           in1=Xg[b2][:, :, lend].unsqueeze(2).to_broadcast([128, 8, 64]),
                                                           op=ALU.mult),
                              reads=[t_S, t_Xg[b2], t_py], writes=[t_S])
                        kb.op(dve, lambda: V.tensor_tensor(out=Sg, in0=Sg, in1=pst[:, :].rearrange("p (r q) -> p r q", r=8),
                                                           op=ALU.add),
                              reads=[t_S, t_pst], writes=[t_S])
                        kb.op(act, lambda: S.copy(out=Sbf[:, hs, :], in_=Sg), reads=[t_S, t_py], writes=[t_Sb])
            ph.close()

        def odd_out_phase(j, src, dst):
            ph = Phase(f"o4_{j}")
            TT = 512 if NTOK % 512 == 0 else 256
            wo = ph.sb("wo", [128, 16, 1024], BF16); t_wo = Trk("wo")
            stg = [ph.sb(f"stg{i}", [128, 1024]) for i in range(3)]
            t_stg = [Trk("stg") for _ in range(3)]
            load_weight_rows(ph, wo, t_wo, lambda r: od_w_out[j, r * 128:(r + 1) * 128, :], 16, 1024, stg, t_stg)
            t_par = Trk("par")
            dcol = ph.sb("dcol", [128, 16])
            for hh in range(2):
                kb.dma(sp, dcol[hh * 64:(hh + 1) * 64, :],
                       ssd_d[j].rearrange("(c h) -> h c", h=2)[hh].partition_broadcast(64), writes=[t_par], st=t_par)
            nw = ph.sb("nw", [128, 16])
            kb.dma(sp, nw[:], ssd_norm[j].rearrange("(c p) -> p c", p=128), writes=[t_par], st=t_par)
            yf = [ph.sb(f"yf{i}", [128, 4, TT]) for i in range(2)]; t_yf = [Trk("yf") for _ in range(2)]
            ybk = [ph.sb(f"ybk{i}", [128, 4, TT]) for i in range(2)]; t_ybk = [Trk("ybk") for _ in range(2)]
            xsg = [ph.sb(f"xsg{i}", [128, 4, TT]) for i in range(2)]; t_xsg = [Trk("xsg") for _ in range(2)]
            zsg = [ph.sb(f"zsg{i}", [128, 4, TT]) for i in range(2)]; t_zsg = [Trk("zsg") for _ in range(2)]
            sqb = ph.sb("sqb", [128, 4, TT], BF16); t_sqb = Trk("sqb")
            rs = ph.sb("rs", [128, TT]); t_rs = Trk("rs")
            rs2 = ph.sb("rs2", [128, TT]); t_rs2 = Trk("rs2")
            ynb = ph.sb("ynb", [128, 16, TT], BF16); t_ynb = Trk("ynb")
            xres = [ph.sb(f"xres{i}", [128, TT]) for i in range(4)]; t_xres = [Trk("xres") for _ in range(4)]
            ores = [ph.sb(f"ores{i}", [128, TT]) for i in range(4)]; t_ores = [Trk("ores") for _ in range(4)]
            pn = ph.ps("pn", [128, 512]); t_pn = Trk("pn")
            po = [ph.ps(f"po{i}", [128, 512]) for i in range(2)]; t_po = [Trk("po") for _ in range(2)]
            srcc = chunked(dram_ap[src]); dstc = chunked(dram_ap[dst])
            Yfc = chunked(Yd[0]); Ybc = chunked(Yd[1]); XSc = chunked(XSfd); ZSc = chunked(ZSd)
            ri = 0
            gi = 0
            for i in range(NTOK // TT):
                ts = slice(i * TT, (i + 1) * TT)
                for g in range(4):
                    b2 = gi % 2
                    gi += 1
                    cs4 = slice(4 * g, 4 * g + 4)
                    kb.dma(sp, yf[b2][:], Yfc[:, cs4, ts], writes=[t_yf[b2]])
                    kb.dma(sp, ybk[b2][:], Ybc[:, cs4, ts], writes=[t_ybk[b2]])
                    kb.dma(sp, xsg[b2][:], XSc[:, cs4, ts], writes=[t_xsg[b2]])
                    kb.dma(sp, zsg[b2][:], ZSc[:, cs4, ts], writes=[t_zsg[b2]])
                    kb.op(pool, lambda: G_.tensor_tensor(out=yf[b2][:], in0=yf[b2][:], in1=ybk[b2][:], op=ALU.add),
                          reads=[t_yf[b2], t_ybk[b2]], writes=[t_yf[b2]])
                    for c in range(4):
                        ch = 4 * g + c
                        kb.op(dve, lambda c=c, ch=ch: V.scalar_tensor_tensor(out=yf[b2][:, c, :], in0=xsg[b2][:, c, :],
                                                                             scalar=dcol[:, ch:ch + 1], in1=yf[b2][:, c, :],
                                                                             op0=ALU.mult, op1=ALU.add),
                              reads=[t_xsg[b2], t_par, t_yf[b2]], writes=[t_yf[b2]])
                    kb.op(pool, lambda: G_.tensor_tensor(out=yf[b2][:], in0=yf[b2][:], in1=zsg[b2][:], op=ALU.mult),
                          reads=[t_yf[b2], t_zsg[b2]], writes=[t_yf[b2]])
                    kb.op(act, lambda: S.activation(out=sqb[:], in_=yf[b2][:], func=AF.Square),
                          reads=[t_yf[b2]], writes=[t_sqb])
                    for c in range(4):
                        kb.op(pe, lambda c=c: PE_.matmul(pn[:, :TT], lhsT=ones_bf[:], rhs=sqb[:, c, :],
                                                         start=(c == 0), stop=(c == 3)),
                              reads=[t_const, t_sqb], writes=[t_pn], pub=(c == 3))
                    kb.op(act, lambda: S.activation(out=rs[:], in_=pn[:, :TT], func=AF.Sqrt, scale=1.0 / 512.0, bias=eps_t[:]),
                          reads=[t_pn, t_const], writes=[t_rs])
                    kb.op(dve, lambda: V.reciprocal(out=rs2[:], in_=rs[:]), reads=[t_rs], writes=[t_rs2])
                    for c in range(4):
                        ch = 4 * g + c
                        kb.op(dve, lambda c=c, ch=ch: V.scalar_tensor_tensor(out=ynb[:, ch, :], in0=yf[b2][:, c, :],
                                                                             scalar=nw[:, ch:ch + 1], in1=rs2[:],
                                                                             op0=ALU.mult, op1=ALU.mult),
                              reads=[t_yf[b2], t_par, t_rs2], writes=[t_ynb])
                for m in range(8):
                    ms = slice(m * 128, (m + 1) * 128)
                    p, tp = po[m % 2], t_po[m % 2]
                    xr, txr = xres[ri % 4], t_xres[ri % 4]
                    orr, tor = ores[ri % 4], t_ores[ri % 4]
                    ri += 1
                    kb.dma(sp, xr[:], srcc[:, m, ts], writes=[txr])
                    for k in range(16):
                        kb.op(pe, lambda k=k: PE_.matmul(p[:, :TT], lhsT=wo[:, k, ms], rhs=ynb[:, k, :],
                                                         start=(k == 0), stop=(k == 15)),
                              reads=[t_wo, t_ynb], writes=[tp], pub=(k == 15))
                    kb.op(dve, lambda: V.tensor_tensor(out=orr[:], in0=p[:, :TT], in1=xr[:], op=ALU.add),
                          reads=[tp, txr], writes=[tor])
                    kb.dma(pool, dstc[:, m, ts], orr[:], reads=[tor])
            ph.close()

        def odd_mixer(j, layer, src, dst):
            odd_in_phase(j, layer, src)
            odd_conv_phase(j)
            ssd_core_phase(j)
            odd_out_phase(j, src, dst)

        cur = "xT"
        nxt = ["XA", "XB"]
        ni = 0
        for layer in layers:
            j = layer // 2
            if inc("ffn1"):
                dst = nxt[ni % 2]; ni += 1
                ffn_phase(layer, 0, cur, dst)
                cur = dst
            if inc("mixer"):
                dst = nxt[ni % 2]; ni += 1
                if layer % 2 == 0:
                    even_in_phase(j, layer, cur)
                    lru_phase(j)
                    s5_phase(j)
                    even_out_phase(j, cur, dst)
                else:
                    odd_mixer(j, layer, cur, dst)
                cur = dst
            if inc("ffn2"):
                dst = nxt[ni % 2]; ni += 1
                ffn_phase(layer, 1, cur, dst)
                cur = dst
        final_phase(cur)
        kb.finish()
    return nc


N_CORES = 8
NON_WEIGHT = ("x_prompt", "x_sample")


def core_inputs(inputs):
    m = {}
    for k, v in inputs.items():
        if k in NON_WEIGHT:
            continue
        m[k] = np.ascontiguousarray(np.asarray(v, dtype=np.float32))
    m["ident"] = np.eye(128, dtype=np.float32)
    u = np.triu(np.ones((128, 128), np.float32))
    m["tri"] = np.ascontiguousarray(np.concatenate([u, u.T], axis=1))
    return m


def kernel(**inputs):
    xp = np.asarray(inputs["x_prompt"], dtype=np.float32)
    xs = np.asarray(inputs["x_sample"], dtype=np.float32)
    SEG = 2048
    NTOK = 12288
    streams = []
    flags = []
    for c in range(N_CORES):
        if c < 4:
            parts = [xp[c], xs[2 * c], xs[2 * c + 1]]
            fl = [1, 1, 1, 0, 0]
        else:
            b = 8 + 6 * (c - 4)
            parts = [xs[b + q] for q in range(6)]
            fl = [0, 0, 0, 0, 0]
        tok = np.concatenate(parts, axis=0)
        streams.append(np.ascontiguousarray(tok.T))
        f = np.zeros((128, 8), np.float32)
        f[:, :5] = np.asarray(fl, np.float32)[None, :]
        flags.append(f)
    nc = build_program(NTOK, SEG, layers=[0, 1, 2, 3])
    wmap = core_inputs(inputs)
    in_maps = []
    for c in range(N_CORES):
        m = dict(wmap)
        m["xT"] = streams[c]
        m["flags"] = flags[c]
        in_maps.append(m)
    res = run_bass_kernel_spmd(nc, in_maps, core_ids=list(range(N_CORES)))
    outs = [np.ascontiguousarray(res.results[c]["yT"].T) for c in range(N_CORES)]
    y_prompt = np.stack([outs[c][:8192] for c in range(4)], axis=0)
    ys = [None] * 32
    for c in range(4):
        ys[2 * c] = outs[c][8192:8192 + 2048]
        ys[2 * c + 1] = outs[c][8192 + 2048:]
    for c in range(4, 8):
        b = 8 + 6 * (c - 4)
        for q in range(6):
            ys[b + q] = outs[c][q * 2048:(q + 1) * 2048]
    y_sample = np.stack(ys, axis=0)
    return (y_prompt.astype(np.float32), y_sample.astype(np.float32))
```

```python
import contextlib
import numpy as np
import concourse.bass as bass
import concourse.mybir as mybir
from concourse.bass_utils import run_bass_kernel_spmd

F32 = mybir.dt.float32
BF16 = mybir.dt.bfloat16
ALU = mybir.AluOpType
AF = mybir.ActivationFunctionType

D = 1024
DFF = 2816
NFC = DFF // 128
EPS = 1e-6


class Trk:
    __slots__ = ("w", "r", "dsem", "dcnt", "name", "dq")

    def __init__(self, name=""):
        self.w = {}
        self.r = {}
        self.dsem = None
        self.dcnt = 0
        self.name = name


class Eng:
    def __init__(self, name, eng, sem):
        self.name = name
        self.eng = eng
        self.sem = sem
        self.cnt = 0
        self.waited = {}


class KB:
    def __init__(self, nc, es):
        self.nc = nc
        self.es = es
        self.nsem = 0
        self.pe = Eng("pe", nc.tensor, self.sem("pe"))
        self.dve = Eng("dve", nc.vector, self.sem("dve"))
        self.act = Eng("act", nc.scalar, self.sem("act"))
        self.pool = Eng("pool", nc.gpsimd, self.sem("pool"))
        self.sp = Eng("sp", nc.sync, self.sem("sp"))
        self.out_tokens = []
        self.dtrks = []
        self.free_dsems = {}

    def sem(self, name):
        self.nsem += 1
        s = self.es.enter_context(self.nc.semaphore(f"s{self.nsem}_{name}"))
        return s

    def sb(self, name, shape, dt):
        return self.es.enter_context(self.nc.sbuf_tensor(name, shape, dt))

    def ps(self, name, shape, dt=F32):
        return self.es.enter_context(self.nc.psum_tensor(name, shape, dt))

    def _waits(self, E, reads, writes):
        need = {}
        for t in reads:
            for s, v in t.w.items():
                if need.get(s, 0) < v:
                    need[s] = v
        for t in writes:
            for s, v in t.w.items():
                if s is E.sem:
                    continue
                if need.get(s, 0) < v:
                    need[s] = v
            for s, v in t.r.items():
                if s is E.sem:
                    continue
                if need.get(s, 0) < v:
                    need[s] = v
        for s, v in need.items():
            if s is E.sem and v > E.cnt:
                continue
            if E.waited.get(id(s), 0) < v:
                E.eng.wait_ge(s, v)
                E.waited[id(s)] = v

    def op(self, E, make, reads=(), writes=(), pub=True):
        self._waits(E, reads, writes)
        inst = make()
        tokv = E.cnt + 1
        if pub:
            inst.then_inc(E.sem, 1)
            E.cnt += 1
        for t in reads:
            if t.r.get(E.sem, 0) < tokv:
                t.r[E.sem] = tokv
        for t in writes:
            t.w = {E.sem: tokv}
            t.r = {}
        return inst

    def dma(self, Q, out, in_, reads=(), writes=(), st=None, is_out=False):
        if st is None:
            st = writes[0] if writes else reads[0]
        if st.dsem is None:
            fl_ = self.free_dsems.setdefault(Q.name, [])
            if fl_:
                st.dsem, st.dcnt = fl_.pop()
            else:
                st.dsem = self.sem("d" + Q.name)
            st.dq = Q.name
            self.dtrks.append(st)
        assert st.dq == Q.name, "one DMA queue per tracker semaphore"
        self._waits(Q, reads, writes)
        inst = Q.eng.dma_start(out=out, in_=in_)
        inst.then_inc(st.dsem, 16)
        st.dcnt += 16
        for t in reads:
            if t.r.get(st.dsem, 0) < st.dcnt:
                t.r[st.dsem] = st.dcnt
        for t in writes:
            t.w = {st.dsem: st.dcnt}
            t.r = {}
        if is_out:
            self.out_tokens.append((st.dsem, st.dcnt))
        return inst

    def barrier(self):
        engs = [self.pe, self.dve, self.act, self.pool, self.sp]
        for E in engs:
            for O in engs:
                if O is E or O.cnt == 0:
                    continue
                if E.waited.get(id(O.sem), 0) < O.cnt:
                    E.eng.wait_ge(O.sem, O.cnt)
                    E.waited[id(O.sem)] = O.cnt
            for t in self.dtrks:
                if t.dcnt and E.waited.get(id(t.dsem), 0) < t.dcnt:
                    E.eng.wait_ge(t.dsem, t.dcnt)
                    E.waited[id(t.dsem)] = t.dcnt

    def release_dsems(self):
        for t in self.dtrks:
            self.free_dsems.setdefault(t.dq, []).append((t.dsem, t.dcnt))
            t.dsem = None
            t.w = {}
            t.r = {}
        self.dtrks = []

    def finish(self):
        need = {}
        for s, v in self.out_tokens:
            if need.get(id(s), (None, 0))[1] < v:
                need[id(s)] = (s, v)
        for s, v in need.values():
            self.sp.eng.wait_ge(s, v)


def build_program(NTOK, SEG, layers, T=256, phases=None):
    nc = bass.Bass("TRN2", target_bir_lowering=False)
    NSEG = NTOK // SEG
    PW = min(512, SEG)
    NPQ = SEG // PW

    def inc(name):
        return phases is None or name in phases

    def din(name, shape):
        return nc.dram_tensor(name, list(shape), F32, kind="ExternalInput").ap()

    def dscr(name, shape, dt=F32):
        return nc.dram_tensor(name, list(shape), dt, kind="Internal").ap()

    xT = din("xT", [D, NTOK])
    flags_d = din("flags", [128, 8])
    ident_d = din("ident", [128, 128])
    ffn_norm = [din("ffn1_norm", [4, D]), din("ffn2_norm", [4, D])]
    ffn_wg = [din("ffn1_w_gate", [4, D, DFF]), din("ffn2_w_gate", [4, D, DFF])]
    ffn_wu = [din("ffn1_w_up", [4, D, DFF]), din("ffn2_w_up", [4, D, DFF])]
    ffn_wd = [din("ffn1_w_down", [4, DFF, D]), din("ffn2_w_down", [4, DFF, D])]
    mix_norm = din("mix_norm", [4, D])
    final_norm = din("final_norm", [D])
    ev_w_in = din("ev_w_in", [2, D, 2560])
    lru_conv_w = din("lru_conv_w", [2, 4, 1024])
    lru_conv_b = din("lru_conv_b", [2, 1024])
    lru_w_a = din("lru_w_a", [2, 2, 16, 64, 64])
    lru_b_a = din("lru_b_a", [2, 2, 1024])
    lru_w_x = din("lru_w_x", [2, 2, 16, 64, 64])
    lru_b_x = din("lru_b_x", [2, 2, 1024])
    lru_lam = din("lru_lam", [2, 2, 1024])
    s5_lam_re = din("s5_lam_re", [2, 2, 32, 64])
    s5_lam_im = din("s5_lam_im", [2, 2, 32, 64])
    s5_log_dt = din("s5_log_dt", [2, 2, 32])
    s5_b_re = din("s5_b_re", [2, 2, 32, 64, 16])
    s5_b_im = din("s5_b_im", [2, 2, 32, 64, 16])
    s5_c_re = din("s5_c_re", [2, 2, 32, 16, 64])
    s5_c_im = din("s5_c_im", [2, 2, 32, 16, 64])
    s5_d = din("s5_d", [2, 512])
    s5_glu_w = din("s5_glu_w", [2, 512, 512])
    s5_glu_b = din("s5_glu_b", [2, 512])
    ev_w_out = din("ev_w_out", [2, 1536, 1024])
    od_w_in = din("od_w_in", [2, D, 5184])
    ssd_conv_w = din("ssd_conv_w", [2, 4, 3072])
    ssd_conv_b = din("ssd_conv_b", [2, 3072])
    ssd_dt_bias = din("ssd_dt_bias", [2, 2, 32])
    ssd_a_log = din("ssd_a_log", [2, 2, 32])
    ssd_d = din("ssd_d", [2, 32])
    ssd_norm = din("ssd_norm", [2, 2048])
    od_w_out = din("od_w_out", [2, 2048, 1024])
    tri_d = din("tri", [128, 256])
    yT = nc.dram_tensor("yT", [D, NTOK], F32, kind="ExternalOutput").ap()
    XA = dscr("XA", [D, NTOK])
    XB = dscr("XB", [D, NTOK])
    Gd = dscr("Gd", [1024, NTOK])
    XRd = dscr("XRd", [1024, NTOK])
    Ud = dscr("Ud", [512, NTOK])
    YMd = dscr("YMd", [1024, NTOK], BF16)
    YSd = dscr("YSd", [512, NTOK])
    ZSd = dscr("ZSd", [2048, NTOK])
    XBCd = dscr("XBCd", [3072, NTOK])
    DCd = dscr("DCd", [128, NTOK])
    XSfd = dscr("XSfd", [2048, NTOK])
    XStd = dscr("XStd", [NTOK, 2048], BF16)
    BFd = dscr("BFd", [4, 128, NTOK], BF16)
    CFd = dscr("CFd", [4, 128, NTOK], BF16)
    BTd = dscr("BTd", [NTOK, 512], BF16)
    Yd = dscr("Yd", [2, 2048, NTOK])
    dram_ap = {"xT": xT, "XA": XA, "XB": XB, "yT": yT}

    with contextlib.ExitStack() as es:
        es.enter_context(nc.allow_non_contiguous_dma(reason="small parameter loads"))
        kb = KB(nc, es)
        pe, dve, act, pool, sp = kb.pe, kb.dve, kb.act, kb.pool, kb.sp
        V, S, G_, PE_ = nc.vector, nc.scalar, nc.gpsimd, nc.tensor

        ones_bf = kb.sb("ones_bf", [128, 128], BF16)
        t_const = Trk("const")
        kb.op(dve, lambda: V.memset(ones_bf[:], 1.0), writes=[t_const])
        eps_t = kb.sb("eps_t", [128, 1], F32)
        kb.op(dve, lambda: V.memset(eps_t[:], EPS), writes=[t_const])
        one_t = kb.sb("one_t", [128, 1], F32)
        kb.op(dve, lambda: V.memset(one_t[:], 1.0), writes=[t_const])
        hpi_t = kb.sb("hpi_t", [128, 1], F32)
        kb.op(dve, lambda: V.memset(hpi_t[:], float(np.pi / 2)), writes=[t_const])
        fl = kb.sb("fl_sb", [128, 8], F32)
        t_fl = Trk("fl")
        kb.dma(sp, fl[:], flags_d[:, :], writes=[t_fl])
        ident = kb.sb("ident_sb", [128, 128], F32)
        t_id = Trk("ident")
        kb.dma(sp, ident[:], ident_d[:, :], writes=[t_id])
        kb.barrier()
        kb.release_dsems()

        def chunked(ap):
            return ap.rearrange("(c p) t -> p c t", p=128)

        class Phase:
            def __init__(self, tag):
                self.tag = tag
                self.pes = contextlib.ExitStack()
                self.n = 0

            def sb(self, name, shape, dt=F32):
                self.n += 1
                return self.pes.enter_context(nc.sbuf_tensor(f"{self.tag}_{name}_{self.n}", list(shape), dt))

            def ps(self, name, shape, dt=F32):
                self.n += 1
                return self.pes.enter_context(nc.psum_tensor(f"{self.tag}_{name}_{self.n}", list(shape), dt))

            def close(self):
                kb.barrier()
                kb.release_dsems()
                self.pes.close()

        cast_rr = [0]

        def cast(out, in_, reads, writes, engs=None):
            engs = engs or [pool, dve, act]
            E = engs[cast_rr[0] % len(engs)]
            cast_rr[0] += 1
            if E is act:
                kb.op(act, lambda: S.copy(out=out, in_=in_), reads=reads, writes=writes)
            elif E is dve:
                kb.op(dve, lambda: V.tensor_copy(out=out, in_=in_), reads=reads, writes=writes)
            else:
                kb.op(pool, lambda: G_.tensor_copy(out=out, in_=in_), reads=reads, writes=writes)

        def load_weight_rows(ph, wdst, t_w, src_rows_fn, nrows, ncols, stg, t_stg):
            W = stg[0].shape[1]
            si = 0
            for r in range(nrows):
                src = src_rows_fn(r)
                for c0 in range(0, ncols, W):
                    c1 = min(ncols, c0 + W)
                    b = si % len(stg)
                    si += 1
                    kb.dma(sp, stg[b][:, :c1 - c0], src[:, c0:c1], writes=[t_stg[b]])
                    cast(wdst[:, r, c0:c1], stg[b][:, :c1 - c0], [t_stg[b]], [t_w])

        class Norm:
            def __init__(self, ph, TT):
                self.TT = TT
                self.sq = ph.sb("sq", [128, 8, TT], BF16)
                self.t_sq = Trk("sq")
                self.rs = ph.sb("rs", [128, TT])
                self.t_rs = Trk("rs")
                self.rs2 = ph.sb("rs2", [128, TT])
                self.t_rs2 = Trk("rs2")
                self.p_n = ph.ps("p_n", [128, 512])
                self.t_pn = Trk("pn")

            def run(self, xt, t_xt, gam, t_gam, out, t_out):
                TT = self.TT
                sq, rs, rs2, p_n = self.sq, self.rs, self.rs2, self.p_n
                for k in range(8):
                    kb.op(act, lambda k=k: S.activation(out=sq[:, k, :], in_=xt[:, k, :], func=AF.Square),
                          reads=[t_xt], writes=[self.t_sq])
                for k in range(8):
                    kb.op(pe, lambda k=k: PE_.matmul(p_n[:, :TT], lhsT=ones_bf[:], rhs=sq[:, k, :],
                                                     start=(k == 0), stop=(k == 7)),
                          reads=[t_const, self.t_sq], writes=[self.t_pn], pub=(k == 7))
                kb.op(act, lambda: S.activation(out=rs[:], in_=p_n[:, :TT], func=AF.Sqrt,
                                                scale=1.0 / D, bias=eps_t[:]),
                      reads=[self.t_pn, t_const], writes=[self.t_rs])
                kb.op(dve, lambda: V.reciprocal(out=rs2[:], in_=rs[:]), reads=[self.t_rs], writes=[self.t_rs2])
                for k in range(8):
                    kb.op(dve, lambda k=k: V.scalar_tensor_tensor(
                        out=out[:, k, :], in0=xt[:, k, :], scalar=gam[:, k:k + 1], in1=rs2[:],
                        op0=ALU.mult, op1=ALU.mult), reads=[t_xt, t_gam, self.t_rs2], writes=[t_out])

        def load_gamma(ph, src_vec):
            gam = ph.sb("gam", [128, 8])
            t_gam = Trk("gam")
            kb.dma(sp, gam[:], src_vec.rearrange("(c p) -> p c", p=128), writes=[t_gam])
            return gam, t_gam

        def ffn_phase(layer, which, src, dst):
            ph = Phase(f"f{layer}{which}")
            NT = NTOK // T
            gam, t_gam = load_gamma(ph, ffn_norm[which][layer])
            wg = ph.sb("wg", [128, 8, DFF], BF16)
            wu = ph.sb("wu", [128, 8, DFF], BF16)
            wd = ph.sb("wd", [128, NFC, D], BF16)
            t_wg, t_wu, t_wd = Trk("wg"), Trk("wu"), Trk("wd")
            HW = DFF // 2
            stg = [ph.sb(f"stg{i}", [128, HW]) for i in range(3)]
            t_stg = [Trk(f"stg{i}") for i in range(3)]
            load_weight_rows(ph, wg, t_wg, lambda r: ffn_wg[which][layer, r * 128:(r + 1) * 128, :], 8, DFF, stg, t_stg)
            load_weight_rows(ph, wu, t_wu, lambda r: ffn_wu[which][layer, r * 128:(r + 1) * 128, :], 8, DFF, stg, t_stg)
            load_weight_rows(ph, wd, t_wd, lambda r: ffn_wd[which][layer, r * 128:(r + 1) * 128, :], NFC, D, stg, t_stg)
            nrm = Norm(ph, T)
            xt = ph.sb("xt", [128, 8, T])
            t_xt = Trk("xt")
            xn = [ph.sb(f"xn{i}", [128, 8, T], BF16) for i in range(2)]
            t_xn = [Trk(f"xn{i}") for i in range(2)]
            h = ph.sb("h", [128, NFC, T], BF16)
            t_h = Trk("h")
            sg = [ph.sb(f"sg{i}", [128, T]) for i in range(2)]
            t_sg = [Trk(f"sg{i}") for i in range(2)]
            xres = [ph.sb(f"xres{i}", [128, T]) for i in range(4)]
            t_xres = [Trk(f"xres{i}") for i in range(4)]
            ores = [ph.sb(f"ores{i}", [128, T]) for i in range(4)]
            t_ores = [Trk(f"ores{i}") for i in range(4)]
            p_g = [ph.ps(f"p_g{i}", [128, 512]) for i in range(2)]
            t_pg = [Trk(f"pg{i}") for i in range(2)]
            p_u = [ph.ps(f"p_u{i}", [128, 512]) for i in range(2)]
            t_pu = [Trk(f"pu{i}") for i in range(2)]
            p_d = [ph.ps(f"p_d{i}", [128, 512]) for i in range(2)]
            t_pd = [Trk(f"pd{i}") for i in range(2)]
            srcc = chunked(dram_ap[src])
            dstc = chunked(dram_ap[dst])
            ri = 0
            for i in range(NT):
                ts = slice(i * T, (i + 1) * T)
                kb.dma(sp, xt[:], srcc[:, :, ts], writes=[t_xt])
                xb, txb = xn[i % 2], t_xn[i % 2]
                nrm.run(xt, t_xt, gam, t_gam, xb, txb)
                for j in range(NFC):
                    js = slice(j * 128, (j + 1) * 128)
                    pg, tpg = p_g[j % 2], t_pg[j % 2]
                    pu, tpu = p_u[j % 2], t_pu[j % 2]
                    for k in range(8):
                        kb.op(pe, lambda k=k: PE_.matmul(pg[:, :T], lhsT=wg[:, k, js], rhs=xb[:, k, :],
                                                         start=(k == 0), stop=(k == 7)),
                              reads=[t_wg, txb], writes=[tpg], pub=(k == 7))
                    for k in range(8):
                        kb.op(pe, lambda k=k: PE_.matmul(pu[:, :T], lhsT=wu[:, k, js], rhs=xb[:, k, :],
                                                         start=(k == 0), stop=(k == 7)),
                              reads=[t_wu, txb], writes=[tpu], pub=(k == 7))
                    sgb, tsg = sg[j % 2], t_sg[j % 2]
                    kb.op(act, lambda: S.activation(out=sgb[:], in_=pg[:, :T], func=AF.Silu),
                          reads=[tpg], writes=[tsg])
                    kb.op(dve, lambda: V.tensor_tensor(out=h[:, j, :], in0=sgb[:], in1=pu[:, :T], op=ALU.mult),
                          reads=[tsg, tpu], writes=[t_h])
                for m in range(8):
                    ms = slice(m * 128, (m + 1) * 128)
                    pd, tpd = p_d[m % 2], t_pd[m % 2]
                    xr, txr = xres[ri % 4], t_xres[ri % 4]
                    orr, tor = ores[ri % 4], t_ores[ri % 4]
                    ri += 1
                    kb.dma(sp, xr[:], srcc[:, m, ts], writes=[txr])
                    for j in range(NFC):
                        kb.op(pe, lambda j=j: PE_.matmul(pd[:, :T], lhsT=wd[:, j, ms], rhs=h[:, j, :],
                                                         start=(j == 0), stop=(j == NFC - 1)),
                              reads=[t_wd, t_h], writes=[tpd], pub=(j == NFC - 1))
                    kb.op(dve, lambda: V.scalar_tensor_tensor(
                        out=orr[:], in0=pd[:, :T], scalar=0.5, in1=xr[:], op0=ALU.mult, op1=ALU.add),
                        reads=[tpd, txr], writes=[tor])
                    kb.dma(pool, dstc[:, m, ts], orr[:], reads=[tor])
            ph.close()

        def final_phase(src):
            ph = Phase("fin")
            gam, t_gam = load_gamma(ph, final_norm)
            TF = 512 if NTOK % 512 == 0 else T
            nrm = Norm(ph, TF)
            xt = [ph.sb(f"fxt{i}", [128, 8, TF]) for i in range(2)]
            t_xt = [Trk("fxt") for _ in range(2)]
            ot = [ph.sb(f"fot{i}", [128, 8, TF]) for i in range(2)]
            t_ot = [Trk("fot") for _ in range(2)]
            srcc = chunked(dram_ap[src])
            dstc = chunked(yT)
            for i in range(NTOK // TF):
                ts = slice(i * TF, (i + 1) * TF)
                xb, txb = xt[i % 2], t_xt[i % 2]
                ob, tob = ot[i % 2], t_ot[i % 2]
                kb.dma(sp, xb[:], srcc[:, :, ts], writes=[txb])
                nrm.run(xb, txb, gam, t_gam, ob, tob)
                kb.dma(pool, dstc[:, :, ts], ob[:], reads=[tob], is_out=True)
            ph.close()

        def even_in_phase(j, layer, src):
            ph = Phase(f"e1_{layer}")
            TT = 512 if NTOK % 512 == 0 else 256
            gam, t_gam = load_gamma(ph, mix_norm[layer])
            NO = 2560
            win = ph.sb("win", [128, 8, NO], BF16)
            t_win = Trk("win")
            stg = [ph.sb(f"stg{i}", [128, NO // 2]) for i in range(3)]
            t_stg = [Trk(f"stg{i}") for i in range(3)]
            load_weight_rows(ph, win, t_win, lambda r: ev_w_in[j, r * 128:(r + 1) * 128, :], 8, NO, stg, t_stg)
            nrm = Norm(ph, TT)
            xt = [ph.sb(f"xt{i}", [128, 8, TT]) for i in range(2)]
            t_xt = [Trk("xt") for _ in range(2)]
            xn = [ph.sb(f"xn{i}", [128, 8, TT], BF16) for i in range(2)]
            t_xn = [Trk("xn") for _ in range(2)]
            ev = [ph.sb(f"ev{i}", [128, TT]) for i in range(4)]
            t_ev = [Trk("ev") for _ in range(4)]
            pp = [ph.ps(f"pp{i}", [128, 512]) for i in range(3)]
            t_pp = [Trk("pp") for _ in range(3)]
            srcc = chunked(dram_ap[src])
            ei = 0
            for i in range(NTOK // TT):
                ts = slice(i * TT, (i + 1) * TT)
                xb, txb = xt[i % 2], t_xt[i % 2]
                kb.dma(sp, xb[:], srcc[:, :, ts], writes=[txb])
                xnb, txnb = xn[i % 2], t_xn[i % 2]
                nrm.run(xb, txb, gam, t_gam, xnb, txnb)
                for oc in range(20):
                    p, tp = pp[oc % 3], t_pp[oc % 3]
                    for k in range(8):
                        kb.op(pe, lambda k=k: PE_.matmul(p[:, :TT], lhsT=win[:, k, oc * 128:(oc + 1) * 128],
                                                         rhs=xnb[:, k, :], start=(k == 0), stop=(k == 7)),
                              reads=[t_win, txnb], writes=[tp], pub=(k == 7))
                    e, te = ev[ei % 4], t_ev[ei % 4]
                    ei += 1
                    if oc < 8:
                        kb.op(act, lambda: S.activation(out=e[:], in_=p[:, :TT], func=AF.Gelu_apprx_tanh),
                              reads=[tp], writes=[te])
                        dst = Gd[oc * 128:(oc + 1) * 128, ts]
                    elif oc < 16:
                        kb.op(dve, lambda: V.tensor_copy(out=e[:], in_=p[:, :TT]), reads=[tp], writes=[te])
                        dst = XRd[(oc - 8) * 128:(oc - 7) * 128, ts]
                    else:
                        kb.op(dve, lambda: V.tensor_copy(out=e[:], in_=p[:, :TT]), reads=[tp], writes=[te])
                        dst = Ud[(oc - 16) * 128:(oc - 15) * 128, ts]
                    kb.dma(pool, dst, e[:], reads=[te])
            ph.close()

        def load_halo(xp, t_xp, src_rows, s):
            lo = s * SEG - 2
            hi = s * SEG + SEG + 1
            clo, chi = max(lo, 0), min(hi, NTOK)
            kb.dma(sp, xp[:, clo - lo:chi - lo], src_rows[:, clo:chi], writes=[t_xp])
            if s == 0:
                kb.op(dve, lambda: V.memset(xp[:, 0:2], 0.0), writes=[t_xp])
            else:
                kb.op(dve, lambda: V.tensor_scalar(out=xp[:, 0:2], in0=xp[:, 0:2], scalar1=fl[:, s - 1:s], scalar2=None,
                                                   op0=ALU.mult), reads=[t_xp, t_fl], writes=[t_xp])
            if s == NSEG - 1:
                kb.op(dve, lambda: V.memset(xp[:, SEG + 2:SEG + 3], 0.0), writes=[t_xp])
            else:
                kb.op(dve, lambda: V.tensor_scalar(out=xp[:, SEG + 2:SEG + 3], in0=xp[:, SEG + 2:SEG + 3],
                                                   scalar1=fl[:, s:s + 1], scalar2=None, op0=ALU.mult),
                      reads=[t_xp, t_fl], writes=[t_xp])

        def conv4(xc, t_xc, xp, t_xp, w4, bcol, t_par):
            kb.op(dve, lambda: V.tensor_scalar(out=xc[:], in0=xp[:, 0:SEG], scalar1=w4[:, 0:1], scalar2=bcol,
                                               op0=ALU.mult, op1=ALU.add), reads=[t_xp, t_par], writes=[t_xc])
            for k in range(1, 4):
                kb.op(dve, lambda k=k: V.scalar_tensor_tensor(out=xc[:], in0=xp[:, k:k + SEG], scalar=w4[:, k:k + 1],
                                                              in1=xc[:], op0=ALU.mult, op1=ALU.add),
                      reads=[t_xp, t_par, t_xc], writes=[t_xc])

        def lru_phase(j):
            ph = Phase(f"e2_{j}")
            t_par = Trk("par")
            cw = ph.sb("cw", [128, 8, 4])
            for k in range(4):
                kb.dma(sp, cw[:, :, k], lru_conv_w[j, k].rearrange("(c p) -> p c", p=128), writes=[t_par], st=t_par)
            cb = ph.sb("cb", [128, 8])
            kb.dma(sp, cb[:], lru_conv_b[j].rearrange("(c p) -> p c", p=128), writes=[t_par])
            ba = ph.sb("ba", [128, 2, 8])
            for d in range(2):
                kb.dma(sp, ba[:, d, :], lru_b_a[j, d].rearrange("(c p) -> p c", p=128), writes=[t_par], st=t_par)
            bx = ph.sb("bx", [128, 2, 8])
            for d in range(2):
                kb.dma(sp, bx[:, d, :], lru_b_x[j, d].rearrange("(c p) -> p c", p=128), writes=[t_par], st=t_par)
            lam = ph.sb("lam", [128, 2, 8])
            t_lam = Trk("lam")
            for d in range(2):
                kb.dma(sp, lam[:, d, :], lru_lam[j, d].rearrange("(c p) -> p c", p=128), writes=[t_lam], st=t_lam)
            l1 = ph.sb("l1", [128, 2, 8])
            c8 = ph.sb("c8", [128, 2, 8])
            c16 = ph.sb("c16", [128, 2, 8])
            kb.op(act, lambda: S.activation(out=l1[:], in_=lam[:], func=AF.Exp, scale=-1.0), reads=[t_lam], writes=[t_lam])
            kb.op(act, lambda: S.activation(out=l1[:], in_=l1[:], func=AF.Ln, bias=1.0, scale=1.0),
                  reads=[t_lam], writes=[t_lam])
            kb.op(dve, lambda: V.tensor_scalar(out=c8[:], in0=l1[:], scalar1=-8.0, scalar2=None, op0=ALU.mult),
                  reads=[t_lam], writes=[t_par])
            kb.op(dve, lambda: V.tensor_scalar(out=c16[:], in0=l1[:], scalar1=-16.0, scalar2=None, op0=ALU.mult),
                  reads=[t_lam], writes=[t_par])
            wbd = ph.sb("wbd", [128, 32, 128], BF16)
            t_wbd = Trk("wbd")
            pp_ = Phase(f"e2p_{j}")
            wst = pp_.sb("wst", [128, 32, 128])
            t_wst = Trk("wst")
            kb.op(dve, lambda: V.memset(wst[:], 0.0), writes=[t_wst])

            def widx(g, d, c):
                return (g * 2 + d) * 8 + c

            for g, Wd_ in enumerate((lru_w_a, lru_w_x)):
                for d in range(2):
                    for c in range(8):
                        for hh in range(2):
                            kb.dma(sp, wst[hh * 64:(hh + 1) * 64, widx(g, d, c), hh * 64:(hh + 1) * 64],
                                   Wd_[j, d, 2 * c + hh], writes=[t_wst], st=t_wst)
            kb.op(dve, lambda: V.tensor_copy(out=wbd[:], in_=wst[:]), reads=[t_wst], writes=[t_wbd])
            pp_.close()

            hb = ph.sb("hb", [128, NTOK])
            t_hb = Trk("hb")
            xp = [ph.sb(f"xp{i}", [128, SEG + 3]) for i in range(2)]
            t_xp = [Trk("xp") for _ in range(2)]
            xc2 = [ph.sb(f"xc{i}", [128, SEG]) for i in range(2)]; t_xc2 = [Trk("xc") for _ in range(2)]
            xcb2 = [ph.sb(f"xcb{i}", [128, SEG], BF16) for i in range(2)]; t_xcb2 = [Trk("xcb") for _ in range(2)]
            Rb2 = [ph.sb(f"Rb{i}", [128, SEG]) for i in range(2)]; t_R2 = [Trk("R") for _ in range(2)]
            Ab2 = [ph.sb(f"Ab{i}", [128, SEG]) for i in range(2)]; t_A2 = [Trk("A") for _ in range(2)]
            Sb2 = [ph.sb(f"Sb{i}", [128, SEG]) for i in range(2)]; t_S2 = [Trk("S") for _ in range(2)]
            Ib2 = [ph.sb(f"Ib{i}", [128, SEG]) for i in range(2)]; t_I2 = [Trk("I") for _ in range(2)]
            hf = [ph.sb(f"hf{i}", [128, SEG]) for i in range(2)]
            t_hf = [Trk("hf") for _ in range(2)]
            Gt = ph.sb("Gt", [128, SEG]); t_G = Trk("G")
            yb = [ph.sb(f"yb{i}", [128, SEG], BF16) for i in range(2)]
            t_yb = [Trk("yb") for _ in range(2)]
            ini = ph.sb("ini", [128, 2]); t_ini = Trk("ini")
            pa = ph.ps("pa", [128, SEG]); t_pa = Trk("pa")
            px = ph.ps("px", [128, SEG]); t_px = Trk("px")
            cnt = 0
            for c in range(8):
                rows = XRd[c * 128:(c + 1) * 128, :]
                for d in (1, 0):
                    segs = list(range(NSEG))[::-1] if d == 1 else list(range(NSEG))
                    for s in segs:
                        ss = slice(s * SEG, (s + 1) * SEG)
                        xpb, txp = xp[cnt % 2], t_xp[cnt % 2]
                        xc, t_xc = xc2[cnt % 2], t_xc2[cnt % 2]
                        xcb, t_xcb = xcb2[cnt % 2], t_xcb2[cnt % 2]
                        Rb, t_R = Rb2[cnt % 2], t_R2[cnt % 2]
                        Ab, t_A = Ab2[cnt % 2], t_A2[cnt % 2]
                        Sb, t_S = Sb2[cnt % 2], t_S2[cnt % 2]
                        Ib, t_I = Ib2[cnt % 2], t_I2[cnt % 2]
                        load_halo(xpb, txp, rows, s)
                        conv4(xc, t_xc, xpb, txp, cw[:, c, :], cb[:, c:c + 1], t_par)
                        kb.op(act, lambda: S.copy(out=xcb[:], in_=xc[:]), reads=[t_xc], writes=[t_xcb])
                        for q in range(NPQ):
                            qs = slice(q * PW, (q + 1) * PW)
                            kb.op(pe, lambda qs=qs: PE_.matmul(pa[:, qs], lhsT=wbd[:, widx(0, d, c), :], rhs=xcb[:, qs],
                                                               start=True, stop=True),
                                  reads=[t_wbd, t_xcb], writes=[t_pa], pub=(q == NPQ - 1))
                        for q in range(NPQ):
                            qs = slice(q * PW, (q + 1) * PW)
                            kb.op(pe, lambda qs=qs: PE_.matmul(px[:, qs], lhsT=wbd[:, widx(1, d, c), :], rhs=xcb[:, qs],
                                                               start=True, stop=True),
                                  reads=[t_wbd, t_xcb], writes=[t_px], pub=(q == NPQ - 1))
                        kb.op(act, lambda: S.activation(out=Rb[:], in_=pa[:], func=AF.Sigmoid, bias=ba[:, d, c:c + 1],
                                                        scale=1.0), reads=[t_pa, t_par], writes=[t_R])
                        kb.op(act, lambda: S.activation(out=Ab[:], in_=Rb[:], func=AF.Exp, scale=c8[:, d, c:c + 1]),
                              reads=[t_R, t_par], writes=[t_A])
                        kb.op(act, lambda: S.activation(out=Sb[:], in_=Rb[:], func=AF.Exp, scale=c16[:, d, c:c + 1]),
                              reads=[t_R, t_par], writes=[t_S])
                        kb.op(act, lambda: S.activation(out=Sb[:], in_=Sb[:], func=AF.Sqrt, scale=-1.0, bias=one_t[:]),
                              reads=[t_S, t_const], writes=[t_S])
                        kb.op(act, lambda: S.activation(out=Ib[:], in_=px[:], func=AF.Sigmoid, bias=bx[:, d, c:c + 1],
                                                        scale=1.0), reads=[t_px, t_par], writes=[t_I])
                        kb.op(dve, lambda: V.tensor_tensor(out=Ib[:], in0=Ib[:], in1=xc[:], op=ALU.mult),
                              reads=[t_I, t_xc], writes=[t_I])
                        kb.op(dve, lambda: V.tensor_tensor(out=Ib[:], in0=Ib[:], in1=Sb[:], op=ALU.mult),
                              reads=[t_I, t_S], writes=[t_I])
                        if d == 1:
                            if s == NSEG - 1:
                                init = 0.0
                                rd = []
                            else:
                                kb.op(dve, lambda: V.tensor_scalar(out=ini[:, 0:1], in0=hb[:, (s + 1) * SEG:(s + 1) * SEG + 1],
                                                                   scalar1=fl[:, s:s + 1], scalar2=None, op0=ALU.mult),
                                      reads=[t_hb, t_fl], writes=[t_ini])
                                init = ini[:, 0:1]
                                rd = [t_ini]
                            kb.op(dve, lambda: V.tensor_tensor_scan(out=hb[:, ss][:, ::-1], data0=Ab[:, ::-1],
                                                                    data1=Ib[:, ::-1], initial=init,
                                                                    op0=ALU.mult, op1=ALU.add),
                                  reads=[t_A, t_I] + rd, writes=[t_hb])
                        else:
                            hfb, thf = hf[cnt % 2], t_hf[cnt % 2]
                            hfp, thfp = hf[(cnt + 1) % 2], t_hf[(cnt + 1) % 2]
                            if s == 0:
                                init = 0.0
                                rd = []
                            else:
                                kb.op(dve, lambda: V.tensor_scalar(out=ini[:, 1:2], in0=hfp[:, SEG - 1:SEG],
                                                                   scalar1=fl[:, s - 1:s], scalar2=None, op0=ALU.mult),
                                      reads=[thfp, t_fl], writes=[t_ini])
                                init = ini[:, 1:2]
                                rd = [t_ini]
                            kb.op(dve, lambda: V.tensor_tensor_scan(out=hfb[:], data0=Ab[:], data1=Ib[:], initial=init,
                                                                    op0=ALU.mult, op1=ALU.add),
                                  reads=[t_A, t_I] + rd, writes=[thf])
                            kb.dma(sp, Gt[:], Gd[c * 128:(c + 1) * 128, ss], writes=[t_G])
                            kb.op(dve, lambda: V.tensor_tensor(out=Ib[:], in0=hfb[:], in1=hb[:, ss], op=ALU.add),
                                  reads=[thf, t_hb], writes=[t_I])
                            ybb, tyb = yb[cnt % 2], t_yb[cnt % 2]
                            kb.op(pool, lambda: G_.tensor_tensor(out=ybb[:], in0=Ib[:], in1=Gt[:], op=ALU.mult),
                                  reads=[t_I, t_G], writes=[tyb])
                            kb.dma(pool, YMd[c * 128:(c + 1) * 128, ss], ybb[:], reads=[tyb])
                        cnt += 1
            ph.close()

        def s5_phase(j):
            ph = Phase(f"e3_{j}")
            t_p = Trk("s5par")
            NC_ = 32
            KB8 = 8
            J = SEG // KB8

            def pt(name, shape=None):
                return ph.sb(name, shape or [128, NC_])

            lre, lim, ldt = pt("lre"), pt("lim"), pt("ldt")
            for d in range(2):
                kb.dma(sp, lre[:, d * 16:(d + 1) * 16],
                       s5_lam_re[j, d].rearrange("(t g) n -> (g n) t", g=2), writes=[t_p], st=t_p)
                kb.dma(sp, lim[:, d * 16:(d + 1) * 16],
                       s5_lam_im[j, d].rearrange("(t g) n -> (g n) t", g=2), writes=[t_p], st=t_p)
                for g in range(2):
                    kb.dma(sp, ldt[g * 64:(g + 1) * 64, d * 16:(d + 1) * 16],
                           s5_log_dt[j, d].rearrange("(t g) -> g t", g=2)[g].partition_broadcast(64), writes=[t_p], st=t_p)

            def vv(out, a, b_, op):
                kb.op(dve, lambda: V.tensor_tensor(out=out, in0=a, in1=b_, op=op), reads=[t_p], writes=[t_p])

            def vs(out, a, s1, op):
                kb.op(dve, lambda: V.tensor_scalar(out=out, in0=a, scalar1=s1, scalar2=None, op0=op),
                      reads=[t_p], writes=[t_p])

            dtt, th, lr, rho = pt("dtt"), pt("th"), pt("lr"), pt("rho")
            kb.op(act, lambda: S.activation(out=dtt[:], in_=ldt[:], func=AF.Exp), reads=[t_p], writes=[t_p])
            vv(th[:], lim[:], dtt[:], ALU.mult)
            vv(lr[:], lre[:], dtt[:], ALU.mult)
            kb.op(act, lambda: S.activation(out=rho[:], in_=lr[:], func=AF.Exp), reads=[t_p], writes=[t_p])
            cs_, sn_ = pt("cs"), pt("sn")
            kb.op(act, lambda: S.activation(out=sn_[:], in_=th[:], func=AF.Sin, scale=1.0 / 32.0), reads=[t_p], writes=[t_p])
            kb.op(act, lambda: S.activation(out=cs_[:], in_=th[:], func=AF.Sin, scale=1.0 / 32.0, bias=hpi_t[:]),
                  reads=[t_p, t_const], writes=[t_p])
            cc, s2, sc = pt("cc"), pt("s2"), pt("sc")

            def double_angle(c_, s_):
                vv(cc[:], c_[:], c_[:], ALU.mult)
                vv(s2[:], s_[:], s_[:], ALU.mult)
                vv(sc[:], s_[:], c_[:], ALU.mult)
                vv(c_[:], cc[:], s2[:], ALU.subtract)
                vs(s_[:], sc[:], 2.0, ALU.mult)

            for _ in range(5):
                double_angle(cs_, sn_)
            c8, s8, rho8 = pt("c8"), pt("s8"), pt("rho8")
            vs(c8[:], cs_[:], 1.0, ALU.mult)
            vs(s8[:], sn_[:], 1.0, ALU.mult)
            for _ in range(3):
                double_angle(c8, s8)
            vv(rho8[:], rho[:], rho[:], ALU.mult)
            vv(rho8[:], rho8[:], rho8[:], ALU.mult)
            vv(rho8[:], rho8[:], rho8[:], ALU.mult)
            abr, abi = pt("abr"), pt("abi")
            vv(abr[:], rho[:], cs_[:], ALU.mult)
            vv(abi[:], rho[:], sn_[:], ALU.mult)
            t1, t2, den, abm1 = pt("t1"), pt("t2"), pt("den"), pt("abm1")
            vv(t1[:], lre[:], lre[:], ALU.mult)
            vv(t2[:], lim[:], lim[:], ALU.mult)
            vv(den[:], t1[:], t2[:], ALU.add)
            kb.op(dve, lambda: V.reciprocal(out=den[:], in_=den[:]), reads=[t_p], writes=[t_p])
            vs(abm1[:], abr[:], -1.0, ALU.add)
            cre, cim = pt("cre"), pt("cim")
            vv(t1[:], abm1[:], lre[:], ALU.mult)
            vv(t2[:], abi[:], lim[:], ALU.mult)
            vv(t1[:], t1[:], t2[:], ALU.add)
            vv(cre[:], t1[:], den[:], ALU.mult)
            vv(t1[:], abi[:], lre[:], ALU.mult)
            vv(t2[:], abm1[:], lim[:], ALU.mult)
            vv(t1[:], t1[:], t2[:], ALU.subtract)
            vv(cim[:], t1[:], den[:], ALU.mult)
            APR = ph.sb("APR", [128, 9, NC_])
            API = ph.sb("API", [128, 9, NC_])
            kb.op(dve, lambda: V.memset(APR[:, 0, :], 1.0), writes=[t_p])
            kb.op(dve, lambda: V.memset(API[:, 0, :], 0.0), writes=[t_p])
            vs(APR[:, 1, :], abr[:], 1.0, ALU.mult)
            vs(API[:, 1, :], abi[:], 1.0, ALU.mult)
            for p in range(2, 9):
                vv(t1[:], APR[:, p - 1, :], abr[:], ALU.mult)
                vv(t2[:], API[:, p - 1, :], abi[:], ALU.mult)
                vv(APR[:, p, :], t1[:], t2[:], ALU.subtract)
                vv(t1[:], APR[:, p - 1, :], abi[:], ALU.mult)
                vv(t2[:], API[:, p - 1, :], abr[:], ALU.mult)
                vv(API[:, p, :], t1[:], t2[:], ALU.add)
            pzr = ph.ps("pzr", [128, 512]); t_pzr = Trk("pzr")
            pzi = ph.ps("pzi", [128, 512]); t_pzi = Trk("pzi")
            py = ph.ps("py", [128, 2048]); t_py = Trk("py")
            pm = ph.ps("pm", [32, 512]); t_pm = Trk("pm")
            bbre = ph.sb("bbre", [128, NC_, 32])
            bbim = ph.sb("bbim", [128, NC_, 32])
            bbimn = ph.sb("bbimn", [128, NC_, 32])
            CTre = ph.sb("CTre", [128, NC_, 32])
            CTim = ph.sb("CTim", [128, NC_, 32])
            dsk = ph.sb("dsk", [32, 16])
            kb.dma(sp, dsk[:], s5_d[j].rearrange("(t c) -> c t", c=32), writes=[t_p], st=t_p)
            pp_ = Phase(f"e3p_{j}")
            bre = pp_.sb("bre", [128, NC_, 32])
            bim = pp_.sb("bim", [128, NC_, 32])
            kb.op(dve, lambda: V.memset(bre[:], 0.0), writes=[t_p])
            kb.op(dve, lambda: V.memset(bim[:], 0.0), writes=[t_p])
            for g in range(2):
                for (dst_, src_) in ((bre, s5_b_re), (bim, s5_b_im)):
                    for d in range(2):
                        kb.dma(sp, dst_[g * 64:(g + 1) * 64, d * 16:(d + 1) * 16, g * 16:(g + 1) * 16],
                               src_[j, d].rearrange("(t g) n c -> g n t c", g=2)[g], writes=[t_p], st=t_p)
            tb = pp_.sb("tb", [128, NC_, 32])

            def bc(col):
                return col[:].unsqueeze(2).to_broadcast([128, NC_, 32])

            vv(bbre[:], bre[:], bc(cre), ALU.mult)
            vv(tb[:], bim[:], bc(cim), ALU.mult)
            vv(bbre[:], bbre[:], tb[:], ALU.subtract)
            vv(bbim[:], bim[:], bc(cre), ALU.mult)
            vv(tb[:], bre[:], bc(cim), ALU.mult)
            vv(bbim[:], bbim[:], tb[:], ALU.add)
            vs(bbimn[:], bbim[:], -1.0, ALU.mult)
            craw = [pp_.sb("crre", [32, NC_, 128]), pp_.sb("crim", [32, NC_, 128])]
            t_c = Trk("craw")
            for ri_, src_ in enumerate((s5_c_re, s5_c_im)):
                kb.op(dve, lambda: V.memset(craw[ri_][:], 0.0), writes=[t_c])
                for g in range(2):
                    for d in range(2):
                        kb.dma(sp, craw[ri_][g * 16:(g + 1) * 16, d * 16:(d + 1) * 16, g * 64:(g + 1) * 64],
                               src_[j, d].rearrange("(t g) c n -> g c t n", g=2)[g], writes=[t_c], st=t_c)
            for ri_, CT in enumerate((CTre, CTim)):
                for q in range(NC_):
                    kb.op(pe, lambda q=q: PE_.transpose(out=py[:, q * 32:(q + 1) * 32], in_=craw[ri_][:, q, :],
                                                        identity=ident[0:32, 0:32]),
                          reads=[t_c, t_id], writes=[t_py], pub=(q == NC_ - 1))
                kb.op(dve, lambda: V.tensor_copy(out=CT[:], in_=py[:, 0:NC_ * 32].rearrange("p (q c) -> p q c", q=NC_)),
                      reads=[t_py], writes=[t_p])
            pp_.close()
            CAre = [ph.sb(f"CAre{d}", [128, 9, 32]) for d in range(2)]
            CAim = [ph.sb(f"CAim{d}", [128, 9, 32]) for d in range(2)]
            tca = ph.sb("tca", [128, 9, 32])
            t_ca = Trk("ca")
            lCre = [ph.sb(f"lCre{d}", [128, 9, 32], BF16) for d in range(2)]
            lCimn = [ph.sb(f"lCimn{d}", [128, 9, 32], BF16) for d in range(2)]
            t_lC = Trk("lC")
            Dgr = ph.sb("Dgr", [128, 8, 128]); Dgi = ph.sb("Dgi", [128, 8, 128]); t_dg = Trk("dg")
            lBzr = [ph.sb(f"lBzr{d}", [32, 8, 128], BF16) for d in range(2)]
            lBzi = [ph.sb(f"lBzi{d}", [32, 8, 128], BF16) for d in range(2)]
            t_lB = Trk("lB")
            Mfb = ph.sb("Mfb", [32, 2, 8, 32], BF16); M0 = ph.sb("M0", [32, 32], BF16); M0f = ph.sb("M0f", [32, 32])
            t_M = Trk("M")
            cosT = ph.sb("cosT", [128, J + 2]); sinT = ph.sb("sinT", [128, J + 2]); t_tab = [Trk("tab0"), Trk("tab1")]
            cosTs = [cosT, ph.sb("cosT1", [128, J + 2])]
            sinTs = [sinT, ph.sb("sinT1", [128, J + 2])]
            RT = [ph.sb(f"RT{d}", [128, J]) for d in range(2)]; t_RT = Trk("RT")
            q1 = ph.sb("q1", [128, J + 2]); q2 = ph.sb("q2", [128, J + 2]); t_q1 = Trk("q1"); t_q2 = Trk("q2")
            uf = [ph.sb(f"uf{i}", [32, SEG]) for i in range(2)]; t_uf = [Trk("uf") for _ in range(2)]
            ub = [ph.sb(f"ub{i}", [32, SEG], BF16) for i in range(2)]; t_ub = [Trk("ub") for _ in range(2)]
            w1 = ph.sb("w1", [128, J + 1]); w2 = ph.sb("w2", [128, J + 1]); t_w1 = Trk("w1"); t_w2 = Trk("w2")
            btre = ph.sb("btre", [128, J]); btim = ph.sb("btim", [128, J]); t_bt = Trk("bt")
            gre = [ph.sb(f"gre{d}", [128, J + 1]) for d in range(2)]
            gim = [ph.sb(f"gim{d}", [128, J + 1]) for d in range(2)]
            t_g = [Trk("g0"), Trk("g1")]
            Hf = [ph.sb("Hfre", [128, J + 1], BF16), ph.sb("Hfim", [128, J + 1], BF16)]; t_Hf = Trk("Hf")
            Hb = [ph.sb("Hbre", [128, NSEG, J + 1], BF16), ph.sb("Hbim", [128, NSEG, J + 1], BF16)]; t_Hb = Trk("Hb")
            ot_ = [ph.sb(f"ot{i}", [32, SEG]) for i in range(2)]; t_ot = [Trk("ot") for _ in range(2)]
            ini = ph.sb("ini", [128, 8]); t_ini = Trk("ini")
            li = 0
            for ti in range(16):
                for d in (1, 0):
                    col = d * 16 + ti
                    cT, sT, ttab = cosTs[d], sinTs[d], t_tab[d]
                    ctr = CTre[:, col, :].unsqueeze(1).to_broadcast([128, 9, 32])
                    cti = CTim[:, col, :].unsqueeze(1).to_broadcast([128, 9, 32])
                    apr = APR[:, :, col].unsqueeze(2).to_broadcast([128, 9, 32])
                    api = API[:, :, col].unsqueeze(2).to_broadcast([128, 9, 32])

                    def ca(out, a, b_, op):
                        kb.op(dve, lambda: V.tensor_tensor(out=out, in0=a, in1=b_, op=op), reads=[t_p, t_ca], writes=[t_ca])

                    ca(CAre[d][:], ctr, apr, ALU.mult)
                    ca(tca[:], cti, api, ALU.mult)
                    ca(CAre[d][:], CAre[d][:], tca[:], ALU.subtract)
                    ca(CAim[d][:], ctr, api, ALU.mult)
                    ca(tca[:], cti, apr, ALU.mult)
                    ca(CAim[d][:], CAim[d][:], tca[:], ALU.add)
                    kb.op(act, lambda: S.copy(out=lCre[d][:], in_=CAre[d][:]), reads=[t_ca], writes=[t_lC])
                    kb.op(act, lambda: S.mul(out=lCimn[d][:], in_=CAim[d][:], mul=-1.0), reads=[t_ca], writes=[t_lC])
                    kb.op(pe, lambda: PE_.matmul(pm[:, 0:256], lhsT=bbre[:, col, :],
                                                 rhs=CAre[d][:, 0:8, :].rearrange("p a c -> p (a c)"), start=True, stop=False),
                          reads=[t_p, t_ca], writes=[t_pm], pub=False)
                    kb.op(pe, lambda: PE_.matmul(pm[:, 0:256], lhsT=bbimn[:, col, :],
                                                 rhs=CAim[d][:, 0:8, :].rearrange("p a c -> p (a c)"), start=False, stop=True),
                          reads=[t_p, t_ca], writes=[t_pm])
                    kb.op(dve, lambda: V.tensor_copy(out=Mfb[:, d, :, :], in_=pm[:, 0:256].rearrange("p (a c) -> p a c", a=8)),
                          reads=[t_pm], writes=[t_M])
                    if d == 1:
                        kb.op(dve, lambda: V.tensor_copy(out=M0f[:], in_=pm[:, 0:32]), reads=[t_pm], writes=[t_M])
                    else:
                        kb.op(dve, lambda: V.tensor_tensor(out=M0[:], in0=M0f[:], in1=pm[:, 0:32], op=ALU.add),
                              reads=[t_pm, t_M], writes=[t_M])
                    idb = ident[:, :].unsqueeze(1).to_broadcast([128, 8, 128])
                    kb.op(dve, lambda: V.tensor_tensor(out=Dgr[:], in0=idb,
                                                       in1=APR[:, 0:8, col].unsqueeze(2).to_broadcast([128, 8, 128]), op=ALU.mult),
                          reads=[t_p, t_id], writes=[t_dg])
                    kb.op(dve, lambda: V.tensor_tensor(out=Dgi[:], in0=idb,
                                                       in1=API[:, 0:8, col].unsqueeze(2).to_broadcast([128, 8, 128]), op=ALU.mult),
                          reads=[t_p, t_id], writes=[t_dg])
                    for hh in range(2):
                        hs_ = slice(hh * 512, (hh + 1) * 512)
                        dgr = Dgr[:].rearrange("p a n -> p (a n)")[:, hs_]
                        dgi = Dgi[:].rearrange("p a n -> p (a n)")[:, hs_]
                        kb.op(pe, lambda: PE_.matmul(py[0:32, hs_], lhsT=bbre[:, col, :], rhs=dgr, start=True, stop=False),
                              reads=[t_p, t_dg], writes=[t_py], pub=False)
                        kb.op(pe, lambda: PE_.matmul(py[0:32, hs_], lhsT=bbimn[:, col, :], rhs=dgi, start=False, stop=True),
                              reads=[t_p, t_dg], writes=[t_py], pub=False)
                        h2 = slice(1024 + hh * 512, 1024 + (hh + 1) * 512)
                        kb.op(pe, lambda: PE_.matmul(py[0:32, h2], lhsT=bbre[:, col, :], rhs=dgi, start=True, stop=False),
                              reads=[t_p, t_dg], writes=[t_py], pub=False)
                        kb.op(pe, lambda: PE_.matmul(py[0:32, h2], lhsT=bbim[:, col, :], rhs=dgr, start=False, stop=True),
                              reads=[t_p, t_dg], writes=[t_py], pub=(hh == 1))
                    kb.op(act, lambda: S.copy(out=lBzr[d][:], in_=py[0:32, 0:1024].rearrange("p (a n) -> p a n", a=8)),
                          reads=[t_py], writes=[t_lB])
                    kb.op(act, lambda: S.copy(out=lBzi[d][:], in_=py[0:32, 1024:2048].rearrange("p (a n) -> p a n", a=8)),
                          reads=[t_py], writes=[t_lB])
                    kb.op(dve, lambda: V.tensor_copy(out=cT[:, 0:1], in_=c8[:, col:col + 1]), reads=[t_p], writes=[ttab])
                    kb.op(dve, lambda: V.tensor_scalar(out=sT[:, 0:1], in0=s8[:, col:col + 1], scalar1=-1.0, scalar2=None,
                                                       op0=ALU.mult), reads=[t_p], writes=[ttab])
                    kb.op(dve, lambda: V.memset(cT[:, 1:2], 1.0), writes=[ttab])
                    kb.op(dve, lambda: V.memset(sT[:, 1:2], 0.0), writes=[ttab])
                    kb.op(dve, lambda: V.tensor_copy(out=cT[:, 2:3], in_=c8[:, col:col + 1]), reads=[t_p], writes=[ttab])
                    kb.op(dve, lambda: V.tensor_copy(out=sT[:, 2:3], in_=s8[:, col:col + 1]), reads=[t_p], writes=[ttab])
                    cV, sV = cT[:, 1:J + 2], sT[:, 1:J + 2]
                    m = 2
                    while m < J + 1:
                        n = min(m - 1, J + 1 - m)
                        cr, ci = cV[:, m - 1:m], sV[:, m - 1:m]
                        kb.op(dve, lambda: V.tensor_scalar(out=q1[:, 0:n], in0=sV[:, 1:1 + n], scalar1=ci, scalar2=None,
                                                           op0=ALU.mult), reads=[ttab], writes=[t_q1])
                        kb.op(dve, lambda: V.tensor_scalar(out=q2[:, 0:n], in0=cV[:, 1:1 + n], scalar1=ci, scalar2=None,
                                                           op0=ALU.mult), reads=[ttab], writes=[t_q2])
                        kb.op(dve, lambda: V.scalar_tensor_tensor(out=cV[:, m:m + n], in0=cV[:, 1:1 + n], scalar=cr,
                                                                  in1=q1[:, 0:n], op0=ALU.mult, op1=ALU.subtract),
                              reads=[ttab, t_q1], writes=[ttab])
                        kb.op(dve, lambda: V.scalar_tensor_tensor(out=sV[:, m:m + n], in0=sV[:, 1:1 + n], scalar=cr,
                                                                  in1=q2[:, 0:n], op0=ALU.mult, op1=ALU.add),
                              reads=[ttab, t_q2], writes=[ttab])
                        m += n
                    kb.op(dve, lambda: V.tensor_copy(out=RT[d][:], in_=rho8[:, col:col + 1].to_broadcast([128, J])),
                          reads=[t_p], writes=[t_RT])
                for d in (1, 0):
                    cT, sT, ttab = cosTs[d], sinTs[d], t_tab[d]
                    cV, sV = cT[:, 1:J + 2], sT[:, 1:J + 2]
                    EQr, EQi = cV[:, J:J + 1], sV[:, J:J + 1]
                    gr, gi_, tg = gre[d], gim[d], t_g[d]
                    segs = list(range(NSEG))[::-1] if d == 1 else list(range(NSEG))
                    for si_, s in enumerate(segs):
                        ss = slice(s * SEG, (s + 1) * SEG)
                        ufb, tuf = uf[li % 2], t_uf[li % 2]
                        ubb, tub = ub[li % 2], t_ub[li % 2]
                        otb, tot = ot_[li % 2], t_ot[li % 2]
                        li += 1
                        kb.dma(sp, ufb[:], Ud[32 * ti:32 * ti + 32, ss], writes=[tuf])
                        kb.op(act, lambda: S.copy(out=ubb[:], in_=ufb[:]), reads=[tuf], writes=[tub])
                        for (pz, tpz, lBz) in ((pzr, t_pzr, lBzr[d]), (pzi, t_pzi, lBzi[d])):
                            for k in range(8):
                                e = (7 - k) if d == 0 else k
                                kb.op(pe, lambda k=k, e=e: PE_.matmul(pz[:, 0:J], lhsT=lBz[:, e, :], rhs=ubb[:, k:SEG:8],
                                                                      start=(k == 0), stop=(k == 7)),
                                      reads=[t_lB, tub], writes=[tpz], pub=(k == 7))
                        if d == 0:
                            Ec, Es = cV[:, 0:J], sV[:, 0:J]
                        else:
                            Ec, Es = cV[:, 0:J][:, ::-1], sV[:, 0:J][:, ::-1]
                        kb.op(dve, lambda: V.tensor_tensor(out=w1[:, 0:J], in0=Ec, in1=pzr[:, 0:J], op=ALU.mult),
                              reads=[ttab, t_pzr], writes=[t_w1])
                        kb.op(dve, lambda: V.tensor_tensor(out=w2[:, 0:J], in0=Es, in1=pzi[:, 0:J], op=ALU.mult),
                              reads=[ttab, t_pzi], writes=[t_w2])
                        kb.op(dve, lambda: V.tensor_tensor(out=btre[:], in0=w1[:, 0:J], in1=w2[:, 0:J], op=ALU.add),
                              reads=[t_w1, t_w2], writes=[t_bt])
                        kb.op(dve, lambda: V.tensor_tensor(out=w1[:, 0:J], in0=Ec, in1=pzi[:, 0:J], op=ALU.mult),
                              reads=[ttab, t_pzi], writes=[t_w1])
                        kb.op(dve, lambda: V.tensor_tensor(out=w2[:, 0:J], in0=Es, in1=pzr[:, 0:J], op=ALU.mult),
                              reads=[ttab, t_pzr], writes=[t_w2])
                        kb.op(dve, lambda: V.tensor_tensor(out=btim[:], in0=w1[:, 0:J], in1=w2[:, 0:J], op=ALU.subtract),
                              reads=[t_w1, t_w2], writes=[t_bt])
                        icol = 0 if d == 0 else J
                        if si_ == 0:
                            kb.op(dve, lambda: V.memset(gr[:, icol:icol + 1], 0.0), writes=[tg])
                            kb.op(dve, lambda: V.memset(gi_[:, icol:icol + 1], 0.0), writes=[tg])
                        else:
                            ecol = J if d == 0 else 0
                            fcol = (s - 1) if d == 0 else s
                            ge_r, ge_i = gr[:, ecol:ecol + 1], gi_[:, ecol:ecol + 1]
                            fcl = fl[:, fcol:fcol + 1]

                            def ts2(out, a, s1):
                                kb.op(dve, lambda: V.tensor_scalar(out=out, in0=a, scalar1=s1, scalar2=fcl,
                                                                   op0=ALU.mult, op1=ALU.mult),
                                      reads=[tg, ttab, t_fl], writes=[t_ini])

                            ts2(ini[:, 0:1], ge_r, EQr)
                            ts2(ini[:, 1:2], ge_i, EQi)
                            ts2(ini[:, 2:3], ge_r, EQi)
                            ts2(ini[:, 3:4], ge_i, EQr)
                            kb.op(dve, lambda: V.tensor_tensor(out=gr[:, icol:icol + 1], in0=ini[:, 0:1], in1=ini[:, 1:2],
                                                               op=ALU.subtract), reads=[t_ini], writes=[tg])
                            kb.op(dve, lambda: V.tensor_tensor(out=gi_[:, icol:icol + 1], in0=ini[:, 2:3], in1=ini[:, 3:4],
                                                               op=ALU.add), reads=[t_ini], writes=[tg])
                        if d == 0:
                            kb.op(dve, lambda: V.tensor_tensor_scan(out=gr[:, 1:J + 1], data0=RT[d][:], data1=btre[:],
                                                                    initial=gr[:, 0:1], op0=ALU.mult, op1=ALU.add),
                                  reads=[t_RT, t_bt, tg], writes=[tg])
                            kb.op(dve, lambda: V.tensor_tensor_scan(out=gi_[:, 1:J + 1], data0=RT[d][:], data1=btim[:],
                                                                    initial=gi_[:, 0:1], op0=ALU.mult, op1=ALU.add),
                                  reads=[t_RT, t_bt, tg], writes=[tg])
                            Tc, Ts = cT[:, 0:J + 1], sT[:, 0:J + 1]
                            Hre, Him, tH = Hf[0][:, :], Hf[1][:, :], t_Hf
                        else:
                            kb.op(dve, lambda: V.tensor_tensor_scan(out=gr[:, 0:J][:, ::-1], data0=RT[d][:], data1=btre[:, ::-1],
                                                                    initial=gr[:, J:J + 1], op0=ALU.mult, op1=ALU.add),
                                  reads=[t_RT, t_bt, tg], writes=[tg])
                            kb.op(dve, lambda: V.tensor_tensor_scan(out=gi_[:, 0:J][:, ::-1], data0=RT[d][:], data1=btim[:, ::-1],
                                                                    initial=gi_[:, J:J + 1], op0=ALU.mult, op1=ALU.add),
                                  reads=[t_RT, t_bt, tg], writes=[tg])
                            Tc, Ts = cT[:, 0:J + 1][:, ::-1], sT[:, 0:J + 1][:, ::-1]
                            Hre, Him, tH = Hb[0][:, s, :], Hb[1][:, s, :], t_Hb
                        kb.op(dve, lambda: V.tensor_tensor(out=w1[:], in0=Tc, in1=gr[:], op=ALU.mult),
                              reads=[ttab, tg], writes=[t_w1])
                        kb.op(dve, lambda: V.tensor_tensor(out=w2[:], in0=Ts, in1=gi_[:], op=ALU.mult),
                              reads=[ttab, tg], writes=[t_w2])
                        kb.op(dve, lambda: V.tensor_tensor(out=Hre, in0=w1[:], in1=w2[:], op=ALU.subtract),
                              reads=[t_w1, t_w2], writes=[tH])
                        kb.op(dve, lambda: V.tensor_tensor(out=w1[:], in0=Ts, in1=gr[:], op=ALU.mult),
                              reads=[ttab, tg], writes=[t_w1])
                        kb.op(dve, lambda: V.tensor_tensor(out=w2[:], in0=Tc, in1=gi_[:], op=ALU.mult),
                              reads=[ttab, tg], writes=[t_w2])
                        kb.op(dve, lambda: V.tensor_tensor(out=Him, in0=w1[:], in1=w2[:], op=ALU.add),
                              reads=[t_w1, t_w2], writes=[tH])
                        if d == 1:
                            continue
                        for k in range(8):
                            ko = slice(k * J, (k + 1) * J)
                            for k2 in range(8):
                                if k2 == k:
                                    lt = M0[:, :]
                                elif k2 < k:
                                    lt = Mfb[:, 0, k - k2, :]
                                else:
                                    lt = Mfb[:, 1, k2 - k, :]
                                kb.op(pe, lambda lt=lt, k2=k2: PE_.matmul(py[0:32, ko], lhsT=lt, rhs=ubb[:, k2:SEG:8],
                                                                          start=(k2 == 0), stop=False),
                                      reads=[t_M, tub], writes=[t_py], pub=False)
                            kb.op(pe, lambda: PE_.matmul(py[0:32, ko], lhsT=lCre[0][:, k + 1, :], rhs=Hf[0][:, 0:J],
                                                         start=False, stop=False), reads=[t_lC, t_Hf], writes=[t_py], pub=False)
                            kb.op(pe, lambda: PE_.matmul(py[0:32, ko], lhsT=lCimn[0][:, k + 1, :], rhs=Hf[1][:, 0:J],
                                                         start=False, stop=False), reads=[t_lC, t_Hf], writes=[t_py], pub=False)
                            kb.op(pe, lambda: PE_.matmul(py[0:32, ko], lhsT=lCre[1][:, 8 - k, :], rhs=Hb[0][:, s, 1:J + 1],
                                                         start=False, stop=False), reads=[t_lC, t_Hb], writes=[t_py], pub=False)
                            kb.op(pe, lambda: PE_.matmul(py[0:32, ko], lhsT=lCimn[1][:, 8 - k, :], rhs=Hb[1][:, s, 1:J + 1],
                                                         start=False, stop=True), reads=[t_lC, t_Hb], writes=[t_py],
                                  pub=(k == 7))
                        kb.op(dve, lambda: V.scalar_tensor_tensor(
                            out=otb[:].rearrange("p (j k) -> p k j", k=8), in0=ufb[:].rearrange("p (j k) -> p k j", k=8),
                            scalar=dsk[:, ti:ti + 1], in1=py[0:32, 0:8 * J].rearrange("p (k j) -> p k j", k=8),
                            op0=ALU.mult, op1=ALU.add), reads=[tuf, t_p, t_py], writes=[tot])
                        kb.dma(pool, YSd[32 * ti:32 * ti + 32, ss], otb[:], reads=[tot])
            ph.close()

        def even_out_phase(j, src, dst):
            ph = Phase(f"e4_{j}")
            TT = 512 if NTOK % 512 == 0 else 256
            gw = ph.sb("gw", [128, 4, 512], BF16); t_gw = Trk("gw")
            wo = ph.sb("wo", [128, 12, 1024], BF16); t_wo = Trk("wo")
            stg = [ph.sb(f"stg{i}", [128, 1024]) for i in range(3)]
            t_stg = [Trk("stg") for _ in range(3)]
            load_weight_rows(ph, gw, t_gw, lambda r: s5_glu_w[j, r * 128:(r + 1) * 128, :], 4, 512, stg, t_stg)
            load_weight_rows(ph, wo, t_wo, lambda r: ev_w_out[j, r * 128:(r + 1) * 128, :], 12, 1024, stg, t_stg)
            gb_ = ph.sb("glub", [128, 4]); t_gb = Trk("glub")
            kb.dma(sp, gb_[:], s5_glu_b[j].rearrange("(c p) -> p c", p=128), writes=[t_gb])
            ys = [ph.sb(f"ys{i}", [128, 4, TT]) for i in range(2)]; t_ysb = [Trk("ys") for _ in range(2)]
            ya = [ph.sb(f"ya{i}", [128, 8, TT], BF16) for i in range(2)]; t_ya = [Trk("ya") for _ in range(2)]
            gf = ph.sb("gf", [128, 4, TT]); t_gf = Trk("gf")
            gbf = ph.sb("gbf", [128, 4, TT], BF16); t_gbf = Trk("gbf")
            sgm = [ph.sb(f"sgm{i}", [128, TT]) for i in range(2)]; t_sgm = [Trk("sgm") for _ in range(2)]
            ymb = ph.sb("ymb", [128, 4, TT], BF16); t_ymb = Trk("ymb")
            xres = [ph.sb(f"xres{i}", [128, TT]) for i in range(4)]; t_xres = [Trk("xres") for _ in range(4)]
            ores = [ph.sb(f"ores{i}", [128, TT]) for i in range(4)]; t_ores = [Trk("ores") for _ in range(4)]
            pg = [ph.ps(f"pg{i}", [128, 512]) for i in range(2)]; t_pg = [Trk("pg") for _ in range(2)]
            po = [ph.ps(f"po{i}", [128, 512]) for i in range(2)]; t_po = [Trk("po") for _ in range(2)]
            srcc = chunked(dram_ap[src]); dstc = chunked(dram_ap[dst])
            ysc = chunked(YSd); yac = chunked(YMd)
            ri = 0
            for i in range(NTOK // TT):
                ts = slice(i * TT, (i + 1) * TT)
                ysb, tys = ys[i % 2], t_ysb[i % 2]
                yab, tya = ya[i % 2], t_ya[i % 2]
                kb.dma(sp, ysb[:], ysc[:, :, ts], writes=[tys])
                kb.dma(sp, yab[:], yac[:, :, ts], writes=[tya])
                for k in range(4):
                    kb.op(act, lambda k=k: S.activation(out=gf[:, k, :], in_=ysb[:, k, :], func=AF.Gelu_apprx_tanh),
                          reads=[tys], writes=[t_gf])
                    kb.op(dve, lambda k=k: V.tensor_copy(out=gbf[:, k, :], in_=gf[:, k, :]), reads=[t_gf], writes=[t_gbf])
                for m in range(4):
                    p, tp = pg[m % 2], t_pg[m % 2]
                    for k in range(4):
                        kb.op(pe, lambda k=k: PE_.matmul(p[:, :TT], lhsT=gw[:, k, m * 128:(m + 1) * 128], rhs=gbf[:, k, :],
                                                         start=(k == 0), stop=(k == 3)),
                              reads=[t_gw, t_gbf], writes=[tp], pub=(k == 3))
                    sg_, tsg = sgm[m % 2], t_sgm[m % 2]
                    kb.op(act, lambda: S.activation(out=sg_[:], in_=p[:, :TT], func=AF.Sigmoid, bias=gb_[:, m:m + 1], scale=1.0),
                          reads=[tp, t_gb], writes=[tsg])
                    kb.op(dve, lambda: V.tensor_tensor(out=ymb[:, m, :], in0=gf[:, m, :], in1=sg_[:], op=ALU.mult),
                          reads=[t_gf, tsg], writes=[t_ymb])
                for m in range(8):
                    ms = slice(m * 128, (m + 1) * 128)
                    p, tp = po[m % 2], t_po[m % 2]
                    xr, txr = xres[ri % 4], t_xres[ri % 4]
                    orr, tor = ores[ri % 4], t_ores[ri % 4]
                    ri += 1
                    kb.dma(sp, xr[:], srcc[:, m, ts], writes=[txr])
                    for k in range(12):
                        rhs = yab[:, k, :] if k < 8 else ymb[:, k - 8, :]
                        kb.op(pe, lambda k=k, rhs=rhs: PE_.matmul(p[:, :TT], lhsT=wo[:, k, ms], rhs=rhs,
                                                                  start=(k == 0), stop=(k == 11)),
                              reads=[t_wo, tya, t_ymb], writes=[tp], pub=(k == 11))
                    kb.op(dve, lambda: V.tensor_tensor(out=orr[:], in0=p[:, :TT], in1=xr[:], op=ALU.add),
                          reads=[tp, txr], writes=[tor])
                    kb.dma(pool, dstc[:, m, ts], orr[:], reads=[tor])
            ph.close()

        def odd_in_phase(j, layer, src):
            ph = Phase(f"o1_{layer}")
            TT = 512 if NTOK % 512 == 0 else 256
            gam, t_gam = load_gamma(ph, mix_norm[layer])
            NO = 5184
            win = ph.sb("win", [128, 8, NO], BF16)
            t_win = Trk("win")
            stg = [ph.sb(f"stg{i}", [128, NO // 3]) for i in range(3)]
            t_stg = [Trk(f"stg{i}") for i in range(3)]
            load_weight_rows(ph, win, t_win, lambda r: od_w_in[j, r * 128:(r + 1) * 128, :], 8, NO, stg, t_stg)
            t_par = Trk("par")
            dtb = ph.sb("dtb", [64, 1])
            kb.dma(sp, dtb[:], ssd_dt_bias[j].rearrange("d (r o) -> (d r) o", o=1), writes=[t_par], st=t_par)
            acol = ph.sb("acol", [64, 1])
            kb.dma(sp, acol[:], ssd_a_log[j].rearrange("d (r o) -> (d r) o", o=1), writes=[t_par], st=t_par)
            kb.op(act, lambda: S.activation(out=acol[:], in_=acol[:], func=AF.Exp), reads=[t_par], writes=[t_par])
            kb.op(dve, lambda: V.tensor_scalar(out=acol[:], in0=acol[:], scalar1=-1.0, scalar2=None, op0=ALU.mult),
                  reads=[t_par], writes=[t_par])
            mk = ph.sb("mk", [64, TT])
            kb.op(dve, lambda: V.memset(mk[:], 1.0), writes=[t_par])
            kb.op(dve, lambda: V.memset(mk[0:32, 0:TT:128], 0.0), writes=[t_par])
            kb.op(dve, lambda: V.memset(mk[32:64, 127:TT:128], 0.0), writes=[t_par])
            nrm = Norm(ph, TT)
            xt = [ph.sb(f"xt{i}", [128, 8, TT]) for i in range(2)]
            t_xt = [Trk("xt") for _ in range(2)]
            xn = [ph.sb(f"xn{i}", [128, 8, TT], BF16) for i in range(2)]
            t_xn = [Trk("xn") for _ in range(2)]
            ev = [ph.sb(f"ev{i}", [128, TT]) for i in range(4)]
            t_ev = [Trk("ev") for _ in range(4)]
            pp = [ph.ps(f"pp{i}", [128, 512]) for i in range(3)]
            t_pp = [Trk("pp") for _ in range(3)]
            pdt = ph.ps("pdt", [64, 512]); t_pdt = Trk("pdt")
            dte = ph.sb("dte", [64, TT]); t_dte = Trk("dte")
            dtv = ph.sb("dtv", [64, TT]); t_dtv = Trk("dtv")
            dta = ph.sb("dta", [64, TT]); t_dta = Trk("dta")
            csb = ph.sb("csb", [64, TT]); t_csb = Trk("csb")
            srcc = chunked(dram_ap[src])
            ei = 0
            for i in range(NTOK // TT):
                ts = slice(i * TT, (i + 1) * TT)
                xb, txb = xt[i % 2], t_xt[i % 2]
                kb.dma(sp, xb[:], srcc[:, :, ts], writes=[txb])
                xnb, txnb = xn[i % 2], t_xn[i % 2]
                nrm.run(xb, txb, gam, t_gam, xnb, txnb)
                for oc in range(40):
                    p, tp = pp[oc % 3], t_pp[oc % 3]
                    for k in range(8):
                        kb.op(pe, lambda k=k: PE_.matmul(p[:, :TT], lhsT=win[:, k, oc * 128:(oc + 1) * 128],
                                                         rhs=xnb[:, k, :], start=(k == 0), stop=(k == 7)),
                              reads=[t_win, txnb], writes=[tp], pub=(k == 7))
                    e, te = ev[ei % 4], t_ev[ei % 4]
                    ei += 1
                    if oc < 16:
                        kb.op(act, lambda: S.activation(out=e[:], in_=p[:, :TT], func=AF.Silu), reads=[tp], writes=[te])
                        dst = ZSd[oc * 128:(oc + 1) * 128, ts]
                    else:
                        kb.op(dve, lambda: V.tensor_copy(out=e[:], in_=p[:, :TT]), reads=[tp], writes=[te])
                        dst = XBCd[(oc - 16) * 128:(oc - 15) * 128, ts]
                    kb.dma(pool, dst, e[:], reads=[te])
                for k in range(8):
                    kb.op(pe, lambda k=k: PE_.matmul(pdt[:, :TT], lhsT=win[:, k, 5120:5184], rhs=xnb[:, k, :],
                                                     start=(k == 0), stop=(k == 7)),
                          reads=[t_win, txnb], writes=[t_pdt], pub=(k == 7))
                kb.op(act, lambda: S.activation(out=dte[:], in_=pdt[:, :TT], func=AF.Exp, bias=dtb[:], scale=1.0),
                      reads=[t_pdt, t_par], writes=[t_dte])
                kb.op(act, lambda: S.activation(out=dtv[:], in_=dte[:], func=AF.Ln, bias=1.0, scale=1.0),
                      reads=[t_dte], writes=[t_dtv])
                kb.op(dve, lambda: V.tensor_scalar(out=dta[:], in0=dtv[:], scalar1=acol[:], scalar2=None, op0=ALU.mult),
                      reads=[t_dtv, t_par], writes=[t_dta])
                kb.op(dve, lambda: V.tensor_tensor_scan(out=csb[0:32, :], data0=mk[0:32, :], data1=dta[0:32, :],
                                                        initial=0.0, op0=ALU.mult, op1=ALU.add),
                      reads=[t_dta, t_par], writes=[t_csb])
                kb.op(dve, lambda: V.tensor_tensor_scan(out=csb[32:64, ::-1], data0=mk[32:64, ::-1], data1=dta[32:64, ::-1],
                                                        initial=0.0, op0=ALU.mult, op1=ALU.add),
                      reads=[t_dta, t_par], writes=[t_csb])
                for d in range(2):
                    kb.dma(pool, DCd[d * 64:d * 64 + 32, ts], csb[d * 32:(d + 1) * 32, :], reads=[t_csb], st=t_csb)
                    kb.dma(pool, DCd[d * 64 + 32:d * 64 + 64, ts], dtv[d * 32:(d + 1) * 32, :], reads=[t_dtv], st=t_dtv)
            ph.close()

        def odd_conv_phase(j):
            ph = Phase(f"o2_{j}")
            t_par = Trk("par")
            cw = ph.sb("cw", [128, 24, 4])
            for k in range(4):
                kb.dma(sp, cw[:, :, k], ssd_conv_w[j, k].rearrange("(c p) -> p c", p=128), writes=[t_par], st=t_par)
            cb = ph.sb("cb", [128, 24])
            kb.dma(sp, cb[:], ssd_conv_b[j].rearrange("(c p) -> p c", p=128), writes=[t_par], st=t_par)
            idb = ph.sb("idb", [128, 128], BF16)
            kb.op(dve, lambda: V.tensor_copy(out=idb[:], in_=ident[:]), writes=[t_par])
            xp = [ph.sb(f"xp{i}", [128, SEG + 3]) for i in range(2)]
            t_xp = [Trk("xp") for _ in range(2)]
            xc2 = [ph.sb(f"xc{i}", [128, SEG]) for i in range(2)]; t_xc2 = [Trk("xc") for _ in range(2)]
            xs_ = [ph.sb(f"xs{i}", [128, SEG]) for i in range(2)]; t_xs = [Trk("xs") for _ in range(2)]
            xb = [ph.sb(f"xb{i}", [128, SEG], BF16) for i in range(2)]; t_xb = [Trk("xb") for _ in range(2)]
            tr = [ph.sb(f"tr{i}", [128, 4, 128], BF16) for i in range(2)]; t_tr = [Trk("tr") for _ in range(2)]
            ptr = [ph.ps(f"ptr{i}", [128, 512], BF16) for i in range(2)]; t_ptr = [Trk("ptr") for _ in range(2)]
            cnt = 0
            tc_ = 0
            NB = SEG // 128
            for cc in range(24):
                rows = XBCd[cc * 128:(cc + 1) * 128, :]
                for s in range(NSEG):
                    ss = slice(s * SEG, (s + 1) * SEG)
                    xpb, txp = xp[cnt % 2], t_xp[cnt % 2]
                    xsb, txs = xs_[cnt % 2], t_xs[cnt % 2]
                    xbb, txb = xb[cnt % 2], t_xb[cnt % 2]
                    xc, t_xc = xc2[cnt % 2], t_xc2[cnt % 2]
                    cnt += 1
                    load_halo(xpb, txp, rows, s)
                    conv4(xc, t_xc, xpb, txp, cw[:, cc, :], cb[:, cc:cc + 1], t_par)
                    kb.op(act, lambda: S.activation(out=xsb[:], in_=xc[:], func=AF.Silu), reads=[t_xc], writes=[txs])
                    kb.op(act, lambda: S.activation(out=xbb[:], in_=xc[:], func=AF.Silu), reads=[t_xc], writes=[txb])
                    if cc < 16:
                        kb.dma(pool, XSfd[cc * 128:(cc + 1) * 128, ss], xsb[:], reads=[txs], st=txs)
                    elif cc < 20:
                        kb.dma(pool, BFd[cc - 16, :, ss], xbb[:], reads=[txb], st=txb)
                    else:
                        kb.dma(pool, CFd[cc - 20, :, ss], xbb[:], reads=[txb], st=txb)
                    if cc < 20:
                        for b0 in range(0, NB, 4):
                            pt_, tpt = ptr[tc_ % 2], t_ptr[tc_ % 2]
                            trb, ttr = tr[tc_ % 2], t_tr[tc_ % 2]
                            tc_ += 1
                            nb = min(4, NB - b0)
                            for q in range(nb):
                                kb.op(pe, lambda q=q: PE_.transpose(out=pt_[:, q * 128:(q + 1) * 128],
                                                                    in_=xbb[:, (b0 + q) * 128:(b0 + q + 1) * 128],
                                                                    identity=idb[:]),
                                      reads=[txb, t_par], writes=[tpt], pub=(q == nb - 1))
                            kb.op(dve, lambda: V.tensor_copy(out=trb[:, 0:nb, :],
                                                             in_=pt_[:, 0:nb * 128].rearrange("p (q c) -> p q c", q=nb)),
                                  reads=[tpt], writes=[ttr])
                            t0 = s * SEG + b0 * 128
                            if cc < 16:
                                dst = XStd[t0:t0 + nb * 128, cc * 128:(cc + 1) * 128]
                            else:
                                dst = BTd[t0:t0 + nb * 128, (cc - 16) * 128:(cc - 15) * 128]
                            kb.dma(pool, dst.rearrange("(b p) c -> p b c", p=128), trb[:, 0:nb, :], reads=[ttr], st=ttr)
            ph.close()

        def ssd_core_phase(j):
            ph = Phase(f"o3_{j}")
            t_c = Trk("c")
            sel = ph.sb("sel", [32, 32, 128])
            kb.op(dve, lambda: V.tensor_copy(out=sel[:], in_=ident[0:32, 0:32].unsqueeze(2).to_broadcast([32, 32, 128])),
                  writes=[t_c])
            tri = ph.sb("tri", [128, 2, 128])
            kb.dma(sp, tri[:], tri_d.rearrange("p (a b) -> p a b", a=2), writes=[t_c], st=t_c)
            Sst = ph.sb("Sst", [128, 32, 64]); t_S = Trk("S")
            Sbf = ph.sb("Sbf", [128, 32, 64], BF16); t_Sb = Trk("Sb")
            xst = [ph.sb(f"xst{i}", [128, 2048], BF16) for i in range(2)]; t_xst = [Trk("xst") for _ in range(2)]
            Bfm = [ph.sb(f"Bfm{i}", [128, 4, 128], BF16) for i in range(2)]; t_Bfm = [Trk("Bfm") for _ in range(2)]
            Cfm = [ph.sb(f"Cfm{i}", [128, 4, 128], BF16) for i in range(2)]; t_Cfm = [Trk("Cfm") for _ in range(2)]
            Btk = [ph.sb(f"Btk{i}", [128, 512], BF16) for i in range(2)]; t_Btk = [Trk("Btk") for _ in range(2)]
            dc = [ph.sb(f"dc{i}", [64, 128]) for i in range(2)]; t_dc = [Trk("dc") for _ in range(2)]
            tok = ph.sb("tok", [128, 64]); t_tok = Trk("tok")
            scm = [ph.sb(f"scm{i}", [128, 128]) for i in range(2)]; t_scm = [Trk("scm") for _ in range(2)]
            Eg = [ph.sb(f"Eg{i}", [128, 8, 128]) for i in range(2)]; t_Eg = [Trk("Eg") for _ in range(2)]
            Xg = [ph.sb(f"Xg{i}", [128, 8, 128]) for i in range(2)]; t_Xg = [Trk("Xg") for _ in range(2)]
            Gg = [ph.sb(f"Gg{i}", [128, 8, 128], BF16) for i in range(2)]; t_Gg = [Trk("Gg") for _ in range(2)]
            CEg = [ph.sb(f"CEg{i}", [128, 8, 128], BF16) for i in range(2)]; t_CEg = [Trk("CEg") for _ in range(2)]
            xdt = [ph.sb(f"xdt{i}", [128, 8, 64], BF16) for i in range(2)]; t_xdt = [Trk("xdt") for _ in range(2)]
            wdt = [ph.sb(f"wdt{i}", [128, 8]) for i in range(2)]; t_wdt = [Trk("wdt") for _ in range(2)]
            wx = [ph.sb(f"wx{i}", [128, 8, 64], BF16) for i in range(2)]; t_wx = [Trk("wx") for _ in range(2)]
            yo = [ph.sb(f"yo{i}", [64, 8, 128]) for i in range(2)]; t_yo = [Trk("yo") for _ in range(2)]
            pTs = ph.ps("pTs", [128, 512]); t_pT = Trk("pT")
            pT = pTs[:, 0:128]
            psc_ = [pTs[:, 128:256], pTs[:, 256:384]]
            t_psc = t_pT
            pcs2 = [ph.ps(f"pcs{i}", [128, 1024]) for i in range(2)]; t_pcs2 = [Trk("pcs") for _ in range(2)]
            lnd = ph.sb("lnd", [128, 32]); csm = ph.sb("csm", [128, 32])
            py = ph.ps("py", [64, 1024]); t_py = Trk("py")
            pst = ph.ps("pst", [128, 512]); t_pst = Trk("pst")
            NCH = NTOK // 128
            CPS = SEG // 128
            gi = 0
            li = 0
            for d in (1, 0):
                order = list(range(NCH))[::-1] if d == 1 else list(range(NCH))
                lend = 127 if d == 0 else 0
                for ci in order:
                    t0 = ci * 128
                    tsl = slice(t0, t0 + 128)
                    s = t0 // SEG
                    at_bound = (ci % CPS == 0) if d == 0 else (ci % CPS == CPS - 1)
                    if at_bound:
                        first = (s == 0) if d == 0 else (s == NSEG - 1)
                        if first:
                            kb.op(dve, lambda: V.memset(Sst[:], 0.0), writes=[t_S])
                        else:
                            fc = (s - 1) if d == 0 else s
                            kb.op(dve, lambda: V.tensor_scalar(out=Sst[:], in0=Sst[:], scalar1=fl[:, fc:fc + 1], scalar2=None,
                                                               op0=ALU.mult), reads=[t_S, t_fl], writes=[t_S])
                        kb.op(act, lambda: S.copy(out=Sbf[:], in_=Sst[:]), reads=[t_S], writes=[t_Sb])
                    xb_, txb = xst[li % 2], t_xst[li % 2]
                    bf_, tbf = Bfm[li % 2], t_Bfm[li % 2]
                    cf_, tcf = Cfm[li % 2], t_Cfm[li % 2]
                    bt_, tbt = Btk[li % 2], t_Btk[li % 2]
                    dc_, tdc = dc[li % 2], t_dc[li % 2]
                    li += 1
                    kb.dma(sp, xb_[:], XStd[tsl, :], writes=[txb])
                    kb.dma(sp, bf_[:], BFd[:, :, tsl].rearrange("g n s -> n g s"), writes=[tbf])
                    kb.dma(sp, cf_[:], CFd[:, :, tsl].rearrange("g n s -> n g s"), writes=[tcf])
                    kb.dma(sp, bt_[:], BTd[tsl, :], writes=[tbt])
                    kb.dma(sp, dc_[:], DCd[d * 64:(d + 1) * 64, tsl], writes=[tdc])
                    kb.op(pe, lambda: PE_.transpose(out=pT[:, 0:64], in_=dc_[:, :], identity=ident[0:64, 0:64]),
                          reads=[tdc, t_id], writes=[t_pT])
                    kb.op(act, lambda: S.copy(out=tok[:], in_=pT[:, 0:64]), reads=[t_pT], writes=[t_tok])
                    dt_tok = tok[:, 32:64]
                    kb.op(act, lambda: S.activation(out=lnd[:], in_=tok[:, 32:64], func=AF.Ln), reads=[t_tok], writes=[t_tok])
                    kb.op(dve, lambda: V.tensor_tensor(out=csm[:], in0=tok[:, 0:32], in1=lnd[:], op=ALU.subtract),
                          reads=[t_tok], writes=[t_tok])
                    cs_tok = csm
                    for g in range(4):
                        b2 = gi % 2
                        gi += 1
                        hs = slice(8 * g, 8 * g + 8)
                        pcs, t_pcs = pcs2[b2], t_pcs2[b2]
                        psc = psc_[b2]
                        for r in range(8):
                            kb.op(pe, lambda r=r: PE_.matmul(pcs[:, r * 128:(r + 1) * 128], lhsT=sel[:, 8 * g + r, :],
                                                             rhs=dc_[0:32, :], start=True, stop=True),
                                  reads=[t_c, tdc], writes=[t_pcs], pub=(r == 7))
                        kb.op(pe, lambda: PE_.matmul(psc, lhsT=bf_[:, g, :], rhs=cf_[:, g, :], start=True, stop=True),
                              reads=[tbf, tcf], writes=[t_psc])
                        kb.op(dve, lambda: V.tensor_tensor(out=scm[b2][:], in0=psc, in1=tri[:, d, :], op=ALU.mult),
                              reads=[t_psc, t_c], writes=[t_scm[b2]])
                        pcs3 = pcs[:, :].rearrange("p (r l) -> p r l", r=8)
                        kb.op(dve, lambda: V.tensor_tensor(out=Eg[b2][:], in0=pcs3,
                                                           in1=cs_tok[:, hs].unsqueeze(2).to_broadcast([128, 8, 128]),
                                                           op=ALU.subtract),
                              reads=[t_pcs, t_tok], writes=[t_Eg[b2]])
                        kb.op(act, lambda: S.activation(out=Eg[b2][:], in_=Eg[b2][:], func=AF.Exp),
                              reads=[t_Eg[b2]], writes=[t_Eg[b2]])
                        kb.op(act, lambda: S.activation(out=Xg[b2][:], in_=pcs3, func=AF.Exp),
                              reads=[t_pcs], writes=[t_Xg[b2]])
                        kb.op(dve, lambda: V.scalar_tensor_tensor(out=Gg[b2][:], in0=Eg[b2][:], scalar=1.0e30,
                                                                  in1=scm[b2][:].unsqueeze(1).to_broadcast([128, 8, 128]),
                                                                  op0=ALU.min, op1=ALU.mult),
                              reads=[t_Eg[b2], t_scm[b2]], writes=[t_Gg[b2]])
                        kb.op(pool, lambda: G_.tensor_tensor(out=CEg[b2][:], in0=Xg[b2][:],
                                                             in1=cf_[:, g, :].unsqueeze(1).to_broadcast([128, 8, 128]),
                                                             op=ALU.mult),
                              reads=[t_Xg[b2], tcf], writes=[t_CEg[b2]])
                        xg = xb_[:, g * 512:(g + 1) * 512].rearrange("p (r q) -> p r q", r=8)
                        kb.op(pool, lambda: G_.tensor_tensor(out=wx[b2][:], in0=xg,
                                                             in1=Eg[b2][:, :, lend].unsqueeze(2).to_broadcast([128, 8, 64]),
                                                             op=ALU.mult),
                              reads=[txb, t_Eg[b2]], writes=[t_wx[b2]])
                        for r in range(8):
                            kb.op(pe, lambda r=r: PE_.matmul(py[:, r * 128:(r + 1) * 128], lhsT=xg[:, r, :],
                                                             rhs=Gg[b2][:, r, :], start=True, stop=False),
                                  reads=[txb, t_Gg[b2]], writes=[t_py], pub=False)
                            kb.op(pe, lambda r=r: PE_.matmul(py[:, r * 128:(r + 1) * 128], lhsT=Sbf[:, 8 * g + r, :],
                                                             rhs=CEg[b2][:, r, :], start=False, stop=True),
                                  reads=[t_Sb, t_CEg[b2]], writes=[t_py], pub=(r == 7))
                        kb.op(act, lambda: S.copy(out=yo[b2][:], in_=py[:, :].rearrange("p (r l) -> p r l", r=8)),
                              reads=[t_py], writes=[t_yo[b2]])
                        kb.dma(pool, Yd[d, g * 512:(g + 1) * 512, tsl].rearrange("(r p) l -> p r l", p=64), yo[b2][:],
                               reads=[t_yo[b2]], st=t_yo[b2])
                        kb.op(pe, lambda: PE_.matmul(pst[:, :], lhsT=bt_[:, g * 128:(g + 1) * 128],
                                                     rhs=wx[b2][:].rearrange("p r q -> p (r q)"), start=True, stop=True),
                              reads=[tbt, t_wx[b2]], writes=[t_pst])
                        Sg = Sst[:, hs, :]
                        kb.op(dve, lambda: V.tensor_tensor(out=Sg, in0=Sg,
                                                           in1=Xg[b2][:, :, lend].unsqueeze(2).to_broadcast([128, 8, 64]),
                                                           op=ALU.mult),
                              reads=[t_S, t_Xg[b2], t_py], writes=[t_S])
                        kb.op(dve, lambda: V.tensor_tensor(out=Sg, in0=Sg, in1=pst[:, :].rearrange("p (r q) -> p r q", r=8),
                                                           op=ALU.add),
                              reads=[t_S, t_pst], writes=[t_S])
                        kb.op(act, lambda: S.copy(out=Sbf[:, hs, :], in_=Sg), reads=[t_S, t_py], writes=[t_Sb])
            ph.close()

        def odd_out_phase(j, src, dst):
            ph = Phase(f"o4_{j}")
            TT = 512 if NTOK % 512 == 0 else 256
            wo = ph.sb("wo", [128, 16, 1024], BF16); t_wo = Trk("wo")
            stg = [ph.sb(f"stg{i}", [128, 1024]) for i in range(3)]
            t_stg = [Trk("stg") for _ in range(3)]
            load_weight_rows(ph, wo, t_wo, lambda r: od_w_out[j, r * 128:(r + 1) * 128, :], 16, 1024, stg, t_stg)
            t_par = Trk("par")
            dcol = ph.sb("dcol", [128, 16])
            for hh in range(2):
                kb.dma(sp, dcol[hh * 64:(hh + 1) * 64, :],
                       ssd_d[j].rearrange("(c h) -> h c", h=2)[hh].partition_broadcast(64), writes=[t_par], st=t_par)
            nw = ph.sb("nw", [128, 16])
            kb.dma(sp, nw[:], ssd_norm[j].rearrange("(c p) -> p c", p=128), writes=[t_par], st=t_par)
            yf = [ph.sb(f"yf{i}", [128, 4, TT]) for i in range(2)]; t_yf = [Trk("yf") for _ in range(2)]
            ybk = [ph.sb(f"ybk{i}", [128, 4, TT]) for i in range(2)]; t_ybk = [Trk("ybk") for _ in range(2)]
            xsg = [ph.sb(f"xsg{i}", [128, 4, TT]) for i in range(2)]; t_xsg = [Trk("xsg") for _ in range(2)]
            zsg = [ph.sb(f"zsg{i}", [128, 4, TT]) for i in range(2)]; t_zsg = [Trk("zsg") for _ in range(2)]
            sqb = ph.sb("sqb", [128, 4, TT], BF16); t_sqb = Trk("sqb")
            rs = ph.sb("rs", [128, TT]); t_rs = Trk("rs")
            rs2 = ph.sb("rs2", [128, TT]); t_rs2 = Trk("rs2")
            ynb = ph.sb("ynb", [128, 16, TT], BF16); t_ynb = Trk("ynb")
            xres = [ph.sb(f"xres{i}", [128, TT]) for i in range(4)]; t_xres = [Trk("xres") for _ in range(4)]
            ores = [ph.sb(f"ores{i}", [128, TT]) for i in range(4)]; t_ores = [Trk("ores") for _ in range(4)]
            pn = ph.ps("pn", [128, 512]); t_pn = Trk("pn")
            po = [ph.ps(f"po{i}", [128, 512]) for i in range(2)]; t_po = [Trk("po") for _ in range(2)]
            srcc = chunked(dram_ap[src]); dstc = chunked(dram_ap[dst])
            Yfc = chunked(Yd[0]); Ybc = chunked(Yd[1]); XSc = chunked(XSfd); ZSc = chunked(ZSd)
            ri = 0
            gi = 0
            for i in range(NTOK // TT):
                ts = slice(i * TT, (i + 1) * TT)
                for g in range(4):
                    b2 = gi % 2
                    gi += 1
                    cs4 = slice(4 * g, 4 * g + 4)
                    kb.dma(sp, yf[b2][:], Yfc[:, cs4, ts], writes=[t_yf[b2]])
                    kb.dma(sp, ybk[b2][:], Ybc[:, cs4, ts], writes=[t_ybk[b2]])
                    kb.dma(sp, xsg[b2][:], XSc[:, cs4, ts], writes=[t_xsg[b2]])
                    kb.dma(sp, zsg[b2][:], ZSc[:, cs4, ts], writes=[t_zsg[b2]])
                    kb.op(pool, lambda: G_.tensor_tensor(out=yf[b2][:], in0=yf[b2][:], in1=ybk[b2][:], op=ALU.add),
                          reads=[t_yf[b2], t_ybk[b2]], writes=[t_yf[b2]])
                    for c in range(4):
                        ch = 4 * g + c
                        kb.op(dve, lambda c=c, ch=ch: V.scalar_tensor_tensor(out=yf[b2][:, c, :], in0=xsg[b2][:, c, :],
                                                                             scalar=dcol[:, ch:ch + 1], in1=yf[b2][:, c, :],
                                                                             op0=ALU.mult, op1=ALU.add),
                              reads=[t_xsg[b2], t_par, t_yf[b2]], writes=[t_yf[b2]])
                    kb.op(dve, lambda: V.tensor_tensor(out=yf[b2][:], in0=yf[b2][:], in1=zsg[b2][:], op=ALU.mult),
                          reads=[t_yf[b2], t_zsg[b2]], writes=[t_yf[b2]])
                    kb.op(act, lambda: S.activation(out=sqb[:], in_=yf[b2][:], func=AF.Square),
                          reads=[t_yf[b2]], writes=[t_sqb])
                    for c in range(4):
                        kb.op(pe, lambda c=c: PE_.matmul(pn[:, :TT], lhsT=ones_bf[:], rhs=sqb[:, c, :],
                                                         start=(c == 0), stop=(c == 3)),
                              reads=[t_const, t_sqb], writes=[t_pn], pub=(c == 3))
                    kb.op(act, lambda: S.activation(out=rs[:], in_=pn[:, :TT], func=AF.Sqrt, scale=1.0 / 512.0, bias=eps_t[:]),
                          reads=[t_pn, t_const], writes=[t_rs])
                    kb.op(dve, lambda: V.reciprocal(out=rs2[:], in_=rs[:]), reads=[t_rs], writes=[t_rs2])
                    for c in range(4):
                        ch = 4 * g + c
                        kb.op(dve, lambda c=c, ch=ch: V.scalar_tensor_tensor(out=ynb[:, ch, :], in0=yf[b2][:, c, :],
                                                                             scalar=nw[:, ch:ch + 1], in1=rs2[:],
                                                                             op0=ALU.mult, op1=ALU.mult),
                              reads=[t_yf[b2], t_par, t_rs2], writes=[t_ynb])
                for m in range(8):
                    ms = slice(m * 128, (m + 1) * 128)
                    p, tp = po[m % 2], t_po[m % 2]
                    xr, txr = xres[ri % 4], t_xres[ri % 4]
                    orr, tor = ores[ri % 4], t_ores[ri % 4]
                    ri += 1
                    kb.dma(sp, xr[:], srcc[:, m, ts], writes=[txr])
                    for k in range(16):
                        kb.op(pe, lambda k=k: PE_.matmul(p[:, :TT], lhsT=wo[:, k, ms], rhs=ynb[:, k, :],
                                                         start=(k == 0), stop=(k == 15)),
                              reads=[t_wo, t_ynb], writes=[tp], pub=(k == 15))
                    kb.op(dve, lambda: V.tensor_tensor(out=orr[:], in0=p[:, :TT], in1=xr[:], op=ALU.add),
                          reads=[tp, txr], writes=[tor])
                    kb.dma(pool, dstc[:, m, ts], orr[:], reads=[tor])
            ph.close()

        def odd_mixer(j, layer, src, dst):
            odd_in_phase(j, layer, src)
            odd_conv_phase(j)
            ssd_core_phase(j)
            odd_out_phase(j, src, dst)

        cur = "xT"
        nxt = ["XA", "XB"]
        ni = 0
        for layer in layers:
            j = layer // 2
            if inc("ffn1"):
                dst = nxt[ni % 2]; ni += 1
                ffn_phase(layer, 0, cur, dst)
                cur = dst
            if phases is None or any(n in phases for n in ("mixer", "e1", "e2", "e3", "e4", "o1", "o2", "o3", "o4")):
                dst = nxt[ni % 2]; ni += 1
                def sub(n):
                    return phases is None or "mixer" in phases or n in phases
                if layer % 2 == 0:
                    if sub("e1"): even_in_phase(j, layer, cur)
                    if sub("e2"): lru_phase(j)
                    if sub("e3"): s5_phase(j)
                    if sub("e4"): even_out_phase(j, cur, dst)
                else:
                    if sub("o1"): odd_in_phase(j, layer, cur)
                    if sub("o2"): odd_conv_phase(j)
                    if sub("o3"): ssd_core_phase(j)
                    if sub("o4"): odd_out_phase(j, cur, dst)
                cur = dst
            if inc("ffn2"):
                dst = nxt[ni % 2]; ni += 1
                ffn_phase(layer, 1, cur, dst)
                cur = dst
        final_phase(cur)
        kb.finish()
    return nc


N_CORES = 8
NON_WEIGHT = ("x_prompt", "x_sample")


def core_inputs(inputs):
    m = {}
    for k, v in inputs.items():
        if k in NON_WEIGHT:
            continue
        m[k] = np.ascontiguousarray(np.asarray(v, dtype=np.float32))
    m["ident"] = np.eye(128, dtype=np.float32)
    u = np.triu(np.ones((128, 128), np.float32))
    m["tri"] = np.ascontiguousarray(np.concatenate([u, u.T], axis=1))
    return m


def kernel(**inputs):
    xp = np.asarray(inputs["x_prompt"], dtype=np.float32)
    xs = np.asarray(inputs["x_sample"], dtype=np.float32)
    SEG = 2048
    NTOK = 12288
    streams = []
    flags = []
    for c in range(N_CORES):
        if c < 4:
            parts = [xp[c], xs[2 * c], xs[2 * c + 1]]
            fl = [1, 1, 1, 0, 0]
        else:
            b = 8 + 6 * (c - 4)
            parts = [xs[b + q] for q in range(6)]
            fl = [0, 0, 0, 0, 0]
        tok = np.concatenate(parts, axis=0)
        streams.append(np.ascontiguousarray(tok.T))
        f = np.zeros((128, 8), np.float32)
        f[:, :5] = np.asarray(fl, np.float32)[None, :]
        flags.append(f)
    nc = build_program(NTOK, SEG, layers=[0, 1, 2, 3])
    wmap = core_inputs(inputs)
    in_maps = []
    for c in range(N_CORES):
        m = dict(wmap)
        m["xT"] = streams[c]
        m["flags"] = flags[c]
        in_maps.append(m)
    res = run_bass_kernel_spmd(nc, in_maps, core_ids=list(range(N_CORES)))
    outs = [np.ascontiguousarray(res.results[c]["yT"].T) for c in range(N_CORES)]
    y_prompt = np.stack([outs[c][:8192] for c in range(4)], axis=0)
    ys = [None] * 32
    for c in range(4):
        ys[2 * c] = outs[c][8192:8192 + 2048]
        ys[2 * c + 1] = outs[c][8192 + 2048:]
    for c in range(4, 8):
        b = 8 + 6 * (c - 4)
        for q in range(6):
            ys[b + q] = outs[c][q * 2048:(q + 1) * 2048]
    y_sample = np.stack(ys, axis=0)
    return (y_prompt.astype(np.float32), y_sample.astype(np.float32))
```

```python
import contextlib
import numpy as np
import concourse.bass as bass
import concourse.mybir as mybir
from concourse.bass_utils import run_bass_kernel_spmd

F32 = mybir.dt.float32
BF16 = mybir.dt.bfloat16
ALU = mybir.AluOpType
AF = mybir.ActivationFunctionType

D = 1024
DFF = 2816
NFC = DFF // 128
EPS = 1e-6


class Trk:
    __slots__ = ("w", "r", "dsem", "dcnt", "name", "dq")

    def __init__(self, name=""):
        self.w = {}
        self.r = {}
        self.dsem = None
        self.dcnt = 0
        self.name = name


class Eng:
    def __init__(self, name, eng, sem):
        self.name = name
        self.eng = eng
        self.sem = sem
        self.cnt = 0
        self.waited = {}


class KB:
    def __init__(self, nc, es):
        self.nc = nc
        self.es = es
        self.nsem = 0
        self.pe = Eng("pe", nc.tensor, self.sem("pe"))
        self.dve = Eng("dve", nc.vector, self.sem("dve"))
        self.act = Eng("act", nc.scalar, self.sem("act"))
        self.pool = Eng("pool", nc.gpsimd, self.sem("pool"))
        self.sp = Eng("sp", nc.sync, self.sem("sp"))
        self.out_tokens = []
        self.dtrks = []
        self.free_dsems = {}

    def sem(self, name):
        self.nsem += 1
        s = self.es.enter_context(self.nc.semaphore(f"s{self.nsem}_{name}"))
        return s

    def sb(self, name, shape, dt):
        return self.es.enter_context(self.nc.sbuf_tensor(name, shape, dt))

    def ps(self, name, shape, dt=F32):
        return self.es.enter_context(self.nc.psum_tensor(name, shape, dt))

    def _waits(self, E, reads, writes):
        need = {}
        for t in reads:
            for s, v in t.w.items():
                if need.get(s, 0) < v:
                    need[s] = v
        for t in writes:
            for s, v in t.w.items():
                if s is E.sem:
                    continue
                if need.get(s, 0) < v:
                    need[s] = v
            for s, v in t.r.items():
                if s is E.sem:
                    continue
                if need.get(s, 0) < v:
                    need[s] = v
        for s, v in need.items():
            if s is E.sem and v > E.cnt:
                continue
            if E.waited.get(id(s), 0) < v:
                E.eng.wait_ge(s, v)
                E.waited[id(s)] = v

    def op(self, E, make, reads=(), writes=(), pub=True):
        self._waits(E, reads, writes)
        inst = make()
        tokv = E.cnt + 1
        if pub:
            inst.then_inc(E.sem, 1)
            E.cnt += 1
        for t in reads:
            if t.r.get(E.sem, 0) < tokv:
                t.r[E.sem] = tokv
        for t in writes:
            t.w = {E.sem: tokv}
            t.r = {}
        return inst

    def dma(self, Q, out, in_, reads=(), writes=(), st=None, is_out=False):
        if st is None:
            st = writes[0] if writes else reads[0]
        if st.dsem is None:
            fl_ = self.free_dsems.setdefault(Q.name, [])
            if fl_:
                st.dsem, st.dcnt = fl_.pop()
            else:
                st.dsem = self.sem("d" + Q.name)
            st.dq = Q.name
            self.dtrks.append(st)
        assert st.dq == Q.name, "one DMA queue per tracker semaphore"
        self._waits(Q, reads, writes)
        inst = Q.eng.dma_start(out=out, in_=in_)
        inst.then_inc(st.dsem, 16)
        st.dcnt += 16
        for t in reads:
            if t.r.get(st.dsem, 0) < st.dcnt:
                t.r[st.dsem] = st.dcnt
        for t in writes:
            t.w = {st.dsem: st.dcnt}
            t.r = {}
        if is_out:
            self.out_tokens.append((st.dsem, st.dcnt))
        return inst

    def barrier(self):
        engs = [self.pe, self.dve, self.act, self.pool, self.sp]
        for E in engs:
            for O in engs:
                if O is E or O.cnt == 0:
                    continue
                if E.waited.get(id(O.sem), 0) < O.cnt:
                    E.eng.wait_ge(O.sem, O.cnt)
                    E.waited[id(O.sem)] = O.cnt
            for t in self.dtrks:
                if t.dcnt and E.waited.get(id(t.dsem), 0) < t.dcnt:
                    E.eng.wait_ge(t.dsem, t.dcnt)
                    E.waited[id(t.dsem)] = t.dcnt

    def release_dsems(self):
        for t in self.dtrks:
            self.free_dsems.setdefault(t.dq, []).append((t.dsem, t.dcnt))
            t.dsem = None
            t.w = {}
            t.r = {}
        self.dtrks = []

    def finish(self):
        need = {}
        for s, v in self.out_tokens:
            if need.get(id(s), (None, 0))[1] < v:
                need[id(s)] = (s, v)
        for s, v in need.values():
            self.sp.eng.wait_ge(s, v)


def build_program(NTOK, SEG, layers, T=256, phases=None):
    nc = bass.Bass("TRN2", target_bir_lowering=False)
    NSEG = NTOK // SEG
    PW = min(512, SEG)
    NPQ = SEG // PW

    def inc(name):
        return phases is None or name in phases

    def din(name, shape):
        return nc.dram_tensor(name, list(shape), F32, kind="ExternalInput").ap()

    def dscr(name, shape, dt=F32):
        return nc.dram_tensor(name, list(shape), dt, kind="Internal").ap()

    xT = din("xT", [D, NTOK])
    flags_d = din("flags", [128, 8])
    ident_d = din("ident", [128, 128])
    ffn_norm = [din("ffn1_norm", [4, D]), din("ffn2_norm", [4, D])]
    ffn_wg = [din("ffn1_w_gate", [4, D, DFF]), din("ffn2_w_gate", [4, D, DFF])]
    ffn_wu = [din("ffn1_w_up", [4, D, DFF]), din("ffn2_w_up", [4, D, DFF])]
    ffn_wd = [din("ffn1_w_down", [4, DFF, D]), din("ffn2_w_down", [4, DFF, D])]
    mix_norm = din("mix_norm", [4, D])
    final_norm = din("final_norm", [D])
    ev_w_in = din("ev_w_in", [2, D, 2560])
    lru_conv_w = din("lru_conv_w", [2, 4, 1024])
    lru_conv_b = din("lru_conv_b", [2, 1024])
    lru_w_a = din("lru_w_a", [2, 2, 16, 64, 64])
    lru_b_a = din("lru_b_a", [2, 2, 1024])
    lru_w_x = din("lru_w_x", [2, 2, 16, 64, 64])
    lru_b_x = din("lru_b_x", [2, 2, 1024])
    lru_lam = din("lru_lam", [2, 2, 1024])
    s5_lam_re = din("s5_lam_re", [2, 2, 32, 64])
    s5_lam_im = din("s5_lam_im", [2, 2, 32, 64])
    s5_log_dt = din("s5_log_dt", [2, 2, 32])
    s5_b_re = din("s5_b_re", [2, 2, 32, 64, 16])
    s5_b_im = din("s5_b_im", [2, 2, 32, 64, 16])
    s5_c_re = din("s5_c_re", [2, 2, 32, 16, 64])
    s5_c_im = din("s5_c_im", [2, 2, 32, 16, 64])
    s5_d = din("s5_d", [2, 512])
    s5_glu_w = din("s5_glu_w", [2, 512, 512])
    s5_glu_b = din("s5_glu_b", [2, 512])
    ev_w_out = din("ev_w_out", [2, 1536, 1024])
    od_w_in = din("od_w_in", [2, D, 5184])
    ssd_conv_w = din("ssd_conv_w", [2, 4, 3072])
    ssd_conv_b = din("ssd_conv_b", [2, 3072])
    ssd_dt_bias = din("ssd_dt_bias", [2, 2, 32])
    ssd_a_log = din("ssd_a_log", [2, 2, 32])
    ssd_d = din("ssd_d", [2, 32])
    ssd_norm = din("ssd_norm", [2, 2048])
    od_w_out = din("od_w_out", [2, 2048, 1024])
    tri_d = din("tri", [128, 256])
    yT = nc.dram_tensor("yT", [D, NTOK], F32, kind="ExternalOutput").ap()
    XA = dscr("XA", [D, NTOK])
    XB = dscr("XB", [D, NTOK])
    Gd = dscr("Gd", [1024, NTOK])
    XRd = dscr("XRd", [1024, NTOK])
    Ud = dscr("Ud", [512, NTOK])
    YMd = dscr("YMd", [1024, NTOK], BF16)
    YSd = dscr("YSd", [512, NTOK])
    ZSd = dscr("ZSd", [2048, NTOK])
    XBCd = dscr("XBCd", [3072, NTOK])
    DCd = dscr("DCd", [128, NTOK])
    XSfd = dscr("XSfd", [2048, NTOK])
    XStd = dscr("XStd", [NTOK, 2048], BF16)
    BFd = dscr("BFd", [4, 128, NTOK], BF16)
    CFd = dscr("CFd", [4, 128, NTOK], BF16)
    BTd = dscr("BTd", [NTOK, 512], BF16)
    Yd = dscr("Yd", [2, 2048, NTOK])
    dram_ap = {"xT": xT, "XA": XA, "XB": XB, "yT": yT}

    with contextlib.ExitStack() as es:
        es.enter_context(nc.allow_non_contiguous_dma(reason="small parameter loads"))
        kb = KB(nc, es)
        pe, dve, act, pool, sp = kb.pe, kb.dve, kb.act, kb.pool, kb.sp
        V, S, G_, PE_ = nc.vector, nc.scalar, nc.gpsimd, nc.tensor

        ones_bf = kb.sb("ones_bf", [128, 128], BF16)
        t_const = Trk("const")
        kb.op(dve, lambda: V.memset(ones_bf[:], 1.0), writes=[t_const])
        eps_t = kb.sb("eps_t", [128, 1], F32)
        kb.op(dve, lambda: V.memset(eps_t[:], EPS), writes=[t_const])
        one_t = kb.sb("one_t", [128, 1], F32)
        kb.op(dve, lambda: V.memset(one_t[:], 1.0), writes=[t_const])
        hpi_t = kb.sb("hpi_t", [128, 1], F32)
        kb.op(dve, lambda: V.memset(hpi_t[:], float(np.pi / 2)), writes=[t_const])
        fl = kb.sb("fl_sb", [128, 8], F32)
        t_fl = Trk("fl")
        kb.dma(sp, fl[:], flags_d[:, :], writes=[t_fl])
        ident = kb.sb("ident_sb", [128, 128], F32)
        t_id = Trk("ident")
        kb.dma(sp, ident[:], ident_d[:, :], writes=[t_id])
        kb.barrier()
        kb.release_dsems()

        def chunked(ap):
            return ap.rearrange("(c p) t -> p c t", p=128)

        class Phase:
            def __init__(self, tag):
                self.tag = tag
                self.pes = contextlib.ExitStack()
                self.n = 0

            def sb(self, name, shape, dt=F32):
                self.n += 1
                return self.pes.enter_context(nc.sbuf_tensor(f"{self.tag}_{name}_{self.n}", list(shape), dt))

            def ps(self, name, shape, dt=F32):
                self.n += 1
                return self.pes.enter_context(nc.psum_tensor(f"{self.tag}_{name}_{self.n}", list(shape), dt))

            def close(self):
                kb.barrier()
                kb.release_dsems()
                self.pes.close()

        cast_rr = [0]

        def cast(out, in_, reads, writes, engs=None):
            engs = engs or [pool, dve, act]
            E = engs[cast_rr[0] % len(engs)]
            cast_rr[0] += 1
            if E is act:
                kb.op(act, lambda: S.copy(out=out, in_=in_), reads=reads, writes=writes)
            elif E is dve:
                kb.op(dve, lambda: V.tensor_copy(out=out, in_=in_), reads=reads, writes=writes)
            else:
                kb.op(pool, lambda: G_.tensor_copy(out=out, in_=in_), reads=reads, writes=writes)

        def load_weight_rows(ph, wdst, t_w, src_rows_fn, nrows, ncols, stg, t_stg):
            W = stg[0].shape[1]
            si = 0
            for r in range(nrows):
                src = src_rows_fn(r)
                for c0 in range(0, ncols, W):
                    c1 = min(ncols, c0 + W)
                    b = si % len(stg)
                    si += 1
                    kb.dma(sp, stg[b][:, :c1 - c0], src[:, c0:c1], writes=[t_stg[b]])
                    cast(wdst[:, r, c0:c1], stg[b][:, :c1 - c0], [t_stg[b]], [t_w])

        class Norm:
            def __init__(self, ph, TT):
                self.TT = TT
                self.sq = ph.sb("sq", [128, 8, TT], BF16)
                self.t_sq = Trk("sq")
                self.rs = ph.sb("rs", [128, TT])
                self.t_rs = Trk("rs")
                self.rs2 = ph.sb("rs2", [128, TT])
                self.t_rs2 = Trk("rs2")
                self.p_n = ph.ps("p_n", [128, 512])
                self.t_pn = Trk("pn")

            def run(self, xt, t_xt, gam, t_gam, out, t_out):
                TT = self.TT
                sq, rs, rs2, p_n = self.sq, self.rs, self.rs2, self.p_n
                for k in range(8):
                    kb.op(act, lambda k=k: S.activation(out=sq[:, k, :], in_=xt[:, k, :], func=AF.Square),
                          reads=[t_xt], writes=[self.t_sq])
                for k in range(8):
                    kb.op(pe, lambda k=k: PE_.matmul(p_n[:, :TT], lhsT=ones_bf[:], rhs=sq[:, k, :],
                                                     start=(k == 0), stop=(k == 7)),
                          reads=[t_const, self.t_sq], writes=[self.t_pn], pub=(k == 7))
                kb.op(act, lambda: S.activation(out=rs[:], in_=p_n[:, :TT], func=AF.Sqrt,
                                                scale=1.0 / D, bias=eps_t[:]),
                      reads=[self.t_pn, t_const], writes=[self.t_rs])
                kb.op(dve, lambda: V.reciprocal(out=rs2[:], in_=rs[:]), reads=[self.t_rs], writes=[self.t_rs2])
                for k in range(8):
                    kb.op(dve, lambda k=k: V.scalar_tensor_tensor(
                        out=out[:, k, :], in0=xt[:, k, :], scalar=gam[:, k:k + 1], in1=rs2[:],
                        op0=ALU.mult, op1=ALU.mult), reads=[t_xt, t_gam, self.t_rs2], writes=[t_out])

        def load_gamma(ph, src_vec):
            gam = ph.sb("gam", [128, 8])
            t_gam = Trk("gam")
            kb.dma(sp, gam[:], src_vec.rearrange("(c p) -> p c", p=128), writes=[t_gam])
            return gam, t_gam

        def ffn_phase(layer, which, src, dst):
            ph = Phase(f"f{layer}{which}")
            NT = NTOK // T
            gam, t_gam = load_gamma(ph, ffn_norm[which][layer])
            wg = ph.sb("wg", [128, 8, DFF], BF16)
            wu = ph.sb("wu", [128, 8, DFF], BF16)
            wd = ph.sb("wd", [128, NFC, D], BF16)
            t_wg, t_wu, t_wd = Trk("wg"), Trk("wu"), Trk("wd")
            HW = DFF // 2
            stg = [ph.sb(f"stg{i}", [128, HW]) for i in range(3)]
            t_stg = [Trk(f"stg{i}") for i in range(3)]
            load_weight_rows(ph, wg, t_wg, lambda r: ffn_wg[which][layer, r * 128:(r + 1) * 128, :], 8, DFF, stg, t_stg)
            load_weight_rows(ph, wu, t_wu, lambda r: ffn_wu[which][layer, r * 128:(r + 1) * 128, :], 8, DFF, stg, t_stg)
            load_weight_rows(ph, wd, t_wd, lambda r: ffn_wd[which][layer, r * 128:(r + 1) * 128, :], NFC, D, stg, t_stg)
            nrm = Norm(ph, T)
            xt = ph.sb("xt", [128, 8, T])
            t_xt = Trk("xt")
            xn = [ph.sb(f"xn{i}", [128, 8, T], BF16) for i in range(2)]
            t_xn = [Trk(f"xn{i}") for i in range(2)]
            h = ph.sb("h", [128, NFC, T], BF16)
            t_h = Trk("h")
            sg = [ph.sb(f"sg{i}", [128, T]) for i in range(2)]
            t_sg = [Trk(f"sg{i}") for i in range(2)]
            xres = [ph.sb(f"xres{i}", [128, T]) for i in range(4)]
            t_xres = [Trk(f"xres{i}") for i in range(4)]
            ores = [ph.sb(f"ores{i}", [128, T]) for i in range(4)]
            t_ores = [Trk(f"ores{i}") for i in range(4)]
            p_g = [ph.ps(f"p_g{i}", [128, 512]) for i in range(2)]
            t_pg = [Trk(f"pg{i}") for i in range(2)]
            p_u = [ph.ps(f"p_u{i}", [128, 512]) for i in range(2)]
            t_pu = [Trk(f"pu{i}") for i in range(2)]
            p_d = [ph.ps(f"p_d{i}", [128, 512]) for i in range(2)]
            t_pd = [Trk(f"pd{i}") for i in range(2)]
            srcc = chunked(dram_ap[src])
            dstc = chunked(dram_ap[dst])
            ri = 0
            for i in range(NT):
                ts = slice(i * T, (i + 1) * T)
                kb.dma(sp, xt[:], srcc[:, :, ts], writes=[t_xt])
                xb, txb = xn[i % 2], t_xn[i % 2]
                nrm.run(xt, t_xt, gam, t_gam, xb, txb)
                for j in range(NFC):
                    js = slice(j * 128, (j + 1) * 128)
                    pg, tpg = p_g[j % 2], t_pg[j % 2]
                    pu, tpu = p_u[j % 2], t_pu[j % 2]
                    for k in range(8):
                        kb.op(pe, lambda k=k: PE_.matmul(pg[:, :T], lhsT=wg[:, k, js], rhs=xb[:, k, :],
                                                         start=(k == 0), stop=(k == 7)),
                              reads=[t_wg, txb], writes=[tpg], pub=(k == 7))
                    for k in range(8):
                        kb.op(pe, lambda k=k: PE_.matmul(pu[:, :T], lhsT=wu[:, k, js], rhs=xb[:, k, :],
                                                         start=(k == 0), stop=(k == 7)),
                              reads=[t_wu, txb], writes=[tpu], pub=(k == 7))
                    sgb, tsg = sg[j % 2], t_sg[j % 2]
                    kb.op(act, lambda: S.activation(out=sgb[:], in_=pg[:, :T], func=AF.Silu),
                          reads=[tpg], writes=[tsg])
                    kb.op(dve, lambda: V.tensor_tensor(out=h[:, j, :], in0=sgb[:], in1=pu[:, :T], op=ALU.mult),
                          reads=[tsg, tpu], writes=[t_h])
                for m in range(8):
                    ms = slice(m * 128, (m + 1) * 128)
                    pd, tpd = p_d[m % 2], t_pd[m % 2]
                    xr, txr = xres[ri % 4], t_xres[ri % 4]
                    orr, tor = ores[ri % 4], t_ores[ri % 4]
                    ri += 1
                    kb.dma(sp, xr[:], srcc[:, m, ts], writes=[txr])
                    for j in range(NFC):
                        kb.op(pe, lambda j=j: PE_.matmul(pd[:, :T], lhsT=wd[:, j, ms], rhs=h[:, j, :],
                                                         start=(j == 0), stop=(j == NFC - 1)),
                              reads=[t_wd, t_h], writes=[tpd], pub=(j == NFC - 1))
                    kb.op(dve, lambda: V.scalar_tensor_tensor(
                        out=orr[:], in0=pd[:, :T], scalar=0.5, in1=xr[:], op0=ALU.mult, op1=ALU.add),
                        reads=[tpd, txr], writes=[tor])
                    kb.dma(pool, dstc[:, m, ts], orr[:], reads=[tor])
            ph.close()

        def final_phase(src):
            ph = Phase("fin")
            gam, t_gam = load_gamma(ph, final_norm)
            TF = 512 if NTOK % 512 == 0 else T
            nrm = Norm(ph, TF)
            xt = [ph.sb(f"fxt{i}", [128, 8, TF]) for i in range(2)]
            t_xt = [Trk("fxt") for _ in range(2)]
            ot = [ph.sb(f"fot{i}", [128, 8, TF]) for i in range(2)]
            t_ot = [Trk("fot") for _ in range(2)]
            srcc = chunked(dram_ap[src])
            dstc = chunked(yT)
            for i in range(NTOK // TF):
                ts = slice(i * TF, (i + 1) * TF)
                xb, txb = xt[i % 2], t_xt[i % 2]
                ob, tob = ot[i % 2], t_ot[i % 2]
                kb.dma(sp, xb[:], srcc[:, :, ts], writes=[txb])
                nrm.run(xb, txb, gam, t_gam, ob, tob)
                kb.dma(pool, dstc[:, :, ts], ob[:], reads=[tob], is_out=True)
            ph.close()

        def even_in_phase(j, layer, src):
            ph = Phase(f"e1_{layer}")
            TT = 512 if NTOK % 512 == 0 else 256
            gam, t_gam = load_gamma(ph, mix_norm[layer])
            NO = 2560
            win = ph.sb("win", [128, 8, NO], BF16)
            t_win = Trk("win")
            stg = [ph.sb(f"stg{i}", [128, NO // 2]) for i in range(3)]
            t_stg = [Trk(f"stg{i}") for i in range(3)]
            load_weight_rows(ph, win, t_win, lambda r: ev_w_in[j, r * 128:(r + 1) * 128, :], 8, NO, stg, t_stg)
            nrm = Norm(ph, TT)
            xt = [ph.sb(f"xt{i}", [128, 8, TT]) for i in range(2)]
            t_xt = [Trk("xt") for _ in range(2)]
            xn = [ph.sb(f"xn{i}", [128, 8, TT], BF16) for i in range(2)]
            t_xn = [Trk("xn") for _ in range(2)]
            ev = [ph.sb(f"ev{i}", [128, TT]) for i in range(4)]
            t_ev = [Trk("ev") for _ in range(4)]
            pp = [ph.ps(f"pp{i}", [128, 512]) for i in range(3)]
            t_pp = [Trk("pp") for _ in range(3)]
            srcc = chunked(dram_ap[src])
            ei = 0
            for i in range(NTOK // TT):
                ts = slice(i * TT, (i + 1) * TT)
                xb, txb = xt[i % 2], t_xt[i % 2]
                kb.dma(sp, xb[:], srcc[:, :, ts], writes=[txb])
                xnb, txnb = xn[i % 2], t_xn[i % 2]
                nrm.run(xb, txb, gam, t_gam, xnb, txnb)
                for oc in range(20):
                    p, tp = pp[oc % 3], t_pp[oc % 3]
                    for k in range(8):
                        kb.op(pe, lambda k=k: PE_.matmul(p[:, :TT], lhsT=win[:, k, oc * 128:(oc + 1) * 128],
                                                         rhs=xnb[:, k, :], start=(k == 0), stop=(k == 7)),
                              reads=[t_win, txnb], writes=[tp], pub=(k == 7))
                    e, te = ev[ei % 4], t_ev[ei % 4]
                    ei += 1
                    if oc < 8:
                        kb.op(act, lambda: S.activation(out=e[:], in_=p[:, :TT], func=AF.Gelu_apprx_tanh),
                              reads=[tp], writes=[te])
                        dst = Gd[oc * 128:(oc + 1) * 128, ts]
                    elif oc < 16:
                        kb.op(dve, lambda: V.tensor_copy(out=e[:], in_=p[:, :TT]), reads=[tp], writes=[te])
                        dst = XRd[(oc - 8) * 128:(oc - 7) * 128, ts]
                    else:
                        kb.op(dve, lambda: V.tensor_copy(out=e[:], in_=p[:, :TT]), reads=[tp], writes=[te])
                        dst = Ud[(oc - 16) * 128:(oc - 15) * 128, ts]
                    kb.dma(pool, dst, e[:], reads=[te])
            ph.close()

        def load_halo(xp, t_xp, src_rows, s):
            lo = s * SEG - 2
            hi = s * SEG + SEG + 1
            clo, chi = max(lo, 0), min(hi, NTOK)
            kb.dma(sp, xp[:, clo - lo:chi - lo], src_rows[:, clo:chi], writes=[t_xp])
            if s == 0:
                kb.op(dve, lambda: V.memset(xp[:, 0:2], 0.0), writes=[t_xp])
            else:
                kb.op(dve, lambda: V.tensor_scalar(out=xp[:, 0:2], in0=xp[:, 0:2], scalar1=fl[:, s - 1:s], scalar2=None,
                                                   op0=ALU.mult), reads=[t_xp, t_fl], writes=[t_xp])
            if s == NSEG - 1:
                kb.op(dve, lambda: V.memset(xp[:, SEG + 2:SEG + 3], 0.0), writes=[t_xp])
            else:
                kb.op(dve, lambda: V.tensor_scalar(out=xp[:, SEG + 2:SEG + 3], in0=xp[:, SEG + 2:SEG + 3],
                                                   scalar1=fl[:, s:s + 1], scalar2=None, op0=ALU.mult),
                      reads=[t_xp, t_fl], writes=[t_xp])

        def conv4(xc, t_xc, xp, t_xp, w4, bcol, t_par):
            kb.op(dve, lambda: V.tensor_scalar(out=xc[:], in0=xp[:, 0:SEG], scalar1=w4[:, 0:1], scalar2=bcol,
                                               op0=ALU.mult, op1=ALU.add), reads=[t_xp, t_par], writes=[t_xc])
            for k in range(1, 4):
                kb.op(dve, lambda k=k: V.scalar_tensor_tensor(out=xc[:], in0=xp[:, k:k + SEG], scalar=w4[:, k:k + 1],
                                                              in1=xc[:], op0=ALU.mult, op1=ALU.add),
                      reads=[t_xp, t_par, t_xc], writes=[t_xc])

        def lru_phase(j):
            ph = Phase(f"e2_{j}")
            t_par = Trk("par")
            cw = ph.sb("cw", [128, 8, 4])
            for k in range(4):
                kb.dma(sp, cw[:, :, k], lru_conv_w[j, k].rearrange("(c p) -> p c", p=128), writes=[t_par], st=t_par)
            cb = ph.sb("cb", [128, 8])
            kb.dma(sp, cb[:], lru_conv_b[j].rearrange("(c p) -> p c", p=128), writes=[t_par])
            ba = ph.sb("ba", [128, 2, 8])
            for d in range(2):
                kb.dma(sp, ba[:, d, :], lru_b_a[j, d].rearrange("(c p) -> p c", p=128), writes=[t_par], st=t_par)
            bx = ph.sb("bx", [128, 2, 8])
            for d in range(2):
                kb.dma(sp, bx[:, d, :], lru_b_x[j, d].rearrange("(c p) -> p c", p=128), writes=[t_par], st=t_par)
            lam = ph.sb("lam", [128, 2, 8])
            t_lam = Trk("lam")
            for d in range(2):
                kb.dma(sp, lam[:, d, :], lru_lam[j, d].rearrange("(c p) -> p c", p=128), writes=[t_lam], st=t_lam)
            l1 = ph.sb("l1", [128, 2, 8])
            c8 = ph.sb("c8", [128, 2, 8])
            c16 = ph.sb("c16", [128, 2, 8])
            kb.op(act, lambda: S.activation(out=l1[:], in_=lam[:], func=AF.Exp, scale=-1.0), reads=[t_lam], writes=[t_lam])
            kb.op(act, lambda: S.activation(out=l1[:], in_=l1[:], func=AF.Ln, bias=1.0, scale=1.0),
                  reads=[t_lam], writes=[t_lam])
            kb.op(dve, lambda: V.tensor_scalar(out=c8[:], in0=l1[:], scalar1=-8.0, scalar2=None, op0=ALU.mult),
                  reads=[t_lam], writes=[t_par])
            kb.op(dve, lambda: V.tensor_scalar(out=c16[:], in0=l1[:], scalar1=-16.0, scalar2=None, op0=ALU.mult),
                  reads=[t_lam], writes=[t_par])
            wbd = ph.sb("wbd", [128, 32, 128], BF16)
            t_wbd = Trk("wbd")
            pp_ = Phase(f"e2p_{j}")
            wst = pp_.sb("wst", [128, 32, 128])
            t_wst = Trk("wst")
            kb.op(dve, lambda: V.memset(wst[:], 0.0), writes=[t_wst])

            def widx(g, d, c):
                return (g * 2 + d) * 8 + c

            for g, Wd_ in enumerate((lru_w_a, lru_w_x)):
                for d in range(2):
                    for c in range(8):
                        for hh in range(2):
                            kb.dma(sp, wst[hh * 64:(hh + 1) * 64, widx(g, d, c), hh * 64:(hh + 1) * 64],
                                   Wd_[j, d, 2 * c + hh], writes=[t_wst], st=t_wst)
            kb.op(dve, lambda: V.tensor_copy(out=wbd[:], in_=wst[:]), reads=[t_wst], writes=[t_wbd])
            pp_.close()

            hb = ph.sb("hb", [128, NTOK])
            t_hb = Trk("hb")
            xp = [ph.sb(f"xp{i}", [128, SEG + 3]) for i in range(2)]
            t_xp = [Trk("xp") for _ in range(2)]
            xc2 = [ph.sb(f"xc{i}", [128, SEG]) for i in range(2)]; t_xc2 = [Trk("xc") for _ in range(2)]
            xcb2 = [ph.sb(f"xcb{i}", [128, SEG], BF16) for i in range(2)]; t_xcb2 = [Trk("xcb") for _ in range(2)]
            Rb2 = [ph.sb(f"Rb{i}", [128, SEG]) for i in range(2)]; t_R2 = [Trk("R") for _ in range(2)]
            Ab2 = [ph.sb(f"Ab{i}", [128, SEG]) for i in range(2)]; t_A2 = [Trk("A") for _ in range(2)]
            Sb2 = [ph.sb(f"Sb{i}", [128, SEG]) for i in range(2)]; t_S2 = [Trk("S") for _ in range(2)]
            Ib2 = [ph.sb(f"Ib{i}", [128, SEG]) for i in range(2)]; t_I2 = [Trk("I") for _ in range(2)]
            hf = [ph.sb(f"hf{i}", [128, SEG]) for i in range(2)]
            t_hf = [Trk("hf") for _ in range(2)]
            Gt = ph.sb("Gt", [128, SEG]); t_G = Trk("G")
            yb = [ph.sb(f"yb{i}", [128, SEG], BF16) for i in range(2)]
            t_yb = [Trk("yb") for _ in range(2)]
            ini = ph.sb("ini", [128, 2]); t_ini = Trk("ini")
            pa = ph.ps("pa", [128, SEG]); t_pa = Trk("pa")
            px = ph.ps("px", [128, SEG]); t_px = Trk("px")
            def stA(cnt, c, d, s):
                rows = XRd[c * 128:(c + 1) * 128, :]
                ss = slice(s * SEG, (s + 1) * SEG)
                xpb, txp = xp[cnt % 2], t_xp[cnt % 2]
                xc, t_xc = xc2[cnt % 2], t_xc2[cnt % 2]
                xcb, t_xcb = xcb2[cnt % 2], t_xcb2[cnt % 2]
                Rb, t_R = Rb2[cnt % 2], t_R2[cnt % 2]
                Ab, t_A = Ab2[cnt % 2], t_A2[cnt % 2]
                Sb, t_S = Sb2[cnt % 2], t_S2[cnt % 2]
                Ib, t_I = Ib2[cnt % 2], t_I2[cnt % 2]
                load_halo(xpb, txp, rows, s)
                conv4(xc, t_xc, xpb, txp, cw[:, c, :], cb[:, c:c + 1], t_par)
                kb.op(act, lambda: S.copy(out=xcb[:], in_=xc[:]), reads=[t_xc], writes=[t_xcb])
                for q in range(NPQ):
                    qs = slice(q * PW, (q + 1) * PW)
                    kb.op(pe, lambda qs=qs: PE_.matmul(pa[:, qs], lhsT=wbd[:, widx(0, d, c), :], rhs=xcb[:, qs],
                                                       start=True, stop=True),
                          reads=[t_wbd, t_xcb], writes=[t_pa], pub=(q == NPQ - 1))
                for q in range(NPQ):
                    qs = slice(q * PW, (q + 1) * PW)
                    kb.op(pe, lambda qs=qs: PE_.matmul(px[:, qs], lhsT=wbd[:, widx(1, d, c), :], rhs=xcb[:, qs],
                                                       start=True, stop=True),
                          reads=[t_wbd, t_xcb], writes=[t_px], pub=(q == NPQ - 1))
                kb.op(act, lambda: S.activation(out=Rb[:], in_=pa[:], func=AF.Sigmoid, bias=ba[:, d, c:c + 1],
                                                scale=1.0), reads=[t_pa, t_par], writes=[t_R])
                kb.op(act, lambda: S.activation(out=Ab[:], in_=Rb[:], func=AF.Exp, scale=c8[:, d, c:c + 1]),
                      reads=[t_R, t_par], writes=[t_A])
                kb.op(act, lambda: S.activation(out=Sb[:], in_=Rb[:], func=AF.Exp, scale=c16[:, d, c:c + 1]),
                      reads=[t_R, t_par], writes=[t_S])
                kb.op(act, lambda: S.activation(out=Sb[:], in_=Sb[:], func=AF.Sqrt, scale=-1.0, bias=one_t[:]),
                      reads=[t_S, t_const], writes=[t_S])
                kb.op(act, lambda: S.activation(out=Ib[:], in_=px[:], func=AF.Sigmoid, bias=bx[:, d, c:c + 1],
                                                scale=1.0), reads=[t_px, t_par], writes=[t_I])

            def stB(cnt, c, d, s):
                ss = slice(s * SEG, (s + 1) * SEG)
                xpb, txp = xp[cnt % 2], t_xp[cnt % 2]
                xc, t_xc = xc2[cnt % 2], t_xc2[cnt % 2]
                xcb, t_xcb = xcb2[cnt % 2], t_xcb2[cnt % 2]
                Rb, t_R = Rb2[cnt % 2], t_R2[cnt % 2]
                Ab, t_A = Ab2[cnt % 2], t_A2[cnt % 2]
                Sb, t_S = Sb2[cnt % 2], t_S2[cnt % 2]
                Ib, t_I = Ib2[cnt % 2], t_I2[cnt % 2]
                kb.op(dve, lambda: V.tensor_tensor(out=Ib[:], in0=Ib[:], in1=xc[:], op=ALU.mult),
                      reads=[t_I, t_xc], writes=[t_I])
                kb.op(dve, lambda: V.tensor_tensor(out=Ib[:], in0=Ib[:], in1=Sb[:], op=ALU.mult),
                      reads=[t_I, t_S], writes=[t_I])
                if d == 1:
                    if s == NSEG - 1:
                        init = 0.0
                        rd = []
                    else:
                        kb.op(dve, lambda: V.tensor_scalar(out=ini[:, 0:1], in0=hb[:, (s + 1) * SEG:(s + 1) * SEG + 1],
                                                           scalar1=fl[:, s:s + 1], scalar2=None, op0=ALU.mult),
                              reads=[t_hb, t_fl], writes=[t_ini])
                        init = ini[:, 0:1]
                        rd = [t_ini]
                    kb.op(dve, lambda: V.tensor_tensor_scan(out=hb[:, ss][:, ::-1], data0=Ab[:, ::-1],
                                                            data1=Ib[:, ::-1], initial=init,
                                                            op0=ALU.mult, op1=ALU.add),
                          reads=[t_A, t_I] + rd, writes=[t_hb])
                else:
                    hfb, thf = hf[cnt % 2], t_hf[cnt % 2]
                    hfp, thfp = hf[(cnt + 1) % 2], t_hf[(cnt + 1) % 2]
                    if s == 0:
                        init = 0.0
                        rd = []
                    else:
                        kb.op(dve, lambda: V.tensor_scalar(out=ini[:, 1:2], in0=hfp[:, SEG - 1:SEG],
                                                           scalar1=fl[:, s - 1:s], scalar2=None, op0=ALU.mult),
                              reads=[thfp, t_fl], writes=[t_ini])
                        init = ini[:, 1:2]
                        rd = [t_ini]
                    kb.op(dve, lambda: V.tensor_tensor_scan(out=hfb[:], data0=Ab[:], data1=Ib[:], initial=init,
                                                            op0=ALU.mult, op1=ALU.add),
                          reads=[t_A, t_I] + rd, writes=[thf])
                    kb.dma(sp, Gt[:], Gd[c * 128:(c + 1) * 128, ss], writes=[t_G])
                    kb.op(dve, lambda: V.tensor_tensor(out=Ib[:], in0=hfb[:], in1=hb[:, ss], op=ALU.add),
                          reads=[thf, t_hb], writes=[t_I])
                    ybb, tyb = yb[cnt % 2], t_yb[cnt % 2]
                    kb.op(pool, lambda: G_.tensor_tensor(out=ybb[:], in0=Ib[:], in1=Gt[:], op=ALU.mult),
                          reads=[t_I, t_G], writes=[tyb])
                    kb.dma(pool, YMd[c * 128:(c + 1) * 128, ss], ybb[:], reads=[tyb])

            units = []
            for c in range(8):
                for d in (1, 0):
                    segs = list(range(NSEG))[::-1] if d == 1 else list(range(NSEG))
                    for s in segs:
                        units.append((len(units), c, d, s))
            for i in range(len(units) + 1):
                if i < len(units):
                    stA(*units[i])
                if i >= 1:
                    stB(*units[i - 1])
            ph.close()

        def s5_phase(j):
            ph = Phase(f"e3_{j}")
            t_p = Trk("s5par")
            NC_ = 32
            KB8 = 8
            J = SEG // KB8

            def pt(name, shape=None):
                return ph.sb(name, shape or [128, NC_])

            lre, lim, ldt = pt("lre"), pt("lim"), pt("ldt")
            for d in range(2):
                kb.dma(sp, lre[:, d * 16:(d + 1) * 16],
                       s5_lam_re[j, d].rearrange("(t g) n -> (g n) t", g=2), writes=[t_p], st=t_p)
                kb.dma(sp, lim[:, d * 16:(d + 1) * 16],
                       s5_lam_im[j, d].rearrange("(t g) n -> (g n) t", g=2), writes=[t_p], st=t_p)
                for g in range(2):
                    kb.dma(sp, ldt[g * 64:(g + 1) * 64, d * 16:(d + 1) * 16],
                           s5_log_dt[j, d].rearrange("(t g) -> g t", g=2)[g].partition_broadcast(64), writes=[t_p], st=t_p)

            def vv(out, a, b_, op):
                kb.op(dve, lambda: V.tensor_tensor(out=out, in0=a, in1=b_, op=op), reads=[t_p], writes=[t_p])

            def vs(out, a, s1, op):
                kb.op(dve, lambda: V.tensor_scalar(out=out, in0=a, scalar1=s1, scalar2=None, op0=op),
                      reads=[t_p], writes=[t_p])

            dtt, th, lr, rho = pt("dtt"), pt("th"), pt("lr"), pt("rho")
            kb.op(act, lambda: S.activation(out=dtt[:], in_=ldt[:], func=AF.Exp), reads=[t_p], writes=[t_p])
            vv(th[:], lim[:], dtt[:], ALU.mult)
            vv(lr[:], lre[:], dtt[:], ALU.mult)
            kb.op(act, lambda: S.activation(out=rho[:], in_=lr[:], func=AF.Exp), reads=[t_p], writes=[t_p])
            cs_, sn_ = pt("cs"), pt("sn")
            kb.op(act, lambda: S.activation(out=sn_[:], in_=th[:], func=AF.Sin, scale=1.0 / 32.0), reads=[t_p], writes=[t_p])
            kb.op(act, lambda: S.activation(out=cs_[:], in_=th[:], func=AF.Sin, scale=1.0 / 32.0, bias=hpi_t[:]),
                  reads=[t_p, t_const], writes=[t_p])
            cc, s2, sc = pt("cc"), pt("s2"), pt("sc")

            def double_angle(c_, s_):
                vv(cc[:], c_[:], c_[:], ALU.mult)
                vv(s2[:], s_[:], s_[:], ALU.mult)
                vv(sc[:], s_[:], c_[:], ALU.mult)
                vv(c_[:], cc[:], s2[:], ALU.subtract)
                vs(s_[:], sc[:], 2.0, ALU.mult)

            for _ in range(5):
                double_angle(cs_, sn_)
            c8, s8, rho8 = pt("c8"), pt("s8"), pt("rho8")
            vs(c8[:], cs_[:], 1.0, ALU.mult)
            vs(s8[:], sn_[:], 1.0, ALU.mult)
            for _ in range(3):
                double_angle(c8, s8)
            vv(rho8[:], rho[:], rho[:], ALU.mult)
            vv(rho8[:], rho8[:], rho8[:], ALU.mult)
            vv(rho8[:], rho8[:], rho8[:], ALU.mult)
            abr, abi = pt("abr"), pt("abi")
            vv(abr[:], rho[:], cs_[:], ALU.mult)
            vv(abi[:], rho[:], sn_[:], ALU.mult)
            t1, t2, den, abm1 = pt("t1"), pt("t2"), pt("den"), pt("abm1")
            vv(t1[:], lre[:], lre[:], ALU.mult)
            vv(t2[:], lim[:], lim[:], ALU.mult)
            vv(den[:], t1[:], t2[:], ALU.add)
            kb.op(dve, lambda: V.reciprocal(out=den[:], in_=den[:]), reads=[t_p], writes=[t_p])
            vs(abm1[:], abr[:], -1.0, ALU.add)
            cre, cim = pt("cre"), pt("cim")
            vv(t1[:], abm1[:], lre[:], ALU.mult)
            vv(t2[:], abi[:], lim[:], ALU.mult)
            vv(t1[:], t1[:], t2[:], ALU.add)
            vv(cre[:], t1[:], den[:], ALU.mult)
            vv(t1[:], abi[:], lre[:], ALU.mult)
            vv(t2[:], abm1[:], lim[:], ALU.mult)
            vv(t1[:], t1[:], t2[:], ALU.subtract)
            vv(cim[:], t1[:], den[:], ALU.mult)
            APR = ph.sb("APR", [128, 9, NC_])
            API = ph.sb("API", [128, 9, NC_])
            kb.op(dve, lambda: V.memset(APR[:, 0, :], 1.0), writes=[t_p])
            kb.op(dve, lambda: V.memset(API[:, 0, :], 0.0), writes=[t_p])
            vs(APR[:, 1, :], abr[:], 1.0, ALU.mult)
            vs(API[:, 1, :], abi[:], 1.0, ALU.mult)
            for p in range(2, 9):
                vv(t1[:], APR[:, p - 1, :], abr[:], ALU.mult)
                vv(t2[:], API[:, p - 1, :], abi[:], ALU.mult)
                vv(APR[:, p, :], t1[:], t2[:], ALU.subtract)
                vv(t1[:], APR[:, p - 1, :], abi[:], ALU.mult)
                vv(t2[:], API[:, p - 1, :], abr[:], ALU.mult)
                vv(API[:, p, :], t1[:], t2[:], ALU.add)
            pzr = ph.ps("pzr", [128, 512]); t_pzr = Trk("pzr")
            pzi = ph.ps("pzi", [128, 512]); t_pzi = Trk("pzi")
            py = ph.ps("py", [128, 2048]); t_py = Trk("py")
            pm = ph.ps("pm", [32, 512]); t_pm = Trk("pm")
            bbre = ph.sb("bbre", [128, NC_, 32])
            bbim = ph.sb("bbim", [128, NC_, 32])
            bbimn = ph.sb("bbimn", [128, NC_, 32])
            CTre = ph.sb("CTre", [128, NC_, 32])
            CTim = ph.sb("CTim", [128, NC_, 32])
            dsk = ph.sb("dsk", [32, 16])
            kb.dma(sp, dsk[:], s5_d[j].rearrange("(t c) -> c t", c=32), writes=[t_p], st=t_p)
            pp_ = Phase(f"e3p_{j}")
            bre = pp_.sb("bre", [128, NC_, 32])
            bim = pp_.sb("bim", [128, NC_, 32])
            kb.op(dve, lambda: V.memset(bre[:], 0.0), writes=[t_p])
            kb.op(dve, lambda: V.memset(bim[:], 0.0), writes=[t_p])
            for g in range(2):
                for (dst_, src_) in ((bre, s5_b_re), (bim, s5_b_im)):
                    for d in range(2):
                        kb.dma(sp, dst_[g * 64:(g + 1) * 64, d * 16:(d + 1) * 16, g * 16:(g + 1) * 16],
                               src_[j, d].rearrange("(t g) n c -> g n t c", g=2)[g], writes=[t_p], st=t_p)
            tb = pp_.sb("tb", [128, NC_, 32])

            def bc(col):
                return col[:].unsqueeze(2).to_broadcast([128, NC_, 32])

            vv(bbre[:], bre[:], bc(cre), ALU.mult)
            vv(tb[:], bim[:], bc(cim), ALU.mult)
            vv(bbre[:], bbre[:], tb[:], ALU.subtract)
            vv(bbim[:], bim[:], bc(cre), ALU.mult)
            vv(tb[:], bre[:], bc(cim), ALU.mult)
            vv(bbim[:], bbim[:], tb[:], ALU.add)
            vs(bbimn[:], bbim[:], -1.0, ALU.mult)
            craw = [pp_.sb("crre", [32, NC_, 128]), pp_.sb("crim", [32, NC_, 128])]
            t_c = Trk("craw")
            for ri_, src_ in enumerate((s5_c_re, s5_c_im)):
                kb.op(dve, lambda: V.memset(craw[ri_][:], 0.0), writes=[t_c])
                for g in range(2):
                    for d in range(2):
                        kb.dma(sp, craw[ri_][g * 16:(g + 1) * 16, d * 16:(d + 1) * 16, g * 64:(g + 1) * 64],
                               src_[j, d].rearrange("(t g) c n -> g c t n", g=2)[g], writes=[t_c], st=t_c)
            for ri_, CT in enumerate((CTre, CTim)):
                for q in range(NC_):
                    kb.op(pe, lambda q=q: PE_.transpose(out=py[:, q * 32:(q + 1) * 32], in_=craw[ri_][:, q, :],
                                                        identity=ident[0:32, 0:32]),
                          reads=[t_c, t_id], writes=[t_py], pub=(q == NC_ - 1))
                kb.op(dve, lambda: V.tensor_copy(out=CT[:], in_=py[:, 0:NC_ * 32].rearrange("p (q c) -> p q c", q=NC_)),
                      reads=[t_py], writes=[t_p])
            pp_.close()
            CAre = [ph.sb(f"CAre{d}", [128, 9, 32]) for d in range(2)]
            CAim = [ph.sb(f"CAim{d}", [128, 9, 32]) for d in range(2)]
            tca = ph.sb("tca", [128, 9, 32])
            t_ca = Trk("ca")
            lCre = [ph.sb(f"lCre{d}", [128, 9, 32], BF16) for d in range(2)]
            lCimn = [ph.sb(f"lCimn{d}", [128, 9, 32], BF16) for d in range(2)]
            t_lC = Trk("lC")
            Dgr = ph.sb("Dgr", [128, 8, 128]); Dgi = ph.sb("Dgi", [128, 8, 128]); t_dg = Trk("dg")
            lBzr = [ph.sb(f"lBzr{d}", [32, 8, 128], BF16) for d in range(2)]
            lBzi = [ph.sb(f"lBzi{d}", [32, 8, 128], BF16) for d in range(2)]
            t_lB = Trk("lB")
            Mfb = ph.sb("Mfb", [32, 2, 8, 32], BF16); M0 = ph.sb("M0", [32, 32], BF16); M0f = ph.sb("M0f", [32, 32])
            t_M = Trk("M")
            cosT = ph.sb("cosT", [128, J + 2]); sinT = ph.sb("sinT", [128, J + 2]); t_tab = [Trk("tab0"), Trk("tab1")]
            cosTs = [cosT, ph.sb("cosT1", [128, J + 2])]
            sinTs = [sinT, ph.sb("sinT1", [128, J + 2])]
            RT = [ph.sb(f"RT{d}", [128, J]) for d in range(2)]; t_RT = Trk("RT")
            q1 = ph.sb("q1", [128, J + 2]); q2 = ph.sb("q2", [128, J + 2]); t_q1 = Trk("q1"); t_q2 = Trk("q2")
            uf = [ph.sb(f"uf{i}", [32, SEG]) for i in range(2)]; t_uf = [Trk("uf") for _ in range(2)]
            ub = [ph.sb(f"ub{i}", [32, SEG], BF16) for i in range(2)]; t_ub = [Trk("ub") for _ in range(2)]
            w1 = ph.sb("w1", [128, J + 1]); w2 = ph.sb("w2", [128, J + 1]); t_w1 = Trk("w1"); t_w2 = Trk("w2")
            btre = ph.sb("btre", [128, J]); btim = ph.sb("btim", [128, J]); t_bt = Trk("bt")
            gre = [ph.sb(f"gre{d}", [128, J + 1]) for d in range(2)]
            gim = [ph.sb(f"gim{d}", [128, J + 1]) for d in range(2)]
            t_g = [Trk("g0"), Trk("g1")]
            Hf = [ph.sb("Hfre", [128, J + 1], BF16), ph.sb("Hfim", [128, J + 1], BF16)]; t_Hf = Trk("Hf")
            Hb = [ph.sb("Hbre", [128, NSEG, J + 1], BF16), ph.sb("Hbim", [128, NSEG, J + 1], BF16)]; t_Hb = Trk("Hb")
            ot_ = [ph.sb(f"ot{i}", [32, SEG]) for i in range(2)]; t_ot = [Trk("ot") for _ in range(2)]
            ini = ph.sb("ini", [128, 8]); t_ini = Trk("ini")
            li = 0
            for ti in range(16):
                for d in (1, 0):
                    col = d * 16 + ti
                    cT, sT, ttab = cosTs[d], sinTs[d], t_tab[d]
                    ctr = CTre[:, col, :].unsqueeze(1).to_broadcast([128, 9, 32])
                    cti = CTim[:, col, :].unsqueeze(1).to_broadcast([128, 9, 32])
                    apr = APR[:, :, col].unsqueeze(2).to_broadcast([128, 9, 32])
                    api = API[:, :, col].unsqueeze(2).to_broadcast([128, 9, 32])

                    def ca(out, a, b_, op):
                        kb.op(dve, lambda: V.tensor_tensor(out=out, in0=a, in1=b_, op=op), reads=[t_p, t_ca], writes=[t_ca])

                    ca(CAre[d][:], ctr, apr, ALU.mult)
                    ca(tca[:], cti, api, ALU.mult)
                    ca(CAre[d][:], CAre[d][:], tca[:], ALU.subtract)
                    ca(CAim[d][:], ctr, api, ALU.mult)
                    ca(tca[:], cti, apr, ALU.mult)
                    ca(CAim[d][:], CAim[d][:], tca[:], ALU.add)
                    kb.op(act, lambda: S.copy(out=lCre[d][:], in_=CAre[d][:]), reads=[t_ca], writes=[t_lC])
                    kb.op(act, lambda: S.mul(out=lCimn[d][:], in_=CAim[d][:], mul=-1.0), reads=[t_ca], writes=[t_lC])
                    kb.op(pe, lambda: PE_.matmul(pm[:, 0:256], lhsT=bbre[:, col, :],
                                                 rhs=CAre[d][:, 0:8, :].rearrange("p a c -> p (a c)"), start=True, stop=False),
                          reads=[t_p, t_ca], writes=[t_pm], pub=False)
                    kb.op(pe, lambda: PE_.matmul(pm[:, 0:256], lhsT=bbimn[:, col, :],
                                                 rhs=CAim[d][:, 0:8, :].rearrange("p a c -> p (a c)"), start=False, stop=True),
                          reads=[t_p, t_ca], writes=[t_pm])
                    kb.op(dve, lambda: V.tensor_copy(out=Mfb[:, d, :, :], in_=pm[:, 0:256].rearrange("p (a c) -> p a c", a=8)),
                          reads=[t_pm], writes=[t_M])
                    if d == 1:
                        kb.op(dve, lambda: V.tensor_copy(out=M0f[:], in_=pm[:, 0:32]), reads=[t_pm], writes=[t_M])
                    else:
                        kb.op(dve, lambda: V.tensor_tensor(out=M0[:], in0=M0f[:], in1=pm[:, 0:32], op=ALU.add),
                              reads=[t_pm, t_M], writes=[t_M])
                    idb = ident[:, :].unsqueeze(1).to_broadcast([128, 8, 128])
                    kb.op(dve, lambda: V.tensor_tensor(out=Dgr[:], in0=idb,
                                                       in1=APR[:, 0:8, col].unsqueeze(2).to_broadcast([128, 8, 128]), op=ALU.mult),
                          reads=[t_p, t_id], writes=[t_dg])
                    kb.op(dve, lambda: V.tensor_tensor(out=Dgi[:], in0=idb,
                                                       in1=API[:, 0:8, col].unsqueeze(2).to_broadcast([128, 8, 128]), op=ALU.mult),
                          reads=[t_p, t_id], writes=[t_dg])
                    for hh in range(2):
                        hs_ = slice(hh * 512, (hh + 1) * 512)
                        dgr = Dgr[:].rearrange("p a n -> p (a n)")[:, hs_]
                        dgi = Dgi[:].rearrange("p a n -> p (a n)")[:, hs_]
                        kb.op(pe, lambda: PE_.matmul(py[0:32, hs_], lhsT=bbre[:, col, :], rhs=dgr, start=True, stop=False),
                              reads=[t_p, t_dg], writes=[t_py], pub=False)
                        kb.op(pe, lambda: PE_.matmul(py[0:32, hs_], lhsT=bbimn[:, col, :], rhs=dgi, start=False, stop=True),
                              reads=[t_p, t_dg], writes=[t_py], pub=False)
                        h2 = slice(1024 + hh * 512, 1024 + (hh + 1) * 512)
                        kb.op(pe, lambda: PE_.matmul(py[0:32, h2], lhsT=bbre[:, col, :], rhs=dgi, start=True, stop=False),
                              reads=[t_p, t_dg], writes=[t_py], pub=False)
                        kb.op(pe, lambda: PE_.matmul(py[0:32, h2], lhsT=bbim[:, col, :], rhs=dgr, start=False, stop=True),
                              reads=[t_p, t_dg], writes=[t_py], pub=(hh == 1))
                    kb.op(act, lambda: S.copy(out=lBzr[d][:], in_=py[0:32, 0:1024].rearrange("p (a n) -> p a n", a=8)),
                          reads=[t_py], writes=[t_lB])
                    kb.op(act, lambda: S.copy(out=lBzi[d][:], in_=py[0:32, 1024:2048].rearrange("p (a n) -> p a n", a=8)),
                          reads=[t_py], writes=[t_lB])
                    kb.op(dve, lambda: V.tensor_copy(out=cT[:, 0:1], in_=c8[:, col:col + 1]), reads=[t_p], writes=[ttab])
                    kb.op(dve, lambda: V.tensor_scalar(out=sT[:, 0:1], in0=s8[:, col:col + 1], scalar1=-1.0, scalar2=None,
                                                       op0=ALU.mult), reads=[t_p], writes=[ttab])
                    kb.op(dve, lambda: V.memset(cT[:, 1:2], 1.0), writes=[ttab])
                    kb.op(dve, lambda: V.memset(sT[:, 1:2], 0.0), writes=[ttab])
                    kb.op(dve, lambda: V.tensor_copy(out=cT[:, 2:3], in_=c8[:, col:col + 1]), reads=[t_p], writes=[ttab])
                    kb.op(dve, lambda: V.tensor_copy(out=sT[:, 2:3], in_=s8[:, col:col + 1]), reads=[t_p], writes=[ttab])
                    cV, sV = cT[:, 1:J + 2], sT[:, 1:J + 2]
                    m = 2
                    while m < J + 1:
                        n = min(m - 1, J + 1 - m)
                        cr, ci = cV[:, m - 1:m], sV[:, m - 1:m]
                        kb.op(dve, lambda: V.tensor_scalar(out=q1[:, 0:n], in0=sV[:, 1:1 + n], scalar1=ci, scalar2=None,
                                                           op0=ALU.mult), reads=[ttab], writes=[t_q1])
                        kb.op(dve, lambda: V.tensor_scalar(out=q2[:, 0:n], in0=cV[:, 1:1 + n], scalar1=ci, scalar2=None,
                                                           op0=ALU.mult), reads=[ttab], writes=[t_q2])
                        kb.op(dve, lambda: V.scalar_tensor_tensor(out=cV[:, m:m + n], in0=cV[:, 1:1 + n], scalar=cr,
                                                                  in1=q1[:, 0:n], op0=ALU.mult, op1=ALU.subtract),
                              reads=[ttab, t_q1], writes=[ttab])
                        kb.op(dve, lambda: V.scalar_tensor_tensor(out=sV[:, m:m + n], in0=sV[:, 1:1 + n], scalar=cr,
                                                                  in1=q2[:, 0:n], op0=ALU.mult, op1=ALU.add),
                              reads=[ttab, t_q2], writes=[ttab])
                        m += n
                    kb.op(dve, lambda: V.tensor_copy(out=RT[d][:], in_=rho8[:, col:col + 1].to_broadcast([128, J])),
                          reads=[t_p], writes=[t_RT])
                for d in (1, 0):
                    cT, sT, ttab = cosTs[d], sinTs[d], t_tab[d]
                    cV, sV = cT[:, 1:J + 2], sT[:, 1:J + 2]
                    EQr, EQi = cV[:, J:J + 1], sV[:, J:J + 1]
                    gr, gi_, tg = gre[d], gim[d], t_g[d]
                    segs = list(range(NSEG))[::-1] if d == 1 else list(range(NSEG))
                    for si_, s in enumerate(segs):
                        ss = slice(s * SEG, (s + 1) * SEG)
                        ufb, tuf = uf[li % 2], t_uf[li % 2]
                        ubb, tub = ub[li % 2], t_ub[li % 2]
                        otb, tot = ot_[li % 2], t_ot[li % 2]
                        li += 1
                        kb.dma(sp, ufb[:], Ud[32 * ti:32 * ti + 32, ss], writes=[tuf])
                        kb.op(act, lambda: S.copy(out=ubb[:], in_=ufb[:]), reads=[tuf], writes=[tub])
                        for (pz, tpz, lBz) in ((pzr, t_pzr, lBzr[d]), (pzi, t_pzi, lBzi[d])):
                            for k in range(8):
                                e = (7 - k) if d == 0 else k
                                kb.op(pe, lambda k=k, e=e: PE_.matmul(pz[:, 0:J], lhsT=lBz[:, e, :], rhs=ubb[:, k:SEG:8],
                                                                      start=(k == 0), stop=(k == 7)),
                                      reads=[t_lB, tub], writes=[tpz], pub=(k == 7))
                        if d == 0:
                            Ec, Es = cV[:, 0:J], sV[:, 0:J]
                        else:
                            Ec, Es = cV[:, 0:J][:, ::-1], sV[:, 0:J][:, ::-1]
                        kb.op(dve, lambda: V.tensor_tensor(out=w1[:, 0:J], in0=Ec, in1=pzr[:, 0:J], op=ALU.mult),
                              reads=[ttab, t_pzr], writes=[t_w1])
                        kb.op(dve, lambda: V.tensor_tensor(out=w2[:, 0:J], in0=Es, in1=pzi[:, 0:J], op=ALU.mult),
                              reads=[ttab, t_pzi], writes=[t_w2])
                        kb.op(dve, lambda: V.tensor_tensor(out=btre[:], in0=w1[:, 0:J], in1=w2[:, 0:J], op=ALU.add),
                              reads=[t_w1, t_w2], writes=[t_bt])
                        kb.op(dve, lambda: V.tensor_tensor(out=w1[:, 0:J], in0=Ec, in1=pzi[:, 0:J], op=ALU.mult),
                              reads=[ttab, t_pzi], writes=[t_w1])
                        kb.op(dve, lambda: V.tensor_tensor(out=w2[:, 0:J], in0=Es, in1=pzr[:, 0:J], op=ALU.mult),
                              reads=[ttab, t_pzr], writes=[t_w2])
                        kb.op(dve, lambda: V.tensor_tensor(out=btim[:], in0=w1[:, 0:J], in1=w2[:, 0:J], op=ALU.subtract),
                              reads=[t_w1, t_w2], writes=[t_bt])
                        icol = 0 if d == 0 else J
                        if si_ == 0:
                            kb.op(dve, lambda: V.memset(gr[:, icol:icol + 1], 0.0), writes=[tg])
                            kb.op(dve, lambda: V.memset(gi_[:, icol:icol + 1], 0.0), writes=[tg])
                        else:
                            ecol = J if d == 0 else 0
                            fcol = (s - 1) if d == 0 else s
                            ge_r, ge_i = gr[:, ecol:ecol + 1], gi_[:, ecol:ecol + 1]
                            fcl = fl[:, fcol:fcol + 1]

                            def ts2(out, a, s1):
                                kb.op(dve, lambda: V.tensor_scalar(out=out, in0=a, scalar1=s1, scalar2=fcl,
                                                                   op0=ALU.mult, op1=ALU.mult),
                                      reads=[tg, ttab, t_fl], writes=[t_ini])

                            ts2(ini[:, 0:1], ge_r, EQr)
                            ts2(ini[:, 1:2], ge_i, EQi)
                            ts2(ini[:, 2:3], ge_r, EQi)
                            ts2(ini[:, 3:4], ge_i, EQr)
                            kb.op(dve, lambda: V.tensor_tensor(out=gr[:, icol:icol + 1], in0=ini[:, 0:1], in1=ini[:, 1:2],
                                                               op=ALU.subtract), reads=[t_ini], writes=[tg])
                            kb.op(dve, lambda: V.tensor_tensor(out=gi_[:, icol:icol + 1], in0=ini[:, 2:3], in1=ini[:, 3:4],
                                                               op=ALU.add), reads=[t_ini], writes=[tg])
                        if d == 0:
                            kb.op(dve, lambda: V.tensor_tensor_scan(out=gr[:, 1:J + 1], data0=RT[d][:], data1=btre[:],
                                                                    initial=gr[:, 0:1], op0=ALU.mult, op1=ALU.add),
                                  reads=[t_RT, t_bt, tg], writes=[tg])
                            kb.op(dve, lambda: V.tensor_tensor_scan(out=gi_[:, 1:J + 1], data0=RT[d][:], data1=btim[:],
                                                                    initial=gi_[:, 0:1], op0=ALU.mult, op1=ALU.add),
                                  reads=[t_RT, t_bt, tg], writes=[tg])
                            Tc, Ts = cT[:, 0:J + 1], sT[:, 0:J + 1]
                            Hre, Him, tH = Hf[0][:, :], Hf[1][:, :], t_Hf
                        else:
                            kb.op(dve, lambda: V.tensor_tensor_scan(out=gr[:, 0:J][:, ::-1], data0=RT[d][:], data1=btre[:, ::-1],
                                                                    initial=gr[:, J:J + 1], op0=ALU.mult, op1=ALU.add),
                                  reads=[t_RT, t_bt, tg], writes=[tg])
                            kb.op(dve, lambda: V.tensor_tensor_scan(out=gi_[:, 0:J][:, ::-1], data0=RT[d][:], data1=btim[:, ::-1],
                                                                    initial=gi_[:, J:J + 1], op0=ALU.mult, op1=ALU.add),
                                  reads=[t_RT, t_bt, tg], writes=[tg])
                            Tc, Ts = cT[:, 0:J + 1][:, ::-1], sT[:, 0:J + 1][:, ::-1]
                            Hre, Him, tH = Hb[0][:, s, :], Hb[1][:, s, :], t_Hb
                        kb.op(dve, lambda: V.tensor_tensor(out=w1[:], in0=Tc, in1=gr[:], op=ALU.mult),
                              reads=[ttab, tg], writes=[t_w1])
                        kb.op(dve, lambda: V.tensor_tensor(out=w2[:], in0=Ts, in1=gi_[:], op=ALU.mult),
                              reads=[ttab, tg], writes=[t_w2])
                        kb.op(dve, lambda: V.tensor_tensor(out=Hre, in0=w1[:], in1=w2[:], op=ALU.subtract),
                              reads=[t_w1, t_w2], writes=[tH])
                        kb.op(dve, lambda: V.tensor_tensor(out=w1[:], in0=Ts, in1=gr[:], op=ALU.mult),
                              reads=[ttab, tg], writes=[t_w1])
                        kb.op(dve, lambda: V.tensor_tensor(out=w2[:], in0=Tc, in1=gi_[:], op=ALU.mult),
                              reads=[ttab, tg], writes=[t_w2])
                        kb.op(dve, lambda: V.tensor_tensor(out=Him, in0=w1[:], in1=w2[:], op=ALU.add),
                              reads=[t_w1, t_w2], writes=[tH])
                        if d == 1:
                            continue
                        for k in range(8):
                            ko = slice(k * J, (k + 1) * J)
                            for k2 in range(8):
                                if k2 == k:
                                    lt = M0[:, :]
                                elif k2 < k:
                                    lt = Mfb[:, 0, k - k2, :]
                                else:
                                    lt = Mfb[:, 1, k2 - k, :]
                                kb.op(pe, lambda lt=lt, k2=k2: PE_.matmul(py[0:32, ko], lhsT=lt, rhs=ubb[:, k2:SEG:8],
                                                                          start=(k2 == 0), stop=False),
                                      reads=[t_M, tub], writes=[t_py], pub=False)
                            kb.op(pe, lambda: PE_.matmul(py[0:32, ko], lhsT=lCre[0][:, k + 1, :], rhs=Hf[0][:, 0:J],
                                                         start=False, stop=False), reads=[t_lC, t_Hf], writes=[t_py], pub=False)
                            kb.op(pe, lambda: PE_.matmul(py[0:32, ko], lhsT=lCimn[0][:, k + 1, :], rhs=Hf[1][:, 0:J],
                                                         start=False, stop=False), reads=[t_lC, t_Hf], writes=[t_py], pub=False)
                            kb.op(pe, lambda: PE_.matmul(py[0:32, ko], lhsT=lCre[1][:, 8 - k, :], rhs=Hb[0][:, s, 1:J + 1],
                                                         start=False, stop=False), reads=[t_lC, t_Hb], writes=[t_py], pub=False)
                            kb.op(pe, lambda: PE_.matmul(py[0:32, ko], lhsT=lCimn[1][:, 8 - k, :], rhs=Hb[1][:, s, 1:J + 1],
                                                         start=False, stop=True), reads=[t_lC, t_Hb], writes=[t_py],
                                  pub=(k == 7))
                        kb.op(dve, lambda: V.scalar_tensor_tensor(
                            out=otb[:].rearrange("p (j k) -> p k j", k=8), in0=ufb[:].rearrange("p (j k) -> p k j", k=8),
                            scalar=dsk[:, ti:ti + 1], in1=py[0:32, 0:8 * J].rearrange("p (k j) -> p k j", k=8),
                            op0=ALU.mult, op1=ALU.add), reads=[tuf, t_p, t_py], writes=[tot])
                        kb.dma(pool, YSd[32 * ti:32 * ti + 32, ss], otb[:], reads=[tot])
            ph.close()

        def even_out_phase(j, src, dst):
            ph = Phase(f"e4_{j}")
            TT = 512 if NTOK % 512 == 0 else 256
            gw = ph.sb("gw", [128, 4, 512], BF16); t_gw = Trk("gw")
            wo = ph.sb("wo", [128, 12, 1024], BF16); t_wo = Trk("wo")
            stg = [ph.sb(f"stg{i}", [128, 1024]) for i in range(3)]
            t_stg = [Trk("stg") for _ in range(3)]
            load_weight_rows(ph, gw, t_gw, lambda r: s5_glu_w[j, r * 128:(r + 1) * 128, :], 4, 512, stg, t_stg)
            load_weight_rows(ph, wo, t_wo, lambda r: ev_w_out[j, r * 128:(r + 1) * 128, :], 12, 1024, stg, t_stg)
            gb_ = ph.sb("glub", [128, 4]); t_gb = Trk("glub")
            kb.dma(sp, gb_[:], s5_glu_b[j].rearrange("(c p) -> p c", p=128), writes=[t_gb])
            ys = [ph.sb(f"ys{i}", [128, 4, TT]) for i in range(2)]; t_ysb = [Trk("ys") for _ in range(2)]
            ya = [ph.sb(f"ya{i}", [128, 8, TT], BF16) for i in range(2)]; t_ya = [Trk("ya") for _ in range(2)]
            gf = ph.sb("gf", [128, 4, TT]); t_gf = Trk("gf")
            gbf = ph.sb("gbf", [128, 4, TT], BF16); t_gbf = Trk("gbf")
            sgm = [ph.sb(f"sgm{i}", [128, TT]) for i in range(2)]; t_sgm = [Trk("sgm") for _ in range(2)]
            ymb = ph.sb("ymb", [128, 4, TT], BF16); t_ymb = Trk("ymb")
            xres = [ph.sb(f"xres{i}", [128, TT]) for i in range(4)]; t_xres = [Trk("xres") for _ in range(4)]
            ores = [ph.sb(f"ores{i}", [128, TT]) for i in range(4)]; t_ores = [Trk("ores") for _ in range(4)]
            pg = [ph.ps(f"pg{i}", [128, 512]) for i in range(2)]; t_pg = [Trk("pg") for _ in range(2)]
            po = [ph.ps(f"po{i}", [128, 512]) for i in range(2)]; t_po = [Trk("po") for _ in range(2)]
            srcc = chunked(dram_ap[src]); dstc = chunked(dram_ap[dst])
            ysc = chunked(YSd); yac = chunked(YMd)
            ri = 0
            for i in range(NTOK // TT):
                ts = slice(i * TT, (i + 1) * TT)
                ysb, tys = ys[i % 2], t_ysb[i % 2]
                yab, tya = ya[i % 2], t_ya[i % 2]
                kb.dma(sp, ysb[:], ysc[:, :, ts], writes=[tys])
                kb.dma(sp, yab[:], yac[:, :, ts], writes=[tya])
                for k in range(4):
                    kb.op(act, lambda k=k: S.activation(out=gf[:, k, :], in_=ysb[:, k, :], func=AF.Gelu_apprx_tanh),
                          reads=[tys], writes=[t_gf])
                    kb.op(dve, lambda k=k: V.tensor_copy(out=gbf[:, k, :], in_=gf[:, k, :]), reads=[t_gf], writes=[t_gbf])
                for m in range(4):
                    p, tp = pg[m % 2], t_pg[m % 2]
                    for k in range(4):
                        kb.op(pe, lambda k=k: PE_.matmul(p[:, :TT], lhsT=gw[:, k, m * 128:(m + 1) * 128], rhs=gbf[:, k, :],
                                                         start=(k == 0), stop=(k == 3)),
                              reads=[t_gw, t_gbf], writes=[tp], pub=(k == 3))
                    sg_, tsg = sgm[m % 2], t_sgm[m % 2]
                    kb.op(act, lambda: S.activation(out=sg_[:], in_=p[:, :TT], func=AF.Sigmoid, bias=gb_[:, m:m + 1], scale=1.0),
                          reads=[tp, t_gb], writes=[tsg])
                    kb.op(dve, lambda: V.tensor_tensor(out=ymb[:, m, :], in0=gf[:, m, :], in1=sg_[:], op=ALU.mult),
                          reads=[t_gf, tsg], writes=[t_ymb])
                for m in range(8):
                    ms = slice(m * 128, (m + 1) * 128)
                    p, tp = po[m % 2], t_po[m % 2]
                    xr, txr = xres[ri % 4], t_xres[ri % 4]
                    orr, tor = ores[ri % 4], t_ores[ri % 4]
                    ri += 1
                    kb.dma(sp, xr[:], srcc[:, m, ts], writes=[txr])
                    for k in range(12):
                        rhs = yab[:, k, :] if k < 8 else ymb[:, k - 8, :]
                        kb.op(pe, lambda k=k, rhs=rhs: PE_.matmul(p[:, :TT], lhsT=wo[:, k, ms], rhs=rhs,
                                                                  start=(k == 0), stop=(k == 11)),
                              reads=[t_wo, tya, t_ymb], writes=[tp], pub=(k == 11))
                    kb.op(dve, lambda: V.tensor_tensor(out=orr[:], in0=p[:, :TT], in1=xr[:], op=ALU.add),
                          reads=[tp, txr], writes=[tor])
                    kb.dma(pool, dstc[:, m, ts], orr[:], reads=[tor])
            ph.close()

        def odd_in_phase(j, layer, src):
            ph = Phase(f"o1_{layer}")
            TT = 512 if NTOK % 512 == 0 else 256
            gam, t_gam = load_gamma(ph, mix_norm[layer])
            NO = 5184
            win = ph.sb("win", [128, 8, NO], BF16)
            t_win = Trk("win")
            stg = [ph.sb(f"stg{i}", [128, NO // 3]) for i in range(3)]
            t_stg = [Trk(f"stg{i}") for i in range(3)]
            load_weight_rows(ph, win, t_win, lambda r: od_w_in[j, r * 128:(r + 1) * 128, :], 8, NO, stg, t_stg)
            t_par = Trk("par")
            dtb = ph.sb("dtb", [64, 1])
            kb.dma(sp, dtb[:], ssd_dt_bias[j].rearrange("d (r o) -> (d r) o", o=1), writes=[t_par], st=t_par)
            acol = ph.sb("acol", [64, 1])
            kb.dma(sp, acol[:], ssd_a_log[j].rearrange("d (r o) -> (d r) o", o=1), writes=[t_par], st=t_par)
            kb.op(act, lambda: S.activation(out=acol[:], in_=acol[:], func=AF.Exp), reads=[t_par], writes=[t_par])
            kb.op(dve, lambda: V.tensor_scalar(out=acol[:], in0=acol[:], scalar1=-1.0, scalar2=None, op0=ALU.mult),
                  reads=[t_par], writes=[t_par])
            mk = ph.sb("mk", [64, TT])
            kb.op(dve, lambda: V.memset(mk[:], 1.0), writes=[t_par])
            kb.op(dve, lambda: V.memset(mk[0:32, 0:TT:128], 0.0), writes=[t_par])
            kb.op(dve, lambda: V.memset(mk[32:64, 127:TT:128], 0.0), writes=[t_par])
            nrm = Norm(ph, TT)
            xt = [ph.sb(f"xt{i}", [128, 8, TT]) for i in range(2)]
            t_xt = [Trk("xt") for _ in range(2)]
            xn = [ph.sb(f"xn{i}", [128, 8, TT], BF16) for i in range(2)]
            t_xn = [Trk("xn") for _ in range(2)]
            ev = [ph.sb(f"ev{i}", [128, TT]) for i in range(4)]
            t_ev = [Trk("ev") for _ in range(4)]
            pp = [ph.ps(f"pp{i}", [128, 512]) for i in range(3)]
            t_pp = [Trk("pp") for _ in range(3)]
            pdt = ph.ps("pdt", [64, 512]); t_pdt = Trk("pdt")
            dte = ph.sb("dte", [64, TT]); t_dte = Trk("dte")
            dtv = ph.sb("dtv", [64, TT]); t_dtv = Trk("dtv")
            dta = ph.sb("dta", [64, TT]); t_dta = Trk("dta")
            csb = ph.sb("csb", [64, TT]); t_csb = Trk("csb")
            srcc = chunked(dram_ap[src])
            ei = 0
            for i in range(NTOK // TT):
                ts = slice(i * TT, (i + 1) * TT)
                xb, txb = xt[i % 2], t_xt[i % 2]
                kb.dma(sp, xb[:], srcc[:, :, ts], writes=[txb])
                xnb, txnb = xn[i % 2], t_xn[i % 2]
                nrm.run(xb, txb, gam, t_gam, xnb, txnb)
                for oc in range(40):
                    p, tp = pp[oc % 3], t_pp[oc % 3]
                    for k in range(8):
                        kb.op(pe, lambda k=k: PE_.matmul(p[:, :TT], lhsT=win[:, k, oc * 128:(oc + 1) * 128],
                                                         rhs=xnb[:, k, :], start=(k == 0), stop=(k == 7)),
                              reads=[t_win, txnb], writes=[tp], pub=(k == 7))
                    e, te = ev[ei % 4], t_ev[ei % 4]
                    ei += 1
                    if oc < 16:
                        kb.op(act, lambda: S.activation(out=e[:], in_=p[:, :TT], func=AF.Silu), reads=[tp], writes=[te])
                        dst = ZSd[oc * 128:(oc + 1) * 128, ts]
                    else:
                        kb.op(dve, lambda: V.tensor_copy(out=e[:], in_=p[:, :TT]), reads=[tp], writes=[te])
                        dst = XBCd[(oc - 16) * 128:(oc - 15) * 128, ts]
                    kb.dma(pool, dst, e[:], reads=[te])
                for k in range(8):
                    kb.op(pe, lambda k=k: PE_.matmul(pdt[:, :TT], lhsT=win[:, k, 5120:5184], rhs=xnb[:, k, :],
                                                     start=(k == 0), stop=(k == 7)),
                          reads=[t_win, txnb], writes=[t_pdt], pub=(k == 7))
                kb.op(act, lambda: S.activation(out=dte[:], in_=pdt[:, :TT], func=AF.Exp, bias=dtb[:], scale=1.0),
                      reads=[t_pdt, t_par], writes=[t_dte])
                kb.op(act, lambda: S.activation(out=dtv[:], in_=dte[:], func=AF.Ln, bias=1.0, scale=1.0),
                      reads=[t_dte], writes=[t_dtv])
                kb.op(dve, lambda: V.tensor_scalar(out=dta[:], in0=dtv[:], scalar1=acol[:], scalar2=None, op0=ALU.mult),
                      reads=[t_dtv, t_par], writes=[t_dta])
                kb.op(dve, lambda: V.tensor_tensor_scan(out=csb[0:32, :], data0=mk[0:32, :], data1=dta[0:32, :],
                                                        initial=0.0, op0=ALU.mult, op1=ALU.add),
                      reads=[t_dta, t_par], writes=[t_csb])
                kb.op(dve, lambda: V.tensor_tensor_scan(out=csb[32:64, ::-1], data0=mk[32:64, ::-1], data1=dta[32:64, ::-1],
                                                        initial=0.0, op0=ALU.mult, op1=ALU.add),
                      reads=[t_dta, t_par], writes=[t_csb])
                for d in range(2):
                    kb.dma(pool, DCd[d * 64:d * 64 + 32, ts], csb[d * 32:(d + 1) * 32, :], reads=[t_csb], st=t_csb)
                    kb.dma(pool, DCd[d * 64 + 32:d * 64 + 64, ts], dtv[d * 32:(d + 1) * 32, :], reads=[t_dtv], st=t_dtv)
            ph.close()

        def odd_conv_phase(j):
            ph = Phase(f"o2_{j}")
            t_par = Trk("par")
            cw = ph.sb("cw", [128, 24, 4])
            for k in range(4):
                kb.dma(sp, cw[:, :, k], ssd_conv_w[j, k].rearrange("(c p) -> p c", p=128), writes=[t_par], st=t_par)
            cb = ph.sb("cb", [128, 24])
            kb.dma(sp, cb[:], ssd_conv_b[j].rearrange("(c p) -> p c", p=128), writes=[t_par], st=t_par)
            idb = ph.sb("idb", [128, 128], BF16)
            kb.op(dve, lambda: V.tensor_copy(out=idb[:], in_=ident[:]), writes=[t_par])
            xp = [ph.sb(f"xp{i}", [128, SEG + 3]) for i in range(2)]
            t_xp = [Trk("xp") for _ in range(2)]
            xc2 = [ph.sb(f"xc{i}", [128, SEG]) for i in range(2)]; t_xc2 = [Trk("xc") for _ in range(2)]
            xs_ = [ph.sb(f"xs{i}", [128, SEG]) for i in range(2)]; t_xs = [Trk("xs") for _ in range(2)]
            xb = [ph.sb(f"xb{i}", [128, SEG], BF16) for i in range(2)]; t_xb = [Trk("xb") for _ in range(2)]
            tr = [ph.sb(f"tr{i}", [128, 4, 128], BF16) for i in range(2)]; t_tr = [Trk("tr") for _ in range(2)]
            ptr = [ph.ps(f"ptr{i}", [128, 512], BF16) for i in range(2)]; t_ptr = [Trk("ptr") for _ in range(2)]
            cnt = 0
            tc_ = 0
            NB = SEG // 128
            for cc in range(24):
                rows = XBCd[cc * 128:(cc + 1) * 128, :]
                for s in range(NSEG):
                    ss = slice(s * SEG, (s + 1) * SEG)
                    xpb, txp = xp[cnt % 2], t_xp[cnt % 2]
                    xsb, txs = xs_[cnt % 2], t_xs[cnt % 2]
                    xbb, txb = xb[cnt % 2], t_xb[cnt % 2]
                    xc, t_xc = xc2[cnt % 2], t_xc2[cnt % 2]
                    cnt += 1
                    load_halo(xpb, txp, rows, s)
                    conv4(xc, t_xc, xpb, txp, cw[:, cc, :], cb[:, cc:cc + 1], t_par)
                    kb.op(act, lambda: S.activation(out=xsb[:], in_=xc[:], func=AF.Silu), reads=[t_xc], writes=[txs])
                    kb.op(act, lambda: S.activation(out=xbb[:], in_=xc[:], func=AF.Silu), reads=[t_xc], writes=[txb])
                    if cc < 16:
                        kb.dma(pool, XSfd[cc * 128:(cc + 1) * 128, ss], xsb[:], reads=[txs], st=txs)
                    elif cc < 20:
                        kb.dma(pool, BFd[cc - 16, :, ss], xbb[:], reads=[txb], st=txb)
                    else:
                        kb.dma(pool, CFd[cc - 20, :, ss], xbb[:], reads=[txb], st=txb)
                    if cc < 20:
                        for b0 in range(0, NB, 4):
                            pt_, tpt = ptr[tc_ % 2], t_ptr[tc_ % 2]
                            trb, ttr = tr[tc_ % 2], t_tr[tc_ % 2]
                            tc_ += 1
                            nb = min(4, NB - b0)
                            for q in range(nb):
                                kb.op(pe, lambda q=q: PE_.transpose(out=pt_[:, q * 128:(q + 1) * 128],
                                                                    in_=xbb[:, (b0 + q) * 128:(b0 + q + 1) * 128],
                                                                    identity=idb[:]),
                                      reads=[txb, t_par], writes=[tpt], pub=(q == nb - 1))
                            kb.op(dve, lambda: V.tensor_copy(out=trb[:, 0:nb, :],
                                                             in_=pt_[:, 0:nb * 128].rearrange("p (q c) -> p q c", q=nb)),
                                  reads=[tpt], writes=[ttr])
                            t0 = s * SEG + b0 * 128
                            if cc < 16:
                                dst = XStd[t0:t0 + nb * 128, cc * 128:(cc + 1) * 128]
                            else:
                                dst = BTd[t0:t0 + nb * 128, (cc - 16) * 128:(cc - 15) * 128]
                            kb.dma(pool, dst.rearrange("(b p) c -> p b c", p=128), trb[:, 0:nb, :], reads=[ttr], st=ttr)
            ph.close()

        def ssd_core_phase(j):
            ph = Phase(f"o3_{j}")
            t_c = Trk("c")
            sel = ph.sb("sel", [32, 32, 128])
            kb.op(dve, lambda: V.tensor_copy(out=sel[:], in_=ident[0:32, 0:32].unsqueeze(2).to_broadcast([32, 32, 128])),
                  writes=[t_c])
            tri = ph.sb("tri", [128, 2, 128])
            kb.dma(sp, tri[:], tri_d.rearrange("p (a b) -> p a b", a=2), writes=[t_c], st=t_c)
            Sst = ph.sb("Sst", [128, 32, 64]); t_S = Trk("S")
            Sbf = ph.sb("Sbf", [128, 32, 64], BF16); t_Sb = Trk("Sb")
            xst = [ph.sb(f"xst{i}", [128, 2048], BF16) for i in range(2)]; t_xst = [Trk("xst") for _ in range(2)]
            Bfm = [ph.sb(f"Bfm{i}", [128, 4, 128], BF16) for i in range(2)]; t_Bfm = [Trk("Bfm") for _ in range(2)]
            Cfm = [ph.sb(f"Cfm{i}", [128, 4, 128], BF16) for i in range(2)]; t_Cfm = [Trk("Cfm") for _ in range(2)]
            Btk = [ph.sb(f"Btk{i}", [128, 512], BF16) for i in range(2)]; t_Btk = [Trk("Btk") for _ in range(2)]
            dc = [ph.sb(f"dc{i}", [64, 128]) for i in range(2)]; t_dc = [Trk("dc") for _ in range(2)]
            tok = ph.sb("tok", [128, 64]); t_tok = Trk("tok")
            scm = [ph.sb(f"scm{i}", [128, 128]) for i in range(2)]; t_scm = [Trk("scm") for _ in range(2)]
            Eg = [ph.sb(f"Eg{i}", [128, 8, 128]) for i in range(2)]; t_Eg = [Trk("Eg") for _ in range(2)]
            Xg = [ph.sb(f"Xg{i}", [128, 8, 128]) for i in range(2)]; t_Xg = [Trk("Xg") for _ in range(2)]
            Gg = [ph.sb(f"Gg{i}", [128, 8, 128], BF16) for i in range(2)]; t_Gg = [Trk("Gg") for _ in range(2)]
            CEg = [ph.sb(f"CEg{i}", [128, 8, 128], BF16) for i in range(2)]; t_CEg = [Trk("CEg") for _ in range(2)]
            xdt = [ph.sb(f"xdt{i}", [128, 8, 64], BF16) for i in range(2)]; t_xdt = [Trk("xdt") for _ in range(2)]
            wdt = [ph.sb(f"wdt{i}", [128, 8]) for i in range(2)]; t_wdt = [Trk("wdt") for _ in range(2)]
            wx = [ph.sb(f"wx{i}", [128, 8, 64], BF16) for i in range(2)]; t_wx = [Trk("wx") for _ in range(2)]
            yo = [ph.sb(f"yo{i}", [64, 8, 128]) for i in range(2)]; t_yo = [Trk("yo") for _ in range(2)]
            pTs = ph.ps("pTs", [128, 512]); t_pT = Trk("pT")
            pT = pTs[:, 0:128]
            psc_ = [pTs[:, 128:256], pTs[:, 256:384]]
            t_psc = t_pT
            pcs2 = [ph.ps(f"pcs{i}", [128, 1024]) for i in range(2)]; t_pcs2 = [Trk("pcs") for _ in range(2)]
            lnd = ph.sb("lnd", [128, 32]); csm = ph.sb("csm", [128, 32])
            py = ph.ps("py", [64, 1024]); t_py = Trk("py")
            pst = ph.ps("pst", [128, 512]); t_pst = Trk("pst")
            NCH = NTOK // 128
            CPS = SEG // 128
            tok2 = [tok, ph.sb("tok1", [128, 64])]; t_tok2 = [t_tok, Trk("tok1")]
            lnd2 = [lnd, ph.sb("lnd1", [128, 32])]; csm2 = [csm, ph.sb("csm1", [128, 32])]

            class Tsk:
                pass

            def prologue(t):
                li = t.chunk_idx
                t.xb_, t.txb = xst[li % 2], t_xst[li % 2]
                t.bf_, t.tbf = Bfm[li % 2], t_Bfm[li % 2]
                t.cf_, t.tcf = Cfm[li % 2], t_Cfm[li % 2]
                t.bt_, t.tbt = Btk[li % 2], t_Btk[li % 2]
                t.dc_, t.tdc = dc[li % 2], t_dc[li % 2]
                t.tok, t.ttok = tok2[li % 2], t_tok2[li % 2]
                t.lnd, t.csm = lnd2[li % 2], csm2[li % 2]
                tsl = t.tsl
                kb.dma(sp, t.xb_[:], XStd[tsl, :], writes=[t.txb])
                kb.dma(sp, t.bf_[:], BFd[:, :, tsl].rearrange("g n s -> n g s"), writes=[t.tbf])
                kb.dma(sp, t.cf_[:], CFd[:, :, tsl].rearrange("g n s -> n g s"), writes=[t.tcf])
                kb.dma(sp, t.bt_[:], BTd[tsl, :], writes=[t.tbt])
                kb.dma(sp, t.dc_[:], DCd[t.d * 64:(t.d + 1) * 64, tsl], writes=[t.tdc])
                kb.op(pe, lambda: PE_.transpose(out=pT[:, 0:64], in_=t.dc_[:, :], identity=ident[0:64, 0:64]),
                      reads=[t.tdc, t_id], writes=[t_pT])
                kb.op(act, lambda: S.copy(out=t.tok[:], in_=pT[:, 0:64]), reads=[t_pT], writes=[t.ttok])
                kb.op(act, lambda: S.activation(out=t.lnd[:], in_=t.tok[:, 32:64], func=AF.Ln), reads=[t.ttok], writes=[t.ttok])
                kb.op(dve, lambda: V.tensor_tensor(out=t.csm[:], in0=t.tok[:, 0:32], in1=t.lnd[:], op=ALU.subtract),
                      reads=[t.ttok], writes=[t.ttok])

            def s1(t):
                g, b2 = t.g, t.b2
                pcs, t_pcs = pcs2[b2], t_pcs2[b2]
                for r in range(8):
                    kb.op(pe, lambda r=r: PE_.matmul(pcs[:, r * 128:(r + 1) * 128], lhsT=sel[:, 8 * g + r, :],
                                                     rhs=t.dc_[0:32, :], start=True, stop=True),
                          reads=[t_c, t.tdc], writes=[t_pcs], pub=(r == 7))
                kb.op(pe, lambda: PE_.matmul(psc_[b2], lhsT=t.bf_[:, g, :], rhs=t.cf_[:, g, :], start=True, stop=True),
                      reads=[t.tbf, t.tcf], writes=[t_psc])

            def s2(t):
                g, b2, d = t.g, t.b2, t.d
                hs = slice(8 * g, 8 * g + 8)
                pcs, t_pcs = pcs2[b2], t_pcs2[b2]
                pcs3 = pcs[:, :].rearrange("p (r l) -> p r l", r=8)
                kb.op(dve, lambda: V.tensor_tensor(out=scm[b2][:], in0=psc_[b2], in1=tri[:, d, :], op=ALU.mult),
                      reads=[t_psc, t_c], writes=[t_scm[b2]])
                kb.op(dve, lambda: V.tensor_tensor(out=Eg[b2][:], in0=pcs3,
                                                   in1=t.csm[:, hs].unsqueeze(2).to_broadcast([128, 8, 128]),
                                                   op=ALU.subtract),
                      reads=[t_pcs, t.ttok], writes=[t_Eg[b2]])
                kb.op(act, lambda: S.activation(out=Eg[b2][:], in_=Eg[b2][:], func=AF.Exp),
                      reads=[t_Eg[b2]], writes=[t_Eg[b2]])
                kb.op(act, lambda: S.activation(out=Xg[b2][:], in_=pcs3, func=AF.Exp),
                      reads=[t_pcs], writes=[t_Xg[b2]])

            def s3(t):
                g, b2, d = t.g, t.b2, t.d
                hs = slice(8 * g, 8 * g + 8)
                lend = 127 if d == 0 else 0
                if t.reset is not None and g == 0:
                    if t.reset == "zero":
                        kb.op(dve, lambda: V.memset(Sst[:], 0.0), writes=[t_S])
                    else:
                        fc = t.reset
                        kb.op(dve, lambda: V.tensor_scalar(out=Sst[:], in0=Sst[:], scalar1=fl[:, fc:fc + 1], scalar2=None,
                                                           op0=ALU.mult), reads=[t_S, t_fl], writes=[t_S])
                    kb.op(act, lambda: S.copy(out=Sbf[:], in_=Sst[:]), reads=[t_S], writes=[t_Sb])
                kb.op(dve, lambda: V.scalar_tensor_tensor(out=Gg[b2][:], in0=Eg[b2][:], scalar=1.0e30,
                                                          in1=scm[b2][:].unsqueeze(1).to_broadcast([128, 8, 128]),
                                                          op0=ALU.min, op1=ALU.mult),
                      reads=[t_Eg[b2], t_scm[b2]], writes=[t_Gg[b2]])
                kb.op(pool, lambda: G_.tensor_tensor(out=CEg[b2][:], in0=Xg[b2][:],
                                                     in1=t.cf_[:, g, :].unsqueeze(1).to_broadcast([128, 8, 128]),
                                                     op=ALU.mult),
                      reads=[t_Xg[b2], t.tcf], writes=[t_CEg[b2]])
                xg = t.xb_[:, g * 512:(g + 1) * 512].rearrange("p (r q) -> p r q", r=8)
                kb.op(pool, lambda: G_.tensor_tensor(out=wx[b2][:], in0=xg,
                                                     in1=Eg[b2][:, :, lend].unsqueeze(2).to_broadcast([128, 8, 64]),
                                                     op=ALU.mult),
                      reads=[t.txb, t_Eg[b2]], writes=[t_wx[b2]])
                for r in range(8):
                    kb.op(pe, lambda r=r: PE_.matmul(py[:, r * 128:(r + 1) * 128], lhsT=xg[:, r, :],
                                                     rhs=Gg[b2][:, r, :], start=True, stop=False),
                          reads=[t.txb, t_Gg[b2]], writes=[t_py], pub=False)
                    kb.op(pe, lambda r=r: PE_.matmul(py[:, r * 128:(r + 1) * 128], lhsT=Sbf[:, 8 * g + r, :],
                                                     rhs=CEg[b2][:, r, :], start=False, stop=True),
                          reads=[t_Sb, t_CEg[b2]], writes=[t_py], pub=(r == 7))
                kb.op(act, lambda: S.copy(out=yo[b2][:], in_=py[:, :].rearrange("p (r l) -> p r l", r=8)),
                      reads=[t_py], writes=[t_yo[b2]])
                kb.dma(pool, Yd[d, g * 512:(g + 1) * 512, t.tsl].rearrange("(r p) l -> p r l", p=64), yo[b2][:],
                       reads=[t_yo[b2]], st=t_yo[b2])
                kb.op(pe, lambda: PE_.matmul(pst[:, :], lhsT=t.bt_[:, g * 128:(g + 1) * 128],
                                             rhs=wx[b2][:].rearrange("p r q -> p (r q)"), start=True, stop=True),
                      reads=[t.tbt, t_wx[b2]], writes=[t_pst])
                Sg = Sst[:, hs, :]
                kb.op(dve, lambda: V.tensor_tensor(out=Sg, in0=Sg,
                                                   in1=Xg[b2][:, :, lend].unsqueeze(2).to_broadcast([128, 8, 64]),
                                                   op=ALU.mult),
                      reads=[t_S, t_Xg[b2], t_py], writes=[t_S])
                kb.op(dve, lambda: V.tensor_tensor(out=Sg, in0=Sg, in1=pst[:, :].rearrange("p (r q) -> p r q", r=8),
                                                   op=ALU.add),
                      reads=[t_S, t_pst], writes=[t_S])
                kb.op(act, lambda: S.copy(out=Sbf[:, hs, :], in_=Sg), reads=[t_S, t_py], writes=[t_Sb])

            tasks = []
            chunk_idx = 0
            for d in (1, 0):
                order = list(range(NCH))[::-1] if d == 1 else list(range(NCH))
                for ci in order:
                    t0 = ci * 128
                    s = t0 // SEG
                    at_bound = (ci % CPS == 0) if d == 0 else (ci % CPS == CPS - 1)
                    reset = None
                    if at_bound:
                        first = (s == 0) if d == 0 else (s == NSEG - 1)
                        reset = "zero" if first else ((s - 1) if d == 0 else s)
                    proto = None
                    for g in range(4):
                        t = Tsk()
                        t.d, t.ci, t.g, t.tsl, t.reset = d, ci, g, slice(t0, t0 + 128), reset
                        t.chunk_idx = chunk_idx
                        t.b2 = len(tasks) % 2
                        t.proto = proto
                        if g == 0:
                            proto = t
                            t.proto = None
                        tasks.append(t)
                    chunk_idx += 1
            n = len(tasks)
            for i in range(n + 2):
                if i < n:
                    t = tasks[i]
                    if t.proto is None:
                        prologue(t)
                    else:
                        for a in ("xb_", "txb", "bf_", "tbf", "cf_", "tcf", "bt_", "tbt", "dc_", "tdc", "tok", "ttok", "lnd", "csm"):
                            setattr(t, a, getattr(t.proto, a))
                    s1(t)
                if 1 <= i <= n:
                    s2(tasks[i - 1])
                if 2 <= i <= n + 1:
                    s3(tasks[i - 2])
            ph.close()

        def odd_out_phase(j, src, dst):
            ph = Phase(f"o4_{j}")
            TT = 512 if NTOK % 512 == 0 else 256
            wo = ph.sb("wo", [128, 16, 1024], BF16); t_wo = Trk("wo")
            stg = [ph.sb(f"stg{i}", [128, 1024]) for i in range(3)]
            t_stg = [Trk("stg") for _ in range(3)]
            load_weight_rows(ph, wo, t_wo, lambda r: od_w_out[j, r * 128:(r + 1) * 128, :], 16, 1024, stg, t_stg)
            t_par = Trk("par")
            dcol = ph.sb("dcol", [128, 16])
            for hh in range(2):
                kb.dma(sp, dcol[hh * 64:(hh + 1) * 64, :],
                       ssd_d[j].rearrange("(c h) -> h c", h=2)[hh].partition_broadcast(64), writes=[t_par], st=t_par)
            nw = ph.sb("nw", [128, 16])
            kb.dma(sp, nw[:], ssd_norm[j].rearrange("(c p) -> p c", p=128), writes=[t_par], st=t_par)
            yf = [ph.sb(f"yf{i}", [128, 4, TT]) for i in range(2)]; t_yf = [Trk("yf") for _ in range(2)]
            ybk = [ph.sb(f"ybk{i}", [128, 4, TT]) for i in range(2)]; t_ybk = [Trk("ybk") for _ in range(2)]
            xsg = [ph.sb(f"xsg{i}", [128, 4, TT]) for i in range(2)]; t_xsg = [Trk("xsg") for _ in range(2)]
            zsg = [ph.sb(f"zsg{i}", [128, 4, TT]) for i in range(2)]; t_zsg = [Trk("zsg") for _ in range(2)]
            sqb = ph.sb("sqb", [128, 4, TT], BF16); t_sqb = Trk("sqb")
            rs = ph.sb("rs", [128, TT]); t_rs = Trk("rs")
            rs2 = ph.sb("rs2", [128, TT]); t_rs2 = Trk("rs2")
            ynb = ph.sb("ynb", [128, 16, TT], BF16); t_ynb = Trk("ynb")
            xres = [ph.sb(f"xres{i}", [128, TT]) for i in range(4)]; t_xres = [Trk("xres") for _ in range(4)]
            ores = [ph.sb(f"ores{i}", [128, TT]) for i in range(4)]; t_ores = [Trk("ores") for _ in range(4)]
            pn = ph.ps("pn", [128, 512]); t_pn = Trk("pn")
            po = [ph.ps(f"po{i}", [128, 512]) for i in range(2)]; t_po = [Trk("po") for _ in range(2)]
            srcc = chunked(dram_ap[src]); dstc = chunked(dram_ap[dst])
            Yfc = chunked(Yd[0]); Ybc = chunked(Yd[1]); XSc = chunked(XSfd); ZSc = chunked(ZSd)
            ri = 0
            gi = 0
            for i in range(NTOK // TT):
                ts = slice(i * TT, (i + 1) * TT)
                for g in range(4):
                    b2 = gi % 2
                    gi += 1
                    cs4 = slice(4 * g, 4 * g + 4)
                    kb.dma(sp, yf[b2][:], Yfc[:, cs4, ts], writes=[t_yf[b2]])
                    kb.dma(sp, ybk[b2][:], Ybc[:, cs4, ts], writes=[t_ybk[b2]])
                    kb.dma(sp, xsg[b2][:], XSc[:, cs4, ts], writes=[t_xsg[b2]])
                    kb.dma(sp, zsg[b2][:], ZSc[:, cs4, ts], writes=[t_zsg[b2]])
                    kb.op(pool, lambda: G_.tensor_tensor(out=yf[b2][:], in0=yf[b2][:], in1=ybk[b2][:], op=ALU.add),
                          reads=[t_yf[b2], t_ybk[b2]], writes=[t_yf[b2]])
                    for c in range(4):
                        ch = 4 * g + c
                        kb.op(dve, lambda c=c, ch=ch: V.scalar_tensor_tensor(out=yf[b2][:, c, :], in0=xsg[b2][:, c, :],
                                                                             scalar=dcol[:, ch:ch + 1], in1=yf[b2][:, c, :],
                                                                             op0=ALU.mult, op1=ALU.add),
                              reads=[t_xsg[b2], t_par, t_yf[b2]], writes=[t_yf[b2]])
                    kb.op(dve, lambda: V.tensor_tensor(out=yf[b2][:], in0=yf[b2][:], in1=zsg[b2][:], op=ALU.mult),
                          reads=[t_yf[b2], t_zsg[b2]], writes=[t_yf[b2]])
                    kb.op(act, lambda: S.activation(out=sqb[:], in_=yf[b2][:], func=AF.Square),
                          reads=[t_yf[b2]], writes=[t_sqb])
                    for c in range(4):
                        kb.op(pe, lambda c=c: PE_.matmul(pn[:, :TT], lhsT=ones_bf[:], rhs=sqb[:, c, :],
                                                         start=(c == 0), stop=(c == 3)),
                              reads=[t_const, t_sqb], writes=[t_pn], pub=(c == 3))
                    kb.op(act, lambda: S.activation(out=rs[:], in_=pn[:, :TT], func=AF.Sqrt, scale=1.0 / 512.0, bias=eps_t[:]),
                          reads=[t_pn, t_const], writes=[t_rs])
                    kb.op(dve, lambda: V.reciprocal(out=rs2[:], in_=rs[:]), reads=[t_rs], writes=[t_rs2])
                    for c in range(4):
                        ch = 4 * g + c
                        kb.op(dve, lambda c=c, ch=ch: V.scalar_tensor_tensor(out=ynb[:, ch, :], in0=yf[b2][:, c, :],
                                                                             scalar=nw[:, ch:ch + 1], in1=rs2[:],
                                                                             op0=ALU.mult, op1=ALU.mult),
                              reads=[t_yf[b2], t_par, t_rs2], writes=[t_ynb])
                for m in range(8):
                    ms = slice(m * 128, (m + 1) * 128)
                    p, tp = po[m % 2], t_po[m % 2]
                    xr, txr = xres[ri % 4], t_xres[ri % 4]
                    orr, tor = ores[ri % 4], t_ores[ri % 4]
                    ri += 1
                    kb.dma(sp, xr[:], srcc[:, m, ts], writes=[txr])
                    for k in range(16):
                        kb.op(pe, lambda k=k: PE_.matmul(p[:, :TT], lhsT=wo[:, k, ms], rhs=ynb[:, k, :],
                                                         start=(k == 0), stop=(k == 15)),
                              reads=[t_wo, t_ynb], writes=[tp], pub=(k == 15))
                    kb.op(dve, lambda: V.tensor_tensor(out=orr[:], in0=p[:, :TT], in1=xr[:], op=ALU.add),
                          reads=[tp, txr], writes=[tor])
                    kb.dma(pool, dstc[:, m, ts], orr[:], reads=[tor])
            ph.close()

        def odd_mixer(j, layer, src, dst):
            odd_in_phase(j, layer, src)
            odd_conv_phase(j)
            ssd_core_phase(j)
            odd_out_phase(j, src, dst)

        cur = "xT"
        nxt = ["XA", "XB"]
        ni = 0
        for layer in layers:
            j = layer // 2
            if inc("ffn1"):
                dst = nxt[ni % 2]; ni += 1
                ffn_phase(layer, 0, cur, dst)
                cur = dst
            if phases is None or any(n in phases for n in ("mixer", "e1", "e2", "e3", "e4", "o1", "o2", "o3", "o4")):
                dst = nxt[ni % 2]; ni += 1
                def sub(n):
                    return phases is None or "mixer" in phases or n in phases
                if layer % 2 == 0:
                    if sub("e1"): even_in_phase(j, layer, cur)
                    if sub("e2"): lru_phase(j)
                    if sub("e3"): s5_phase(j)
                    if sub("e4"): even_out_phase(j, cur, dst)
                else:
                    if sub("o1"): odd_in_phase(j, layer, cur)
                    if sub("o2"): odd_conv_phase(j)
                    if sub("o3"): ssd_core_phase(j)
                    if sub("o4"): odd_out_phase(j, cur, dst)
                cur = dst
            if inc("ffn2"):
                dst = nxt[ni % 2]; ni += 1
                ffn_phase(layer, 1, cur, dst)
                cur = dst
        final_phase(cur)
        kb.finish()
    return nc


N_CORES = 8
NON_WEIGHT = ("x_prompt", "x_sample")


def core_inputs(inputs):
    m = {}
    for k, v in inputs.items():
        if k in NON_WEIGHT:
            continue
        m[k] = np.ascontiguousarray(np.asarray(v, dtype=np.float32))
    m["ident"] = np.eye(128, dtype=np.float32)
    u = np.triu(np.ones((128, 128), np.float32))
    m["tri"] = np.ascontiguousarray(np.concatenate([u, u.T], axis=1))
    return m


def kernel(**inputs):
    xp = np.asarray(inputs["x_prompt"], dtype=np.float32)
    xs = np.asarray(inputs["x_sample"], dtype=np.float32)
    SEG = 2048
    NTOK = 12288
    streams = []
    flags = []
    for c in range(N_CORES):
        if c < 4:
            parts = [xp[c], xs[2 * c], xs[2 * c + 1]]
            fl = [1, 1, 1, 0, 0]
        else:
            b = 8 + 6 * (c - 4)
            parts = [xs[b + q] for q in range(6)]
            fl = [0, 0, 0, 0, 0]
        tok = np.concatenate(parts, axis=0)
        streams.append(np.ascontiguousarray(tok.T))
        f = np.zeros((128, 8), np.float32)
        f[:, :5] = np.asarray(fl, np.float32)[None, :]
        flags.append(f)
    nc = build_program(NTOK, SEG, layers=[0, 1, 2, 3])
    wmap = core_inputs(inputs)
    in_maps = []
    for c in range(N_CORES):
        m = dict(wmap)
        m["xT"] = streams[c]
        m["flags"] = flags[c]
        in_maps.append(m)
    res = run_bass_kernel_spmd(nc, in_maps, core_ids=list(range(N_CORES)))
    outs = [np.ascontiguousarray(res.results[c]["yT"].T) for c in range(N_CORES)]
    y_prompt = np.stack([outs[c][:8192] for c in range(4)], axis=0)
    ys = [None] * 32
    for c in range(4):
        ys[2 * c] = outs[c][8192:8192 + 2048]
        ys[2 * c + 1] = outs[c][8192 + 2048:]
    for c in range(4, 8):
        b = 8 + 6 * (c - 4)
        for q in range(6):
            ys[b + q] = outs[c][q * 2048:(q + 1) * 2048]
    y_sample = np.stack(ys, axis=0)
    return (y_prompt.astype(np.float32), y_sample.astype(np.float32))
```

```python
import contextlib
import numpy as np
import concourse.bass as bass
import concourse.mybir as mybir
from concourse.bass_utils import run_bass_kernel_spmd

F32 = mybir.dt.float32
BF16 = mybir.dt.bfloat16
ALU = mybir.AluOpType
AF = mybir.ActivationFunctionType

D = 1024
DFF = 2816
NFC = DFF // 128
EPS = 1e-6


class Trk:
    __slots__ = ("w", "r", "dsem", "dcnt", "name", "dq")

    def __init__(self, name=""):
        self.w = {}
        self.r = {}
        self.dsem = None
        self.dcnt = 0
        self.name = name


class Eng:
    def __init__(self, name, eng, sem):
        self.name = name
        self.eng = eng
        self.sem = sem
        self.cnt = 0
        self.waited = {}


class KB:
    def __init__(self, nc, es):
        self.nc = nc
        self.es = es
        self.nsem = 0
        self.pe = Eng("pe", nc.tensor, self.sem("pe"))
        self.dve = Eng("dve", nc.vector, self.sem("dve"))
        self.act = Eng("act", nc.scalar, self.sem("act"))
        self.pool = Eng("pool", nc.gpsimd, self.sem("pool"))
        self.sp = Eng("sp", nc.sync, self.sem("sp"))
        self.out_tokens = []
        self.dtrks = []
        self.free_dsems = {}

    def sem(self, name):
        self.nsem += 1
        s = self.es.enter_context(self.nc.semaphore(f"s{self.nsem}_{name}"))
        return s

    def sb(self, name, shape, dt):
        return self.es.enter_context(self.nc.sbuf_tensor(name, shape, dt))

    def ps(self, name, shape, dt=F32):
        return self.es.enter_context(self.nc.psum_tensor(name, shape, dt))

    def _waits(self, E, reads, writes):
        need = {}
        for t in reads:
            for s, v in t.w.items():
                if need.get(s, 0) < v:
                    need[s] = v
        for t in writes:
            for s, v in t.w.items():
                if s is E.sem:
                    continue
                if need.get(s, 0) < v:
                    need[s] = v
            for s, v in t.r.items():
                if s is E.sem:
                    continue
                if need.get(s, 0) < v:
                    need[s] = v
        for s, v in need.items():
            if s is E.sem and v > E.cnt:
                continue
            if E.waited.get(id(s), 0) < v:
                E.eng.wait_ge(s, v)
                E.waited[id(s)] = v

    def op(self, E, make, reads=(), writes=(), pub=True):
        self._waits(E, reads, writes)
        inst = make()
        tokv = E.cnt + 1
        if pub:
            inst.then_inc(E.sem, 1)
            E.cnt += 1
        for t in reads:
            if t.r.get(E.sem, 0) < tokv:
                t.r[E.sem] = tokv
        for t in writes:
            t.w = {E.sem: tokv}
            t.r = {}
        return inst

    def dma(self, Q, out, in_, reads=(), writes=(), st=None, is_out=False):
        if st is None:
            st = writes[0] if writes else reads[0]
        if st.dsem is None:
            fl_ = self.free_dsems.setdefault(Q.name, [])
            if fl_:
                st.dsem, st.dcnt = fl_.pop()
            else:
                st.dsem = self.sem("d" + Q.name)
            st.dq = Q.name
            self.dtrks.append(st)
        assert st.dq == Q.name, "one DMA queue per tracker semaphore"
        self._waits(Q, reads, writes)
        inst = Q.eng.dma_start(out=out, in_=in_)
        inst.then_inc(st.dsem, 16)
        st.dcnt += 16
        for t in reads:
            if t.r.get(st.dsem, 0) < st.dcnt:
                t.r[st.dsem] = st.dcnt
        for t in writes:
            t.w = {st.dsem: st.dcnt}
            t.r = {}
        if is_out:
            self.out_tokens.append((st.dsem, st.dcnt))
        return inst

    def barrier(self):
        engs = [self.pe, self.dve, self.act, self.pool, self.sp]
        for E in engs:
            for O in engs:
                if O is E or O.cnt == 0:
                    continue
                if E.waited.get(id(O.sem), 0) < O.cnt:
                    E.eng.wait_ge(O.sem, O.cnt)
                    E.waited[id(O.sem)] = O.cnt
            for t in self.dtrks:
                if t.dcnt and E.waited.get(id(t.dsem), 0) < t.dcnt:
                    E.eng.wait_ge(t.dsem, t.dcnt)
                    E.waited[id(t.dsem)] = t.dcnt

    def release_dsems(self):
        for t in self.dtrks:
            self.free_dsems.setdefault(t.dq, []).append((t.dsem, t.dcnt))
            t.dsem = None
            t.w = {}
            t.r = {}
        self.dtrks = []

    def finish(self):
        need = {}
        for s, v in self.out_tokens:
            if need.get(id(s), (None, 0))[1] < v:
                need[id(s)] = (s, v)
        for s, v in need.values():
            self.sp.eng.wait_ge(s, v)


def build_program(NTOK, SEG, layers, T=256, phases=None):
    nc = bass.Bass("TRN2", target_bir_lowering=False)
    NSEG = NTOK // SEG
    PW = min(512, SEG)
    NPQ = SEG // PW

    def inc(name):
        return phases is None or name in phases

    def din(name, shape):
        return nc.dram_tensor(name, list(shape), F32, kind="ExternalInput").ap()

    def dscr(name, shape, dt=F32):
        return nc.dram_tensor(name, list(shape), dt, kind="Internal").ap()

    xT = din("xT", [D, NTOK])
    flags_d = din("flags", [128, 8])
    ident_d = din("ident", [128, 128])
    ffn_norm = [din("ffn1_norm", [4, D]), din("ffn2_norm", [4, D])]
    ffn_wg = [din("ffn1_w_gate", [4, D, DFF]), din("ffn2_w_gate", [4, D, DFF])]
    ffn_wu = [din("ffn1_w_up", [4, D, DFF]), din("ffn2_w_up", [4, D, DFF])]
    ffn_wd = [din("ffn1_w_down", [4, DFF, D]), din("ffn2_w_down", [4, DFF, D])]
    mix_norm = din("mix_norm", [4, D])
    final_norm = din("final_norm", [D])
    ev_w_in = din("ev_w_in", [2, D, 2560])
    lru_conv_w = din("lru_conv_w", [2, 4, 1024])
    lru_conv_b = din("lru_conv_b", [2, 1024])
    lru_w_a = din("lru_w_a", [2, 2, 16, 64, 64])
    lru_b_a = din("lru_b_a", [2, 2, 1024])
    lru_w_x = din("lru_w_x", [2, 2, 16, 64, 64])
    lru_b_x = din("lru_b_x", [2, 2, 1024])
    lru_lam = din("lru_lam", [2, 2, 1024])
    s5_lam_re = din("s5_lam_re", [2, 2, 32, 64])
    s5_lam_im = din("s5_lam_im", [2, 2, 32, 64])
    s5_log_dt = din("s5_log_dt", [2, 2, 32])
    s5_b_re = din("s5_b_re", [2, 2, 32, 64, 16])
    s5_b_im = din("s5_b_im", [2, 2, 32, 64, 16])
    s5_c_re = din("s5_c_re", [2, 2, 32, 16, 64])
    s5_c_im = din("s5_c_im", [2, 2, 32, 16, 64])
    s5_d = din("s5_d", [2, 512])
    s5_glu_w = din("s5_glu_w", [2, 512, 512])
    s5_glu_b = din("s5_glu_b", [2, 512])
    ev_w_out = din("ev_w_out", [2, 1536, 1024])
    od_w_in = din("od_w_in", [2, D, 5184])
    ssd_conv_w = din("ssd_conv_w", [2, 4, 3072])
    ssd_conv_b = din("ssd_conv_b", [2, 3072])
    ssd_dt_bias = din("ssd_dt_bias", [2, 2, 32])
    ssd_a_log = din("ssd_a_log", [2, 2, 32])
    ssd_d = din("ssd_d", [2, 32])
    ssd_norm = din("ssd_norm", [2, 2048])
    od_w_out = din("od_w_out", [2, 2048, 1024])
    tri_d = din("tri", [128, 256])
    yT = nc.dram_tensor("yT", [D, NTOK], F32, kind="ExternalOutput").ap()
    XA = dscr("XA", [D, NTOK])
    XB = dscr("XB", [D, NTOK])
    Gd = dscr("Gd", [1024, NTOK])
    XRd = dscr("XRd", [1024, NTOK])
    Ud = dscr("Ud", [512, NTOK])
    YMd = dscr("YMd", [1024, NTOK], BF16)
    YSd = dscr("YSd", [512, NTOK])
    ZSd = dscr("ZSd", [2048, NTOK])
    XBCd = dscr("XBCd", [3072, NTOK])
    DCd = dscr("DCd", [128, NTOK])
    XSfd = dscr("XSfd", [2048, NTOK])
    XStd = dscr("XStd", [NTOK, 2048], BF16)
    BFd = dscr("BFd", [4, 128, NTOK], BF16)
    CFd = dscr("CFd", [4, 128, NTOK], BF16)
    BTd = dscr("BTd", [NTOK, 512], BF16)
    Yd = dscr("Yd", [2, 2048, NTOK])
    dram_ap = {"xT": xT, "XA": XA, "XB": XB, "yT": yT}

    with contextlib.ExitStack() as es:
        es.enter_context(nc.allow_non_contiguous_dma(reason="small parameter loads"))
        kb = KB(nc, es)
        pe, dve, act, pool, sp = kb.pe, kb.dve, kb.act, kb.pool, kb.sp
        V, S, G_, PE_ = nc.vector, nc.scalar, nc.gpsimd, nc.tensor

        ones_bf = kb.sb("ones_bf", [128, 128], BF16)
        t_const = Trk("const")
        kb.op(dve, lambda: V.memset(ones_bf[:], 1.0), writes=[t_const])
        eps_t = kb.sb("eps_t", [128, 1], F32)
        kb.op(dve, lambda: V.memset(eps_t[:], EPS), writes=[t_const])
        one_t = kb.sb("one_t", [128, 1], F32)
        kb.op(dve, lambda: V.memset(one_t[:], 1.0), writes=[t_const])
        hpi_t = kb.sb("hpi_t", [128, 1], F32)
        kb.op(dve, lambda: V.memset(hpi_t[:], float(np.pi / 2)), writes=[t_const])
        fl = kb.sb("fl_sb", [128, 8], F32)
        t_fl = Trk("fl")
        kb.dma(sp, fl[:], flags_d[:, :], writes=[t_fl])
        ident = kb.sb("ident_sb", [128, 128], F32)
        t_id = Trk("ident")
        kb.dma(sp, ident[:], ident_d[:, :], writes=[t_id])
        kb.barrier()
        kb.release_dsems()

        def chunked(ap):
            return ap.rearrange("(c p) t -> p c t", p=128)

        class Phase:
            def __init__(self, tag):
                self.tag = tag
                self.pes = contextlib.ExitStack()
                self.n = 0

            def sb(self, name, shape, dt=F32):
                self.n += 1
                return self.pes.enter_context(nc.sbuf_tensor(f"{self.tag}_{name}_{self.n}", list(shape), dt))

            def ps(self, name, shape, dt=F32):
                self.n += 1
                return self.pes.enter_context(nc.psum_tensor(f"{self.tag}_{name}_{self.n}", list(shape), dt))

            def close(self):
                kb.barrier()
                kb.release_dsems()
                self.pes.close()

        cast_rr = [0]

        def cast(out, in_, reads, writes, engs=None):
            engs = engs or [pool, dve, act]
            E = engs[cast_rr[0] % len(engs)]
            cast_rr[0] += 1
            if E is act:
                kb.op(act, lambda: S.copy(out=out, in_=in_), reads=reads, writes=writes)
            elif E is dve:
                kb.op(dve, lambda: V.tensor_copy(out=out, in_=in_), reads=reads, writes=writes)
            else:
                kb.op(pool, lambda: G_.tensor_copy(out=out, in_=in_), reads=reads, writes=writes)

        def load_weight_rows(ph, wdst, t_w, src_rows_fn, nrows, ncols, stg, t_stg):
            W = stg[0].shape[1]
            si = 0
            for r in range(nrows):
                src = src_rows_fn(r)
                for c0 in range(0, ncols, W):
                    c1 = min(ncols, c0 + W)
                    b = si % len(stg)
                    si += 1
                    kb.dma(sp, stg[b][:, :c1 - c0], src[:, c0:c1], writes=[t_stg[b]])
                    cast(wdst[:, r, c0:c1], stg[b][:, :c1 - c0], [t_stg[b]], [t_w])

        class Norm:
            def __init__(self, ph, TT):
                self.TT = TT
                self.sq = ph.sb("sq", [128, 8, TT], BF16)
                self.t_sq = Trk("sq")
                self.rs = ph.sb("rs", [128, TT])
                self.t_rs = Trk("rs")
                self.rs2 = ph.sb("rs2", [128, TT])
                self.t_rs2 = Trk("rs2")
                self.p_n = ph.ps("p_n", [128, 512])
                self.t_pn = Trk("pn")

            def run(self, xt, t_xt, gam, t_gam, out, t_out):
                self.part1(xt, t_xt)
                self.part2(xt, t_xt, gam, t_gam, out, t_out)

            def part1(self, xt, t_xt):
                sq = self.sq
                for k in range(8):
                    kb.op(act, lambda k=k: S.activation(out=sq[:, k, :], in_=xt[:, k, :], func=AF.Square),
                          reads=[t_xt], writes=[self.t_sq])

            def part2(self, xt, t_xt, gam, t_gam, out, t_out):
                TT = self.TT
                sq, rs, rs2, p_n = self.sq, self.rs, self.rs2, self.p_n
                for k in range(8):
                    kb.op(pe, lambda k=k: PE_.matmul(p_n[:, :TT], lhsT=ones_bf[:], rhs=sq[:, k, :],
                                                     start=(k == 0), stop=(k == 7)),
                          reads=[t_const, self.t_sq], writes=[self.t_pn], pub=(k == 7))
                kb.op(act, lambda: S.activation(out=rs[:], in_=p_n[:, :TT], func=AF.Sqrt,
                                                scale=1.0 / D, bias=eps_t[:]),
                      reads=[self.t_pn, t_const], writes=[self.t_rs])
                kb.op(dve, lambda: V.reciprocal(out=rs2[:], in_=rs[:]), reads=[self.t_rs], writes=[self.t_rs2])
                for k in range(8):
                    kb.op(dve, lambda k=k: V.scalar_tensor_tensor(
                        out=out[:, k, :], in0=xt[:, k, :], scalar=gam[:, k:k + 1], in1=rs2[:],
                        op0=ALU.mult, op1=ALU.mult), reads=[t_xt, t_gam, self.t_rs2], writes=[t_out])

        def load_gamma(ph, src_vec):
            gam = ph.sb("gam", [128, 8])
            t_gam = Trk("gam")
            kb.dma(sp, gam[:], src_vec.rearrange("(c p) -> p c", p=128), writes=[t_gam])
            return gam, t_gam

        def ffn_phase(layer, which, src, dst):
            ph = Phase(f"f{layer}{which}")
            NT = NTOK // T
            gam, t_gam = load_gamma(ph, ffn_norm[which][layer])
            wg = ph.sb("wg", [128, 8, DFF], BF16)
            wu = ph.sb("wu", [128, 8, DFF], BF16)
            wd = ph.sb("wd", [128, NFC, D], BF16)
            t_wg, t_wu, t_wd = Trk("wg"), Trk("wu"), Trk("wd")
            HW = DFF // 2
            stg = [ph.sb(f"stg{i}", [128, HW]) for i in range(3)]
            t_stg = [Trk(f"stg{i}") for i in range(3)]
            load_weight_rows(ph, wg, t_wg, lambda r: ffn_wg[which][layer, r * 128:(r + 1) * 128, :], 8, DFF, stg, t_stg)
            load_weight_rows(ph, wu, t_wu, lambda r: ffn_wu[which][layer, r * 128:(r + 1) * 128, :], 8, DFF, stg, t_stg)
            load_weight_rows(ph, wd, t_wd, lambda r: ffn_wd[which][layer, r * 128:(r + 1) * 128, :], NFC, D, stg, t_stg)
            nrm = Norm(ph, T)
            xt = ph.sb("xt", [128, 8, T])
            t_xt = Trk("xt")
            xn = [ph.sb(f"xn{i}", [128, 8, T], BF16) for i in range(2)]
            t_xn = [Trk(f"xn{i}") for i in range(2)]
            h = ph.sb("h", [128, NFC, T], BF16)
            t_h = Trk("h")
            sg = [ph.sb(f"sg{i}", [128, T]) for i in range(2)]
            t_sg = [Trk(f"sg{i}") for i in range(2)]
            xres = [ph.sb(f"xres{i}", [128, T]) for i in range(4)]
            t_xres = [Trk(f"xres{i}") for i in range(4)]
            ores = [ph.sb(f"ores{i}", [128, T]) for i in range(4)]
            t_ores = [Trk(f"ores{i}") for i in range(4)]
            p_g = [ph.ps(f"p_g{i}", [128, 512]) for i in range(2)]
            t_pg = [Trk(f"pg{i}") for i in range(2)]
            p_u = [ph.ps(f"p_u{i}", [128, 512]) for i in range(2)]
            t_pu = [Trk(f"pu{i}") for i in range(2)]
            p_d = [ph.ps(f"p_d{i}", [128, 512]) for i in range(2)]
            t_pd = [Trk(f"pd{i}") for i in range(2)]
            srcc = chunked(dram_ap[src])
            dstc = chunked(dram_ap[dst])
            ri = 0
            kb.dma(sp, xt[:], srcc[:, :, 0:T], writes=[t_xt])
            nrm.run(xt, t_xt, gam, t_gam, xn[0], t_xn[0])
            for i in range(NT):
                ts = slice(i * T, (i + 1) * T)
                xb, txb = xn[i % 2], t_xn[i % 2]
                for j in range(NFC):
                    js = slice(j * 128, (j + 1) * 128)
                    pg, tpg = p_g[j % 2], t_pg[j % 2]
                    pu, tpu = p_u[j % 2], t_pu[j % 2]
                    for k in range(8):
                        kb.op(pe, lambda k=k: PE_.matmul(pg[:, :T], lhsT=wg[:, k, js], rhs=xb[:, k, :],
                                                         start=(k == 0), stop=(k == 7)),
                              reads=[t_wg, txb], writes=[tpg], pub=(k == 7))
                    for k in range(8):
                        kb.op(pe, lambda k=k: PE_.matmul(pu[:, :T], lhsT=wu[:, k, js], rhs=xb[:, k, :],
                                                         start=(k == 0), stop=(k == 7)),
                              reads=[t_wu, txb], writes=[tpu], pub=(k == 7))
                    sgb, tsg = sg[j % 2], t_sg[j % 2]
                    kb.op(act, lambda: S.activation(out=sgb[:], in_=pg[:, :T], func=AF.Silu),
                          reads=[tpg], writes=[tsg])
                    kb.op(dve, lambda: V.tensor_tensor(out=h[:, j, :], in0=sgb[:], in1=pu[:, :T], op=ALU.mult),
                          reads=[tsg, tpu], writes=[t_h])
                if i + 1 < NT:
                    kb.dma(sp, xt[:], srcc[:, :, (i + 1) * T:(i + 2) * T], writes=[t_xt])
                    nrm.part1(xt, t_xt)
                for m in range(8):
                    ms = slice(m * 128, (m + 1) * 128)
                    pd, tpd = p_d[m % 2], t_pd[m % 2]
                    xr, txr = xres[ri % 4], t_xres[ri % 4]
                    orr, tor = ores[ri % 4], t_ores[ri % 4]
                    ri += 1
                    if m == 4 and i + 1 < NT:
                        nrm.part2(xt, t_xt, gam, t_gam, xn[(i + 1) % 2], t_xn[(i + 1) % 2])
                    kb.dma(sp, xr[:], srcc[:, m, ts], writes=[txr])
                    for j in range(NFC):
                        kb.op(pe, lambda j=j: PE_.matmul(pd[:, :T], lhsT=wd[:, j, ms], rhs=h[:, j, :],
                                                         start=(j == 0), stop=(j == NFC - 1)),
                              reads=[t_wd, t_h], writes=[tpd], pub=(j == NFC - 1))
                    kb.op(dve, lambda: V.scalar_tensor_tensor(
                        out=orr[:], in0=pd[:, :T], scalar=0.5, in1=xr[:], op0=ALU.mult, op1=ALU.add),
                        reads=[tpd, txr], writes=[tor])
                    kb.dma(pool, dstc[:, m, ts], orr[:], reads=[tor])
            ph.close()

        def final_phase(src):
            ph = Phase("fin")
            gam, t_gam = load_gamma(ph, final_norm)
            TF = 512 if NTOK % 512 == 0 else T
            nrm = Norm(ph, TF)
            xt = [ph.sb(f"fxt{i}", [128, 8, TF]) for i in range(2)]
            t_xt = [Trk("fxt") for _ in range(2)]
            ot = [ph.sb(f"fot{i}", [128, 8, TF]) for i in range(2)]
            t_ot = [Trk("fot") for _ in range(2)]
            srcc = chunked(dram_ap[src])
            dstc = chunked(yT)
            for i in range(NTOK // TF):
                ts = slice(i * TF, (i + 1) * TF)
                xb, txb = xt[i % 2], t_xt[i % 2]
                ob, tob = ot[i % 2], t_ot[i % 2]
                kb.dma(sp, xb[:], srcc[:, :, ts], writes=[txb])
                nrm.run(xb, txb, gam, t_gam, ob, tob)
                kb.dma(pool, dstc[:, :, ts], ob[:], reads=[tob], is_out=True)
            ph.close()

        def even_in_phase(j, layer, src):
            ph = Phase(f"e1_{layer}")
            TT = 512 if NTOK % 512 == 0 else 256
            gam, t_gam = load_gamma(ph, mix_norm[layer])
            NO = 2560
            win = ph.sb("win", [128, 8, NO], BF16)
            t_win = Trk("win")
            stg = [ph.sb(f"stg{i}", [128, NO // 2]) for i in range(3)]
            t_stg = [Trk(f"stg{i}") for i in range(3)]
            load_weight_rows(ph, win, t_win, lambda r: ev_w_in[j, r * 128:(r + 1) * 128, :], 8, NO, stg, t_stg)
            nrm = Norm(ph, TT)
            xt = [ph.sb(f"xt{i}", [128, 8, TT]) for i in range(2)]
            t_xt = [Trk("xt") for _ in range(2)]
            xn = [ph.sb(f"xn{i}", [128, 8, TT], BF16) for i in range(2)]
            t_xn = [Trk("xn") for _ in range(2)]
            ev = [ph.sb(f"ev{i}", [128, TT]) for i in range(4)]
            t_ev = [Trk("ev") for _ in range(4)]
            pp = [ph.ps(f"pp{i}", [128, 512]) for i in range(3)]
            t_pp = [Trk("pp") for _ in range(3)]
            srcc = chunked(dram_ap[src])
            ei = 0
            NTT = NTOK // TT

            def prep(i):
                kb.dma(sp, xt[i % 2][:], srcc[:, :, i * TT:(i + 1) * TT], writes=[t_xt[i % 2]])
                nrm.run(xt[i % 2], t_xt[i % 2], gam, t_gam, xn[i % 2], t_xn[i % 2])

            prep(0)
            for i in range(NTT):
                ts = slice(i * TT, (i + 1) * TT)
                xnb, txnb = xn[i % 2], t_xn[i % 2]
                for oc in range(20):
                    if oc == 10 and i + 1 < NTT:
                        prep(i + 1)
                    p, tp = pp[oc % 3], t_pp[oc % 3]
                    for k in range(8):
                        kb.op(pe, lambda k=k: PE_.matmul(p[:, :TT], lhsT=win[:, k, oc * 128:(oc + 1) * 128],
                                                         rhs=xnb[:, k, :], start=(k == 0), stop=(k == 7)),
                              reads=[t_win, txnb], writes=[tp], pub=(k == 7))
                    e, te = ev[ei % 4], t_ev[ei % 4]
                    ei += 1
                    if oc < 8:
                        kb.op(act, lambda: S.activation(out=e[:], in_=p[:, :TT], func=AF.Gelu_apprx_tanh),
                              reads=[tp], writes=[te])
                        dst = Gd[oc * 128:(oc + 1) * 128, ts]
                    elif oc < 16:
                        kb.op(dve, lambda: V.tensor_copy(out=e[:], in_=p[:, :TT]), reads=[tp], writes=[te])
                        dst = XRd[(oc - 8) * 128:(oc - 7) * 128, ts]
                    else:
                        kb.op(dve, lambda: V.tensor_copy(out=e[:], in_=p[:, :TT]), reads=[tp], writes=[te])
                        dst = Ud[(oc - 16) * 128:(oc - 15) * 128, ts]
                    kb.dma(pool, dst, e[:], reads=[te])
            ph.close()

        def load_halo(xp, t_xp, src_rows, s):
            lo = s * SEG - 2
            hi = s * SEG + SEG + 1
            clo, chi = max(lo, 0), min(hi, NTOK)
            kb.dma(sp, xp[:, clo - lo:chi - lo], src_rows[:, clo:chi], writes=[t_xp])
            if s == 0:
                kb.op(dve, lambda: V.memset(xp[:, 0:2], 0.0), writes=[t_xp])
            else:
                kb.op(dve, lambda: V.tensor_scalar(out=xp[:, 0:2], in0=xp[:, 0:2], scalar1=fl[:, s - 1:s], scalar2=None,
                                                   op0=ALU.mult), reads=[t_xp, t_fl], writes=[t_xp])
            if s == NSEG - 1:
                kb.op(dve, lambda: V.memset(xp[:, SEG + 2:SEG + 3], 0.0), writes=[t_xp])
            else:
                kb.op(dve, lambda: V.tensor_scalar(out=xp[:, SEG + 2:SEG + 3], in0=xp[:, SEG + 2:SEG + 3],
                                                   scalar1=fl[:, s:s + 1], scalar2=None, op0=ALU.mult),
                      reads=[t_xp, t_fl], writes=[t_xp])

        def conv4(xc, t_xc, xp, t_xp, w4, bcol, t_par):
            kb.op(dve, lambda: V.tensor_scalar(out=xc[:], in0=xp[:, 0:SEG], scalar1=w4[:, 0:1], scalar2=bcol,
                                               op0=ALU.mult, op1=ALU.add), reads=[t_xp, t_par], writes=[t_xc])
            for k in range(1, 4):
                kb.op(dve, lambda k=k: V.scalar_tensor_tensor(out=xc[:], in0=xp[:, k:k + SEG], scalar=w4[:, k:k + 1],
                                                              in1=xc[:], op0=ALU.mult, op1=ALU.add),
                      reads=[t_xp, t_par, t_xc], writes=[t_xc])

        def lru_phase(j):
            ph = Phase(f"e2_{j}")
            t_par = Trk("par")
            cw = ph.sb("cw", [128, 8, 4])
            for k in range(4):
                kb.dma(sp, cw[:, :, k], lru_conv_w[j, k].rearrange("(c p) -> p c", p=128), writes=[t_par], st=t_par)
            cb = ph.sb("cb", [128, 8])
            kb.dma(sp, cb[:], lru_conv_b[j].rearrange("(c p) -> p c", p=128), writes=[t_par])
            ba = ph.sb("ba", [128, 2, 8])
            for d in range(2):
                kb.dma(sp, ba[:, d, :], lru_b_a[j, d].rearrange("(c p) -> p c", p=128), writes=[t_par], st=t_par)
            bx = ph.sb("bx", [128, 2, 8])
            for d in range(2):
                kb.dma(sp, bx[:, d, :], lru_b_x[j, d].rearrange("(c p) -> p c", p=128), writes=[t_par], st=t_par)
            lam = ph.sb("lam", [128, 2, 8])
            t_lam = Trk("lam")
            for d in range(2):
                kb.dma(sp, lam[:, d, :], lru_lam[j, d].rearrange("(c p) -> p c", p=128), writes=[t_lam], st=t_lam)
            l1 = ph.sb("l1", [128, 2, 8])
            c8 = ph.sb("c8", [128, 2, 8])
            c16 = ph.sb("c16", [128, 2, 8])
            kb.op(act, lambda: S.activation(out=l1[:], in_=lam[:], func=AF.Exp, scale=-1.0), reads=[t_lam], writes=[t_lam])
            kb.op(act, lambda: S.activation(out=l1[:], in_=l1[:], func=AF.Ln, bias=1.0, scale=1.0),
                  reads=[t_lam], writes=[t_lam])
            kb.op(dve, lambda: V.tensor_scalar(out=c8[:], in0=l1[:], scalar1=-8.0, scalar2=None, op0=ALU.mult),
                  reads=[t_lam], writes=[t_par])
            kb.op(dve, lambda: V.tensor_scalar(out=c16[:], in0=l1[:], scalar1=-16.0, scalar2=None, op0=ALU.mult),
                  reads=[t_lam], writes=[t_par])
            wbd = ph.sb("wbd", [128, 32, 128], BF16)
            t_wbd = Trk("wbd")
            pp_ = Phase(f"e2p_{j}")
            wst = pp_.sb("wst", [128, 32, 128])
            t_wst = Trk("wst")
            kb.op(dve, lambda: V.memset(wst[:], 0.0), writes=[t_wst])

            def widx(g, d, c):
                return (g * 2 + d) * 8 + c

            for g, Wd_ in enumerate((lru_w_a, lru_w_x)):
                for d in range(2):
                    for c in range(8):
                        for hh in range(2):
                            kb.dma(sp, wst[hh * 64:(hh + 1) * 64, widx(g, d, c), hh * 64:(hh + 1) * 64],
                                   Wd_[j, d, 2 * c + hh], writes=[t_wst], st=t_wst)
            kb.op(dve, lambda: V.tensor_copy(out=wbd[:], in_=wst[:]), reads=[t_wst], writes=[t_wbd])
            pp_.close()

            hb = ph.sb("hb", [128, NTOK])
            t_hb = Trk("hb")
            xp = [ph.sb(f"xp{i}", [128, SEG + 3]) for i in range(2)]
            t_xp = [Trk("xp") for _ in range(2)]
            xc2 = [ph.sb(f"xc{i}", [128, SEG]) for i in range(2)]; t_xc2 = [Trk("xc") for _ in range(2)]
            xcb2 = [ph.sb(f"xcb{i}", [128, SEG], BF16) for i in range(2)]; t_xcb2 = [Trk("xcb") for _ in range(2)]
            Rb2 = [ph.sb(f"Rb{i}", [128, SEG]) for i in range(2)]; t_R2 = [Trk("R") for _ in range(2)]
            Ab2 = [ph.sb(f"Ab{i}", [128, SEG]) for i in range(2)]; t_A2 = [Trk("A") for _ in range(2)]
            Sb2 = [ph.sb(f"Sb{i}", [128, SEG]) for i in range(2)]; t_S2 = [Trk("S") for _ in range(2)]
            Ib2 = [ph.sb(f"Ib{i}", [128, SEG]) for i in range(2)]; t_I2 = [Trk("I") for _ in range(2)]
            hf = [ph.sb(f"hf{i}", [128, SEG]) for i in range(2)]
            t_hf = [Trk("hf") for _ in range(2)]
            Gt = ph.sb("Gt", [128, SEG]); t_G = Trk("G")
            yb = [ph.sb(f"yb{i}", [128, SEG], BF16) for i in range(2)]
            t_yb = [Trk("yb") for _ in range(2)]
            ini = ph.sb("ini", [128, 2]); t_ini = Trk("ini")
            pa = ph.ps("pa", [128, SEG]); t_pa = Trk("pa")
            px = ph.ps("px", [128, SEG]); t_px = Trk("px")
            def stA(cnt, c, d, s):
                rows = XRd[c * 128:(c + 1) * 128, :]
                ss = slice(s * SEG, (s + 1) * SEG)
                xpb, txp = xp[cnt % 2], t_xp[cnt % 2]
                xc, t_xc = xc2[cnt % 2], t_xc2[cnt % 2]
                xcb, t_xcb = xcb2[cnt % 2], t_xcb2[cnt % 2]
                Rb, t_R = Rb2[cnt % 2], t_R2[cnt % 2]
                Ab, t_A = Ab2[cnt % 2], t_A2[cnt % 2]
                Sb, t_S = Sb2[cnt % 2], t_S2[cnt % 2]
                Ib, t_I = Ib2[cnt % 2], t_I2[cnt % 2]
                load_halo(xpb, txp, rows, s)
                conv4(xc, t_xc, xpb, txp, cw[:, c, :], cb[:, c:c + 1], t_par)
                kb.op(act, lambda: S.copy(out=xcb[:], in_=xc[:]), reads=[t_xc], writes=[t_xcb])
                for q in range(NPQ):
                    qs = slice(q * PW, (q + 1) * PW)
                    kb.op(pe, lambda qs=qs: PE_.matmul(pa[:, qs], lhsT=wbd[:, widx(0, d, c), :], rhs=xcb[:, qs],
                                                       start=True, stop=True),
                          reads=[t_wbd, t_xcb], writes=[t_pa], pub=(q == NPQ - 1))
                for q in range(NPQ):
                    qs = slice(q * PW, (q + 1) * PW)
                    kb.op(pe, lambda qs=qs: PE_.matmul(px[:, qs], lhsT=wbd[:, widx(1, d, c), :], rhs=xcb[:, qs],
                                                       start=True, stop=True),
                          reads=[t_wbd, t_xcb], writes=[t_px], pub=(q == NPQ - 1))
                kb.op(act, lambda: S.activation(out=Rb[:], in_=pa[:], func=AF.Sigmoid, bias=ba[:, d, c:c + 1],
                                                scale=1.0), reads=[t_pa, t_par], writes=[t_R])
                kb.op(act, lambda: S.activation(out=Ab[:], in_=Rb[:], func=AF.Exp, scale=c8[:, d, c:c + 1]),
                      reads=[t_R, t_par], writes=[t_A])
                kb.op(act, lambda: S.activation(out=Sb[:], in_=Rb[:], func=AF.Exp, scale=c16[:, d, c:c + 1]),
                      reads=[t_R, t_par], writes=[t_S])
                kb.op(act, lambda: S.activation(out=Sb[:], in_=Sb[:], func=AF.Sqrt, scale=-1.0, bias=one_t[:]),
                      reads=[t_S, t_const], writes=[t_S])
                kb.op(act, lambda: S.activation(out=Ib[:], in_=px[:], func=AF.Sigmoid, bias=bx[:, d, c:c + 1],
                                                scale=1.0), reads=[t_px, t_par], writes=[t_I])

            def stB(cnt, c, d, s):
                ss = slice(s * SEG, (s + 1) * SEG)
                xpb, txp = xp[cnt % 2], t_xp[cnt % 2]
                xc, t_xc = xc2[cnt % 2], t_xc2[cnt % 2]
                xcb, t_xcb = xcb2[cnt % 2], t_xcb2[cnt % 2]
                Rb, t_R = Rb2[cnt % 2], t_R2[cnt % 2]
                Ab, t_A = Ab2[cnt % 2], t_A2[cnt % 2]
                Sb, t_S = Sb2[cnt % 2], t_S2[cnt % 2]
                Ib, t_I = Ib2[cnt % 2], t_I2[cnt % 2]
                kb.op(dve, lambda: V.tensor_tensor(out=Ib[:], in0=Ib[:], in1=xc[:], op=ALU.mult),
                      reads=[t_I, t_xc], writes=[t_I])
                kb.op(dve, lambda: V.tensor_tensor(out=Ib[:], in0=Ib[:], in1=Sb[:], op=ALU.mult),
                      reads=[t_I, t_S], writes=[t_I])
                if d == 1:
                    if s == NSEG - 1:
                        init = 0.0
                        rd = []
                    else:
                        kb.op(dve, lambda: V.tensor_scalar(out=ini[:, 0:1], in0=hb[:, (s + 1) * SEG:(s + 1) * SEG + 1],
                                                           scalar1=fl[:, s:s + 1], scalar2=None, op0=ALU.mult),
                              reads=[t_hb, t_fl], writes=[t_ini])
                        init = ini[:, 0:1]
                        rd = [t_ini]
                    kb.op(dve, lambda: V.tensor_tensor_scan(out=hb[:, ss][:, ::-1], data0=Ab[:, ::-1],
                                                            data1=Ib[:, ::-1], initial=init,
                                                            op0=ALU.mult, op1=ALU.add),
                          reads=[t_A, t_I] + rd, writes=[t_hb])
                else:
                    hfb, thf = hf[cnt % 2], t_hf[cnt % 2]
                    hfp, thfp = hf[(cnt + 1) % 2], t_hf[(cnt + 1) % 2]
                    if s == 0:
                        init = 0.0
                        rd = []
                    else:
                        kb.op(dve, lambda: V.tensor_scalar(out=ini[:, 1:2], in0=hfp[:, SEG - 1:SEG],
                                                           scalar1=fl[:, s - 1:s], scalar2=None, op0=ALU.mult),
                              reads=[thfp, t_fl], writes=[t_ini])
                        init = ini[:, 1:2]
                        rd = [t_ini]
                    kb.op(dve, lambda: V.tensor_tensor_scan(out=hfb[:], data0=Ab[:], data1=Ib[:], initial=init,
                                                            op0=ALU.mult, op1=ALU.add),
                          reads=[t_A, t_I] + rd, writes=[thf])
                    kb.dma(sp, Gt[:], Gd[c * 128:(c + 1) * 128, ss], writes=[t_G])
                    kb.op(dve, lambda: V.tensor_tensor(out=Ib[:], in0=hfb[:], in1=hb[:, ss], op=ALU.add),
                          reads=[thf, t_hb], writes=[t_I])
                    ybb, tyb = yb[cnt % 2], t_yb[cnt % 2]
                    kb.op(pool, lambda: G_.tensor_tensor(out=ybb[:], in0=Ib[:], in1=Gt[:], op=ALU.mult),
                          reads=[t_I, t_G], writes=[tyb])
                    kb.dma(pool, YMd[c * 128:(c + 1) * 128, ss], ybb[:], reads=[tyb])

            units = []
            for c in range(8):
                for d in (1, 0):
                    segs = list(range(NSEG))[::-1] if d == 1 else list(range(NSEG))
                    for s in segs:
                        units.append((len(units), c, d, s))
            for i in range(len(units) + 1):
                if i < len(units):
                    stA(*units[i])
                if i >= 1:
                    stB(*units[i - 1])
            ph.close()

        def s5_phase(j):
            ph = Phase(f"e3_{j}")
            t_p = Trk("s5par")
            NC_ = 32
            KB8 = 8
            J = SEG // KB8

            def pt(name, shape=None):
                return ph.sb(name, shape or [128, NC_])

            lre, lim, ldt = pt("lre"), pt("lim"), pt("ldt")
            for d in range(2):
                kb.dma(sp, lre[:, d * 16:(d + 1) * 16],
                       s5_lam_re[j, d].rearrange("(t g) n -> (g n) t", g=2), writes=[t_p], st=t_p)
                kb.dma(sp, lim[:, d * 16:(d + 1) * 16],
                       s5_lam_im[j, d].rearrange("(t g) n -> (g n) t", g=2), writes=[t_p], st=t_p)
                for g in range(2):
                    kb.dma(sp, ldt[g * 64:(g + 1) * 64, d * 16:(d + 1) * 16],
                           s5_log_dt[j, d].rearrange("(t g) -> g t", g=2)[g].partition_broadcast(64), writes=[t_p], st=t_p)

            def vv(out, a, b_, op):
                kb.op(dve, lambda: V.tensor_tensor(out=out, in0=a, in1=b_, op=op), reads=[t_p], writes=[t_p])

            def vs(out, a, s1, op):
                kb.op(dve, lambda: V.tensor_scalar(out=out, in0=a, scalar1=s1, scalar2=None, op0=op),
                      reads=[t_p], writes=[t_p])

            dtt, th, lr, rho = pt("dtt"), pt("th"), pt("lr"), pt("rho")
            kb.op(act, lambda: S.activation(out=dtt[:], in_=ldt[:], func=AF.Exp), reads=[t_p], writes=[t_p])
            vv(th[:], lim[:], dtt[:], ALU.mult)
            vv(lr[:], lre[:], dtt[:], ALU.mult)
            kb.op(act, lambda: S.activation(out=rho[:], in_=lr[:], func=AF.Exp), reads=[t_p], writes=[t_p])
            cs_, sn_ = pt("cs"), pt("sn")
            kb.op(act, lambda: S.activation(out=sn_[:], in_=th[:], func=AF.Sin, scale=1.0 / 32.0), reads=[t_p], writes=[t_p])
            kb.op(act, lambda: S.activation(out=cs_[:], in_=th[:], func=AF.Sin, scale=1.0 / 32.0, bias=hpi_t[:]),
                  reads=[t_p, t_const], writes=[t_p])
            cc, s2, sc = pt("cc"), pt("s2"), pt("sc")

            def double_angle(c_, s_):
                vv(cc[:], c_[:], c_[:], ALU.mult)
                vv(s2[:], s_[:], s_[:], ALU.mult)
                vv(sc[:], s_[:], c_[:], ALU.mult)
                vv(c_[:], cc[:], s2[:], ALU.subtract)
                vs(s_[:], sc[:], 2.0, ALU.mult)

            for _ in range(5):
                double_angle(cs_, sn_)
            c8, s8, rho8 = pt("c8"), pt("s8"), pt("rho8")
            vs(c8[:], cs_[:], 1.0, ALU.mult)
            vs(s8[:], sn_[:], 1.0, ALU.mult)
            for _ in range(3):
                double_angle(c8, s8)
            vv(rho8[:], rho[:], rho[:], ALU.mult)
            vv(rho8[:], rho8[:], rho8[:], ALU.mult)
            vv(rho8[:], rho8[:], rho8[:], ALU.mult)
            abr, abi = pt("abr"), pt("abi")
            vv(abr[:], rho[:], cs_[:], ALU.mult)
            vv(abi[:], rho[:], sn_[:], ALU.mult)
            t1, t2, den, abm1 = pt("t1"), pt("t2"), pt("den"), pt("abm1")
            vv(t1[:], lre[:], lre[:], ALU.mult)
            vv(t2[:], lim[:], lim[:], ALU.mult)
            vv(den[:], t1[:], t2[:], ALU.add)
            kb.op(dve, lambda: V.reciprocal(out=den[:], in_=den[:]), reads=[t_p], writes=[t_p])
            vs(abm1[:], abr[:], -1.0, ALU.add)
            cre, cim = pt("cre"), pt("cim")
            vv(t1[:], abm1[:], lre[:], ALU.mult)
            vv(t2[:], abi[:], lim[:], ALU.mult)
            vv(t1[:], t1[:], t2[:], ALU.add)
            vv(cre[:], t1[:], den[:], ALU.mult)
            vv(t1[:], abi[:], lre[:], ALU.mult)
            vv(t2[:], abm1[:], lim[:], ALU.mult)
            vv(t1[:], t1[:], t2[:], ALU.subtract)
            vv(cim[:], t1[:], den[:], ALU.mult)
            APR = ph.sb("APR", [128, 9, NC_])
            API = ph.sb("API", [128, 9, NC_])
            kb.op(dve, lambda: V.memset(APR[:, 0, :], 1.0), writes=[t_p])
            kb.op(dve, lambda: V.memset(API[:, 0, :], 0.0), writes=[t_p])
            vs(APR[:, 1, :], abr[:], 1.0, ALU.mult)
            vs(API[:, 1, :], abi[:], 1.0, ALU.mult)
            for p in range(2, 9):
                vv(t1[:], APR[:, p - 1, :], abr[:], ALU.mult)
                vv(t2[:], API[:, p - 1, :], abi[:], ALU.mult)
                vv(APR[:, p, :], t1[:], t2[:], ALU.subtract)
                vv(t1[:], APR[:, p - 1, :], abi[:], ALU.mult)
                vv(t2[:], API[:, p - 1, :], abr[:], ALU.mult)
                vv(API[:, p, :], t1[:], t2[:], ALU.add)
            pzr = ph.ps("pzr", [128, 512]); t_pzr = Trk("pzr")
            pzi = ph.ps("pzi", [128, 512]); t_pzi = Trk("pzi")
            py = ph.ps("py", [128, 2048]); t_py = Trk("py")
            pm = ph.ps("pm", [32, 512]); t_pm = Trk("pm")
            bbre = ph.sb("bbre", [128, NC_, 32])
            bbim = ph.sb("bbim", [128, NC_, 32])
            bbimn = ph.sb("bbimn", [128, NC_, 32])
            CTre = ph.sb("CTre", [128, NC_, 32])
            CTim = ph.sb("CTim", [128, NC_, 32])
            dsk = ph.sb("dsk", [32, 16])
            kb.dma(sp, dsk[:], s5_d[j].rearrange("(t c) -> c t", c=32), writes=[t_p], st=t_p)
            pp_ = Phase(f"e3p_{j}")
            bre = pp_.sb("bre", [128, NC_, 32])
            bim = pp_.sb("bim", [128, NC_, 32])
            kb.op(dve, lambda: V.memset(bre[:], 0.0), writes=[t_p])
            kb.op(dve, lambda: V.memset(bim[:], 0.0), writes=[t_p])
            for g in range(2):
                for (dst_, src_) in ((bre, s5_b_re), (bim, s5_b_im)):
                    for d in range(2):
                        kb.dma(sp, dst_[g * 64:(g + 1) * 64, d * 16:(d + 1) * 16, g * 16:(g + 1) * 16],
                               src_[j, d].rearrange("(t g) n c -> g n t c", g=2)[g], writes=[t_p], st=t_p)
            tb = pp_.sb("tb", [128, NC_, 32])

            def bc(col):
                return col[:].unsqueeze(2).to_broadcast([128, NC_, 32])

            vv(bbre[:], bre[:], bc(cre), ALU.mult)
            vv(tb[:], bim[:], bc(cim), ALU.mult)
            vv(bbre[:], bbre[:], tb[:], ALU.subtract)
            vv(bbim[:], bim[:], bc(cre), ALU.mult)
            vv(tb[:], bre[:], bc(cim), ALU.mult)
            vv(bbim[:], bbim[:], tb[:], ALU.add)
            vs(bbimn[:], bbim[:], -1.0, ALU.mult)
            craw = [pp_.sb("crre", [32, NC_, 128]), pp_.sb("crim", [32, NC_, 128])]
            t_c = Trk("craw")
            for ri_, src_ in enumerate((s5_c_re, s5_c_im)):
                kb.op(dve, lambda: V.memset(craw[ri_][:], 0.0), writes=[t_c])
                for g in range(2):
                    for d in range(2):
                        kb.dma(sp, craw[ri_][g * 16:(g + 1) * 16, d * 16:(d + 1) * 16, g * 64:(g + 1) * 64],
                               src_[j, d].rearrange("(t g) c n -> g c t n", g=2)[g], writes=[t_c], st=t_c)
            for ri_, CT in enumerate((CTre, CTim)):
                for q in range(NC_):
                    kb.op(pe, lambda q=q: PE_.transpose(out=py[:, q * 32:(q + 1) * 32], in_=craw[ri_][:, q, :],
                                                        identity=ident[0:32, 0:32]),
                          reads=[t_c, t_id], writes=[t_py], pub=(q == NC_ - 1))
                kb.op(dve, lambda: V.tensor_copy(out=CT[:], in_=py[:, 0:NC_ * 32].rearrange("p (q c) -> p q c", q=NC_)),
                      reads=[t_py], writes=[t_p])
            pp_.close()
            CAre = [ph.sb(f"CAre{d}", [128, 9, 32]) for d in range(2)]
            CAim = [ph.sb(f"CAim{d}", [128, 9, 32]) for d in range(2)]
            tca = ph.sb("tca", [128, 9, 32])
            t_ca = Trk("ca")
            lCre = [ph.sb(f"lCre{d}", [128, 9, 32], BF16) for d in range(2)]
            lCimn = [ph.sb(f"lCimn{d}", [128, 9, 32], BF16) for d in range(2)]
            t_lC = Trk("lC")
            Dgr = ph.sb("Dgr", [128, 8, 128]); Dgi = ph.sb("Dgi", [128, 8, 128]); t_dg = Trk("dg")
            lBzr = [ph.sb(f"lBzr{d}", [32, 8, 128], BF16) for d in range(2)]
            lBzi = [ph.sb(f"lBzi{d}", [32, 8, 128], BF16) for d in range(2)]
            t_lB = Trk("lB")
            Mfb = ph.sb("Mfb", [32, 2, 8, 32], BF16); M0 = ph.sb("M0", [32, 32], BF16); M0f = ph.sb("M0f", [32, 32])
            t_M = Trk("M")
            cosT = ph.sb("cosT", [128, J + 2]); sinT = ph.sb("sinT", [128, J + 2]); t_tab = [Trk("tab0"), Trk("tab1")]
            cosTs = [cosT, ph.sb("cosT1", [128, J + 2])]
            sinTs = [sinT, ph.sb("sinT1", [128, J + 2])]
            RT = [ph.sb(f"RT{d}", [128, J]) for d in range(2)]; t_RT = Trk("RT")
            q1 = ph.sb("q1", [128, J + 2]); q2 = ph.sb("q2", [128, J + 2]); t_q1 = Trk("q1"); t_q2 = Trk("q2")
            uf = [ph.sb(f"uf{i}", [32, SEG]) for i in range(2)]; t_uf = [Trk("uf") for _ in range(2)]
            ub = [ph.sb(f"ub{i}", [32, SEG], BF16) for i in range(2)]; t_ub = [Trk("ub") for _ in range(2)]
            w1 = ph.sb("w1", [128, J + 1]); w2 = ph.sb("w2", [128, J + 1]); t_w1 = Trk("w1"); t_w2 = Trk("w2")
            btre = ph.sb("btre", [128, J]); btim = ph.sb("btim", [128, J]); t_bt = Trk("bt")
            gre = [ph.sb(f"gre{d}", [128, J + 1]) for d in range(2)]
            gim = [ph.sb(f"gim{d}", [128, J + 1]) for d in range(2)]
            t_g = [Trk("g0"), Trk("g1")]
            Hf = [ph.sb("Hfre", [128, J + 1], BF16), ph.sb("Hfim", [128, J + 1], BF16)]; t_Hf = Trk("Hf")
            Hb = [ph.sb("Hbre", [128, NSEG, J + 1], BF16), ph.sb("Hbim", [128, NSEG, J + 1], BF16)]; t_Hb = Trk("Hb")
            ot_ = [ph.sb(f"ot{i}", [32, SEG]) for i in range(2)]; t_ot = [Trk("ot") for _ in range(2)]
            ini = ph.sb("ini", [128, 8]); t_ini = Trk("ini")
            li = 0
            for ti in range(16):
                for d in (1, 0):
                    col = d * 16 + ti
                    cT, sT, ttab = cosTs[d], sinTs[d], t_tab[d]
                    ctr = CTre[:, col, :].unsqueeze(1).to_broadcast([128, 9, 32])
                    cti = CTim[:, col, :].unsqueeze(1).to_broadcast([128, 9, 32])
                    apr = APR[:, :, col].unsqueeze(2).to_broadcast([128, 9, 32])
                    api = API[:, :, col].unsqueeze(2).to_broadcast([128, 9, 32])

                    def ca(out, a, b_, op):
                        kb.op(dve, lambda: V.tensor_tensor(out=out, in0=a, in1=b_, op=op), reads=[t_p, t_ca], writes=[t_ca])

                    ca(CAre[d][:], ctr, apr, ALU.mult)
                    ca(tca[:], cti, api, ALU.mult)
                    ca(CAre[d][:], CAre[d][:], tca[:], ALU.subtract)
                    ca(CAim[d][:], ctr, api, ALU.mult)
                    ca(tca[:], cti, apr, ALU.mult)
                    ca(CAim[d][:], CAim[d][:], tca[:], ALU.add)
                    kb.op(act, lambda: S.copy(out=lCre[d][:], in_=CAre[d][:]), reads=[t_ca], writes=[t_lC])
                    kb.op(act, lambda: S.mul(out=lCimn[d][:], in_=CAim[d][:], mul=-1.0), reads=[t_ca], writes=[t_lC])
                    kb.op(pe, lambda: PE_.matmul(pm[:, 0:256], lhsT=bbre[:, col, :],
                                                 rhs=CAre[d][:, 0:8, :].rearrange("p a c -> p (a c)"), start=True, stop=False),
                          reads=[t_p, t_ca], writes=[t_pm], pub=False)
                    kb.op(pe, lambda: PE_.matmul(pm[:, 0:256], lhsT=bbimn[:, col, :],
                                                 rhs=CAim[d][:, 0:8, :].rearrange("p a c -> p (a c)"), start=False, stop=True),
                          reads=[t_p, t_ca], writes=[t_pm])
                    kb.op(dve, lambda: V.tensor_copy(out=Mfb[:, d, :, :], in_=pm[:, 0:256].rearrange("p (a c) -> p a c", a=8)),
                          reads=[t_pm], writes=[t_M])
                    if d == 1:
                        kb.op(dve, lambda: V.tensor_copy(out=M0f[:], in_=pm[:, 0:32]), reads=[t_pm], writes=[t_M])
                    else:
                        kb.op(dve, lambda: V.tensor_tensor(out=M0[:], in0=M0f[:], in1=pm[:, 0:32], op=ALU.add),
                              reads=[t_pm, t_M], writes=[t_M])
                    idb = ident[:, :].unsqueeze(1).to_broadcast([128, 8, 128])
                    kb.op(dve, lambda: V.tensor_tensor(out=Dgr[:], in0=idb,
                                                       in1=APR[:, 0:8, col].unsqueeze(2).to_broadcast([128, 8, 128]), op=ALU.mult),
                          reads=[t_p, t_id], writes=[t_dg])
                    kb.op(dve, lambda: V.tensor_tensor(out=Dgi[:], in0=idb,
                                                       in1=API[:, 0:8, col].unsqueeze(2).to_broadcast([128, 8, 128]), op=ALU.mult),
                          reads=[t_p, t_id], writes=[t_dg])
                    for hh in range(2):
                        hs_ = slice(hh * 512, (hh + 1) * 512)
                        dgr = Dgr[:].rearrange("p a n -> p (a n)")[:, hs_]
                        dgi = Dgi[:].rearrange("p a n -> p (a n)")[:, hs_]
                        kb.op(pe, lambda: PE_.matmul(py[0:32, hs_], lhsT=bbre[:, col, :], rhs=dgr, start=True, stop=False),
                              reads=[t_p, t_dg], writes=[t_py], pub=False)
                        kb.op(pe, lambda: PE_.matmul(py[0:32, hs_], lhsT=bbimn[:, col, :], rhs=dgi, start=False, stop=True),
                              reads=[t_p, t_dg], writes=[t_py], pub=False)
                        h2 = slice(1024 + hh * 512, 1024 + (hh + 1) * 512)
                        kb.op(pe, lambda: PE_.matmul(py[0:32, h2], lhsT=bbre[:, col, :], rhs=dgi, start=True, stop=False),
                              reads=[t_p, t_dg], writes=[t_py], pub=False)
                        kb.op(pe, lambda: PE_.matmul(py[0:32, h2], lhsT=bbim[:, col, :], rhs=dgr, start=False, stop=True),
                              reads=[t_p, t_dg], writes=[t_py], pub=(hh == 1))
                    kb.op(act, lambda: S.copy(out=lBzr[d][:], in_=py[0:32, 0:1024].rearrange("p (a n) -> p a n", a=8)),
                          reads=[t_py], writes=[t_lB])
                    kb.op(act, lambda: S.copy(out=lBzi[d][:], in_=py[0:32, 1024:2048].rearrange("p (a n) -> p a n", a=8)),
                          reads=[t_py], writes=[t_lB])
                    kb.op(dve, lambda: V.tensor_copy(out=cT[:, 0:1], in_=c8[:, col:col + 1]), reads=[t_p], writes=[ttab])
                    kb.op(dve, lambda: V.tensor_scalar(out=sT[:, 0:1], in0=s8[:, col:col + 1], scalar1=-1.0, scalar2=None,
                                                       op0=ALU.mult), reads=[t_p], writes=[ttab])
                    kb.op(dve, lambda: V.memset(cT[:, 1:2], 1.0), writes=[ttab])
                    kb.op(dve, lambda: V.memset(sT[:, 1:2], 0.0), writes=[ttab])
                    kb.op(dve, lambda: V.tensor_copy(out=cT[:, 2:3], in_=c8[:, col:col + 1]), reads=[t_p], writes=[ttab])
                    kb.op(dve, lambda: V.tensor_copy(out=sT[:, 2:3], in_=s8[:, col:col + 1]), reads=[t_p], writes=[ttab])
                    cV, sV = cT[:, 1:J + 2], sT[:, 1:J + 2]
                    m = 2
                    while m < J + 1:
                        n = min(m - 1, J + 1 - m)
                        cr, ci = cV[:, m - 1:m], sV[:, m - 1:m]
                        kb.op(dve, lambda: V.tensor_scalar(out=q1[:, 0:n], in0=sV[:, 1:1 + n], scalar1=ci, scalar2=None,
                                                           op0=ALU.mult), reads=[ttab], writes=[t_q1])
                        kb.op(dve, lambda: V.tensor_scalar(out=q2[:, 0:n], in0=cV[:, 1:1 + n], scalar1=ci, scalar2=None,
                                                           op0=ALU.mult), reads=[ttab], writes=[t_q2])
                        kb.op(dve, lambda: V.scalar_tensor_tensor(out=cV[:, m:m + n], in0=cV[:, 1:1 + n], scalar=cr,
                                                                  in1=q1[:, 0:n], op0=ALU.mult, op1=ALU.subtract),
                              reads=[ttab, t_q1], writes=[ttab])
                        kb.op(dve, lambda: V.scalar_tensor_tensor(out=sV[:, m:m + n], in0=sV[:, 1:1 + n], scalar=cr,
                                                                  in1=q2[:, 0:n], op0=ALU.mult, op1=ALU.add),
                              reads=[ttab, t_q2], writes=[ttab])
                        m += n
                    kb.op(dve, lambda: V.tensor_copy(out=RT[d][:], in_=rho8[:, col:col + 1].to_broadcast([128, J])),
                          reads=[t_p], writes=[t_RT])
                for d in (1, 0):
                    cT, sT, ttab = cosTs[d], sinTs[d], t_tab[d]
                    cV, sV = cT[:, 1:J + 2], sT[:, 1:J + 2]
                    EQr, EQi = cV[:, J:J + 1], sV[:, J:J + 1]
                    gr, gi_, tg = gre[d], gim[d], t_g[d]
                    segs = list(range(NSEG))[::-1] if d == 1 else list(range(NSEG))
                    for si_, s in enumerate(segs):
                        ss = slice(s * SEG, (s + 1) * SEG)
                        ufb, tuf = uf[li % 2], t_uf[li % 2]
                        ubb, tub = ub[li % 2], t_ub[li % 2]
                        otb, tot = ot_[li % 2], t_ot[li % 2]
                        li += 1
                        kb.dma(sp, ufb[:], Ud[32 * ti:32 * ti + 32, ss], writes=[tuf])
                        kb.op(act, lambda: S.copy(out=ubb[:], in_=ufb[:]), reads=[tuf], writes=[tub])
                        for (pz, tpz, lBz) in ((pzr, t_pzr, lBzr[d]), (pzi, t_pzi, lBzi[d])):
                            for k in range(8):
                                e = (7 - k) if d == 0 else k
                                kb.op(pe, lambda k=k, e=e: PE_.matmul(pz[:, 0:J], lhsT=lBz[:, e, :], rhs=ubb[:, k:SEG:8],
                                                                      start=(k == 0), stop=(k == 7)),
                                      reads=[t_lB, tub], writes=[tpz], pub=(k == 7))
                        if d == 0:
                            Ec, Es = cV[:, 0:J], sV[:, 0:J]
                        else:
                            Ec, Es = cV[:, 0:J][:, ::-1], sV[:, 0:J][:, ::-1]
                        kb.op(dve, lambda: V.tensor_tensor(out=w1[:, 0:J], in0=Ec, in1=pzr[:, 0:J], op=ALU.mult),
                              reads=[ttab, t_pzr], writes=[t_w1])
                        kb.op(dve, lambda: V.tensor_tensor(out=w2[:, 0:J], in0=Es, in1=pzi[:, 0:J], op=ALU.mult),
                              reads=[ttab, t_pzi], writes=[t_w2])
                        kb.op(dve, lambda: V.tensor_tensor(out=btre[:], in0=w1[:, 0:J], in1=w2[:, 0:J], op=ALU.add),
                              reads=[t_w1, t_w2], writes=[t_bt])
                        kb.op(dve, lambda: V.tensor_tensor(out=w1[:, 0:J], in0=Ec, in1=pzi[:, 0:J], op=ALU.mult),
                              reads=[ttab, t_pzi], writes=[t_w1])
                        kb.op(dve, lambda: V.tensor_tensor(out=w2[:, 0:J], in0=Es, in1=pzr[:, 0:J], op=ALU.mult),
                              reads=[ttab, t_pzr], writes=[t_w2])
                        kb.op(dve, lambda: V.tensor_tensor(out=btim[:], in0=w1[:, 0:J], in1=w2[:, 0:J], op=ALU.subtract),
                              reads=[t_w1, t_w2], writes=[t_bt])
                        icol = 0 if d == 0 else J
                        if si_ == 0:
                            kb.op(dve, lambda: V.memset(gr[:, icol:icol + 1], 0.0), writes=[tg])
                            kb.op(dve, lambda: V.memset(gi_[:, icol:icol + 1], 0.0), writes=[tg])
                        else:
                            ecol = J if d == 0 else 0
                            fcol = (s - 1) if d == 0 else s
                            ge_r, ge_i = gr[:, ecol:ecol + 1], gi_[:, ecol:ecol + 1]
                            fcl = fl[:, fcol:fcol + 1]

                            def ts2(out, a, s1):
                                kb.op(dve, lambda: V.tensor_scalar(out=out, in0=a, scalar1=s1, scalar2=fcl,
                                                                   op0=ALU.mult, op1=ALU.mult),
                                      reads=[tg, ttab, t_fl], writes=[t_ini])

                            ts2(ini[:, 0:1], ge_r, EQr)
                            ts2(ini[:, 1:2], ge_i, EQi)
                            ts2(ini[:, 2:3], ge_r, EQi)
                            ts2(ini[:, 3:4], ge_i, EQr)
                            kb.op(dve, lambda: V.tensor_tensor(out=gr[:, icol:icol + 1], in0=ini[:, 0:1], in1=ini[:, 1:2],
                                                               op=ALU.subtract), reads=[t_ini], writes=[tg])
                            kb.op(dve, lambda: V.tensor_tensor(out=gi_[:, icol:icol + 1], in0=ini[:, 2:3], in1=ini[:, 3:4],
                                                               op=ALU.add), reads=[t_ini], writes=[tg])
                        if d == 0:
                            kb.op(dve, lambda: V.tensor_tensor_scan(out=gr[:, 1:J + 1], data0=RT[d][:], data1=btre[:],
                                                                    initial=gr[:, 0:1], op0=ALU.mult, op1=ALU.add),
                                  reads=[t_RT, t_bt, tg], writes=[tg])
                            kb.op(dve, lambda: V.tensor_tensor_scan(out=gi_[:, 1:J + 1], data0=RT[d][:], data1=btim[:],
                                                                    initial=gi_[:, 0:1], op0=ALU.mult, op1=ALU.add),
                                  reads=[t_RT, t_bt, tg], writes=[tg])
                            Tc, Ts = cT[:, 0:J + 1], sT[:, 0:J + 1]
                            Hre, Him, tH = Hf[0][:, :], Hf[1][:, :], t_Hf
                        else:
                            kb.op(dve, lambda: V.tensor_tensor_scan(out=gr[:, 0:J][:, ::-1], data0=RT[d][:], data1=btre[:, ::-1],
                                                                    initial=gr[:, J:J + 1], op0=ALU.mult, op1=ALU.add),
                                  reads=[t_RT, t_bt, tg], writes=[tg])
                            kb.op(dve, lambda: V.tensor_tensor_scan(out=gi_[:, 0:J][:, ::-1], data0=RT[d][:], data1=btim[:, ::-1],
                                                                    initial=gi_[:, J:J + 1], op0=ALU.mult, op1=ALU.add),
                                  reads=[t_RT, t_bt, tg], writes=[tg])
                            Tc, Ts = cT[:, 0:J + 1][:, ::-1], sT[:, 0:J + 1][:, ::-1]
                            Hre, Him, tH = Hb[0][:, s, :], Hb[1][:, s, :], t_Hb
                        kb.op(dve, lambda: V.tensor_tensor(out=w1[:], in0=Tc, in1=gr[:], op=ALU.mult),
                              reads=[ttab, tg], writes=[t_w1])
                        kb.op(dve, lambda: V.tensor_tensor(out=w2[:], in0=Ts, in1=gi_[:], op=ALU.mult),
                              reads=[ttab, tg], writes=[t_w2])
                        kb.op(dve, lambda: V.tensor_tensor(out=Hre, in0=w1[:], in1=w2[:], op=ALU.subtract),
                              reads=[t_w1, t_w2], writes=[tH])
                        kb.op(dve, lambda: V.tensor_tensor(out=w1[:], in0=Ts, in1=gr[:], op=ALU.mult),
                              reads=[ttab, tg], writes=[t_w1])
                        kb.op(dve, lambda: V.tensor_tensor(out=w2[:], in0=Tc, in1=gi_[:], op=ALU.mult),
                              reads=[ttab, tg], writes=[t_w2])
                        kb.op(dve, lambda: V.tensor_tensor(out=Him, in0=w1[:], in1=w2[:], op=ALU.add),
                              reads=[t_w1, t_w2], writes=[tH])
                        if d == 1:
                            continue
                        for k in range(8):
                            ko = slice(k * J, (k + 1) * J)
                            for k2 in range(8):
                                if k2 == k:
                                    lt = M0[:, :]
                                elif k2 < k:
                                    lt = Mfb[:, 0, k - k2, :]
                                else:
                                    lt = Mfb[:, 1, k2 - k, :]
                                kb.op(pe, lambda lt=lt, k2=k2: PE_.matmul(py[0:32, ko], lhsT=lt, rhs=ubb[:, k2:SEG:8],
                                                                          start=(k2 == 0), stop=False),
                                      reads=[t_M, tub], writes=[t_py], pub=False)
                            kb.op(pe, lambda: PE_.matmul(py[0:32, ko], lhsT=lCre[0][:, k + 1, :], rhs=Hf[0][:, 0:J],
                                                         start=False, stop=False), reads=[t_lC, t_Hf], writes=[t_py], pub=False)
                            kb.op(pe, lambda: PE_.matmul(py[0:32, ko], lhsT=lCimn[0][:, k + 1, :], rhs=Hf[1][:, 0:J],
                                                         start=False, stop=False), reads=[t_lC, t_Hf], writes=[t_py], pub=False)
                            kb.op(pe, lambda: PE_.matmul(py[0:32, ko], lhsT=lCre[1][:, 8 - k, :], rhs=Hb[0][:, s, 1:J + 1],
                                                         start=False, stop=False), reads=[t_lC, t_Hb], writes=[t_py], pub=False)
                            kb.op(pe, lambda: PE_.matmul(py[0:32, ko], lhsT=lCimn[1][:, 8 - k, :], rhs=Hb[1][:, s, 1:J + 1],
                                                         start=False, stop=True), reads=[t_lC, t_Hb], writes=[t_py],
                                  pub=(k == 7))
                        kb.op(dve, lambda: V.scalar_tensor_tensor(
                            out=otb[:].rearrange("p (j k) -> p k j", k=8), in0=ufb[:].rearrange("p (j k) -> p k j", k=8),
                            scalar=dsk[:, ti:ti + 1], in1=py[0:32, 0:8 * J].rearrange("p (k j) -> p k j", k=8),
                            op0=ALU.mult, op1=ALU.add), reads=[tuf, t_p, t_py], writes=[tot])
                        kb.dma(pool, YSd[32 * ti:32 * ti + 32, ss], otb[:], reads=[tot])
            ph.close()

        def even_out_phase(j, src, dst):
            ph = Phase(f"e4_{j}")
            TT = 512 if NTOK % 512 == 0 else 256
            gw = ph.sb("gw", [128, 4, 512], BF16); t_gw = Trk("gw")
            wo = ph.sb("wo", [128, 12, 1024], BF16); t_wo = Trk("wo")
            stg = [ph.sb(f"stg{i}", [128, 1024]) for i in range(3)]
            t_stg = [Trk("stg") for _ in range(3)]
            load_weight_rows(ph, gw, t_gw, lambda r: s5_glu_w[j, r * 128:(r + 1) * 128, :], 4, 512, stg, t_stg)
            load_weight_rows(ph, wo, t_wo, lambda r: ev_w_out[j, r * 128:(r + 1) * 128, :], 12, 1024, stg, t_stg)
            gb_ = ph.sb("glub", [128, 4]); t_gb = Trk("glub")
            kb.dma(sp, gb_[:], s5_glu_b[j].rearrange("(c p) -> p c", p=128), writes=[t_gb])
            ys = [ph.sb(f"ys{i}", [128, 4, TT]) for i in range(2)]; t_ysb = [Trk("ys") for _ in range(2)]
            ya = [ph.sb(f"ya{i}", [128, 8, TT], BF16) for i in range(2)]; t_ya = [Trk("ya") for _ in range(2)]
            gf = ph.sb("gf", [128, 4, TT]); t_gf = Trk("gf")
            gbf = ph.sb("gbf", [128, 4, TT], BF16); t_gbf = Trk("gbf")
            sgm = [ph.sb(f"sgm{i}", [128, TT]) for i in range(2)]; t_sgm = [Trk("sgm") for _ in range(2)]
            ymb = ph.sb("ymb", [128, 4, TT], BF16); t_ymb = Trk("ymb")
            xres = [ph.sb(f"xres{i}", [128, TT]) for i in range(4)]; t_xres = [Trk("xres") for _ in range(4)]
            ores = [ph.sb(f"ores{i}", [128, TT]) for i in range(4)]; t_ores = [Trk("ores") for _ in range(4)]
            pg = [ph.ps(f"pg{i}", [128, 512]) for i in range(2)]; t_pg = [Trk("pg") for _ in range(2)]
            po = [ph.ps(f"po{i}", [128, 512]) for i in range(2)]; t_po = [Trk("po") for _ in range(2)]
            srcc = chunked(dram_ap[src]); dstc = chunked(dram_ap[dst])
            ysc = chunked(YSd); yac = chunked(YMd)
            ri = 0
            for i in range(NTOK // TT):
                ts = slice(i * TT, (i + 1) * TT)
                ysb, tys = ys[i % 2], t_ysb[i % 2]
                yab, tya = ya[i % 2], t_ya[i % 2]
                kb.dma(sp, ysb[:], ysc[:, :, ts], writes=[tys])
                kb.dma(sp, yab[:], yac[:, :, ts], writes=[tya])
                for k in range(4):
                    kb.op(act, lambda k=k: S.activation(out=gf[:, k, :], in_=ysb[:, k, :], func=AF.Gelu_apprx_tanh),
                          reads=[tys], writes=[t_gf])
                    kb.op(dve, lambda k=k: V.tensor_copy(out=gbf[:, k, :], in_=gf[:, k, :]), reads=[t_gf], writes=[t_gbf])
                for m in range(4):
                    p, tp = pg[m % 2], t_pg[m % 2]
                    for k in range(4):
                        kb.op(pe, lambda k=k: PE_.matmul(p[:, :TT], lhsT=gw[:, k, m * 128:(m + 1) * 128], rhs=gbf[:, k, :],
                                                         start=(k == 0), stop=(k == 3)),
                              reads=[t_gw, t_gbf], writes=[tp], pub=(k == 3))
                    sg_, tsg = sgm[m % 2], t_sgm[m % 2]
                    kb.op(act, lambda: S.activation(out=sg_[:], in_=p[:, :TT], func=AF.Sigmoid, bias=gb_[:, m:m + 1], scale=1.0),
                          reads=[tp, t_gb], writes=[tsg])
                    kb.op(dve, lambda: V.tensor_tensor(out=ymb[:, m, :], in0=gf[:, m, :], in1=sg_[:], op=ALU.mult),
                          reads=[t_gf, tsg], writes=[t_ymb])
                for m in range(8):
                    ms = slice(m * 128, (m + 1) * 128)
                    p, tp = po[m % 2], t_po[m % 2]
                    xr, txr = xres[ri % 4], t_xres[ri % 4]
                    orr, tor = ores[ri % 4], t_ores[ri % 4]
                    ri += 1
                    kb.dma(sp, xr[:], srcc[:, m, ts], writes=[txr])
                    for k in range(12):
                        rhs = yab[:, k, :] if k < 8 else ymb[:, k - 8, :]
                        kb.op(pe, lambda k=k, rhs=rhs: PE_.matmul(p[:, :TT], lhsT=wo[:, k, ms], rhs=rhs,
                                                                  start=(k == 0), stop=(k == 11)),
                              reads=[t_wo, tya, t_ymb], writes=[tp], pub=(k == 11))
                    kb.op(dve, lambda: V.tensor_tensor(out=orr[:], in0=p[:, :TT], in1=xr[:], op=ALU.add),
                          reads=[tp, txr], writes=[tor])
                    kb.dma(pool, dstc[:, m, ts], orr[:], reads=[tor])
            ph.close()

        def odd_in_phase(j, layer, src):
            ph = Phase(f"o1_{layer}")
            TT = 512 if NTOK % 512 == 0 else 256
            gam, t_gam = load_gamma(ph, mix_norm[layer])
            NO = 5184
            win = ph.sb("win", [128, 8, NO], BF16)
            t_win = Trk("win")
            stg = [ph.sb(f"stg{i}", [128, NO // 3]) for i in range(3)]
            t_stg = [Trk(f"stg{i}") for i in range(3)]
            load_weight_rows(ph, win, t_win, lambda r: od_w_in[j, r * 128:(r + 1) * 128, :], 8, NO, stg, t_stg)
            t_par = Trk("par")
            dtb = ph.sb("dtb", [64, 1])
            kb.dma(sp, dtb[:], ssd_dt_bias[j].rearrange("d (r o) -> (d r) o", o=1), writes=[t_par], st=t_par)
            acol = ph.sb("acol", [64, 1])
            kb.dma(sp, acol[:], ssd_a_log[j].rearrange("d (r o) -> (d r) o", o=1), writes=[t_par], st=t_par)
            kb.op(act, lambda: S.activation(out=acol[:], in_=acol[:], func=AF.Exp), reads=[t_par], writes=[t_par])
            kb.op(dve, lambda: V.tensor_scalar(out=acol[:], in0=acol[:], scalar1=-1.0, scalar2=None, op0=ALU.mult),
                  reads=[t_par], writes=[t_par])
            mk = ph.sb("mk", [64, TT])
            kb.op(dve, lambda: V.memset(mk[:], 1.0), writes=[t_par])
            kb.op(dve, lambda: V.memset(mk[0:32, 0:TT:128], 0.0), writes=[t_par])
            kb.op(dve, lambda: V.memset(mk[32:64, 127:TT:128], 0.0), writes=[t_par])
            nrm = Norm(ph, TT)
            xt = [ph.sb(f"xt{i}", [128, 8, TT]) for i in range(2)]
            t_xt = [Trk("xt") for _ in range(2)]
            xn = [ph.sb(f"xn{i}", [128, 8, TT], BF16) for i in range(2)]
            t_xn = [Trk("xn") for _ in range(2)]
            ev = [ph.sb(f"ev{i}", [128, TT]) for i in range(4)]
            t_ev = [Trk("ev") for _ in range(4)]
            pp = [ph.ps(f"pp{i}", [128, 512]) for i in range(3)]
            t_pp = [Trk("pp") for _ in range(3)]
            pdt = ph.ps("pdt", [64, 512]); t_pdt = Trk("pdt")
            dte = ph.sb("dte", [64, TT]); t_dte = Trk("dte")
            dtv = ph.sb("dtv", [64, TT]); t_dtv = Trk("dtv")
            dta = ph.sb("dta", [64, TT]); t_dta = Trk("dta")
            csb = ph.sb("csb", [64, TT]); t_csb = Trk("csb")
            srcc = chunked(dram_ap[src])
            ei = 0
            NTT = NTOK // TT

            def prep(i):
                kb.dma(sp, xt[i % 2][:], srcc[:, :, i * TT:(i + 1) * TT], writes=[t_xt[i % 2]])
                nrm.run(xt[i % 2], t_xt[i % 2], gam, t_gam, xn[i % 2], t_xn[i % 2])

            prep(0)
            for i in range(NTT):
                ts = slice(i * TT, (i + 1) * TT)
                xnb, txnb = xn[i % 2], t_xn[i % 2]
                for oc in range(40):
                    if oc == 24 and i + 1 < NTT:
                        prep(i + 1)
                    p, tp = pp[oc % 3], t_pp[oc % 3]
                    for k in range(8):
                        kb.op(pe, lambda k=k: PE_.matmul(p[:, :TT], lhsT=win[:, k, oc * 128:(oc + 1) * 128],
                                                         rhs=xnb[:, k, :], start=(k == 0), stop=(k == 7)),
                              reads=[t_win, txnb], writes=[tp], pub=(k == 7))
                    e, te = ev[ei % 4], t_ev[ei % 4]
                    ei += 1
                    if oc < 16:
                        kb.op(act, lambda: S.activation(out=e[:], in_=p[:, :TT], func=AF.Silu), reads=[tp], writes=[te])
                        dst = ZSd[oc * 128:(oc + 1) * 128, ts]
                    else:
                        kb.op(dve, lambda: V.tensor_copy(out=e[:], in_=p[:, :TT]), reads=[tp], writes=[te])
                        dst = XBCd[(oc - 16) * 128:(oc - 15) * 128, ts]
                    kb.dma(pool, dst, e[:], reads=[te])
                for k in range(8):
                    kb.op(pe, lambda k=k: PE_.matmul(pdt[:, :TT], lhsT=win[:, k, 5120:5184], rhs=xnb[:, k, :],
                                                     start=(k == 0), stop=(k == 7)),
                          reads=[t_win, txnb], writes=[t_pdt], pub=(k == 7))
                kb.op(act, lambda: S.activation(out=dte[:], in_=pdt[:, :TT], func=AF.Exp, bias=dtb[:], scale=1.0),
                      reads=[t_pdt, t_par], writes=[t_dte])
                kb.op(act, lambda: S.activation(out=dtv[:], in_=dte[:], func=AF.Ln, bias=1.0, scale=1.0),
                      reads=[t_dte], writes=[t_dtv])
                kb.op(dve, lambda: V.tensor_scalar(out=dta[:], in0=dtv[:], scalar1=acol[:], scalar2=None, op0=ALU.mult),
                      reads=[t_dtv, t_par], writes=[t_dta])
                kb.op(dve, lambda: V.tensor_tensor_scan(out=csb[0:32, :], data0=mk[0:32, :], data1=dta[0:32, :],
                                                        initial=0.0, op0=ALU.mult, op1=ALU.add),
                      reads=[t_dta, t_par], writes=[t_csb])
                kb.op(dve, lambda: V.tensor_tensor_scan(out=csb[32:64, ::-1], data0=mk[32:64, ::-1], data1=dta[32:64, ::-1],
                                                        initial=0.0, op0=ALU.mult, op1=ALU.add),
                      reads=[t_dta, t_par], writes=[t_csb])
                for d in range(2):
                    kb.dma(pool, DCd[d * 64:d * 64 + 32, ts], csb[d * 32:(d + 1) * 32, :], reads=[t_csb], st=t_csb)
                    kb.dma(pool, DCd[d * 64 + 32:d * 64 + 64, ts], dtv[d * 32:(d + 1) * 32, :], reads=[t_dtv], st=t_dtv)
            ph.close()

        def odd_conv_phase(j):
            ph = Phase(f"o2_{j}")
            t_par = Trk("par")
            cw = ph.sb("cw", [128, 24, 4])
            for k in range(4):
                kb.dma(sp, cw[:, :, k], ssd_conv_w[j, k].rearrange("(c p) -> p c", p=128), writes=[t_par], st=t_par)
            cb = ph.sb("cb", [128, 24])
            kb.dma(sp, cb[:], ssd_conv_b[j].rearrange("(c p) -> p c", p=128), writes=[t_par], st=t_par)
            idb = ph.sb("idb", [128, 128], BF16)
            kb.op(dve, lambda: V.tensor_copy(out=idb[:], in_=ident[:]), writes=[t_par])
            xp = [ph.sb(f"xp{i}", [128, SEG + 3]) for i in range(2)]
            t_xp = [Trk("xp") for _ in range(2)]
            xc2 = [ph.sb(f"xc{i}", [128, SEG]) for i in range(2)]; t_xc2 = [Trk("xc") for _ in range(2)]
            xs_ = [ph.sb(f"xs{i}", [128, SEG]) for i in range(2)]; t_xs = [Trk("xs") for _ in range(2)]
            xb = [ph.sb(f"xb{i}", [128, SEG], BF16) for i in range(2)]; t_xb = [Trk("xb") for _ in range(2)]
            tr = [ph.sb(f"tr{i}", [128, 4, 128], BF16) for i in range(2)]; t_tr = [Trk("tr") for _ in range(2)]
            ptr = [ph.ps(f"ptr{i}", [128, 512], BF16) for i in range(2)]; t_ptr = [Trk("ptr") for _ in range(2)]
            tc_ = [0]
            NB = SEG // 128

            def sel_(cnt):
                return (xp[cnt % 2], t_xp[cnt % 2], xs_[cnt % 2], t_xs[cnt % 2], xb[cnt % 2], t_xb[cnt % 2],
                        xc2[cnt % 2], t_xc2[cnt % 2])

            def stA(cnt, cc, s):
                xpb, txp, xsb, txs, xbb, txb, xc, t_xc = sel_(cnt)
                rows = XBCd[cc * 128:(cc + 1) * 128, :]
                load_halo(xpb, txp, rows, s)
                conv4(xc, t_xc, xpb, txp, cw[:, cc, :], cb[:, cc:cc + 1], t_par)

            def stB(cnt, cc, s):
                xpb, txp, xsb, txs, xbb, txb, xc, t_xc = sel_(cnt)
                ss = slice(s * SEG, (s + 1) * SEG)
                if cc < 16:
                    kb.op(act, lambda: S.activation(out=xsb[:], in_=xc[:], func=AF.Silu), reads=[t_xc], writes=[txs])
                    kb.dma(pool, XSfd[cc * 128:(cc + 1) * 128, ss], xsb[:], reads=[txs], st=txs)
                kb.op(act, lambda: S.activation(out=xbb[:], in_=xc[:], func=AF.Silu), reads=[t_xc], writes=[txb])
                if 16 <= cc < 20:
                    kb.dma(pool, BFd[cc - 16, :, ss], xbb[:], reads=[txb], st=txb)
                elif cc >= 20:
                    kb.dma(pool, CFd[cc - 20, :, ss], xbb[:], reads=[txb], st=txb)
                if cc < 20:
                    for b0 in range(0, NB, 4):
                        pt_, tpt = ptr[tc_[0] % 2], t_ptr[tc_[0] % 2]
                        trb, ttr = tr[tc_[0] % 2], t_tr[tc_[0] % 2]
                        tc_[0] += 1
                        nb = min(4, NB - b0)
                        for q in range(nb):
                            kb.op(pe, lambda q=q: PE_.transpose(out=pt_[:, q * 128:(q + 1) * 128],
                                                                in_=xbb[:, (b0 + q) * 128:(b0 + q + 1) * 128],
                                                                identity=idb[:]),
                                  reads=[txb, t_par], writes=[tpt], pub=(q == nb - 1))
                        kb.op(dve, lambda: V.tensor_copy(out=trb[:, 0:nb, :],
                                                         in_=pt_[:, 0:nb * 128].rearrange("p (q c) -> p q c", q=nb)),
                              reads=[tpt], writes=[ttr])
                        t0 = s * SEG + b0 * 128
                        if cc < 16:
                            dst = XStd[t0:t0 + nb * 128, cc * 128:(cc + 1) * 128]
                        else:
                            dst = BTd[t0:t0 + nb * 128, (cc - 16) * 128:(cc - 15) * 128]
                        kb.dma(pool, dst.rearrange("(b p) c -> p b c", p=128), trb[:, 0:nb, :], reads=[ttr], st=ttr)

            units = [(u, u // NSEG, u % NSEG) for u in range(24 * NSEG)]
            for u in range(len(units) + 1):
                if u < len(units):
                    stA(*units[u])
                if u >= 1:
                    stB(*units[u - 1])
            ph.close()

        def ssd_core_phase(j):
            ph = Phase(f"o3_{j}")
            t_c = Trk("c")
            sel = ph.sb("sel", [32, 32, 128])
            kb.op(dve, lambda: V.tensor_copy(out=sel[:], in_=ident[0:32, 0:32].unsqueeze(2).to_broadcast([32, 32, 128])),
                  writes=[t_c])
            tri = ph.sb("tri", [128, 2, 128])
            kb.dma(sp, tri[:], tri_d.rearrange("p (a b) -> p a b", a=2), writes=[t_c], st=t_c)
            Sst = ph.sb("Sst", [128, 32, 64]); t_S = Trk("S")
            Sbf = ph.sb("Sbf", [128, 32, 64], BF16); t_Sb = Trk("Sb")
            xst = [ph.sb(f"xst{i}", [128, 2048], BF16) for i in range(2)]; t_xst = [Trk("xst") for _ in range(2)]
            Bfm = [ph.sb(f"Bfm{i}", [128, 4, 128], BF16) for i in range(2)]; t_Bfm = [Trk("Bfm") for _ in range(2)]
            Cfm = [ph.sb(f"Cfm{i}", [128, 4, 128], BF16) for i in range(2)]; t_Cfm = [Trk("Cfm") for _ in range(2)]
            Btk = [ph.sb(f"Btk{i}", [128, 512], BF16) for i in range(2)]; t_Btk = [Trk("Btk") for _ in range(2)]
            dc = [ph.sb(f"dc{i}", [64, 128]) for i in range(2)]; t_dc = [Trk("dc") for _ in range(2)]
            tok = ph.sb("tok", [128, 64]); t_tok = Trk("tok")
            scm = [ph.sb(f"scm{i}", [128, 128]) for i in range(2)]; t_scm = [Trk("scm") for _ in range(2)]
            Eg = [ph.sb(f"Eg{i}", [128, 8, 128]) for i in range(2)]; t_Eg = [Trk("Eg") for _ in range(2)]
            Xg = [ph.sb(f"Xg{i}", [128, 8, 128]) for i in range(2)]; t_Xg = [Trk("Xg") for _ in range(2)]
            Gg = [ph.sb(f"Gg{i}", [128, 8, 128], BF16) for i in range(2)]; t_Gg = [Trk("Gg") for _ in range(2)]
            CEg = [ph.sb(f"CEg{i}", [128, 8, 128], BF16) for i in range(2)]; t_CEg = [Trk("CEg") for _ in range(2)]
            xdt = [ph.sb(f"xdt{i}", [128, 8, 64], BF16) for i in range(2)]; t_xdt = [Trk("xdt") for _ in range(2)]
            wdt = [ph.sb(f"wdt{i}", [128, 8]) for i in range(2)]; t_wdt = [Trk("wdt") for _ in range(2)]
            wx = [ph.sb(f"wx{i}", [128, 8, 64], BF16) for i in range(2)]; t_wx = [Trk("wx") for _ in range(2)]
            yo = [ph.sb(f"yo{i}", [64, 8, 128]) for i in range(2)]; t_yo = [Trk("yo") for _ in range(2)]
            pTs = ph.ps("pTs", [128, 512]); t_pT = Trk("pT")
            pT = pTs[:, 0:128]
            psc_ = [pTs[:, 128:256], pTs[:, 256:384]]
            t_psc = t_pT
            pcs2 = [ph.ps(f"pcs{i}", [128, 1024]) for i in range(2)]; t_pcs2 = [Trk("pcs") for _ in range(2)]
            lnd = ph.sb("lnd", [128, 32]); csm = ph.sb("csm", [128, 32])
            py = ph.ps("py", [64, 1024]); t_py = Trk("py")
            pst = ph.ps("pst", [128, 512]); t_pst = Trk("pst")
            NCH = NTOK // 128
            CPS = SEG // 128
            tok2 = [tok, ph.sb("tok1", [128, 64])]; t_tok2 = [t_tok, Trk("tok1")]
            lnd2 = [lnd, ph.sb("lnd1", [128, 32])]; csm2 = [csm, ph.sb("csm1", [128, 32])]

            class Tsk:
                pass

            def prologue(t):
                li = t.chunk_idx
                t.xb_, t.txb = xst[li % 2], t_xst[li % 2]
                t.bf_, t.tbf = Bfm[li % 2], t_Bfm[li % 2]
                t.cf_, t.tcf = Cfm[li % 2], t_Cfm[li % 2]
                t.bt_, t.tbt = Btk[li % 2], t_Btk[li % 2]
                t.dc_, t.tdc = dc[li % 2], t_dc[li % 2]
                t.tok, t.ttok = tok2[li % 2], t_tok2[li % 2]
                t.lnd, t.csm = lnd2[li % 2], csm2[li % 2]
                tsl = t.tsl
                kb.dma(sp, t.xb_[:], XStd[tsl, :], writes=[t.txb])
                kb.dma(sp, t.bf_[:], BFd[:, :, tsl].rearrange("g n s -> n g s"), writes=[t.tbf])
                kb.dma(sp, t.cf_[:], CFd[:, :, tsl].rearrange("g n s -> n g s"), writes=[t.tcf])
                kb.dma(sp, t.bt_[:], BTd[tsl, :], writes=[t.tbt])
                kb.dma(sp, t.dc_[:], DCd[t.d * 64:(t.d + 1) * 64, tsl], writes=[t.tdc])
                kb.op(pe, lambda: PE_.transpose(out=pT[:, 0:64], in_=t.dc_[:, :], identity=ident[0:64, 0:64]),
                      reads=[t.tdc, t_id], writes=[t_pT])
                kb.op(act, lambda: S.copy(out=t.tok[:], in_=pT[:, 0:64]), reads=[t_pT], writes=[t.ttok])
                kb.op(act, lambda: S.activation(out=t.lnd[:], in_=t.tok[:, 32:64], func=AF.Ln), reads=[t.ttok], writes=[t.ttok])
                kb.op(dve, lambda: V.tensor_tensor(out=t.csm[:], in0=t.tok[:, 0:32], in1=t.lnd[:], op=ALU.subtract),
                      reads=[t.ttok], writes=[t.ttok])

            def s1(t):
                g, b2 = t.g, t.b2
                pcs, t_pcs = pcs2[b2], t_pcs2[b2]
                for r in range(8):
                    kb.op(pe, lambda r=r: PE_.matmul(pcs[:, r * 128:(r + 1) * 128], lhsT=sel[:, 8 * g + r, :],
                                                     rhs=t.dc_[0:32, :], start=True, stop=True),
                          reads=[t_c, t.tdc], writes=[t_pcs], pub=(r == 7))
                kb.op(pe, lambda: PE_.matmul(psc_[b2], lhsT=t.bf_[:, g, :], rhs=t.cf_[:, g, :], start=True, stop=True),
                      reads=[t.tbf, t.tcf], writes=[t_psc])

            def s2(t):
                g, b2, d = t.g, t.b2, t.d
                hs = slice(8 * g, 8 * g + 8)
                pcs, t_pcs = pcs2[b2], t_pcs2[b2]
                pcs3 = pcs[:, :].rearrange("p (r l) -> p r l", r=8)
                kb.op(dve, lambda: V.tensor_tensor(out=scm[b2][:], in0=psc_[b2], in1=tri[:, d, :], op=ALU.mult),
                      reads=[t_psc, t_c], writes=[t_scm[b2]])
                kb.op(dve, lambda: V.tensor_tensor(out=Eg[b2][:], in0=pcs3,
                                                   in1=t.csm[:, hs].unsqueeze(2).to_broadcast([128, 8, 128]),
                                                   op=ALU.subtract),
                      reads=[t_pcs, t.ttok], writes=[t_Eg[b2]])
                kb.op(act, lambda: S.activation(out=Eg[b2][:], in_=Eg[b2][:], func=AF.Exp),
                      reads=[t_Eg[b2]], writes=[t_Eg[b2]])
                kb.op(act, lambda: S.activation(out=Xg[b2][:], in_=pcs3, func=AF.Exp),
                      reads=[t_pcs], writes=[t_Xg[b2]])

            def s3(t):
                g, b2, d = t.g, t.b2, t.d
                hs = slice(8 * g, 8 * g + 8)
                lend = 127 if d == 0 else 0
                if t.reset is not None and g == 0:
                    if t.reset == "zero":
                        kb.op(dve, lambda: V.memset(Sst[:], 0.0), writes=[t_S])
                    else:
                        fc = t.reset
                        kb.op(dve, lambda: V.tensor_scalar(out=Sst[:], in0=Sst[:], scalar1=fl[:, fc:fc + 1], scalar2=None,
                                                           op0=ALU.mult), reads=[t_S, t_fl], writes=[t_S])
                    kb.op(act, lambda: S.copy(out=Sbf[:], in_=Sst[:]), reads=[t_S], writes=[t_Sb])
                kb.op(dve, lambda: V.scalar_tensor_tensor(out=Gg[b2][:], in0=Eg[b2][:], scalar=1.0e30,
                                                          in1=scm[b2][:].unsqueeze(1).to_broadcast([128, 8, 128]),
                                                          op0=ALU.min, op1=ALU.mult),
                      reads=[t_Eg[b2], t_scm[b2]], writes=[t_Gg[b2]])
                kb.op(pool, lambda: G_.tensor_tensor(out=CEg[b2][:], in0=Xg[b2][:],
                                                     in1=t.cf_[:, g, :].unsqueeze(1).to_broadcast([128, 8, 128]),
                                                     op=ALU.mult),
                      reads=[t_Xg[b2], t.tcf], writes=[t_CEg[b2]])
                xg = t.xb_[:, g * 512:(g + 1) * 512].rearrange("p (r q) -> p r q", r=8)
                kb.op(pool, lambda: G_.tensor_tensor(out=wx[b2][:], in0=xg,
                                                     in1=Eg[b2][:, :, lend].unsqueeze(2).to_broadcast([128, 8, 64]),
                                                     op=ALU.mult),
                      reads=[t.txb, t_Eg[b2]], writes=[t_wx[b2]])
                for r in range(8):
                    kb.op(pe, lambda r=r: PE_.matmul(py[:, r * 128:(r + 1) * 128], lhsT=xg[:, r, :],
                                                     rhs=Gg[b2][:, r, :], start=True, stop=False),
                          reads=[t.txb, t_Gg[b2]], writes=[t_py], pub=False)
                    kb.op(pe, lambda r=r: PE_.matmul(py[:, r * 128:(r + 1) * 128], lhsT=Sbf[:, 8 * g + r, :],
                                                     rhs=CEg[b2][:, r, :], start=False, stop=True),
                          reads=[t_Sb, t_CEg[b2]], writes=[t_py], pub=(r == 7))
                kb.op(act, lambda: S.copy(out=yo[b2][:], in_=py[:, :].rearrange("p (r l) -> p r l", r=8)),
                      reads=[t_py], writes=[t_yo[b2]])
                kb.dma(pool, Yd[d, g * 512:(g + 1) * 512, t.tsl].rearrange("(r p) l -> p r l", p=64), yo[b2][:],
                       reads=[t_yo[b2]], st=t_yo[b2])
                kb.op(pe, lambda: PE_.matmul(pst[:, :], lhsT=t.bt_[:, g * 128:(g + 1) * 128],
                                             rhs=wx[b2][:].rearrange("p r q -> p (r q)"), start=True, stop=True),
                      reads=[t.tbt, t_wx[b2]], writes=[t_pst])
                Sg = Sst[:, hs, :]
                kb.op(dve, lambda: V.tensor_tensor(out=Sg, in0=Sg,
                                                   in1=Xg[b2][:, :, lend].unsqueeze(2).to_broadcast([128, 8, 64]),
                                                   op=ALU.mult),
                      reads=[t_S, t_Xg[b2], t_py], writes=[t_S])
                kb.op(dve, lambda: V.tensor_tensor(out=Sg, in0=Sg, in1=pst[:, :].rearrange("p (r q) -> p r q", r=8),
                                                   op=ALU.add),
                      reads=[t_S, t_pst], writes=[t_S])
                kb.op(act, lambda: S.copy(out=Sbf[:, hs, :], in_=Sg), reads=[t_S, t_py], writes=[t_Sb])

            tasks = []
            chunk_idx = 0
            for d in (1, 0):
                order = list(range(NCH))[::-1] if d == 1 else list(range(NCH))
                for ci in order:
                    t0 = ci * 128
                    s = t0 // SEG
                    at_bound = (ci % CPS == 0) if d == 0 else (ci % CPS == CPS - 1)
                    reset = None
                    if at_bound:
                        first = (s == 0) if d == 0 else (s == NSEG - 1)
                        reset = "zero" if first else ((s - 1) if d == 0 else s)
                    proto = None
                    for g in range(4):
                        t = Tsk()
                        t.d, t.ci, t.g, t.tsl, t.reset = d, ci, g, slice(t0, t0 + 128), reset
                        t.chunk_idx = chunk_idx
                        t.b2 = len(tasks) % 2
                        t.proto = proto
                        if g == 0:
                            proto = t
                            t.proto = None
                        tasks.append(t)
                    chunk_idx += 1
            n = len(tasks)
            for i in range(n + 2):
                if i < n:
                    t = tasks[i]
                    if t.proto is None:
                        prologue(t)
                    else:
                        for a in ("xb_", "txb", "bf_", "tbf", "cf_", "tcf", "bt_", "tbt", "dc_", "tdc", "tok", "ttok", "lnd", "csm"):
                            setattr(t, a, getattr(t.proto, a))
                    s1(t)
                if 1 <= i <= n:
                    s2(tasks[i - 1])
                if 2 <= i <= n + 1:
                    s3(tasks[i - 2])
            ph.close()

        def odd_out_phase(j, src, dst):
            ph = Phase(f"o4_{j}")
            TT = 512 if NTOK % 512 == 0 else 256
            wo = ph.sb("wo", [128, 16, 1024], BF16); t_wo = Trk("wo")
            stg = [ph.sb(f"stg{i}", [128, 1024]) for i in range(3)]
            t_stg = [Trk("stg") for _ in range(3)]
            load_weight_rows(ph, wo, t_wo, lambda r: od_w_out[j, r * 128:(r + 1) * 128, :], 16, 1024, stg, t_stg)
            t_par = Trk("par")
            dcol = ph.sb("dcol", [128, 16])
            for hh in range(2):
                kb.dma(sp, dcol[hh * 64:(hh + 1) * 64, :],
                       ssd_d[j].rearrange("(c h) -> h c", h=2)[hh].partition_broadcast(64), writes=[t_par], st=t_par)
            nw = ph.sb("nw", [128, 16])
            kb.dma(sp, nw[:], ssd_norm[j].rearrange("(c p) -> p c", p=128), writes=[t_par], st=t_par)
            yf = [ph.sb(f"yf{i}", [128, 4, TT]) for i in range(2)]; t_yf = [Trk("yf") for _ in range(2)]
            ybk = [ph.sb(f"ybk{i}", [128, 4, TT]) for i in range(2)]; t_ybk = [Trk("ybk") for _ in range(2)]
            xsg = [ph.sb(f"xsg{i}", [128, 4, TT]) for i in range(2)]; t_xsg = [Trk("xsg") for _ in range(2)]
            zsg = [ph.sb(f"zsg{i}", [128, 4, TT]) for i in range(2)]; t_zsg = [Trk("zsg") for _ in range(2)]
            sqb = ph.sb("sqb", [128, 4, TT], BF16); t_sqb = Trk("sqb")
            rs = ph.sb("rs", [128, TT]); t_rs = Trk("rs")
            rs2 = ph.sb("rs2", [128, TT]); t_rs2 = Trk("rs2")
            ynb = ph.sb("ynb", [128, 16, TT], BF16); t_ynb = Trk("ynb")
            xres = [ph.sb(f"xres{i}", [128, TT]) for i in range(4)]; t_xres = [Trk("xres") for _ in range(4)]
            ores = [ph.sb(f"ores{i}", [128, TT]) for i in range(4)]; t_ores = [Trk("ores") for _ in range(4)]
            pn = ph.ps("pn", [128, 512]); t_pn = Trk("pn")
            po = [ph.ps(f"po{i}", [128, 512]) for i in range(2)]; t_po = [Trk("po") for _ in range(2)]
            srcc = chunked(dram_ap[src]); dstc = chunked(dram_ap[dst])
            Yfc = chunked(Yd[0]); Ybc = chunked(Yd[1]); XSc = chunked(XSfd); ZSc = chunked(ZSd)
            ri = 0
            gi = 0
            for i in range(NTOK // TT):
                ts = slice(i * TT, (i + 1) * TT)
                for g in range(4):
                    b2 = gi % 2
                    gi += 1
                    cs4 = slice(4 * g, 4 * g + 4)
                    kb.dma(sp, yf[b2][:], Yfc[:, cs4, ts], writes=[t_yf[b2]])
                    kb.dma(sp, ybk[b2][:], Ybc[:, cs4, ts], writes=[t_ybk[b2]])
                    kb.dma(sp, xsg[b2][:], XSc[:, cs4, ts], writes=[t_xsg[b2]])
                    kb.dma(sp, zsg[b2][:], ZSc[:, cs4, ts], writes=[t_zsg[b2]])
                    kb.op(dve, lambda: V.tensor_tensor(out=yf[b2][:], in0=yf[b2][:], in1=ybk[b2][:], op=ALU.add),
                          reads=[t_yf[b2], t_ybk[b2]], writes=[t_yf[b2]])
                    for c in range(4):
                        ch = 4 * g + c
                        kb.op(dve, lambda c=c, ch=ch: V.scalar_tensor_tensor(out=yf[b2][:, c, :], in0=xsg[b2][:, c, :],
                                                                             scalar=dcol[:, ch:ch + 1], in1=yf[b2][:, c, :],
                                                                             op0=ALU.mult, op1=ALU.add),
                              reads=[t_xsg[b2], t_par, t_yf[b2]], writes=[t_yf[b2]])
                    kb.op(dve, lambda: V.tensor_tensor(out=yf[b2][:], in0=yf[b2][:], in1=zsg[b2][:], op=ALU.mult),
                          reads=[t_yf[b2], t_zsg[b2]], writes=[t_yf[b2]])
                    kb.op(act, lambda: S.activation(out=sqb[:], in_=yf[b2][:], func=AF.Square),
                          reads=[t_yf[b2]], writes=[t_sqb])
                    for c in range(4):
                        kb.op(pe, lambda c=c: PE_.matmul(pn[:, :TT], lhsT=ones_bf[:], rhs=sqb[:, c, :],
                                                         start=(c == 0), stop=(c == 3)),
                              reads=[t_const, t_sqb], writes=[t_pn], pub=(c == 3))
                    kb.op(act, lambda: S.activation(out=rs[:], in_=pn[:, :TT], func=AF.Sqrt, scale=1.0 / 512.0, bias=eps_t[:]),
                          reads=[t_pn, t_const], writes=[t_rs])
                    kb.op(dve, lambda: V.reciprocal(out=rs2[:], in_=rs[:]), reads=[t_rs], writes=[t_rs2])
                    for c in range(4):
                        ch = 4 * g + c
                        kb.op(dve, lambda c=c, ch=ch: V.scalar_tensor_tensor(out=ynb[:, ch, :], in0=yf[b2][:, c, :],
                                                                             scalar=nw[:, ch:ch + 1], in1=rs2[:],
                                                                             op0=ALU.mult, op1=ALU.mult),
                              reads=[t_yf[b2], t_par, t_rs2], writes=[t_ynb])
                for m in range(8):
                    ms = slice(m * 128, (m + 1) * 128)
                    p, tp = po[m % 2], t_po[m % 2]
                    xr, txr = xres[ri % 4], t_xres[ri % 4]
                    orr, tor = ores[ri % 4], t_ores[ri % 4]
                    ri += 1
                    kb.dma(sp, xr[:], srcc[:, m, ts], writes=[txr])
                    for k in range(16):
                        kb.op(pe, lambda k=k: PE_.matmul(p[:, :TT], lhsT=wo[:, k, ms], rhs=ynb[:, k, :],
                                                         start=(k == 0), stop=(k == 15)),
                              reads=[t_wo, t_ynb], writes=[tp], pub=(k == 15))
                    kb.op(dve, lambda: V.tensor_tensor(out=orr[:], in0=p[:, :TT], in1=xr[:], op=ALU.add),
                          reads=[tp, txr], writes=[tor])
                    kb.dma(pool, dstc[:, m, ts], orr[:], reads=[tor])
            ph.close()

        def odd_mixer(j, layer, src, dst):
            odd_in_phase(j, layer, src)
            odd_conv_phase(j)
            ssd_core_phase(j)
            odd_out_phase(j, src, dst)

        cur = "xT"
        nxt = ["XA", "XB"]
        ni = 0
        for layer in layers:
            j = layer // 2
            if inc("ffn1"):
                dst = nxt[ni % 2]; ni += 1
                ffn_phase(layer, 0, cur, dst)
                cur = dst
            if phases is None or any(n in phases for n in ("mixer", "e1", "e2", "e3", "e4", "o1", "o2", "o3", "o4")):
                dst = nxt[ni % 2]; ni += 1
                def sub(n):
                    return phases is None or "mixer" in phases or n in phases
                if layer % 2 == 0:
                    if sub("e1"): even_in_phase(j, layer, cur)
                    if sub("e2"): lru_phase(j)
                    if sub("e3"): s5_phase(j)
                    if sub("e4"): even_out_phase(j, cur, dst)
                else:
                    if sub("o1"): odd_in_phase(j, layer, cur)
                    if sub("o2"): odd_conv_phase(j)
                    if sub("o3"): ssd_core_phase(j)
                    if sub("o4"): odd_out_phase(j, cur, dst)
                cur = dst
            if inc("ffn2"):
                dst = nxt[ni % 2]; ni += 1
                ffn_phase(layer, 1, cur, dst)
                cur = dst
        final_phase(cur)
        kb.finish()
    return nc


N_CORES = 8
NON_WEIGHT = ("x_prompt", "x_sample")


def core_inputs(inputs):
    m = {}
    for k, v in inputs.items():
        if k in NON_WEIGHT:
            continue
        m[k] = np.ascontiguousarray(np.asarray(v, dtype=np.float32))
    m["ident"] = np.eye(128, dtype=np.float32)
    u = np.triu(np.ones((128, 128), np.float32))
    m["tri"] = np.ascontiguousarray(np.concatenate([u, u.T], axis=1))
    return m


def kernel(**inputs):
    xp = np.asarray(inputs["x_prompt"], dtype=np.float32)
    xs = np.asarray(inputs["x_sample"], dtype=np.float32)
    SEG = 2048
    NTOK = 12288
    streams = []
    flags = []
    for c in range(N_CORES):
        if c < 4:
            parts = [xp[c], xs[2 * c], xs[2 * c + 1]]
            fl = [1, 1, 1, 0, 0]
        else:
            b = 8 + 6 * (c - 4)
            parts = [xs[b + q] for q in range(6)]
            fl = [0, 0, 0, 0, 0]
        tok = np.concatenate(parts, axis=0)
        streams.append(np.ascontiguousarray(tok.T))
        f = np.zeros((128, 8), np.float32)
        f[:, :5] = np.asarray(fl, np.float32)[None, :]
        flags.append(f)
    nc = build_program(NTOK, SEG, layers=[0, 1, 2, 3])
    wmap = core_inputs(inputs)
    in_maps = []
    for c in range(N_CORES):
        m = dict(wmap)
        m["xT"] = streams[c]
        m["flags"] = flags[c]
        in_maps.append(m)
    res = run_bass_kernel_spmd(nc, in_maps, core_ids=list(range(N_CORES)))
    outs = [np.ascontiguousarray(res.results[c]["yT"].T) for c in range(N_CORES)]
    y_prompt = np.stack([outs[c][:8192] for c in range(4)], axis=0)
    ys = [None] * 32
    for c in range(4):
        ys[2 * c] = outs[c][8192:8192 + 2048]
        ys[2 * c + 1] = outs[c][8192 + 2048:]
    for c in range(4, 8):
        b = 8 + 6 * (c - 4)
        for q in range(6):
            ys[b + q] = outs[c][q * 2048:(q + 1) * 2048]
    y_sample = np.stack(ys, axis=0)
    return (y_prompt.astype(np.float32), y_sample.astype(np.float32))
```

```python
import contextlib
import numpy as np
import concourse.bass as bass
import concourse.mybir as mybir
from concourse.bass_utils import run_bass_kernel_spmd

F32 = mybir.dt.float32
BF16 = mybir.dt.bfloat16
ALU = mybir.AluOpType
AF = mybir.ActivationFunctionType

D = 1024
DFF = 2816
NFC = DFF // 128
EPS = 1e-6


class Trk:
    __slots__ = ("w", "r", "dsem", "dcnt", "name", "dq")

    def __init__(self, name=""):
        self.w = {}
        self.r = {}
        self.dsem = None
        self.dcnt = 0
        self.name = name


class Eng:
    def __init__(self, name, eng, sem):
        self.name = name
        self.eng = eng
        self.sem = sem
        self.cnt = 0
        self.waited = {}


class KB:
    def __init__(self, nc, es):
        self.nc = nc
        self.es = es
        self.nsem = 0
        self.pe = Eng("pe", nc.tensor, self.sem("pe"))
        self.dve = Eng("dve", nc.vector, self.sem("dve"))
        self.act = Eng("act", nc.scalar, self.sem("act"))
        self.pool = Eng("pool", nc.gpsimd, self.sem("pool"))
        self.sp = Eng("sp", nc.sync, self.sem("sp"))
        self.out_tokens = []
        self.dtrks = []
        self.free_dsems = {}

    def sem(self, name):
        self.nsem += 1
        s = self.es.enter_context(self.nc.semaphore(f"s{self.nsem}_{name}"))
        return s

    def sb(self, name, shape, dt):
        return self.es.enter_context(self.nc.sbuf_tensor(name, shape, dt))

    def ps(self, name, shape, dt=F32):
        return self.es.enter_context(self.nc.psum_tensor(name, shape, dt))

    def _waits(self, E, reads, writes):
        need = {}
        for t in reads:
            for s, v in t.w.items():
                if need.get(s, 0) < v:
                    need[s] = v
        for t in writes:
            for s, v in t.w.items():
                if s is E.sem:
                    continue
                if need.get(s, 0) < v:
                    need[s] = v
            for s, v in t.r.items():
                if s is E.sem:
                    continue
                if need.get(s, 0) < v:
                    need[s] = v
        for s, v in need.items():
            if s is E.sem and v > E.cnt:
                continue
            if E.waited.get(id(s), 0) < v:
                E.eng.wait_ge(s, v)
                E.waited[id(s)] = v

    def op(self, E, make, reads=(), writes=(), pub=True):
        self._waits(E, reads, writes)
        inst = make()
        tokv = E.cnt + 1
        if pub:
            inst.then_inc(E.sem, 1)
            E.cnt += 1
        for t in reads:
            if t.r.get(E.sem, 0) < tokv:
                t.r[E.sem] = tokv
        for t in writes:
            t.w = {E.sem: tokv}
            t.r = {}
        return inst

    def dma(self, Q, out, in_, reads=(), writes=(), st=None, is_out=False):
        if st is None:
            st = writes[0] if writes else reads[0]
        if st.dsem is None:
            fl_ = self.free_dsems.setdefault(Q.name, [])
            if fl_:
                st.dsem, st.dcnt = fl_.pop()
            else:
                st.dsem = self.sem("d" + Q.name)
            st.dq = Q.name
            self.dtrks.append(st)
        assert st.dq == Q.name, "one DMA queue per tracker semaphore"
        self._waits(Q, reads, writes)
        inst = Q.eng.dma_start(out=out, in_=in_)
        inst.then_inc(st.dsem, 16)
        st.dcnt += 16
        for t in reads:
            if t.r.get(st.dsem, 0) < st.dcnt:
                t.r[st.dsem] = st.dcnt
        for t in writes:
            t.w = {st.dsem: st.dcnt}
            t.r = {}
        if is_out:
            self.out_tokens.append((st.dsem, st.dcnt))
        return inst

    def barrier(self):
        engs = [self.pe, self.dve, self.act, self.pool, self.sp]
        for E in engs:
            for O in engs:
                if O is E or O.cnt == 0:
                    continue
                if E.waited.get(id(O.sem), 0) < O.cnt:
                    E.eng.wait_ge(O.sem, O.cnt)
                    E.waited[id(O.sem)] = O.cnt
            for t in self.dtrks:
                if t.dcnt and E.waited.get(id(t.dsem), 0) < t.dcnt:
                    E.eng.wait_ge(t.dsem, t.dcnt)
                    E.waited[id(t.dsem)] = t.dcnt

    def release_dsems(self):
        for t in self.dtrks:
            self.free_dsems.setdefault(t.dq, []).append((t.dsem, t.dcnt))
            t.dsem = None
            t.w = {}
            t.r = {}
        self.dtrks = []

    def finish(self):
        need = {}
        for s, v in self.out_tokens:
            if need.get(id(s), (None, 0))[1] < v:
                need[id(s)] = (s, v)
        for s, v in need.values():
            self.sp.eng.wait_ge(s, v)


def build_program(NTOK, SEG, layers, T=256, phases=None):
    nc = bass.Bass("TRN2", target_bir_lowering=False)
    NSEG = NTOK // SEG
    PW = min(512, SEG)
    NPQ = SEG // PW

    def inc(name):
        return phases is None or name in phases

    def din(name, shape):
        return nc.dram_tensor(name, list(shape), F32, kind="ExternalInput").ap()

    def dscr(name, shape, dt=F32):
        return nc.dram_tensor(name, list(shape), dt, kind="Internal").ap()

    xT = din("xT", [D, NTOK])
    flags_d = din("flags", [128, 8])
    ident_d = din("ident", [128, 128])
    ffn_norm = [din("ffn1_norm", [4, D]), din("ffn2_norm", [4, D])]
    ffn_wg = [din("ffn1_w_gate", [4, D, DFF]), din("ffn2_w_gate", [4, D, DFF])]
    ffn_wu = [din("ffn1_w_up", [4, D, DFF]), din("ffn2_w_up", [4, D, DFF])]
    ffn_wd = [din("ffn1_w_down", [4, DFF, D]), din("ffn2_w_down", [4, DFF, D])]
    mix_norm = din("mix_norm", [4, D])
    final_norm = din("final_norm", [D])
    ev_w_in = din("ev_w_in", [2, D, 2560])
    lru_conv_w = din("lru_conv_w", [2, 4, 1024])
    lru_conv_b = din("lru_conv_b", [2, 1024])
    lru_w_a = din("lru_w_a", [2, 2, 16, 64, 64])
    lru_b_a = din("lru_b_a", [2, 2, 1024])
    lru_w_x = din("lru_w_x", [2, 2, 16, 64, 64])
    lru_b_x = din("lru_b_x", [2, 2, 1024])
    lru_lam = din("lru_lam", [2, 2, 1024])
    s5_lam_re = din("s5_lam_re", [2, 2, 32, 64])
    s5_lam_im = din("s5_lam_im", [2, 2, 32, 64])
    s5_log_dt = din("s5_log_dt", [2, 2, 32])
    s5_b_re = din("s5_b_re", [2, 2, 32, 64, 16])
    s5_b_im = din("s5_b_im", [2, 2, 32, 64, 16])
    s5_c_re = din("s5_c_re", [2, 2, 32, 16, 64])
    s5_c_im = din("s5_c_im", [2, 2, 32, 16, 64])
    s5_d = din("s5_d", [2, 512])
    s5_glu_w = din("s5_glu_w", [2, 512, 512])
    s5_glu_b = din("s5_glu_b", [2, 512])
    ev_w_out = din("ev_w_out", [2, 1536, 1024])
    od_w_in = din("od_w_in", [2, D, 5184])
    ssd_conv_w = din("ssd_conv_w", [2, 4, 3072])
    ssd_conv_b = din("ssd_conv_b", [2, 3072])
    ssd_dt_bias = din("ssd_dt_bias", [2, 2, 32])
    ssd_a_log = din("ssd_a_log", [2, 2, 32])
    ssd_d = din("ssd_d", [2, 32])
    ssd_norm = din("ssd_norm", [2, 2048])
    od_w_out = din("od_w_out", [2, 2048, 1024])
    tri_d = din("tri", [128, 256])
    yT = nc.dram_tensor("yT", [D, NTOK], F32, kind="ExternalOutput").ap()
    XA = dscr("XA", [D, NTOK])
    XB = dscr("XB", [D, NTOK])
    Gd = dscr("Gd", [1024, NTOK])
    XRd = dscr("XRd", [1024, NTOK])
    Ud = dscr("Ud", [512, NTOK])
    YMd = dscr("YMd", [1024, NTOK], BF16)
    YSd = dscr("YSd", [512, NTOK])
    ZSd = dscr("ZSd", [2048, NTOK])
    XBCd = dscr("XBCd", [3072, NTOK])
    DCd = dscr("DCd", [128, NTOK])
    XSfd = dscr("XSfd", [2048, NTOK])
    XStd = dscr("XStd", [NTOK, 2048], BF16)
    BFd = dscr("BFd", [4, 128, NTOK], BF16)
    CFd = dscr("CFd", [4, 128, NTOK], BF16)
    BTd = dscr("BTd", [NTOK, 512], BF16)
    Yd = dscr("Yd", [2, 2048, NTOK])
    dram_ap = {"xT": xT, "XA": XA, "XB": XB, "yT": yT}

    with contextlib.ExitStack() as es:
        es.enter_context(nc.allow_non_contiguous_dma(reason="small parameter loads"))
        kb = KB(nc, es)
        pe, dve, act, pool, sp = kb.pe, kb.dve, kb.act, kb.pool, kb.sp
        V, S, G_, PE_ = nc.vector, nc.scalar, nc.gpsimd, nc.tensor

        ones_bf = kb.sb("ones_bf", [128, 128], BF16)
        t_const = Trk("const")
        kb.op(dve, lambda: V.memset(ones_bf[:], 1.0), writes=[t_const])
        eps_t = kb.sb("eps_t", [128, 1], F32)
        kb.op(dve, lambda: V.memset(eps_t[:], EPS), writes=[t_const])
        one_t = kb.sb("one_t", [128, 1], F32)
        kb.op(dve, lambda: V.memset(one_t[:], 1.0), writes=[t_const])
        hpi_t = kb.sb("hpi_t", [128, 1], F32)
        kb.op(dve, lambda: V.memset(hpi_t[:], float(np.pi / 2)), writes=[t_const])
        fl = kb.sb("fl_sb", [128, 8], F32)
        t_fl = Trk("fl")
        kb.dma(sp, fl[:], flags_d[:, :], writes=[t_fl])
        ident = kb.sb("ident_sb", [128, 128], F32)
        t_id = Trk("ident")
        kb.dma(sp, ident[:], ident_d[:, :], writes=[t_id])
        kb.barrier()
        kb.release_dsems()

        def chunked(ap):
            return ap.rearrange("(c p) t -> p c t", p=128)

        class Phase:
            def __init__(self, tag):
                self.tag = tag
                self.pes = contextlib.ExitStack()
                self.n = 0

            def sb(self, name, shape, dt=F32):
                self.n += 1
                return self.pes.enter_context(nc.sbuf_tensor(f"{self.tag}_{name}_{self.n}", list(shape), dt))

            def ps(self, name, shape, dt=F32):
                self.n += 1
                return self.pes.enter_context(nc.psum_tensor(f"{self.tag}_{name}_{self.n}", list(shape), dt))

            def close(self):
                kb.barrier()
                kb.release_dsems()
                self.pes.close()

        cast_rr = [0]

        def cast(out, in_, reads, writes, engs=None):
            engs = engs or [pool, dve, act]
            E = engs[cast_rr[0] % len(engs)]
            cast_rr[0] += 1
            if E is act:
                kb.op(act, lambda: S.copy(out=out, in_=in_), reads=reads, writes=writes)
            elif E is dve:
                kb.op(dve, lambda: V.tensor_copy(out=out, in_=in_), reads=reads, writes=writes)
            else:
                kb.op(pool, lambda: G_.tensor_copy(out=out, in_=in_), reads=reads, writes=writes)

        def load_weight_rows(ph, wdst, t_w, src_rows_fn, nrows, ncols, stg, t_stg):
            W = stg[0].shape[1]
            si = 0
            for r in range(nrows):
                src = src_rows_fn(r)
                for c0 in range(0, ncols, W):
                    c1 = min(ncols, c0 + W)
                    b = si % len(stg)
                    si += 1
                    kb.dma(sp, stg[b][:, :c1 - c0], src[:, c0:c1], writes=[t_stg[b]])
                    cast(wdst[:, r, c0:c1], stg[b][:, :c1 - c0], [t_stg[b]], [t_w])

        class Norm:
            def __init__(self, ph, TT):
                self.TT = TT
                self.sq = ph.sb("sq", [128, 8, TT], BF16)
                self.t_sq = Trk("sq")
                self.rs = ph.sb("rs", [128, TT])
                self.t_rs = Trk("rs")
                self.rs2 = ph.sb("rs2", [128, TT])
                self.t_rs2 = Trk("rs2")
                self.p_n = ph.ps("p_n", [128, 512])
                self.t_pn = Trk("pn")

            def run(self, xt, t_xt, gam, t_gam, out, t_out):
                self.part1(xt, t_xt)
                self.part2(xt, t_xt, gam, t_gam, out, t_out)

            def part1(self, xt, t_xt):
                sq = self.sq
                for k in range(8):
                    kb.op(act, lambda k=k: S.activation(out=sq[:, k, :], in_=xt[:, k, :], func=AF.Square),
                          reads=[t_xt], writes=[self.t_sq])

            def part2(self, xt, t_xt, gam, t_gam, out, t_out):
                TT = self.TT
                sq, rs, rs2, p_n = self.sq, self.rs, self.rs2, self.p_n
                for k in range(8):
                    kb.op(pe, lambda k=k: PE_.matmul(p_n[:, :TT], lhsT=ones_bf[:], rhs=sq[:, k, :],
                                                     start=(k == 0), stop=(k == 7)),
                          reads=[t_const, self.t_sq], writes=[self.t_pn], pub=(k == 7))
                kb.op(act, lambda: S.activation(out=rs[:], in_=p_n[:, :TT], func=AF.Sqrt,
                                                scale=1.0 / D, bias=eps_t[:]),
                      reads=[self.t_pn, t_const], writes=[self.t_rs])
                kb.op(dve, lambda: V.reciprocal(out=rs2[:], in_=rs[:]), reads=[self.t_rs], writes=[self.t_rs2])
                for k in range(8):
                    kb.op(dve, lambda k=k: V.scalar_tensor_tensor(
                        out=out[:, k, :], in0=xt[:, k, :], scalar=gam[:, k:k + 1], in1=rs2[:],
                        op0=ALU.mult, op1=ALU.mult), reads=[t_xt, t_gam, self.t_rs2], writes=[t_out])

        def load_gamma(ph, src_vec):
            gam = ph.sb("gam", [128, 8])
            t_gam = Trk("gam")
            kb.dma(sp, gam[:], src_vec.rearrange("(c p) -> p c", p=128), writes=[t_gam])
            return gam, t_gam

        def ffn_phase(layer, which, src, dst):
            ph = Phase(f"f{layer}{which}")
            NT = NTOK // T
            gam, t_gam = load_gamma(ph, ffn_norm[which][layer])
            wg = ph.sb("wg", [128, 8, DFF], BF16)
            wu = ph.sb("wu", [128, 8, DFF], BF16)
            wd = ph.sb("wd", [128, NFC, D], BF16)
            t_wg, t_wu, t_wd = Trk("wg"), Trk("wu"), Trk("wd")
            HW = DFF // 2
            stg = [ph.sb(f"stg{i}", [128, HW]) for i in range(3)]
            t_stg = [Trk(f"stg{i}") for i in range(3)]
            load_weight_rows(ph, wg, t_wg, lambda r: ffn_wg[which][layer, r * 128:(r + 1) * 128, :], 8, DFF, stg, t_stg)
            load_weight_rows(ph, wu, t_wu, lambda r: ffn_wu[which][layer, r * 128:(r + 1) * 128, :], 8, DFF, stg, t_stg)
            load_weight_rows(ph, wd, t_wd, lambda r: ffn_wd[which][layer, r * 128:(r + 1) * 128, :], NFC, D, stg, t_stg)
            nrm = Norm(ph, T)
            xt = ph.sb("xt", [128, 8, T])
            t_xt = Trk("xt")
            xn = [ph.sb(f"xn{i}", [128, 8, T], BF16) for i in range(2)]
            t_xn = [Trk(f"xn{i}") for i in range(2)]
            h = ph.sb("h", [128, NFC, T], BF16)
            t_h = Trk("h")
            sg = [ph.sb(f"sg{i}", [128, T]) for i in range(2)]
            t_sg = [Trk(f"sg{i}") for i in range(2)]
            xres = [ph.sb(f"xres{i}", [128, T]) for i in range(4)]
            t_xres = [Trk(f"xres{i}") for i in range(4)]
            ores = [ph.sb(f"ores{i}", [128, T]) for i in range(4)]
            t_ores = [Trk(f"ores{i}") for i in range(4)]
            p_g = [ph.ps(f"p_g{i}", [128, 512]) for i in range(2)]
            t_pg = [Trk(f"pg{i}") for i in range(2)]
            p_u = [ph.ps(f"p_u{i}", [128, 512]) for i in range(2)]
            t_pu = [Trk(f"pu{i}") for i in range(2)]
            p_d = [ph.ps(f"p_d{i}", [128, 512]) for i in range(2)]
            t_pd = [Trk(f"pd{i}") for i in range(2)]
            srcc = chunked(dram_ap[src])
            dstc = chunked(dram_ap[dst])
            ri = 0
            kb.dma(sp, xt[:], srcc[:, :, 0:T], writes=[t_xt])
            nrm.run(xt, t_xt, gam, t_gam, xn[0], t_xn[0])
            for i in range(NT):
                ts = slice(i * T, (i + 1) * T)
                xb, txb = xn[i % 2], t_xn[i % 2]
                for j in range(NFC):
                    js = slice(j * 128, (j + 1) * 128)
                    pg, tpg = p_g[j % 2], t_pg[j % 2]
                    pu, tpu = p_u[j % 2], t_pu[j % 2]
                    for k in range(8):
                        kb.op(pe, lambda k=k: PE_.matmul(pg[:, :T], lhsT=wg[:, k, js], rhs=xb[:, k, :],
                                                         start=(k == 0), stop=(k == 7)),
                              reads=[t_wg, txb], writes=[tpg], pub=(k == 7))
                    for k in range(8):
                        kb.op(pe, lambda k=k: PE_.matmul(pu[:, :T], lhsT=wu[:, k, js], rhs=xb[:, k, :],
                                                         start=(k == 0), stop=(k == 7)),
                              reads=[t_wu, txb], writes=[tpu], pub=(k == 7))
                    sgb, tsg = sg[j % 2], t_sg[j % 2]
                    kb.op(act, lambda: S.activation(out=sgb[:], in_=pg[:, :T], func=AF.Silu),
                          reads=[tpg], writes=[tsg])
                    kb.op(dve, lambda: V.tensor_tensor(out=h[:, j, :], in0=sgb[:], in1=pu[:, :T], op=ALU.mult),
                          reads=[tsg, tpu], writes=[t_h])
                if i + 1 < NT:
                    kb.dma(sp, xt[:], srcc[:, :, (i + 1) * T:(i + 2) * T], writes=[t_xt])
                    nrm.part1(xt, t_xt)
                for m in range(8):
                    ms = slice(m * 128, (m + 1) * 128)
                    pd, tpd = p_d[m % 2], t_pd[m % 2]
                    xr, txr = xres[ri % 4], t_xres[ri % 4]
                    orr, tor = ores[ri % 4], t_ores[ri % 4]
                    ri += 1
                    if m == 4 and i + 1 < NT:
                        nrm.part2(xt, t_xt, gam, t_gam, xn[(i + 1) % 2], t_xn[(i + 1) % 2])
                    kb.dma(sp, xr[:], srcc[:, m, ts], writes=[txr])
                    for j in range(NFC):
                        kb.op(pe, lambda j=j: PE_.matmul(pd[:, :T], lhsT=wd[:, j, ms], rhs=h[:, j, :],
                                                         start=(j == 0), stop=(j == NFC - 1)),
                              reads=[t_wd, t_h], writes=[tpd], pub=(j == NFC - 1))
                    kb.op(dve, lambda: V.scalar_tensor_tensor(
                        out=orr[:], in0=pd[:, :T], scalar=0.5, in1=xr[:], op0=ALU.mult, op1=ALU.add),
                        reads=[tpd, txr], writes=[tor])
                    kb.dma(pool, dstc[:, m, ts], orr[:], reads=[tor])
            ph.close()

        def final_phase(src):
            ph = Phase("fin")
            gam, t_gam = load_gamma(ph, final_norm)
            TF = 512 if NTOK % 512 == 0 else T
            nrm = Norm(ph, TF)
            xt = [ph.sb(f"fxt{i}", [128, 8, TF]) for i in range(2)]
            t_xt = [Trk("fxt") for _ in range(2)]
            ot = [ph.sb(f"fot{i}", [128, 8, TF]) for i in range(2)]
            t_ot = [Trk("fot") for _ in range(2)]
            srcc = chunked(dram_ap[src])
            dstc = chunked(yT)
            for i in range(NTOK // TF):
                ts = slice(i * TF, (i + 1) * TF)
                xb, txb = xt[i % 2], t_xt[i % 2]
                ob, tob = ot[i % 2], t_ot[i % 2]
                kb.dma(sp, xb[:], srcc[:, :, ts], writes=[txb])
                nrm.run(xb, txb, gam, t_gam, ob, tob)
                kb.dma(pool, dstc[:, :, ts], ob[:], reads=[tob], is_out=True)
            ph.close()

        def even_in_phase(j, layer, src):
            ph = Phase(f"e1_{layer}")
            TT = 512 if NTOK % 512 == 0 else 256
            gam, t_gam = load_gamma(ph, mix_norm[layer])
            NO = 2560
            win = ph.sb("win", [128, 8, NO], BF16)
            t_win = Trk("win")
            stg = [ph.sb(f"stg{i}", [128, NO // 2]) for i in range(3)]
            t_stg = [Trk(f"stg{i}") for i in range(3)]
            load_weight_rows(ph, win, t_win, lambda r: ev_w_in[j, r * 128:(r + 1) * 128, :], 8, NO, stg, t_stg)
            nrm = Norm(ph, TT)
            xt = [ph.sb(f"xt{i}", [128, 8, TT]) for i in range(2)]
            t_xt = [Trk("xt") for _ in range(2)]
            xn = [ph.sb(f"xn{i}", [128, 8, TT], BF16) for i in range(2)]
            t_xn = [Trk("xn") for _ in range(2)]
            ev = [ph.sb(f"ev{i}", [128, TT]) for i in range(4)]
            t_ev = [Trk("ev") for _ in range(4)]
            pp = [ph.ps(f"pp{i}", [128, 512]) for i in range(3)]
            t_pp = [Trk("pp") for _ in range(3)]
            srcc = chunked(dram_ap[src])
            ei = 0
            NTT = NTOK // TT

            def prep(i):
                kb.dma(sp, xt[i % 2][:], srcc[:, :, i * TT:(i + 1) * TT], writes=[t_xt[i % 2]])
                nrm.run(xt[i % 2], t_xt[i % 2], gam, t_gam, xn[i % 2], t_xn[i % 2])

            prep(0)
            for i in range(NTT):
                ts = slice(i * TT, (i + 1) * TT)
                xnb, txnb = xn[i % 2], t_xn[i % 2]
                for oc in range(20):
                    if oc == 10 and i + 1 < NTT:
                        prep(i + 1)
                    p, tp = pp[oc % 3], t_pp[oc % 3]
                    for k in range(8):
                        kb.op(pe, lambda k=k: PE_.matmul(p[:, :TT], lhsT=win[:, k, oc * 128:(oc + 1) * 128],
                                                         rhs=xnb[:, k, :], start=(k == 0), stop=(k == 7)),
                              reads=[t_win, txnb], writes=[tp], pub=(k == 7))
                    e, te = ev[ei % 4], t_ev[ei % 4]
                    ei += 1
                    if oc < 8:
                        kb.op(act, lambda: S.activation(out=e[:], in_=p[:, :TT], func=AF.Gelu_apprx_tanh),
                              reads=[tp], writes=[te])
                        dst = Gd[oc * 128:(oc + 1) * 128, ts]
                    elif oc < 16:
                        kb.op(dve, lambda: V.tensor_copy(out=e[:], in_=p[:, :TT]), reads=[tp], writes=[te])
                        dst = XRd[(oc - 8) * 128:(oc - 7) * 128, ts]
                    else:
                        kb.op(dve, lambda: V.tensor_copy(out=e[:], in_=p[:, :TT]), reads=[tp], writes=[te])
                        dst = Ud[(oc - 16) * 128:(oc - 15) * 128, ts]
                    kb.dma(pool, dst, e[:], reads=[te])
            ph.close()

        def load_halo(xp, t_xp, src_rows, s):
            lo = s * SEG - 2
            hi = s * SEG + SEG + 1
            clo, chi = max(lo, 0), min(hi, NTOK)
            kb.dma(sp, xp[:, clo - lo:chi - lo], src_rows[:, clo:chi], writes=[t_xp])
            if s == 0:
                kb.op(dve, lambda: V.memset(xp[:, 0:2], 0.0), writes=[t_xp])
            else:
                kb.op(dve, lambda: V.tensor_scalar(out=xp[:, 0:2], in0=xp[:, 0:2], scalar1=fl[:, s - 1:s], scalar2=None,
                                                   op0=ALU.mult), reads=[t_xp, t_fl], writes=[t_xp])
            if s == NSEG - 1:
                kb.op(dve, lambda: V.memset(xp[:, SEG + 2:SEG + 3], 0.0), writes=[t_xp])
            else:
                kb.op(dve, lambda: V.tensor_scalar(out=xp[:, SEG + 2:SEG + 3], in0=xp[:, SEG + 2:SEG + 3],
                                                   scalar1=fl[:, s:s + 1], scalar2=None, op0=ALU.mult),
                      reads=[t_xp, t_fl], writes=[t_xp])

        def conv4(xc, t_xc, xp, t_xp, w4, bcol, t_par):
            kb.op(dve, lambda: V.tensor_scalar(out=xc[:], in0=xp[:, 0:SEG], scalar1=w4[:, 0:1], scalar2=bcol,
                                               op0=ALU.mult, op1=ALU.add), reads=[t_xp, t_par], writes=[t_xc])
            for k in range(1, 4):
                kb.op(dve, lambda k=k: V.scalar_tensor_tensor(out=xc[:], in0=xp[:, k:k + SEG], scalar=w4[:, k:k + 1],
                                                              in1=xc[:], op0=ALU.mult, op1=ALU.add),
                      reads=[t_xp, t_par, t_xc], writes=[t_xc])

        def lru_phase(j):
            ph = Phase(f"e2_{j}")
            t_par = Trk("par")
            cw = ph.sb("cw", [128, 8, 4])
            for k in range(4):
                kb.dma(sp, cw[:, :, k], lru_conv_w[j, k].rearrange("(c p) -> p c", p=128), writes=[t_par], st=t_par)
            cb = ph.sb("cb", [128, 8])
            kb.dma(sp, cb[:], lru_conv_b[j].rearrange("(c p) -> p c", p=128), writes=[t_par])
            ba = ph.sb("ba", [128, 2, 8])
            for d in range(2):
                kb.dma(sp, ba[:, d, :], lru_b_a[j, d].rearrange("(c p) -> p c", p=128), writes=[t_par], st=t_par)
            bx = ph.sb("bx", [128, 2, 8])
            for d in range(2):
                kb.dma(sp, bx[:, d, :], lru_b_x[j, d].rearrange("(c p) -> p c", p=128), writes=[t_par], st=t_par)
            lam = ph.sb("lam", [128, 2, 8])
            t_lam = Trk("lam")
            for d in range(2):
                kb.dma(sp, lam[:, d, :], lru_lam[j, d].rearrange("(c p) -> p c", p=128), writes=[t_lam], st=t_lam)
            l1 = ph.sb("l1", [128, 2, 8])
            c8 = ph.sb("c8", [128, 2, 8])
            c16 = ph.sb("c16", [128, 2, 8])
            kb.op(act, lambda: S.activation(out=l1[:], in_=lam[:], func=AF.Exp, scale=-1.0), reads=[t_lam], writes=[t_lam])
            kb.op(act, lambda: S.activation(out=l1[:], in_=l1[:], func=AF.Ln, bias=1.0, scale=1.0),
                  reads=[t_lam], writes=[t_lam])
            kb.op(dve, lambda: V.tensor_scalar(out=c8[:], in0=l1[:], scalar1=-8.0, scalar2=None, op0=ALU.mult),
                  reads=[t_lam], writes=[t_par])
            kb.op(dve, lambda: V.tensor_scalar(out=c16[:], in0=l1[:], scalar1=-16.0, scalar2=None, op0=ALU.mult),
                  reads=[t_lam], writes=[t_par])
            wbd = ph.sb("wbd", [128, 32, 128], BF16)
            t_wbd = Trk("wbd")
            pp_ = Phase(f"e2p_{j}")
            wst = pp_.sb("wst", [128, 32, 128])
            t_wst = Trk("wst")
            kb.op(dve, lambda: V.memset(wst[:], 0.0), writes=[t_wst])

            def widx(g, d, c):
                return (g * 2 + d) * 8 + c

            for g, Wd_ in enumerate((lru_w_a, lru_w_x)):
                for d in range(2):
                    for c in range(8):
                        for hh in range(2):
                            kb.dma(sp, wst[hh * 64:(hh + 1) * 64, widx(g, d, c), hh * 64:(hh + 1) * 64],
                                   Wd_[j, d, 2 * c + hh], writes=[t_wst], st=t_wst)
            kb.op(dve, lambda: V.tensor_copy(out=wbd[:], in_=wst[:]), reads=[t_wst], writes=[t_wbd])
            pp_.close()

            hb = ph.sb("hb", [128, NTOK])
            t_hb = Trk("hb")
            xp = [ph.sb(f"xp{i}", [128, SEG + 3]) for i in range(2)]
            t_xp = [Trk("xp") for _ in range(2)]
            xc2 = [ph.sb(f"xc{i}", [128, SEG]) for i in range(2)]; t_xc2 = [Trk("xc") for _ in range(2)]
            xcb2 = [ph.sb(f"xcb{i}", [128, SEG], BF16) for i in range(2)]; t_xcb2 = [Trk("xcb") for _ in range(2)]
            Rb2 = [ph.sb(f"Rb{i}", [128, SEG]) for i in range(2)]; t_R2 = [Trk("R") for _ in range(2)]
            Ab2 = [ph.sb(f"Ab{i}", [128, SEG]) for i in range(2)]; t_A2 = [Trk("A") for _ in range(2)]
            Sb2 = [ph.sb(f"Sb{i}", [128, SEG]) for i in range(2)]; t_S2 = [Trk("S") for _ in range(2)]
            Ib2 = [ph.sb(f"Ib{i}", [128, SEG]) for i in range(2)]; t_I2 = [Trk("I") for _ in range(2)]
            hf = [ph.sb(f"hf{i}", [128, SEG]) for i in range(2)]
            t_hf = [Trk("hf") for _ in range(2)]
            Gt = ph.sb("Gt", [128, SEG]); t_G = Trk("G")
            yb = [ph.sb(f"yb{i}", [128, SEG], BF16) for i in range(2)]
            t_yb = [Trk("yb") for _ in range(2)]
            ini = ph.sb("ini", [128, 2]); t_ini = Trk("ini")
            pa = ph.ps("pa", [128, SEG]); t_pa = Trk("pa")
            px = ph.ps("px", [128, SEG]); t_px = Trk("px")
            def stA(cnt, c, d, s):
                rows = XRd[c * 128:(c + 1) * 128, :]
                ss = slice(s * SEG, (s + 1) * SEG)
                xpb, txp = xp[cnt % 2], t_xp[cnt % 2]
                xc, t_xc = xc2[cnt % 2], t_xc2[cnt % 2]
                xcb, t_xcb = xcb2[cnt % 2], t_xcb2[cnt % 2]
                Rb, t_R = Rb2[cnt % 2], t_R2[cnt % 2]
                Ab, t_A = Ab2[cnt % 2], t_A2[cnt % 2]
                Sb, t_S = Sb2[cnt % 2], t_S2[cnt % 2]
                Ib, t_I = Ib2[cnt % 2], t_I2[cnt % 2]
                load_halo(xpb, txp, rows, s)
                conv4(xc, t_xc, xpb, txp, cw[:, c, :], cb[:, c:c + 1], t_par)
                kb.op(act, lambda: S.copy(out=xcb[:], in_=xc[:]), reads=[t_xc], writes=[t_xcb])
                for q in range(NPQ):
                    qs = slice(q * PW, (q + 1) * PW)
                    kb.op(pe, lambda qs=qs: PE_.matmul(pa[:, qs], lhsT=wbd[:, widx(0, d, c), :], rhs=xcb[:, qs],
                                                       start=True, stop=True),
                          reads=[t_wbd, t_xcb], writes=[t_pa], pub=(q == NPQ - 1))
                for q in range(NPQ):
                    qs = slice(q * PW, (q + 1) * PW)
                    kb.op(pe, lambda qs=qs: PE_.matmul(px[:, qs], lhsT=wbd[:, widx(1, d, c), :], rhs=xcb[:, qs],
                                                       start=True, stop=True),
                          reads=[t_wbd, t_xcb], writes=[t_px], pub=(q == NPQ - 1))
                kb.op(act, lambda: S.activation(out=Rb[:], in_=pa[:], func=AF.Sigmoid, bias=ba[:, d, c:c + 1],
                                                scale=1.0), reads=[t_pa, t_par], writes=[t_R])
                kb.op(act, lambda: S.activation(out=Ab[:], in_=Rb[:], func=AF.Exp, scale=c8[:, d, c:c + 1]),
                      reads=[t_R, t_par], writes=[t_A])
                kb.op(act, lambda: S.activation(out=Sb[:], in_=Rb[:], func=AF.Exp, scale=c16[:, d, c:c + 1]),
                      reads=[t_R, t_par], writes=[t_S])
                kb.op(act, lambda: S.activation(out=Sb[:], in_=Sb[:], func=AF.Sqrt, scale=-1.0, bias=one_t[:]),
                      reads=[t_S, t_const], writes=[t_S])
                kb.op(act, lambda: S.activation(out=Ib[:], in_=px[:], func=AF.Sigmoid, bias=bx[:, d, c:c + 1],
                                                scale=1.0), reads=[t_px, t_par], writes=[t_I])

            def stB(cnt, c, d, s):
                ss = slice(s * SEG, (s + 1) * SEG)
                xpb, txp = xp[cnt % 2], t_xp[cnt % 2]
                xc, t_xc = xc2[cnt % 2], t_xc2[cnt % 2]
                xcb, t_xcb = xcb2[cnt % 2], t_xcb2[cnt % 2]
                Rb, t_R = Rb2[cnt % 2], t_R2[cnt % 2]
                Ab, t_A = Ab2[cnt % 2], t_A2[cnt % 2]
                Sb, t_S = Sb2[cnt % 2], t_S2[cnt % 2]
                Ib, t_I = Ib2[cnt % 2], t_I2[cnt % 2]
                kb.op(dve, lambda: V.tensor_tensor(out=Ib[:], in0=Ib[:], in1=xc[:], op=ALU.mult),
                      reads=[t_I, t_xc], writes=[t_I])
                kb.op(dve, lambda: V.tensor_tensor(out=Ib[:], in0=Ib[:], in1=Sb[:], op=ALU.mult),
                      reads=[t_I, t_S], writes=[t_I])
                if d == 1:
                    if s == NSEG - 1:
                        init = 0.0
                        rd = []
                    else:
                        kb.op(dve, lambda: V.tensor_scalar(out=ini[:, 0:1], in0=hb[:, (s + 1) * SEG:(s + 1) * SEG + 1],
                                                           scalar1=fl[:, s:s + 1], scalar2=None, op0=ALU.mult),
                              reads=[t_hb, t_fl], writes=[t_ini])
                        init = ini[:, 0:1]
                        rd = [t_ini]
                    kb.op(dve, lambda: V.tensor_tensor_scan(out=hb[:, ss][:, ::-1], data0=Ab[:, ::-1],
                                                            data1=Ib[:, ::-1], initial=init,
                                                            op0=ALU.mult, op1=ALU.add),
                          reads=[t_A, t_I] + rd, writes=[t_hb])
                else:
                    hfb, thf = hf[cnt % 2], t_hf[cnt % 2]
                    hfp, thfp = hf[(cnt + 1) % 2], t_hf[(cnt + 1) % 2]
                    if s == 0:
                        init = 0.0
                        rd = []
                    else:
                        kb.op(dve, lambda: V.tensor_scalar(out=ini[:, 1:2], in0=hfp[:, SEG - 1:SEG],
                                                           scalar1=fl[:, s - 1:s], scalar2=None, op0=ALU.mult),
                              reads=[thfp, t_fl], writes=[t_ini])
                        init = ini[:, 1:2]
                        rd = [t_ini]
                    kb.op(dve, lambda: V.tensor_tensor_scan(out=hfb[:], data0=Ab[:], data1=Ib[:], initial=init,
                                                            op0=ALU.mult, op1=ALU.add),
                          reads=[t_A, t_I] + rd, writes=[thf])
                    kb.dma(sp, Gt[:], Gd[c * 128:(c + 1) * 128, ss], writes=[t_G])
                    kb.op(dve, lambda: V.tensor_tensor(out=Ib[:], in0=hfb[:], in1=hb[:, ss], op=ALU.add),
                          reads=[thf, t_hb], writes=[t_I])
                    ybb, tyb = yb[cnt % 2], t_yb[cnt % 2]
                    kb.op(pool, lambda: G_.tensor_tensor(out=ybb[:], in0=Ib[:], in1=Gt[:], op=ALU.mult),
                          reads=[t_I, t_G], writes=[tyb])
                    kb.dma(pool, YMd[c * 128:(c + 1) * 128, ss], ybb[:], reads=[tyb])

            units = []
            for c in range(8):
                for d in (1, 0):
                    segs = list(range(NSEG))[::-1] if d == 1 else list(range(NSEG))
                    for s in segs:
                        units.append((len(units), c, d, s))
            for i in range(len(units) + 1):
                if i < len(units):
                    stA(*units[i])
                if i >= 1:
                    stB(*units[i - 1])
            ph.close()

        def s5_phase(j):
            ph = Phase(f"e3_{j}")
            t_p = Trk("s5par")
            NC_ = 32
            KB8 = 8
            J = SEG // KB8

            def pt(name, shape=None):
                return ph.sb(name, shape or [128, NC_])

            lre, lim, ldt = pt("lre"), pt("lim"), pt("ldt")
            for d in range(2):
                kb.dma(sp, lre[:, d * 16:(d + 1) * 16],
                       s5_lam_re[j, d].rearrange("(t g) n -> (g n) t", g=2), writes=[t_p], st=t_p)
                kb.dma(sp, lim[:, d * 16:(d + 1) * 16],
                       s5_lam_im[j, d].rearrange("(t g) n -> (g n) t", g=2), writes=[t_p], st=t_p)
                for g in range(2):
                    kb.dma(sp, ldt[g * 64:(g + 1) * 64, d * 16:(d + 1) * 16],
                           s5_log_dt[j, d].rearrange("(t g) -> g t", g=2)[g].partition_broadcast(64), writes=[t_p], st=t_p)

            def vv(out, a, b_, op):
                kb.op(dve, lambda: V.tensor_tensor(out=out, in0=a, in1=b_, op=op), reads=[t_p], writes=[t_p])

            def vs(out, a, s1, op):
                kb.op(dve, lambda: V.tensor_scalar(out=out, in0=a, scalar1=s1, scalar2=None, op0=op),
                      reads=[t_p], writes=[t_p])

            dtt, th, lr, rho = pt("dtt"), pt("th"), pt("lr"), pt("rho")
            kb.op(act, lambda: S.activation(out=dtt[:], in_=ldt[:], func=AF.Exp), reads=[t_p], writes=[t_p])
            vv(th[:], lim[:], dtt[:], ALU.mult)
            vv(lr[:], lre[:], dtt[:], ALU.mult)
            kb.op(act, lambda: S.activation(out=rho[:], in_=lr[:], func=AF.Exp), reads=[t_p], writes=[t_p])
            cs_, sn_ = pt("cs"), pt("sn")
            kb.op(act, lambda: S.activation(out=sn_[:], in_=th[:], func=AF.Sin, scale=1.0 / 32.0), reads=[t_p], writes=[t_p])
            kb.op(act, lambda: S.activation(out=cs_[:], in_=th[:], func=AF.Sin, scale=1.0 / 32.0, bias=hpi_t[:]),
                  reads=[t_p, t_const], writes=[t_p])
            cc, s2, sc = pt("cc"), pt("s2"), pt("sc")

            def double_angle(c_, s_):
                vv(cc[:], c_[:], c_[:], ALU.mult)
                vv(s2[:], s_[:], s_[:], ALU.mult)
                vv(sc[:], s_[:], c_[:], ALU.mult)
                vv(c_[:], cc[:], s2[:], ALU.subtract)
                vs(s_[:], sc[:], 2.0, ALU.mult)

            for _ in range(5):
                double_angle(cs_, sn_)
            c8, s8, rho8 = pt("c8"), pt("s8"), pt("rho8")
            vs(c8[:], cs_[:], 1.0, ALU.mult)
            vs(s8[:], sn_[:], 1.0, ALU.mult)
            for _ in range(3):
                double_angle(c8, s8)
            vv(rho8[:], rho[:], rho[:], ALU.mult)
            vv(rho8[:], rho8[:], rho8[:], ALU.mult)
            vv(rho8[:], rho8[:], rho8[:], ALU.mult)
            abr, abi = pt("abr"), pt("abi")
            vv(abr[:], rho[:], cs_[:], ALU.mult)
            vv(abi[:], rho[:], sn_[:], ALU.mult)
            t1, t2, den, abm1 = pt("t1"), pt("t2"), pt("den"), pt("abm1")
            vv(t1[:], lre[:], lre[:], ALU.mult)
            vv(t2[:], lim[:], lim[:], ALU.mult)
            vv(den[:], t1[:], t2[:], ALU.add)
            kb.op(dve, lambda: V.reciprocal(out=den[:], in_=den[:]), reads=[t_p], writes=[t_p])
            vs(abm1[:], abr[:], -1.0, ALU.add)
            cre, cim = pt("cre"), pt("cim")
            vv(t1[:], abm1[:], lre[:], ALU.mult)
            vv(t2[:], abi[:], lim[:], ALU.mult)
            vv(t1[:], t1[:], t2[:], ALU.add)
            vv(cre[:], t1[:], den[:], ALU.mult)
            vv(t1[:], abi[:], lre[:], ALU.mult)
            vv(t2[:], abm1[:], lim[:], ALU.mult)
            vv(t1[:], t1[:], t2[:], ALU.subtract)
            vv(cim[:], t1[:], den[:], ALU.mult)
            APR = ph.sb("APR", [128, 9, NC_])
            API = ph.sb("API", [128, 9, NC_])
            kb.op(dve, lambda: V.memset(APR[:, 0, :], 1.0), writes=[t_p])
            kb.op(dve, lambda: V.memset(API[:, 0, :], 0.0), writes=[t_p])
            vs(APR[:, 1, :], abr[:], 1.0, ALU.mult)
            vs(API[:, 1, :], abi[:], 1.0, ALU.mult)
            for p in range(2, 9):
                vv(t1[:], APR[:, p - 1, :], abr[:], ALU.mult)
                vv(t2[:], API[:, p - 1, :], abi[:], ALU.mult)
                vv(APR[:, p, :], t1[:], t2[:], ALU.subtract)
                vv(t1[:], APR[:, p - 1, :], abi[:], ALU.mult)
                vv(t2[:], API[:, p - 1, :], abr[:], ALU.mult)
                vv(API[:, p, :], t1[:], t2[:], ALU.add)
            pzr = ph.ps("pzr", [128, 512]); t_pzr = Trk("pzr")
            pzi = ph.ps("pzi", [128, 512]); t_pzi = Trk("pzi")
            py = ph.ps("py", [128, 2048]); t_py = Trk("py")
            pm = ph.ps("pm", [32, 512]); t_pm = Trk("pm")
            bbre = ph.sb("bbre", [128, NC_, 32])
            bbim = ph.sb("bbim", [128, NC_, 32])
            bbimn = ph.sb("bbimn", [128, NC_, 32])
            CTre = ph.sb("CTre", [128, NC_, 32])
            CTim = ph.sb("CTim", [128, NC_, 32])
            dsk = ph.sb("dsk", [32, 16])
            kb.dma(sp, dsk[:], s5_d[j].rearrange("(t c) -> c t", c=32), writes=[t_p], st=t_p)
            pp_ = Phase(f"e3p_{j}")
            bre = pp_.sb("bre", [128, NC_, 32])
            bim = pp_.sb("bim", [128, NC_, 32])
            kb.op(dve, lambda: V.memset(bre[:], 0.0), writes=[t_p])
            kb.op(dve, lambda: V.memset(bim[:], 0.0), writes=[t_p])
            for g in range(2):
                for (dst_, src_) in ((bre, s5_b_re), (bim, s5_b_im)):
                    for d in range(2):
                        kb.dma(sp, dst_[g * 64:(g + 1) * 64, d * 16:(d + 1) * 16, g * 16:(g + 1) * 16],
                               src_[j, d].rearrange("(t g) n c -> g n t c", g=2)[g], writes=[t_p], st=t_p)
            tb = pp_.sb("tb", [128, NC_, 32])

            def bc(col):
                return col[:].unsqueeze(2).to_broadcast([128, NC_, 32])

            vv(bbre[:], bre[:], bc(cre), ALU.mult)
            vv(tb[:], bim[:], bc(cim), ALU.mult)
            vv(bbre[:], bbre[:], tb[:], ALU.subtract)
            vv(bbim[:], bim[:], bc(cre), ALU.mult)
            vv(tb[:], bre[:], bc(cim), ALU.mult)
            vv(bbim[:], bbim[:], tb[:], ALU.add)
            vs(bbimn[:], bbim[:], -1.0, ALU.mult)
            craw = [pp_.sb("crre", [32, NC_, 128]), pp_.sb("crim", [32, NC_, 128])]
            t_c = Trk("craw")
            for ri_, src_ in enumerate((s5_c_re, s5_c_im)):
                kb.op(dve, lambda: V.memset(craw[ri_][:], 0.0), writes=[t_c])
                for g in range(2):
                    for d in range(2):
                        kb.dma(sp, craw[ri_][g * 16:(g + 1) * 16, d * 16:(d + 1) * 16, g * 64:(g + 1) * 64],
                               src_[j, d].rearrange("(t g) c n -> g c t n", g=2)[g], writes=[t_c], st=t_c)
            for ri_, CT in enumerate((CTre, CTim)):
                for q in range(NC_):
                    kb.op(pe, lambda q=q: PE_.transpose(out=py[:, q * 32:(q + 1) * 32], in_=craw[ri_][:, q, :],
                                                        identity=ident[0:32, 0:32]),
                          reads=[t_c, t_id], writes=[t_py], pub=(q == NC_ - 1))
                kb.op(dve, lambda: V.tensor_copy(out=CT[:], in_=py[:, 0:NC_ * 32].rearrange("p (q c) -> p q c", q=NC_)),
                      reads=[t_py], writes=[t_p])
            pp_.close()
            CAre = [ph.sb(f"CAre{d}", [128, 9, 32]) for d in range(2)]
            CAim = [ph.sb(f"CAim{d}", [128, 9, 32]) for d in range(2)]
            tca = ph.sb("tca", [128, 9, 32])
            t_ca = Trk("ca")
            lCre = [ph.sb(f"lCre{d}", [128, 9, 32], BF16) for d in range(2)]
            lCimn = [ph.sb(f"lCimn{d}", [128, 9, 32], BF16) for d in range(2)]
            t_lC = Trk("lC")
            Dgr = ph.sb("Dgr", [128, 8, 128]); Dgi = ph.sb("Dgi", [128, 8, 128]); t_dg = Trk("dg")
            lBzr = [ph.sb(f"lBzr{d}", [32, 8, 128], BF16) for d in range(2)]
            lBzi = [ph.sb(f"lBzi{d}", [32, 8, 128], BF16) for d in range(2)]
            t_lB = Trk("lB")
            Mfb = ph.sb("Mfb", [32, 2, 8, 32], BF16); M0 = ph.sb("M0", [32, 32], BF16); M0f = ph.sb("M0f", [32, 32])
            t_M = Trk("M")
            cosT = ph.sb("cosT", [128, J + 2]); sinT = ph.sb("sinT", [128, J + 2]); t_tab = [Trk("tab0"), Trk("tab1")]
            cosTs = [cosT, ph.sb("cosT1", [128, J + 2])]
            sinTs = [sinT, ph.sb("sinT1", [128, J + 2])]
            RT = [ph.sb(f"RT{d}", [128, J]) for d in range(2)]; t_RT = Trk("RT")
            q1 = ph.sb("q1", [128, J + 2]); q2 = ph.sb("q2", [128, J + 2]); t_q1 = Trk("q1"); t_q2 = Trk("q2")
            uf = [ph.sb(f"uf{i}", [32, SEG]) for i in range(2)]; t_uf = [Trk("uf") for _ in range(2)]
            ub = [ph.sb(f"ub{i}", [32, SEG], BF16) for i in range(2)]; t_ub = [Trk("ub") for _ in range(2)]
            w1 = ph.sb("w1", [128, J + 1]); w2 = ph.sb("w2", [128, J + 1]); t_w1 = Trk("w1"); t_w2 = Trk("w2")
            btre = ph.sb("btre", [128, J]); btim = ph.sb("btim", [128, J]); t_bt = Trk("bt")
            gre = [ph.sb(f"gre{d}", [128, J + 1]) for d in range(2)]
            gim = [ph.sb(f"gim{d}", [128, J + 1]) for d in range(2)]
            t_g = [Trk("g0"), Trk("g1")]
            Hf = [ph.sb("Hfre", [128, J + 1], BF16), ph.sb("Hfim", [128, J + 1], BF16)]; t_Hf = Trk("Hf")
            Hb = [ph.sb("Hbre", [128, NSEG, J + 1], BF16), ph.sb("Hbim", [128, NSEG, J + 1], BF16)]; t_Hb = Trk("Hb")
            ot_ = [ph.sb(f"ot{i}", [32, SEG]) for i in range(2)]; t_ot = [Trk("ot") for _ in range(2)]
            ini = ph.sb("ini", [128, 8]); t_ini = Trk("ini")
            li = 0
            for ti in range(16):
                for d in (1, 0):
                    col = d * 16 + ti
                    cT, sT, ttab = cosTs[d], sinTs[d], t_tab[d]
                    ctr = CTre[:, col, :].unsqueeze(1).to_broadcast([128, 9, 32])
                    cti = CTim[:, col, :].unsqueeze(1).to_broadcast([128, 9, 32])
                    apr = APR[:, :, col].unsqueeze(2).to_broadcast([128, 9, 32])
                    api = API[:, :, col].unsqueeze(2).to_broadcast([128, 9, 32])

                    def ca(out, a, b_, op):
                        kb.op(dve, lambda: V.tensor_tensor(out=out, in0=a, in1=b_, op=op), reads=[t_p, t_ca], writes=[t_ca])

                    ca(CAre[d][:], ctr, apr, ALU.mult)
                    ca(tca[:], cti, api, ALU.mult)
                    ca(CAre[d][:], CAre[d][:], tca[:], ALU.subtract)
                    ca(CAim[d][:], ctr, api, ALU.mult)
                    ca(tca[:], cti, apr, ALU.mult)
                    ca(CAim[d][:], CAim[d][:], tca[:], ALU.add)
                    kb.op(act, lambda: S.copy(out=lCre[d][:], in_=CAre[d][:]), reads=[t_ca], writes=[t_lC])
                    kb.op(act, lambda: S.mul(out=lCimn[d][:], in_=CAim[d][:], mul=-1.0), reads=[t_ca], writes=[t_lC])
                    kb.op(pe, lambda: PE_.matmul(pm[:, 0:256], lhsT=bbre[:, col, :],
                                                 rhs=CAre[d][:, 0:8, :].rearrange("p a c -> p (a c)"), start=True, stop=False),
                          reads=[t_p, t_ca], writes=[t_pm], pub=False)
                    kb.op(pe, lambda: PE_.matmul(pm[:, 0:256], lhsT=bbimn[:, col, :],
                                                 rhs=CAim[d][:, 0:8, :].rearrange("p a c -> p (a c)"), start=False, stop=True),
                          reads=[t_p, t_ca], writes=[t_pm])
                    kb.op(dve, lambda: V.tensor_copy(out=Mfb[:, d, :, :], in_=pm[:, 0:256].rearrange("p (a c) -> p a c", a=8)),
                          reads=[t_pm], writes=[t_M])
                    if d == 1:
                        kb.op(dve, lambda: V.tensor_copy(out=M0f[:], in_=pm[:, 0:32]), reads=[t_pm], writes=[t_M])
                    else:
                        kb.op(dve, lambda: V.tensor_tensor(out=M0[:], in0=M0f[:], in1=pm[:, 0:32], op=ALU.add),
                              reads=[t_pm, t_M], writes=[t_M])
                    idb = ident[:, :].unsqueeze(1).to_broadcast([128, 8, 128])
                    kb.op(dve, lambda: V.tensor_tensor(out=Dgr[:], in0=idb,
                                                       in1=APR[:, 0:8, col].unsqueeze(2).to_broadcast([128, 8, 128]), op=ALU.mult),
                          reads=[t_p, t_id], writes=[t_dg])
                    kb.op(dve, lambda: V.tensor_tensor(out=Dgi[:], in0=idb,
                                                       in1=API[:, 0:8, col].unsqueeze(2).to_broadcast([128, 8, 128]), op=ALU.mult),
                          reads=[t_p, t_id], writes=[t_dg])
                    for hh in range(2):
                        hs_ = slice(hh * 512, (hh + 1) * 512)
                        dgr = Dgr[:].rearrange("p a n -> p (a n)")[:, hs_]
                        dgi = Dgi[:].rearrange("p a n -> p (a n)")[:, hs_]
                        kb.op(pe, lambda: PE_.matmul(py[0:32, hs_], lhsT=bbre[:, col, :], rhs=dgr, start=True, stop=False),
                              reads=[t_p, t_dg], writes=[t_py], pub=False)
                        kb.op(pe, lambda: PE_.matmul(py[0:32, hs_], lhsT=bbimn[:, col, :], rhs=dgi, start=False, stop=True),
                              reads=[t_p, t_dg], writes=[t_py], pub=False)
                        h2 = slice(1024 + hh * 512, 1024 + (hh + 1) * 512)
                        kb.op(pe, lambda: PE_.matmul(py[0:32, h2], lhsT=bbre[:, col, :], rhs=dgi, start=True, stop=False),
                              reads=[t_p, t_dg], writes=[t_py], pub=False)
                        kb.op(pe, lambda: PE_.matmul(py[0:32, h2], lhsT=bbim[:, col, :], rhs=dgr, start=False, stop=True),
                              reads=[t_p, t_dg], writes=[t_py], pub=(hh == 1))
                    kb.op(act, lambda: S.copy(out=lBzr[d][:], in_=py[0:32, 0:1024].rearrange("p (a n) -> p a n", a=8)),
                          reads=[t_py], writes=[t_lB])
                    kb.op(act, lambda: S.copy(out=lBzi[d][:], in_=py[0:32, 1024:2048].rearrange("p (a n) -> p a n", a=8)),
                          reads=[t_py], writes=[t_lB])
                    kb.op(dve, lambda: V.tensor_copy(out=cT[:, 0:1], in_=c8[:, col:col + 1]), reads=[t_p], writes=[ttab])
                    kb.op(dve, lambda: V.tensor_scalar(out=sT[:, 0:1], in0=s8[:, col:col + 1], scalar1=-1.0, scalar2=None,
                                                       op0=ALU.mult), reads=[t_p], writes=[ttab])
                    kb.op(dve, lambda: V.memset(cT[:, 1:2], 1.0), writes=[ttab])
                    kb.op(dve, lambda: V.memset(sT[:, 1:2], 0.0), writes=[ttab])
                    kb.op(dve, lambda: V.tensor_copy(out=cT[:, 2:3], in_=c8[:, col:col + 1]), reads=[t_p], writes=[ttab])
                    kb.op(dve, lambda: V.tensor_copy(out=sT[:, 2:3], in_=s8[:, col:col + 1]), reads=[t_p], writes=[ttab])
                    cV, sV = cT[:, 1:J + 2], sT[:, 1:J + 2]
                    m = 2
                    while m < J + 1:
                        n = min(m - 1, J + 1 - m)
                        cr, ci = cV[:, m - 1:m], sV[:, m - 1:m]
                        kb.op(dve, lambda: V.tensor_scalar(out=q1[:, 0:n], in0=sV[:, 1:1 + n], scalar1=ci, scalar2=None,
                                                           op0=ALU.mult), reads=[ttab], writes=[t_q1])
                        kb.op(dve, lambda: V.tensor_scalar(out=q2[:, 0:n], in0=cV[:, 1:1 + n], scalar1=ci, scalar2=None,
                                                           op0=ALU.mult), reads=[ttab], writes=[t_q2])
                        kb.op(dve, lambda: V.scalar_tensor_tensor(out=cV[:, m:m + n], in0=cV[:, 1:1 + n], scalar=cr,
                                                                  in1=q1[:, 0:n], op0=ALU.mult, op1=ALU.subtract),
                              reads=[ttab, t_q1], writes=[ttab])
                        kb.op(dve, lambda: V.scalar_tensor_tensor(out=sV[:, m:m + n], in0=sV[:, 1:1 + n], scalar=cr,
                                                                  in1=q2[:, 0:n], op0=ALU.mult, op1=ALU.add),
                              reads=[ttab, t_q2], writes=[ttab])
                        m += n
                    kb.op(dve, lambda: V.tensor_copy(out=RT[d][:], in_=rho8[:, col:col + 1].to_broadcast([128, J])),
                          reads=[t_p], writes=[t_RT])
                for d in (1, 0):
                    cT, sT, ttab = cosTs[d], sinTs[d], t_tab[d]
                    cV, sV = cT[:, 1:J + 2], sT[:, 1:J + 2]
                    EQr, EQi = cV[:, J:J + 1], sV[:, J:J + 1]
                    gr, gi_, tg = gre[d], gim[d], t_g[d]
                    segs = list(range(NSEG))[::-1] if d == 1 else list(range(NSEG))
                    for si_, s in enumerate(segs):
                        ss = slice(s * SEG, (s + 1) * SEG)
                        ufb, tuf = uf[li % 2], t_uf[li % 2]
                        ubb, tub = ub[li % 2], t_ub[li % 2]
                        otb, tot = ot_[li % 2], t_ot[li % 2]
                        li += 1
                        kb.dma(sp, ufb[:], Ud[32 * ti:32 * ti + 32, ss], writes=[tuf])
                        kb.op(act, lambda: S.copy(out=ubb[:], in_=ufb[:]), reads=[tuf], writes=[tub])
                        for (pz, tpz, lBz) in ((pzr, t_pzr, lBzr[d]), (pzi, t_pzi, lBzi[d])):
                            for k in range(8):
                                e = (7 - k) if d == 0 else k
                                kb.op(pe, lambda k=k, e=e: PE_.matmul(pz[:, 0:J], lhsT=lBz[:, e, :], rhs=ubb[:, k:SEG:8],
                                                                      start=(k == 0), stop=(k == 7)),
                                      reads=[t_lB, tub], writes=[tpz], pub=(k == 7))
                        if d == 0:
                            Ec, Es = cV[:, 0:J], sV[:, 0:J]
                        else:
                            Ec, Es = cV[:, 0:J][:, ::-1], sV[:, 0:J][:, ::-1]
                        kb.op(dve, lambda: V.tensor_tensor(out=w1[:, 0:J], in0=Ec, in1=pzr[:, 0:J], op=ALU.mult),
                              reads=[ttab, t_pzr], writes=[t_w1])
                        kb.op(dve, lambda: V.tensor_tensor(out=w2[:, 0:J], in0=Es, in1=pzi[:, 0:J], op=ALU.mult),
                              reads=[ttab, t_pzi], writes=[t_w2])
                        kb.op(dve, lambda: V.tensor_tensor(out=btre[:], in0=w1[:, 0:J], in1=w2[:, 0:J], op=ALU.add),
                              reads=[t_w1, t_w2], writes=[t_bt])
                        kb.op(dve, lambda: V.tensor_tensor(out=w1[:, 0:J], in0=Ec, in1=pzi[:, 0:J], op=ALU.mult),
                              reads=[ttab, t_pzi], writes=[t_w1])
                        kb.op(dve, lambda: V.tensor_tensor(out=w2[:, 0:J], in0=Es, in1=pzr[:, 0:J], op=ALU.mult),
                              reads=[ttab, t_pzr], writes=[t_w2])
                        kb.op(dve, lambda: V.tensor_tensor(out=btim[:], in0=w1[:, 0:J], in1=w2[:, 0:J], op=ALU.subtract),
                              reads=[t_w1, t_w2], writes=[t_bt])
                        icol = 0 if d == 0 else J
                        if si_ == 0:
                            kb.op(dve, lambda: V.memset(gr[:, icol:icol + 1], 0.0), writes=[tg])
                            kb.op(dve, lambda: V.memset(gi_[:, icol:icol + 1], 0.0), writes=[tg])
                        else:
                            ecol = J if d == 0 else 0
                            fcol = (s - 1) if d == 0 else s
                            ge_r, ge_i = gr[:, ecol:ecol + 1], gi_[:, ecol:ecol + 1]
                            fcl = fl[:, fcol:fcol + 1]

                            def ts2(out, a, s1):
                                kb.op(dve, lambda: V.tensor_scalar(out=out, in0=a, scalar1=s1, scalar2=fcl,
                                                                   op0=ALU.mult, op1=ALU.mult),
                                      reads=[tg, ttab, t_fl], writes=[t_ini])

                            ts2(ini[:, 0:1], ge_r, EQr)
                            ts2(ini[:, 1:2], ge_i, EQi)
                            ts2(ini[:, 2:3], ge_r, EQi)
                            ts2(ini[:, 3:4], ge_i, EQr)
                            kb.op(dve, lambda: V.tensor_tensor(out=gr[:, icol:icol + 1], in0=ini[:, 0:1], in1=ini[:, 1:2],
                                                               op=ALU.subtract), reads=[t_ini], writes=[tg])
                            kb.op(dve, lambda: V.tensor_tensor(out=gi_[:, icol:icol + 1], in0=ini[:, 2:3], in1=ini[:, 3:4],
                                                               op=ALU.add), reads=[t_ini], writes=[tg])
                        if d == 0:
                            kb.op(dve, lambda: V.tensor_tensor_scan(out=gr[:, 1:J + 1], data0=RT[d][:], data1=btre[:],
                                                                    initial=gr[:, 0:1], op0=ALU.mult, op1=ALU.add),
                                  reads=[t_RT, t_bt, tg], writes=[tg])
                            kb.op(dve, lambda: V.tensor_tensor_scan(out=gi_[:, 1:J + 1], data0=RT[d][:], data1=btim[:],
                                                                    initial=gi_[:, 0:1], op0=ALU.mult, op1=ALU.add),
                                  reads=[t_RT, t_bt, tg], writes=[tg])
                            Tc, Ts = cT[:, 0:J + 1], sT[:, 0:J + 1]
                            Hre, Him, tH = Hf[0][:, :], Hf[1][:, :], t_Hf
                        else:
                            kb.op(dve, lambda: V.tensor_tensor_scan(out=gr[:, 0:J][:, ::-1], data0=RT[d][:], data1=btre[:, ::-1],
                                                                    initial=gr[:, J:J + 1], op0=ALU.mult, op1=ALU.add),
                                  reads=[t_RT, t_bt, tg], writes=[tg])
                            kb.op(dve, lambda: V.tensor_tensor_scan(out=gi_[:, 0:J][:, ::-1], data0=RT[d][:], data1=btim[:, ::-1],
                                                                    initial=gi_[:, J:J + 1], op0=ALU.mult, op1=ALU.add),
                                  reads=[t_RT, t_bt, tg], writes=[tg])
                            Tc, Ts = cT[:, 0:J + 1][:, ::-1], sT[:, 0:J + 1][:, ::-1]
                            Hre, Him, tH = Hb[0][:, s, :], Hb[1][:, s, :], t_Hb
                        kb.op(dve, lambda: V.tensor_tensor(out=w1[:], in0=Tc, in1=gr[:], op=ALU.mult),
                              reads=[ttab, tg], writes=[t_w1])
                        kb.op(dve, lambda: V.tensor_tensor(out=w2[:], in0=Ts, in1=gi_[:], op=ALU.mult),
                              reads=[ttab, tg], writes=[t_w2])
                        kb.op(dve, lambda: V.tensor_tensor(out=Hre, in0=w1[:], in1=w2[:], op=ALU.subtract),
                              reads=[t_w1, t_w2], writes=[tH])
                        kb.op(dve, lambda: V.tensor_tensor(out=w1[:], in0=Ts, in1=gr[:], op=ALU.mult),
                              reads=[ttab, tg], writes=[t_w1])
                        kb.op(dve, lambda: V.tensor_tensor(out=w2[:], in0=Tc, in1=gi_[:], op=ALU.mult),
                              reads=[ttab, tg], writes=[t_w2])
                        kb.op(dve, lambda: V.tensor_tensor(out=Him, in0=w1[:], in1=w2[:], op=ALU.add),
                              reads=[t_w1, t_w2], writes=[tH])
                        if d == 1:
                            continue
                        for k in range(8):
                            ko = slice(k * J, (k + 1) * J)
                            for k2 in range(8):
                                if k2 == k:
                                    lt = M0[:, :]
                                elif k2 < k:
                                    lt = Mfb[:, 0, k - k2, :]
                                else:
                                    lt = Mfb[:, 1, k2 - k, :]
                                kb.op(pe, lambda lt=lt, k2=k2: PE_.matmul(py[0:32, ko], lhsT=lt, rhs=ubb[:, k2:SEG:8],
                                                                          start=(k2 == 0), stop=False),
                                      reads=[t_M, tub], writes=[t_py], pub=False)
                            kb.op(pe, lambda: PE_.matmul(py[0:32, ko], lhsT=lCre[0][:, k + 1, :], rhs=Hf[0][:, 0:J],
                                                         start=False, stop=False), reads=[t_lC, t_Hf], writes=[t_py], pub=False)
                            kb.op(pe, lambda: PE_.matmul(py[0:32, ko], lhsT=lCimn[0][:, k + 1, :], rhs=Hf[1][:, 0:J],
                                                         start=False, stop=False), reads=[t_lC, t_Hf], writes=[t_py], pub=False)
                            kb.op(pe, lambda: PE_.matmul(py[0:32, ko], lhsT=lCre[1][:, 8 - k, :], rhs=Hb[0][:, s, 1:J + 1],
                                                         start=False, stop=False), reads=[t_lC, t_Hb], writes=[t_py], pub=False)
                            kb.op(pe, lambda: PE_.matmul(py[0:32, ko], lhsT=lCimn[1][:, 8 - k, :], rhs=Hb[1][:, s, 1:J + 1],
                                                         start=False, stop=True), reads=[t_lC, t_Hb], writes=[t_py],
                                  pub=(k == 7))
                        kb.op(dve, lambda: V.scalar_tensor_tensor(
                            out=otb[:].rearrange("p (j k) -> p k j", k=8), in0=ufb[:].rearrange("p (j k) -> p k j", k=8),
                            scalar=dsk[:, ti:ti + 1], in1=py[0:32, 0:8 * J].rearrange("p (k j) -> p k j", k=8),
                            op0=ALU.mult, op1=ALU.add), reads=[tuf, t_p, t_py], writes=[tot])
                        kb.dma(pool, YSd[32 * ti:32 * ti + 32, ss], otb[:], reads=[tot])
            ph.close()

        def even_out_phase(j, src, dst):
            ph = Phase(f"e4_{j}")
            TT = 512 if NTOK % 512 == 0 else 256
            gw = ph.sb("gw", [128, 4, 512], BF16); t_gw = Trk("gw")
            wo = ph.sb("wo", [128, 12, 1024], BF16); t_wo = Trk("wo")
            stg = [ph.sb(f"stg{i}", [128, 1024]) for i in range(3)]
            t_stg = [Trk("stg") for _ in range(3)]
            load_weight_rows(ph, gw, t_gw, lambda r: s5_glu_w[j, r * 128:(r + 1) * 128, :], 4, 512, stg, t_stg)
            load_weight_rows(ph, wo, t_wo, lambda r: ev_w_out[j, r * 128:(r + 1) * 128, :], 12, 1024, stg, t_stg)
            gb_ = ph.sb("glub", [128, 4]); t_gb = Trk("glub")
            kb.dma(sp, gb_[:], s5_glu_b[j].rearrange("(c p) -> p c", p=128), writes=[t_gb])
            ys = [ph.sb(f"ys{i}", [128, 4, TT]) for i in range(2)]; t_ysb = [Trk("ys") for _ in range(2)]
            ya = [ph.sb(f"ya{i}", [128, 8, TT], BF16) for i in range(2)]; t_ya = [Trk("ya") for _ in range(2)]
            gf = ph.sb("gf", [128, 4, TT]); t_gf = Trk("gf")
            gbf = ph.sb("gbf", [128, 4, TT], BF16); t_gbf = Trk("gbf")
            sgm = [ph.sb(f"sgm{i}", [128, TT]) for i in range(2)]; t_sgm = [Trk("sgm") for _ in range(2)]
            ymb = ph.sb("ymb", [128, 4, TT], BF16); t_ymb = Trk("ymb")
            xres = [ph.sb(f"xres{i}", [128, TT]) for i in range(4)]; t_xres = [Trk("xres") for _ in range(4)]
            ores = [ph.sb(f"ores{i}", [128, TT]) for i in range(4)]; t_ores = [Trk("ores") for _ in range(4)]
            pg = [ph.ps(f"pg{i}", [128, 512]) for i in range(2)]; t_pg = [Trk("pg") for _ in range(2)]
            po = [ph.ps(f"po{i}", [128, 512]) for i in range(2)]; t_po = [Trk("po") for _ in range(2)]
            srcc = chunked(dram_ap[src]); dstc = chunked(dram_ap[dst])
            ysc = chunked(YSd); yac = chunked(YMd)
            ri = 0
            for i in range(NTOK // TT):
                ts = slice(i * TT, (i + 1) * TT)
                ysb, tys = ys[i % 2], t_ysb[i % 2]
                yab, tya = ya[i % 2], t_ya[i % 2]
                kb.dma(sp, ysb[:], ysc[:, :, ts], writes=[tys])
                kb.dma(sp, yab[:], yac[:, :, ts], writes=[tya])
                for k in range(4):
                    kb.op(act, lambda k=k: S.activation(out=gf[:, k, :], in_=ysb[:, k, :], func=AF.Gelu_apprx_tanh),
                          reads=[tys], writes=[t_gf])
                    kb.op(dve, lambda k=k: V.tensor_copy(out=gbf[:, k, :], in_=gf[:, k, :]), reads=[t_gf], writes=[t_gbf])
                for m in range(4):
                    p, tp = pg[m % 2], t_pg[m % 2]
                    for k in range(4):
                        kb.op(pe, lambda k=k: PE_.matmul(p[:, :TT], lhsT=gw[:, k, m * 128:(m + 1) * 128], rhs=gbf[:, k, :],
                                                         start=(k == 0), stop=(k == 3)),
                              reads=[t_gw, t_gbf], writes=[tp], pub=(k == 3))
                    sg_, tsg = sgm[m % 2], t_sgm[m % 2]
                    kb.op(act, lambda: S.activation(out=sg_[:], in_=p[:, :TT], func=AF.Sigmoid, bias=gb_[:, m:m + 1], scale=1.0),
                          reads=[tp, t_gb], writes=[tsg])
                    kb.op(dve, lambda: V.tensor_tensor(out=ymb[:, m, :], in0=gf[:, m, :], in1=sg_[:], op=ALU.mult),
                          reads=[t_gf, tsg], writes=[t_ymb])
                for m in range(8):
                    ms = slice(m * 128, (m + 1) * 128)
                    p, tp = po[m % 2], t_po[m % 2]
                    xr, txr = xres[ri % 4], t_xres[ri % 4]
                    orr, tor = ores[ri % 4], t_ores[ri % 4]
                    ri += 1
                    kb.dma(sp, xr[:], srcc[:, m, ts], writes=[txr])
                    for k in range(12):
                        rhs = yab[:, k, :] if k < 8 else ymb[:, k - 8, :]
                        kb.op(pe, lambda k=k, rhs=rhs: PE_.matmul(p[:, :TT], lhsT=wo[:, k, ms], rhs=rhs,
                                                                  start=(k == 0), stop=(k == 11)),
                              reads=[t_wo, tya, t_ymb], writes=[tp], pub=(k == 11))
                    kb.op(dve, lambda: V.tensor_tensor(out=orr[:], in0=p[:, :TT], in1=xr[:], op=ALU.add),
                          reads=[tp, txr], writes=[tor])
                    kb.dma(pool, dstc[:, m, ts], orr[:], reads=[tor])
            ph.close()

        def odd_in_phase(j, layer, src):
            ph = Phase(f"o1_{layer}")
            TT = 512 if NTOK % 512 == 0 else 256
            gam, t_gam = load_gamma(ph, mix_norm[layer])
            NO = 5184
            win = ph.sb("win", [128, 8, NO], BF16)
            t_win = Trk("win")
            stg = [ph.sb(f"stg{i}", [128, NO // 3]) for i in range(3)]
            t_stg = [Trk(f"stg{i}") for i in range(3)]
            load_weight_rows(ph, win, t_win, lambda r: od_w_in[j, r * 128:(r + 1) * 128, :], 8, NO, stg, t_stg)
            t_par = Trk("par")
            dtb = ph.sb("dtb", [64, 1])
            kb.dma(sp, dtb[:], ssd_dt_bias[j].rearrange("d (r o) -> (d r) o", o=1), writes=[t_par], st=t_par)
            acol = ph.sb("acol", [64, 1])
            kb.dma(sp, acol[:], ssd_a_log[j].rearrange("d (r o) -> (d r) o", o=1), writes=[t_par], st=t_par)
            kb.op(act, lambda: S.activation(out=acol[:], in_=acol[:], func=AF.Exp), reads=[t_par], writes=[t_par])
            kb.op(dve, lambda: V.tensor_scalar(out=acol[:], in0=acol[:], scalar1=-1.0, scalar2=None, op0=ALU.mult),
                  reads=[t_par], writes=[t_par])
            mk = ph.sb("mk", [64, TT])
            kb.op(dve, lambda: V.memset(mk[:], 1.0), writes=[t_par])
            kb.op(dve, lambda: V.memset(mk[0:32, 0:TT:128], 0.0), writes=[t_par])
            kb.op(dve, lambda: V.memset(mk[32:64, 127:TT:128], 0.0), writes=[t_par])
            nrm = Norm(ph, TT)
            xt = [ph.sb(f"xt{i}", [128, 8, TT]) for i in range(2)]
            t_xt = [Trk("xt") for _ in range(2)]
            xn = [ph.sb(f"xn{i}", [128, 8, TT], BF16) for i in range(2)]
            t_xn = [Trk("xn") for _ in range(2)]
            ev = [ph.sb(f"ev{i}", [128, TT]) for i in range(4)]
            t_ev = [Trk("ev") for _ in range(4)]
            pp = [ph.ps(f"pp{i}", [128, 512]) for i in range(3)]
            t_pp = [Trk("pp") for _ in range(3)]
            pdt = ph.ps("pdt", [64, 512]); t_pdt = Trk("pdt")
            dte = ph.sb("dte", [64, TT]); t_dte = Trk("dte")
            dtv = ph.sb("dtv", [64, TT]); t_dtv = Trk("dtv")
            dta = ph.sb("dta", [64, TT]); t_dta = Trk("dta")
            csb = ph.sb("csb", [64, TT]); t_csb = Trk("csb")
            srcc = chunked(dram_ap[src])
            ei = 0
            NTT = NTOK // TT

            def prep(i):
                kb.dma(sp, xt[i % 2][:], srcc[:, :, i * TT:(i + 1) * TT], writes=[t_xt[i % 2]])
                nrm.run(xt[i % 2], t_xt[i % 2], gam, t_gam, xn[i % 2], t_xn[i % 2])

            prep(0)
            for i in range(NTT):
                ts = slice(i * TT, (i + 1) * TT)
                xnb, txnb = xn[i % 2], t_xn[i % 2]
                for oc in range(40):
                    if oc == 24 and i + 1 < NTT:
                        prep(i + 1)
                    p, tp = pp[oc % 3], t_pp[oc % 3]
                    for k in range(8):
                        kb.op(pe, lambda k=k: PE_.matmul(p[:, :TT], lhsT=win[:, k, oc * 128:(oc + 1) * 128],
                                                         rhs=xnb[:, k, :], start=(k == 0), stop=(k == 7)),
                              reads=[t_win, txnb], writes=[tp], pub=(k == 7))
                    e, te = ev[ei % 4], t_ev[ei % 4]
                    ei += 1
                    if oc < 16:
                        kb.op(act, lambda: S.activation(out=e[:], in_=p[:, :TT], func=AF.Silu), reads=[tp], writes=[te])
                        dst = ZSd[oc * 128:(oc + 1) * 128, ts]
                    else:
                        kb.op(dve, lambda: V.tensor_copy(out=e[:], in_=p[:, :TT]), reads=[tp], writes=[te])
                        dst = XBCd[(oc - 16) * 128:(oc - 15) * 128, ts]
                    kb.dma(pool, dst, e[:], reads=[te])
                for k in range(8):
                    kb.op(pe, lambda k=k: PE_.matmul(pdt[:, :TT], lhsT=win[:, k, 5120:5184], rhs=xnb[:, k, :],
                                                     start=(k == 0), stop=(k == 7)),
                          reads=[t_win, txnb], writes=[t_pdt], pub=(k == 7))
                kb.op(act, lambda: S.activation(out=dte[:], in_=pdt[:, :TT], func=AF.Exp, bias=dtb[:], scale=1.0),
                      reads=[t_pdt, t_par], writes=[t_dte])
                kb.op(act, lambda: S.activation(out=dtv[:], in_=dte[:], func=AF.Ln, bias=1.0, scale=1.0),
                      reads=[t_dte], writes=[t_dtv])
                kb.op(dve, lambda: V.tensor_scalar(out=dta[:], in0=dtv[:], scalar1=acol[:], scalar2=None, op0=ALU.mult),
                      reads=[t_dtv, t_par], writes=[t_dta])
                kb.op(dve, lambda: V.tensor_tensor_scan(out=csb[0:32, :], data0=mk[0:32, :], data1=dta[0:32, :],
                                                        initial=0.0, op0=ALU.mult, op1=ALU.add),
                      reads=[t_dta, t_par], writes=[t_csb])
                kb.op(dve, lambda: V.tensor_tensor_scan(out=csb[32:64, ::-1], data0=mk[32:64, ::-1], data1=dta[32:64, ::-1],
                                                        initial=0.0, op0=ALU.mult, op1=ALU.add),
                      reads=[t_dta, t_par], writes=[t_csb])
                for d in range(2):
                    kb.dma(pool, DCd[d * 64:d * 64 + 32, ts], csb[d * 32:(d + 1) * 32, :], reads=[t_csb], st=t_csb)
                    kb.dma(pool, DCd[d * 64 + 32:d * 64 + 64, ts], dtv[d * 32:(d + 1) * 32, :], reads=[t_dtv], st=t_dtv)
            ph.close()

        def odd_conv_phase(j):
            ph = Phase(f"o2_{j}")
            t_par = Trk("par")
            cw = ph.sb("cw", [128, 24, 4])
            for k in range(4):
                kb.dma(sp, cw[:, :, k], ssd_conv_w[j, k].rearrange("(c p) -> p c", p=128), writes=[t_par], st=t_par)
            cb = ph.sb("cb", [128, 24])
            kb.dma(sp, cb[:], ssd_conv_b[j].rearrange("(c p) -> p c", p=128), writes=[t_par], st=t_par)
            idb = ph.sb("idb", [128, 128], BF16)
            kb.op(dve, lambda: V.tensor_copy(out=idb[:], in_=ident[:]), writes=[t_par])
            xp = [ph.sb(f"xp{i}", [128, SEG + 3]) for i in range(2)]
            t_xp = [Trk("xp") for _ in range(2)]
            xc2 = [ph.sb(f"xc{i}", [128, SEG]) for i in range(2)]; t_xc2 = [Trk("xc") for _ in range(2)]
            xs_ = [ph.sb(f"xs{i}", [128, SEG]) for i in range(2)]; t_xs = [Trk("xs") for _ in range(2)]
            xb = [ph.sb(f"xb{i}", [128, SEG], BF16) for i in range(2)]; t_xb = [Trk("xb") for _ in range(2)]
            tr = [ph.sb(f"tr{i}", [128, 4, 128], BF16) for i in range(2)]; t_tr = [Trk("tr") for _ in range(2)]
            ptr = [ph.ps(f"ptr{i}", [128, 512], BF16) for i in range(2)]; t_ptr = [Trk("ptr") for _ in range(2)]
            tc_ = [0]
            NB = SEG // 128

            def sel_(cnt):
                return (xp[cnt % 2], t_xp[cnt % 2], xs_[cnt % 2], t_xs[cnt % 2], xb[cnt % 2], t_xb[cnt % 2],
                        xc2[cnt % 2], t_xc2[cnt % 2])

            def stA(cnt, cc, s):
                xpb, txp, xsb, txs, xbb, txb, xc, t_xc = sel_(cnt)
                rows = XBCd[cc * 128:(cc + 1) * 128, :]
                load_halo(xpb, txp, rows, s)
                conv4(xc, t_xc, xpb, txp, cw[:, cc, :], cb[:, cc:cc + 1], t_par)

            def stB(cnt, cc, s):
                xpb, txp, xsb, txs, xbb, txb, xc, t_xc = sel_(cnt)
                ss = slice(s * SEG, (s + 1) * SEG)
                if cc < 16:
                    kb.op(act, lambda: S.activation(out=xsb[:], in_=xc[:], func=AF.Silu), reads=[t_xc], writes=[txs])
                    kb.dma(pool, XSfd[cc * 128:(cc + 1) * 128, ss], xsb[:], reads=[txs], st=txs)
                kb.op(act, lambda: S.activation(out=xbb[:], in_=xc[:], func=AF.Silu), reads=[t_xc], writes=[txb])
                if 16 <= cc < 20:
                    kb.dma(pool, BFd[cc - 16, :, ss], xbb[:], reads=[txb], st=txb)
                elif cc >= 20:
                    kb.dma(pool, CFd[cc - 20, :, ss], xbb[:], reads=[txb], st=txb)
                if cc < 20:
                    for b0 in range(0, NB, 4):
                        pt_, tpt = ptr[tc_[0] % 2], t_ptr[tc_[0] % 2]
                        trb, ttr = tr[tc_[0] % 2], t_tr[tc_[0] % 2]
                        tc_[0] += 1
                        nb = min(4, NB - b0)
                        for q in range(nb):
                            kb.op(pe, lambda q=q: PE_.transpose(out=pt_[:, q * 128:(q + 1) * 128],
                                                                in_=xbb[:, (b0 + q) * 128:(b0 + q + 1) * 128],
                                                                identity=idb[:]),
                                  reads=[txb, t_par], writes=[tpt], pub=(q == nb - 1))
                        kb.op(dve, lambda: V.tensor_copy(out=trb[:, 0:nb, :],
                                                         in_=pt_[:, 0:nb * 128].rearrange("p (q c) -> p q c", q=nb)),
                              reads=[tpt], writes=[ttr])
                        t0 = s * SEG + b0 * 128
                        if cc < 16:
                            dst = XStd[t0:t0 + nb * 128, cc * 128:(cc + 1) * 128]
                        else:
                            dst = BTd[t0:t0 + nb * 128, (cc - 16) * 128:(cc - 15) * 128]
                        kb.dma(pool, dst.rearrange("(b p) c -> p b c", p=128), trb[:, 0:nb, :], reads=[ttr], st=ttr)

            units = [(u, u // NSEG, u % NSEG) for u in range(24 * NSEG)]
            for u in range(len(units) + 1):
                if u < len(units):
                    stA(*units[u])
                if u >= 1:
                    stB(*units[u - 1])
            ph.close()

        def ssd_core_phase(j):
            ph = Phase(f"o3_{j}")
            t_c = Trk("c")
            sel = ph.sb("sel", [32, 32, 128])
            kb.op(dve, lambda: V.tensor_copy(out=sel[:], in_=ident[0:32, 0:32].unsqueeze(2).to_broadcast([32, 32, 128])),
                  writes=[t_c])
            tri = ph.sb("tri", [128, 2, 128])
            kb.dma(sp, tri[:], tri_d.rearrange("p (a b) -> p a b", a=2), writes=[t_c], st=t_c)
            Sst = ph.sb("Sst", [128, 32, 64]); t_S = Trk("S")
            Sbf = ph.sb("Sbf", [128, 32, 64], BF16); t_Sb = Trk("Sb")
            xst = [ph.sb(f"xst{i}", [128, 2048], BF16) for i in range(2)]; t_xst = [Trk("xst") for _ in range(2)]
            Bfm = [ph.sb(f"Bfm{i}", [128, 4, 128], BF16) for i in range(2)]; t_Bfm = [Trk("Bfm") for _ in range(2)]
            Cfm = [ph.sb(f"Cfm{i}", [128, 4, 128], BF16) for i in range(2)]; t_Cfm = [Trk("Cfm") for _ in range(2)]
            Btk = [ph.sb(f"Btk{i}", [128, 512], BF16) for i in range(2)]; t_Btk = [Trk("Btk") for _ in range(2)]
            dc = [ph.sb(f"dc{i}", [64, 128]) for i in range(2)]; t_dc = [Trk("dc") for _ in range(2)]
            tok = ph.sb("tok", [128, 64]); t_tok = Trk("tok")
            scm = [ph.sb(f"scm{i}", [128, 128]) for i in range(2)]; t_scm = [Trk("scm") for _ in range(2)]
            Eg = [ph.sb(f"Eg{i}", [128, 8, 128]) for i in range(2)]; t_Eg = [Trk("Eg") for _ in range(2)]
            Xg = [ph.sb(f"Xg{i}", [128, 8, 128]) for i in range(2)]; t_Xg = [Trk("Xg") for _ in range(2)]
            Gg = [ph.sb(f"Gg{i}", [128, 8, 128], BF16) for i in range(2)]; t_Gg = [Trk("Gg") for _ in range(2)]
            CEg = [ph.sb(f"CEg{i}", [128, 8, 128], BF16) for i in range(2)]; t_CEg = [Trk("CEg") for _ in range(2)]
            xdt = [ph.sb(f"xdt{i}", [128, 8, 64], BF16) for i in range(2)]; t_xdt = [Trk("xdt") for _ in range(2)]
            wdt = [ph.sb(f"wdt{i}", [128, 8]) for i in range(2)]; t_wdt = [Trk("wdt") for _ in range(2)]
            wx = [ph.sb(f"wx{i}", [128, 8, 64], BF16) for i in range(2)]; t_wx = [Trk("wx") for _ in range(2)]
            yo = [ph.sb(f"yo{i}", [64, 8, 128]) for i in range(2)]; t_yo = [Trk("yo") for _ in range(2)]
            pTs = ph.ps("pTs", [128, 512]); t_pT = Trk("pT")
            pT = pTs[:, 0:128]
            psc_ = [pTs[:, 128:256], pTs[:, 256:384]]
            t_psc = t_pT
            pcs2 = [ph.ps(f"pcs{i}", [128, 1024]) for i in range(2)]; t_pcs2 = [Trk("pcs") for _ in range(2)]
            lnd = ph.sb("lnd", [128, 32]); csm = ph.sb("csm", [128, 32])
            py = ph.ps("py", [64, 1024]); t_py = Trk("py")
            pst = ph.ps("pst", [128, 512]); t_pst = Trk("pst")
            NCH = NTOK // 128
            CPS = SEG // 128
            tok2 = [tok, ph.sb("tok1", [128, 64])]; t_tok2 = [t_tok, Trk("tok1")]
            lnd2 = [lnd, ph.sb("lnd1", [128, 32])]; csm2 = [csm, ph.sb("csm1", [128, 32])]

            class Tsk:
                pass

            def prologue(t):
                li = t.chunk_idx
                t.xb_, t.txb = xst[li % 2], t_xst[li % 2]
                t.bf_, t.tbf = Bfm[li % 2], t_Bfm[li % 2]
                t.cf_, t.tcf = Cfm[li % 2], t_Cfm[li % 2]
                t.bt_, t.tbt = Btk[li % 2], t_Btk[li % 2]
                t.dc_, t.tdc = dc[li % 2], t_dc[li % 2]
                t.tok, t.ttok = tok2[li % 2], t_tok2[li % 2]
                t.lnd, t.csm = lnd2[li % 2], csm2[li % 2]
                tsl = t.tsl
                kb.dma(sp, t.xb_[:], XStd[tsl, :], writes=[t.txb])
                kb.dma(sp, t.bf_[:], BFd[:, :, tsl].rearrange("g n s -> n g s"), writes=[t.tbf])
                kb.dma(sp, t.cf_[:], CFd[:, :, tsl].rearrange("g n s -> n g s"), writes=[t.tcf])
                kb.dma(sp, t.bt_[:], BTd[tsl, :], writes=[t.tbt])
                kb.dma(sp, t.dc_[:], DCd[t.d * 64:(t.d + 1) * 64, tsl], writes=[t.tdc])
                kb.op(pe, lambda: PE_.transpose(out=pT[:, 0:64], in_=t.dc_[:, :], identity=ident[0:64, 0:64]),
                      reads=[t.tdc, t_id], writes=[t_pT])
                kb.op(act, lambda: S.copy(out=t.tok[:], in_=pT[:, 0:64]), reads=[t_pT], writes=[t.ttok])
                kb.op(act, lambda: S.activation(out=t.lnd[:], in_=t.tok[:, 32:64], func=AF.Ln), reads=[t.ttok], writes=[t.ttok])
                kb.op(dve, lambda: V.tensor_tensor(out=t.csm[:], in0=t.tok[:, 0:32], in1=t.lnd[:], op=ALU.subtract),
                      reads=[t.ttok], writes=[t.ttok])

            def s1(t):
                g, b2 = t.g, t.b2
                pcs, t_pcs = pcs2[b2], t_pcs2[b2]
                for r in range(8):
                    kb.op(pe, lambda r=r: PE_.matmul(pcs[:, r * 128:(r + 1) * 128], lhsT=sel[:, 8 * g + r, :],
                                                     rhs=t.dc_[0:32, :], start=True, stop=True),
                          reads=[t_c, t.tdc], writes=[t_pcs], pub=(r == 7))
                kb.op(pe, lambda: PE_.matmul(psc_[b2], lhsT=t.bf_[:, g, :], rhs=t.cf_[:, g, :], start=True, stop=True),
                      reads=[t.tbf, t.tcf], writes=[t_psc])

            def s2(t):
                g, b2, d = t.g, t.b2, t.d
                hs = slice(8 * g, 8 * g + 8)
                pcs, t_pcs = pcs2[b2], t_pcs2[b2]
                pcs3 = pcs[:, :].rearrange("p (r l) -> p r l", r=8)
                kb.op(dve, lambda: V.tensor_tensor(out=scm[b2][:], in0=psc_[b2], in1=tri[:, d, :], op=ALU.mult),
                      reads=[t_psc, t_c], writes=[t_scm[b2]])
                kb.op(dve, lambda: V.tensor_tensor(out=Eg[b2][:], in0=pcs3,
                                                   in1=t.csm[:, hs].unsqueeze(2).to_broadcast([128, 8, 128]),
                                                   op=ALU.subtract),
                      reads=[t_pcs, t.ttok], writes=[t_Eg[b2]])
                kb.op(act, lambda: S.activation(out=Eg[b2][:], in_=Eg[b2][:], func=AF.Exp),
                      reads=[t_Eg[b2]], writes=[t_Eg[b2]])
                kb.op(act, lambda: S.activation(out=Xg[b2][:], in_=pcs3, func=AF.Exp),
                      reads=[t_pcs], writes=[t_Xg[b2]])
                lend = 127 if d == 0 else 0
                kb.op(dve, lambda: V.scalar_tensor_tensor(out=Gg[b2][:], in0=Eg[b2][:], scalar=1.0e30,
                                                          in1=scm[b2][:].unsqueeze(1).to_broadcast([128, 8, 128]),
                                                          op0=ALU.min, op1=ALU.mult),
                      reads=[t_Eg[b2], t_scm[b2]], writes=[t_Gg[b2]])
                kb.op(pool, lambda: G_.tensor_tensor(out=CEg[b2][:], in0=Xg[b2][:],
                                                     in1=t.cf_[:, g, :].unsqueeze(1).to_broadcast([128, 8, 128]),
                                                     op=ALU.mult),
                      reads=[t_Xg[b2], t.tcf], writes=[t_CEg[b2]])
                xg = t.xb_[:, g * 512:(g + 1) * 512].rearrange("p (r q) -> p r q", r=8)
                kb.op(pool, lambda: G_.tensor_tensor(out=wx[b2][:], in0=xg,
                                                     in1=Eg[b2][:, :, lend].unsqueeze(2).to_broadcast([128, 8, 64]),
                                                     op=ALU.mult),
                      reads=[t.txb, t_Eg[b2]], writes=[t_wx[b2]])

            def s3(t):
                g, b2, d = t.g, t.b2, t.d
                hs = slice(8 * g, 8 * g + 8)
                lend = 127 if d == 0 else 0
                if t.reset is not None and g == 0:
                    if t.reset == "zero":
                        kb.op(dve, lambda: V.memset(Sst[:], 0.0), writes=[t_S])
                    else:
                        fc = t.reset
                        kb.op(dve, lambda: V.tensor_scalar(out=Sst[:], in0=Sst[:], scalar1=fl[:, fc:fc + 1], scalar2=None,
                                                           op0=ALU.mult), reads=[t_S, t_fl], writes=[t_S])
                    kb.op(act, lambda: S.copy(out=Sbf[:], in_=Sst[:]), reads=[t_S], writes=[t_Sb])
                xg = t.xb_[:, g * 512:(g + 1) * 512].rearrange("p (r q) -> p r q", r=8)
                for r in range(8):
                    kb.op(pe, lambda r=r: PE_.matmul(py[:, r * 128:(r + 1) * 128], lhsT=xg[:, r, :],
                                                     rhs=Gg[b2][:, r, :], start=True, stop=False),
                          reads=[t.txb, t_Gg[b2]], writes=[t_py], pub=False)
                    kb.op(pe, lambda r=r: PE_.matmul(py[:, r * 128:(r + 1) * 128], lhsT=Sbf[:, 8 * g + r, :],
                                                     rhs=CEg[b2][:, r, :], start=False, stop=True),
                          reads=[t_Sb, t_CEg[b2]], writes=[t_py], pub=(r == 7))
                kb.op(act, lambda: S.copy(out=yo[b2][:], in_=py[:, :].rearrange("p (r l) -> p r l", r=8)),
                      reads=[t_py], writes=[t_yo[b2]])
                kb.dma(pool, Yd[d, g * 512:(g + 1) * 512, t.tsl].rearrange("(r p) l -> p r l", p=64), yo[b2][:],
                       reads=[t_yo[b2]], st=t_yo[b2])
                kb.op(pe, lambda: PE_.matmul(pst[:, :], lhsT=t.bt_[:, g * 128:(g + 1) * 128],
                                             rhs=wx[b2][:].rearrange("p r q -> p (r q)"), start=True, stop=True),
                      reads=[t.tbt, t_wx[b2]], writes=[t_pst])
                Sg = Sst[:, hs, :]
                kb.op(dve, lambda: V.tensor_tensor(out=Sg, in0=Sg,
                                                   in1=Xg[b2][:, :, lend].unsqueeze(2).to_broadcast([128, 8, 64]),
                                                   op=ALU.mult),
                      reads=[t_S, t_Xg[b2], t_py], writes=[t_S])
                kb.op(dve, lambda: V.tensor_tensor(out=Sg, in0=Sg, in1=pst[:, :].rearrange("p (r q) -> p r q", r=8),
                                                   op=ALU.add),
                      reads=[t_S, t_pst], writes=[t_S])
                kb.op(act, lambda: S.copy(out=Sbf[:, hs, :], in_=Sg), reads=[t_S, t_py], writes=[t_Sb])

            tasks = []
            chunk_idx = 0
            for d in (1, 0):
                order = list(range(NCH))[::-1] if d == 1 else list(range(NCH))
                for ci in order:
                    t0 = ci * 128
                    s = t0 // SEG
                    at_bound = (ci % CPS == 0) if d == 0 else (ci % CPS == CPS - 1)
                    reset = None
                    if at_bound:
                        first = (s == 0) if d == 0 else (s == NSEG - 1)
                        reset = "zero" if first else ((s - 1) if d == 0 else s)
                    proto = None
                    for g in range(4):
                        t = Tsk()
                        t.d, t.ci, t.g, t.tsl, t.reset = d, ci, g, slice(t0, t0 + 128), reset
                        t.chunk_idx = chunk_idx
                        t.b2 = len(tasks) % 2
                        t.proto = proto
                        if g == 0:
                            proto = t
                            t.proto = None
                        tasks.append(t)
                    chunk_idx += 1
            n = len(tasks)
            for i in range(n + 2):
                if i < n:
                    t = tasks[i]
                    if t.proto is None:
                        prologue(t)
                    else:
                        for a in ("xb_", "txb", "bf_", "tbf", "cf_", "tcf", "bt_", "tbt", "dc_", "tdc", "tok", "ttok", "lnd", "csm"):
                            setattr(t, a, getattr(t.proto, a))
                    s1(t)
                if 1 <= i <= n:
                    s2(tasks[i - 1])
                if 2 <= i <= n + 1:
                    s3(tasks[i - 2])
            ph.close()

        def odd_out_phase(j, src, dst):
            ph = Phase(f"o4_{j}")
            TT = 512 if NTOK % 512 == 0 else 256
            wo = ph.sb("wo", [128, 16, 1024], BF16); t_wo = Trk("wo")
            stg = [ph.sb(f"stg{i}", [128, 1024]) for i in range(3)]
            t_stg = [Trk("stg") for _ in range(3)]
            load_weight_rows(ph, wo, t_wo, lambda r: od_w_out[j, r * 128:(r + 1) * 128, :], 16, 1024, stg, t_stg)
            t_par = Trk("par")
            dcol = ph.sb("dcol", [128, 16])
            for hh in range(2):
                kb.dma(sp, dcol[hh * 64:(hh + 1) * 64, :],
                       ssd_d[j].rearrange("(c h) -> h c", h=2)[hh].partition_broadcast(64), writes=[t_par], st=t_par)
            nw = ph.sb("nw", [128, 16])
            kb.dma(sp, nw[:], ssd_norm[j].rearrange("(c p) -> p c", p=128), writes=[t_par], st=t_par)
            yf = [ph.sb(f"yf{i}", [128, 4, TT]) for i in range(2)]; t_yf = [Trk("yf") for _ in range(2)]
            ybk = [ph.sb(f"ybk{i}", [128, 4, TT]) for i in range(2)]; t_ybk = [Trk("ybk") for _ in range(2)]
            xsg = [ph.sb(f"xsg{i}", [128, 4, TT]) for i in range(2)]; t_xsg = [Trk("xsg") for _ in range(2)]
            zsg = [ph.sb(f"zsg{i}", [128, 4, TT]) for i in range(2)]; t_zsg = [Trk("zsg") for _ in range(2)]
            sqb = ph.sb("sqb", [128, 4, TT], BF16); t_sqb = Trk("sqb")
            rs = ph.sb("rs", [128, TT]); t_rs = Trk("rs")
            rs2 = ph.sb("rs2", [128, TT]); t_rs2 = Trk("rs2")
            ynb = ph.sb("ynb", [128, 16, TT], BF16); t_ynb = Trk("ynb")
            xres = [ph.sb(f"xres{i}", [128, TT]) for i in range(4)]; t_xres = [Trk("xres") for _ in range(4)]
            ores = [ph.sb(f"ores{i}", [128, TT]) for i in range(4)]; t_ores = [Trk("ores") for _ in range(4)]
            pn = ph.ps("pn", [128, 512]); t_pn = Trk("pn")
            po = [ph.ps(f"po{i}", [128, 512]) for i in range(2)]; t_po = [Trk("po") for _ in range(2)]
            srcc = chunked(dram_ap[src]); dstc = chunked(dram_ap[dst])
            Yfc = chunked(Yd[0]); Ybc = chunked(Yd[1]); XSc = chunked(XSfd); ZSc = chunked(ZSd)
            ri = 0
            gi = 0
            for i in range(NTOK // TT):
                ts = slice(i * TT, (i + 1) * TT)
                for g in range(4):
                    b2 = gi % 2
                    gi += 1
                    cs4 = slice(4 * g, 4 * g + 4)
                    kb.dma(sp, yf[b2][:], Yfc[:, cs4, ts], writes=[t_yf[b2]])
                    kb.dma(sp, ybk[b2][:], Ybc[:, cs4, ts], writes=[t_ybk[b2]])
                    kb.dma(sp, xsg[b2][:], XSc[:, cs4, ts], writes=[t_xsg[b2]])
                    kb.dma(sp, zsg[b2][:], ZSc[:, cs4, ts], writes=[t_zsg[b2]])
                    kb.op(dve, lambda: V.tensor_tensor(out=yf[b2][:], in0=yf[b2][:], in1=ybk[b2][:], op=ALU.add),
                          reads=[t_yf[b2], t_ybk[b2]], writes=[t_yf[b2]])
                    for c in range(4):
                        ch = 4 * g + c
                        kb.op(dve, lambda c=c, ch=ch: V.scalar_tensor_tensor(out=yf[b2][:, c, :], in0=xsg[b2][:, c, :],
                                                                             scalar=dcol[:, ch:ch + 1], in1=yf[b2][:, c, :],
                                                                             op0=ALU.mult, op1=ALU.add),
                              reads=[t_xsg[b2], t_par, t_yf[b2]], writes=[t_yf[b2]])
                    kb.op(dve, lambda: V.tensor_tensor(out=yf[b2][:], in0=yf[b2][:], in1=zsg[b2][:], op=ALU.mult),
                          reads=[t_yf[b2], t_zsg[b2]], writes=[t_yf[b2]])
                    kb.op(act, lambda: S.activation(out=sqb[:], in_=yf[b2][:], func=AF.Square),
                          reads=[t_yf[b2]], writes=[t_sqb])
                    for c in range(4):
                        kb.op(pe, lambda c=c: PE_.matmul(pn[:, :TT], lhsT=ones_bf[:], rhs=sqb[:, c, :],
                                                         start=(c == 0), stop=(c == 3)),
                              reads=[t_const, t_sqb], writes=[t_pn], pub=(c == 3))
                    kb.op(act, lambda: S.activation(out=rs[:], in_=pn[:, :TT], func=AF.Sqrt, scale=1.0 / 512.0, bias=eps_t[:]),
                          reads=[t_pn, t_const], writes=[t_rs])
                    kb.op(dve, lambda: V.reciprocal(out=rs2[:], in_=rs[:]), reads=[t_rs], writes=[t_rs2])
                    for c in range(4):
                        ch = 4 * g + c
                        kb.op(dve, lambda c=c, ch=ch: V.scalar_tensor_tensor(out=ynb[:, ch, :], in0=yf[b2][:, c, :],
                                                                             scalar=nw[:, ch:ch + 1], in1=rs2[:],
                                                                             op0=ALU.mult, op1=ALU.mult),
                              reads=[t_yf[b2], t_par, t_rs2], writes=[t_ynb])
                for m in range(8):
                    ms = slice(m * 128, (m + 1) * 128)
                    p, tp = po[m % 2], t_po[m % 2]
                    xr, txr = xres[ri % 4], t_xres[ri % 4]
                    orr, tor = ores[ri % 4], t_ores[ri % 4]
                    ri += 1
                    kb.dma(sp, xr[:], srcc[:, m, ts], writes=[txr])
                    for k in range(16):
                        kb.op(pe, lambda k=k: PE_.matmul(p[:, :TT], lhsT=wo[:, k, ms], rhs=ynb[:, k, :],
                                                         start=(k == 0), stop=(k == 15)),
                              reads=[t_wo, t_ynb], writes=[tp], pub=(k == 15))
                    kb.op(dve, lambda: V.tensor_tensor(out=orr[:], in0=p[:, :TT], in1=xr[:], op=ALU.add),
                          reads=[tp, txr], writes=[tor])
                    kb.dma(pool, dstc[:, m, ts], orr[:], reads=[tor])
            ph.close()

        def odd_mixer(j, layer, src, dst):
            odd_in_phase(j, layer, src)
            odd_conv_phase(j)
            ssd_core_phase(j)
            odd_out_phase(j, src, dst)

        cur = "xT"
        nxt = ["XA", "XB"]
        ni = 0
        for layer in layers:
            j = layer // 2
            if inc("ffn1"):
                dst = nxt[ni % 2]; ni += 1
                ffn_phase(layer, 0, cur, dst)
                cur = dst
            if phases is None or any(n in phases for n in ("mixer", "e1", "e2", "e3", "e4", "o1", "o2", "o3", "o4")):
                dst = nxt[ni % 2]; ni += 1
                def sub(n):
                    return phases is None or "mixer" in phases or n in phases
                if layer % 2 == 0:
                    if sub("e1"): even_in_phase(j, layer, cur)
                    if sub("e2"): lru_phase(j)
                    if sub("e3"): s5_phase(j)
                    if sub("e4"): even_out_phase(j, cur, dst)
                else:
                    if sub("o1"): odd_in_phase(j, layer, cur)
                    if sub("o2"): odd_conv_phase(j)
                    if sub("o3"): ssd_core_phase(j)
                    if sub("o4"): odd_out_phase(j, cur, dst)
                cur = dst
            if inc("ffn2"):
                dst = nxt[ni % 2]; ni += 1
                ffn_phase(layer, 1, cur, dst)
                cur = dst
        final_phase(cur)
        kb.finish()
    return nc


N_CORES = 8
NON_WEIGHT = ("x_prompt", "x_sample")


def core_inputs(inputs):
    m = {}
    for k, v in inputs.items():
        if k in NON_WEIGHT:
            continue
        m[k] = np.ascontiguousarray(np.asarray(v, dtype=np.float32))
    m["ident"] = np.eye(128, dtype=np.float32)
    u = np.triu(np.ones((128, 128), np.float32))
    m["tri"] = np.ascontiguousarray(np.concatenate([u, u.T], axis=1))
    return m


def kernel(**inputs):
    xp = np.asarray(inputs["x_prompt"], dtype=np.float32)
    xs = np.asarray(inputs["x_sample"], dtype=np.float32)
    SEG = 2048
    NTOK = 12288
    streams = []
    flags = []
    for c in range(N_CORES):
        if c < 4:
            parts = [xp[c], xs[2 * c], xs[2 * c + 1]]
            fl = [1, 1, 1, 0, 0]
        else:
            b = 8 + 6 * (c - 4)
            parts = [xs[b + q] for q in range(6)]
            fl = [0, 0, 0, 0, 0]
        tok = np.concatenate(parts, axis=0)
        streams.append(np.ascontiguousarray(tok.T))
        f = np.zeros((128, 8), np.float32)
        f[:, :5] = np.asarray(fl, np.float32)[None, :]
        flags.append(f)
    nc = build_program(NTOK, SEG, layers=[0, 1, 2, 3])
    wmap = core_inputs(inputs)
    in_maps = []
    for c in range(N_CORES):
        m = dict(wmap)
        m["xT"] = streams[c]
        m["flags"] = flags[c]
        in_maps.append(m)
    res = run_bass_kernel_spmd(nc, in_maps, core_ids=list(range(N_CORES)))
    outs = [np.ascontiguousarray(res.results[c]["yT"].T) for c in range(N_CORES)]
    y_prompt = np.stack([outs[c][:8192] for c in range(4)], axis=0)
    ys = [None] * 32
    for c in range(4):
        ys[2 * c] = outs[c][8192:8192 + 2048]
        ys[2 * c + 1] = outs[c][8192 + 2048:]
    for c in range(4, 8):
        b = 8 + 6 * (c - 4)
        for q in range(6):
            ys[b + q] = outs[c][q * 2048:(q + 1) * 2048]
    y_sample = np.stack(ys, axis=0)
    return (y_prompt.astype(np.float32), y_sample.astype(np.float32))
```
